# Optimizing a Trainium2 kernel written in Bass

```python
import math
import jax, jax.numpy as jnp
from jax import lax
import numpy as np

D_MODEL = 1024
BATCH = 1
SEQ = 16384
DEPTH = 2

CHUNK = 64
Q_BLOCK = 128
N_MIXERS = 2
ATTN_HEADS = 8
ATTN_HEAD_DIM = 64
ATTN_WIDTH = ATTN_HEADS * 2 * ATTN_HEAD_DIM
ROPE_DIM = ATTN_HEAD_DIM // 4
ROPE_THETA = 500000.0
CONV_WIDTH = 3
CONV_CH = D_MODEL
D_FF = -(-8 * D_MODEL // (3 * 256)) * 256
ALPHA = (2 * DEPTH) ** 0.25
BETA = (8 * DEPTH) ** -0.25
N_ATTN_LAYERS = (DEPTH + 1) // 2
N_CONV_LAYERS = DEPTH // 2
LN_EPS = 1e-5
NEG_INF = -1e30

kernel_name = "hybrid_diffattn_shortconv_deepnorm"


def layer_norm(x, g, b):
    xf = x.astype(jnp.float32)
    mu = jnp.mean(xf, axis=-1, keepdims=True)
    xc = xf - mu
    var = jnp.mean(xc * xc, axis=-1, keepdims=True)
    y = xc * lax.rsqrt(var + LN_EPS) * g.astype(jnp.float32) + b.astype(jnp.float32)
    return y.astype(x.dtype)


def rms_norm(x, g):
    xf = x.astype(jnp.float32)
    y = xf * lax.rsqrt(jnp.mean(xf * xf, axis=-1, keepdims=True) + LN_EPS) * g.astype(jnp.float32)
    return y.astype(x.dtype)


def rope_tables(seq):
    pos = jnp.arange(seq, dtype=jnp.float32)
    inv = ROPE_THETA ** (-jnp.arange(0, ROPE_DIM, 2, dtype=jnp.float32) / ROPE_DIM)
    ang = pos[:, None] * inv[None, :]
    return jnp.cos(ang), jnp.sin(ang)


def partial_rotary(x, cos, sin):
    half = ROPE_DIM // 2
    c = cos[None, :, None, None, :].astype(x.dtype)
    s = sin[None, :, None, None, :].astype(x.dtype)
    x1 = x[..., :half]
    x2 = x[..., half:ROPE_DIM]
    rot = jnp.concatenate([x1 * c - x2 * s, x2 * c + x1 * s], axis=-1)
    return jnp.concatenate([rot, x[..., ROPE_DIM:]], axis=-1)


def lambda_init_fn(layer_idx):
    return 0.8 - 0.6 * math.exp(-0.3 * layer_idx)


def diff_attention(x, w_qkv, w_o, lq1, lk1, lq2, lk2, g_sub, lam_init):
    B, S, _ = x.shape
    H, Dh = ATTN_HEADS, ATTN_HEAD_DIM
    qkv = jnp.einsum('bsd,de->bse', x, w_qkv)
    q = qkv[..., :ATTN_WIDTH].reshape(B, S, H, 2, Dh)
    k = qkv[..., ATTN_WIDTH:2 * ATTN_WIDTH].reshape(B, S, H, 2, Dh)
    v = qkv[..., 2 * ATTN_WIDTH:].reshape(B, S, H, 2 * Dh)
    cos, sin = rope_tables(S)
    q = partial_rotary(q, cos, sin) * (Dh ** -0.5)
    k = partial_rotary(k, cos, sin)
    f32 = jnp.float32
    lam = (jnp.exp(jnp.sum(lq1.astype(f32) * lk1.astype(f32)))
           - jnp.exp(jnp.sum(lq2.astype(f32) * lk2.astype(f32))) + lam_init)
    nb = S // Q_BLOCK
    q_blocks = jnp.moveaxis(q.reshape(B, nb, Q_BLOCK, H, 2, Dh), 1, 0)
    key_chunk = jnp.arange(S) // CHUNK

    def one_block(args):
        q_blk, bi = args
        s = jnp.einsum('bqhcd,bkhcd->bhcqk', q_blk, k).astype(f32)
        q_chunk = (bi * Q_BLOCK + jnp.arange(Q_BLOCK)) // CHUNK
        mask = key_chunk[None, :] <= q_chunk[:, None]
        s = jnp.where(mask, s, NEG_INF)
        p = jax.nn.softmax(s, axis=-1)
        a = p[:, :, 0] - lam * p[:, :, 1]
        return jnp.einsum('bhqk,bkhe->bqhe', a.astype(v.dtype), v)

    o = lax.map(one_block, (q_blocks, jnp.arange(nb)))
    o = jnp.moveaxis(o, 0, 1).reshape(B, S, H, 2 * Dh)
    o = rms_norm(o, g_sub) * (1.0 - lam_init)
    return jnp.einsum('bse,ed->bsd', o.reshape(B, S, ATTN_WIDTH), w_o)


def short_gated_conv(x, w_in, w_conv, w_out):
    bcx = jnp.einsum('bsd,de->bse', x, w_in)
    b_gate = bcx[..., :CONV_CH]
    c_gate = bcx[..., CONV_CH:2 * CONV_CH]
    xt = bcx[..., 2 * CONV_CH:]
    u = c_gate * xt
    y = lax.conv_general_dilated(
        u, w_conv[:, None, :], window_strides=(1,),
        padding=[(CONV_WIDTH - 1, 0)],
        dimension_numbers=('NWC', 'WIO', 'NWC'),
        feature_group_count=CONV_CH)
    return jnp.einsum('bsc,cd->bsd', b_gate * y, w_out)


def swiglu(x, w_gate, w_up, w_down):
    h = jax.nn.silu(jnp.einsum('bsd,df->bsf', x, w_gate)) * jnp.einsum('bsd,df->bsf', x, w_up)
    return jnp.einsum('bsf,fd->bsd', h, w_down)


def setup_inputs(seed: int = 0) -> dict:
    key = jax.random.key(seed)
    ks = jax.random.split(key, 18)
    f32 = jnp.float32

    def nrm(k, shape, scale):
        return jax.random.normal(k, shape, f32) * scale

    NA, NC, Dh = N_ATTN_LAYERS, N_CONV_LAYERS, ATTN_HEAD_DIM
    return {
        'x': nrm(ks[0], (BATCH, SEQ, D_MODEL), 1.0),
        'attn_w_qkv': nrm(ks[1], (NA, D_MODEL, 3 * ATTN_WIDTH), D_MODEL ** -0.5),
        'attn_w_o': nrm(ks[2], (NA, ATTN_WIDTH, D_MODEL), ATTN_WIDTH ** -0.5 * BETA),
        'attn_lambda_q1': nrm(ks[3], (NA, Dh), 0.1),
        'attn_lambda_k1': nrm(ks[4], (NA, Dh), 0.1),
        'attn_lambda_q2': nrm(ks[5], (NA, Dh), 0.1),
        'attn_lambda_k2': nrm(ks[6], (NA, Dh), 0.1),
        'attn_subln_g': 1.0 + nrm(ks[7], (NA, 2 * Dh), 0.02),
        'conv_w_in': nrm(ks[8], (NC, D_MODEL, 3 * CONV_CH), D_MODEL ** -0.5),
        'conv_w': nrm(ks[9], (NC, CONV_WIDTH, CONV_CH), CONV_WIDTH ** -0.5),
        'conv_w_out': nrm(ks[10], (NC, CONV_CH, D_MODEL), CONV_CH ** -0.5 * BETA),
        'ffn_w_gate': nrm(ks[11], (DEPTH, D_MODEL, D_FF), D_MODEL ** -0.5),
        'ffn_w_up': nrm(ks[12], (DEPTH, D_MODEL, D_FF), D_MODEL ** -0.5),
        'ffn_w_down': nrm(ks[13], (DEPTH, D_FF, D_MODEL), D_FF ** -0.5 * BETA),
        'ln1_g': 1.0 + nrm(ks[14], (DEPTH, D_MODEL), 0.02),
        'ln1_b': nrm(ks[15], (DEPTH, D_MODEL), 0.02),
        'ln2_g': 1.0 + nrm(ks[16], (DEPTH, D_MODEL), 0.02),
        'ln2_b': nrm(ks[17], (DEPTH, D_MODEL), 0.02),
    }


def reference(x, attn_w_qkv, attn_w_o, attn_lambda_q1, attn_lambda_k1, attn_lambda_q2,
              attn_lambda_k2, attn_subln_g, conv_w_in, conv_w, conv_w_out,
              ffn_w_gate, ffn_w_up, ffn_w_down, ln1_g, ln1_b, ln2_g, ln2_b):
    h = x
    for i in range(DEPTH):
        j = i // N_MIXERS
        if i % N_MIXERS == 0:
            m = diff_attention(h, attn_w_qkv[j], attn_w_o[j], attn_lambda_q1[j],
                               attn_lambda_k1[j], attn_lambda_q2[j], attn_lambda_k2[j],
                               attn_subln_g[j], lambda_init_fn(i))
        else:
            m = short_gated_conv(h, conv_w_in[j], conv_w[j], conv_w_out[j])
        h = layer_norm(ALPHA * h + m, ln1_g[i], ln1_b[i])
        h = layer_norm(ALPHA * h + swiglu(h, ffn_w_gate[i], ffn_w_up[i], ffn_w_down[i]),
                       ln2_g[i], ln2_b[i])
    return h
```

```python
import math
from contextlib import ExitStack

import numpy as np
import concourse.bass as bass
import concourse.mybir as mybir
from concourse.bass_utils import run_bass_kernel_spmd

F32 = mybir.dt.float32
BF16 = mybir.dt.bfloat16
AF = mybir.ActivationFunctionType
ALU = mybir.AluOpType
AX = mybir.AxisListType

D = 1024
S = 16384
NCORES = 8
DFF = 2816
ALPHA = (2 * 2) ** 0.25
LN_EPS = 1e-5
ROPE_THETA = 500000.0
LAM_INIT0 = 0.8 - 0.6 * math.exp(-0.3 * 0)


class _Op:
    __slots__ = ("eng", "fn", "deps", "dma", "sig", "sem", "val", "prev")


class Sched:
    ENGS = ("sp", "pe", "act", "dve", "pool")
    NDS = 12

    def __init__(self, nc, stack):
        self.nc = nc
        self.stack = stack
        self.ops = []
        self.last_w = {}
        self.readers = {}

    def add(self, eng, fn, reads=(), writes=(), dma=False):
        op = _Op()
        op.eng, op.fn, op.dma, op.sig = eng, fn, dma, False
        op.sem = None
        op.val = 0
        op.prev = 0
        deps = []
        seen = set()

        def dep(o):
            if o is not None and id(o) not in seen:
                seen.add(id(o))
                deps.append(o)

        for r in reads:
            dep(self.last_w.get(r))
            if isinstance(r, tuple) and r[0] == "B":
                for rd in self.readers.get(r, ()):
                    if rd.eng != eng:
                        dep(rd)
        for w in writes:
            dep(self.last_w.get(w))
            for rd in self.readers.get(w, ()):
                dep(rd)
        op.deps = deps
        for r in reads:
            self.readers.setdefault(r, []).append(op)
        for w in writes:
            self.last_w[w] = op
            self.readers[w] = []
        self.ops.append(op)
        return op

    def emit(self):
        nc = self.nc
        st = self.stack
        esem = {e: st.enter_context(nc.semaphore("e_" + e)) for e in ("pe", "act", "dve", "pool")}
        dsem = {e: [st.enter_context(nc.semaphore("d_%s%d" % (e, i))) for i in range(self.NDS)]
                for e in ("sp", "act", "pool")}
        for op in self.ops:
            for d in op.deps:
                if d.dma:
                    continue
                if d.eng == "pe" and op.eng == "pe" and not op.dma:
                    continue
                d.sig = True
        cnt = {e: 0 for e in self.ENGS}
        dcnt = {e: 0 for e in self.ENGS}
        for op in self.ops:
            if op.dma:
                n = dcnt[op.eng]
                dcnt[op.eng] += 1
                op.sem = dsem[op.eng][n % self.NDS]
                op.val = 16 * (n // self.NDS + 1)
                op.prev = 16 * (n // self.NDS)
            elif op.sig:
                cnt[op.eng] += 1
                op.sem = esem[op.eng]
                op.val = cnt[op.eng]
        finals = []
        for e in dsem:
            n = dcnt[e]
            for i in range(min(n, self.NDS)):
                tot = (n - 1 - i) // self.NDS + 1
                finals.append((dsem[e][i], 16 * tot))

        block = st.enter_context(nc.Block())
        decos = {"sp": block.sync, "pe": block.tensor, "act": block.scalar,
                 "dve": block.vector, "pool": block.gpsimd}
        for ename in self.ENGS:
            ops_e = [op for op in self.ops if op.eng == ename]

            def body(eng, ops_e=ops_e, ename=ename):
                waited = {}

                def wait(sem, val):
                    if val > 0 and waited.get(id(sem), 0) < val:
                        eng.wait_ge(sem, val)
                        waited[id(sem)] = val

                for op in ops_e:
                    need = {}
                    for d in op.deps:
                        if (not d.dma) and d.eng == "pe" and ename == "pe" and not op.dma:
                            continue
                        k = id(d.sem)
                        if k not in need or need[k][1] < d.val:
                            need[k] = (d.sem, d.val)
                    if op.dma:
                        wait(op.sem, op.prev)
                    for sem, val in need.values():
                        wait(sem, val)
                    ins = op.fn(eng)
                    if op.dma:
                        ins.then_inc(op.sem, 16)
                    elif op.sig:
                        ins.then_inc(op.sem, 1)
                if ename == "sp":
                    for sem, val in finals:
                        wait(sem, val)

            decos[ename](body)


def _sb(nc, st, name, shape, dt):
    return st.enter_context(nc.sbuf_tensor(name, list(shape), dt))


def _ps(nc, st, name):
    return st.enter_context(nc.psum_tensor(name, [128, 512], F32))


def build_A(nt=S // 512, stage=99):
    TT = 512
    SS = nt * TT
    nc = bass.Bass("TRN2", target_bir_lowering=False)
    xT = nc.dram_tensor("xT", [D, SS], F32, kind="ExternalInput").ap()
    wqkv = nc.dram_tensor("wqkv", [D, 384], F32, kind="ExternalInput").ap()
    ropeC = nc.dram_tensor("ropeC", [128, SS], F32, kind="ExternalInput").ap()
    ropeS = nc.dram_tensor("ropeS", [128, SS], F32, kind="ExternalInput").ap()
    perm = nc.dram_tensor("perm", [128, 128], F32, kind="ExternalInput").ap()
    lam4 = nc.dram_tensor("lam4", [1, 256], F32, kind="ExternalInput").ap()
    subg = nc.dram_tensor("subg", [128, 1], F32, kind="ExternalInput").ap()
    out = nc.dram_tensor("attnT", [128, SS], F32, kind="ExternalOutput").ap()
    xTr = xT.rearrange("(k p) t -> p k t", p=128)
    wr = wqkv.rearrange("(k p) n -> p k n", p=128)

    with ExitStack() as st:
        sc = Sched(nc, st)
        qT = _sb(nc, st, "qT", [128, SS], BF16)
        kT = _sb(nc, st, "kT", [128, SS], BF16)
        vS = _sb(nc, st, "vS", [128, nt * 4, 128], BF16)
        xb = [_sb(nc, st, "xb%d" % i, [128, 8, TT], BF16) for i in range(2)]
        wsb = _sb(nc, st, "wsb", [128, 8, 384], BF16)
        pm = _sb(nc, st, "pm", [128, 128], BF16)
        ones = _sb(nc, st, "ones", [128, 128], BF16)
        rc = [_sb(nc, st, "rc%d" % i, [128, TT], F32) for i in range(2)]
        rs = [_sb(nc, st, "rs%d" % i, [128, TT], F32) for i in range(2)]
        qb = _sb(nc, st, "qb", [128, TT], BF16)
        kb_ = _sb(nc, st, "kb_", [128, TT], BF16)
        t1 = [_sb(nc, st, "t1%d" % i, [128, TT], F32) for i in range(2)]
        t2 = [_sb(nc, st, "t2%d" % i, [128, TT], F32) for i in range(2)]
        pT = [[_sb(nc, st, "pT%d%d" % (c, s), [128, TT], BF16) for s in range(2)] for c in range(2)]
        rz = [_sb(nc, st, "rz%d" % i, [128, TT], F32) for i in range(2)]
        oo = [_sb(nc, st, "oo%d" % i, [128, TT], F32) for i in range(2)]
        od = _sb(nc, st, "od", [128, TT], F32)
        osq = _sb(nc, st, "osq", [128, TT], BF16)
        lnv = _sb(nc, st, "lnv", [128, TT], F32)
        rstd = _sb(nc, st, "rstd", [128, TT], F32)
        res = [_sb(nc, st, "res%d" % i, [128, TT], F32) for i in range(2)]
        lamt = _sb(nc, st, "lamt", [128, 256], F32)
        lprod = _sb(nc, st, "lprod", [128, 128], F32)
        lsum = _sb(nc, st, "lsum", [128, 2], F32)
        lexp = _sb(nc, st, "lexp", [128, 2], F32)
        neglam = _sb(nc, st, "neglam", [128, 1], F32)
        gsc = _sb(nc, st, "gsc", [128, 1], F32)
        gld = _sb(nc, st, "gld", [128, 1], F32)
        B = [_ps(nc, st, "B%d" % i) for i in range(8)]

        sc.add("pool", lambda e: e.dma_start(out=wsb[:], in_=wr), writes=["wsb"], dma=True)
        sc.add("pool", lambda e: e.dma_start(out=pm[:], in_=perm), writes=["pm"], dma=True)
        sc.add("sp", lambda e: e.dma_start(out=lamt[:], in_=lam4.partition_broadcast(128)),
               writes=["lamt"], dma=True)
        sc.add("sp", lambda e: e.dma_start(out=gld[:], in_=subg), writes=["gld"], dma=True)
        sc.add("pool", lambda e: e.memset(ones[:], 1.0), writes=["ones"])
        sc.add("dve", lambda e: e.tensor_tensor(out=lprod[:, 0:64], in0=lamt[:, 0:64], in1=lamt[:, 64:128],
                                                op=ALU.mult), reads=["lamt"], writes=["lprod"])
        sc.add("dve", lambda e: e.tensor_tensor(out=lprod[:, 64:128], in0=lamt[:, 128:192], in1=lamt[:, 192:256],
                                                op=ALU.mult), reads=["lamt"], writes=["lprod"])
        sc.add("dve", lambda e: e.reduce_sum(out=lsum[:, 0:1], in_=lprod[:, 0:64], axis=AX.X),
               reads=["lprod"], writes=["lsum"])
        sc.add("dve", lambda e: e.reduce_sum(out=lsum[:, 1:2], in_=lprod[:, 64:128], axis=AX.X),
               reads=["lprod"], writes=["lsum"])
        sc.add("act", lambda e: e.activation(out=lexp[:], in_=lsum[:], func=AF.Exp),
               reads=["lsum"], writes=["lexp"])
        sc.add("dve", lambda e: e.tensor_tensor(out=neglam[:], in0=lexp[:, 1:2], in1=lexp[:, 0:1],
                                                op=ALU.subtract), reads=["lexp"], writes=["neglam"])
        sc.add("dve", lambda e: e.tensor_scalar(out=neglam[:], in0=neglam[:], scalar1=-LAM_INIT0, scalar2=None,
                                                op0=ALU.add), reads=["neglam"], writes=["neglam"])
        sc.add("dve", lambda e: e.tensor_scalar(out=gsc[:], in0=gld[:], scalar1=(1.0 - LAM_INIT0), scalar2=None,
                                                op0=ALU.mult), reads=["gld"], writes=["gsc"])

        def load_x(tt):
            s = tt % 2
            c0 = tt * TT
            sc.add("pool", lambda e: e.dma_start(out=xb[s][:], in_=xTr[:, :, c0:c0 + TT]),
                   writes=[("xb", s)], dma=True)
            sc.add("sp", lambda e: e.dma_start(out=rc[s][:], in_=ropeC[:, c0:c0 + TT]),
                   writes=[("rc", s)], dma=True)
            sc.add("sp", lambda e: e.dma_start(out=rs[s][:], in_=ropeS[:, c0:c0 + TT]),
                   writes=[("rs", s)], dma=True)

        def mm(outp, lhsT, rhs, start, stop, reads, writes):
            sc.add("pe", lambda e: e.matmul(outp, lhsT, rhs, start=start, stop=stop),
                   reads=reads, writes=writes)

        def qkv(tt):
            s = tt % 2
            c0 = tt * TT
            xs = xb[s]
            for k in range(8):
                mm(B[0][:, :], wsb[:, k, 0:128], xs[:, k, :], k == 0, k == 7, ["wsb", ("xb", s)], [("B", 0)])
            for k in range(8):
                mm(B[1][:, :], wsb[:, k, 128:256], xs[:, k, :], k == 0, k == 7, ["wsb", ("xb", s)], [("B", 1)])
            for sub in range(4):
                for k in range(8):
                    mm(B[2][:, sub * 128:(sub + 1) * 128], xs[:, k, sub * 128:(sub + 1) * 128],
                       wsb[:, k, 256:384], k == 0, k == 7, ["wsb", ("xb", s)], [("B", 2)])
            sc.add("act", lambda e: e.copy(out=qb[:], in_=B[0][:, :]), reads=[("B", 0)], writes=["qb"])
            sc.add("act", lambda e: e.copy(out=kb_[:], in_=B[1][:, :]), reads=[("B", 1)], writes=["kb_"])
            sc.add("act", lambda e: e.copy(out=vS[:, tt * 4:(tt + 1) * 4, :],
                                           in_=B[2][:, :].rearrange("p (a b) -> p a b", a=4)),
                   reads=[("B", 2)], writes=[("vS", tt)])
            mm(B[3][:, :], pm[:, :], qb[:, :], True, True, ["pm", "qb"], [("B", 3)])
            sc.add("dve", lambda e: e.tensor_tensor(out=t1[0][:], in0=B[0][:, :], in1=rc[s][:], op=ALU.mult),
                   reads=[("B", 0), ("rc", s)], writes=["t1q"])
            sc.add("dve", lambda e: e.tensor_tensor(out=t2[0][:], in0=B[3][:, :], in1=rs[s][:], op=ALU.mult),
                   reads=[("B", 3), ("rs", s)], writes=["t2q"])
            sc.add("pool", lambda e: e.tensor_tensor(out=qT[:, c0:c0 + TT], in0=t1[0][:], in1=t2[0][:], op=ALU.add),
                   reads=["t1q", "t2q"], writes=[("qT", tt)])
            mm(B[2][:, :], pm[:, :], kb_[:, :], True, True, ["pm", "kb_"], [("B", 2)])
            sc.add("dve", lambda e: e.tensor_tensor(out=t1[1][:], in0=B[1][:, :], in1=rc[s][:], op=ALU.mult),
                   reads=[("B", 1), ("rc", s)], writes=["t1k"])
            sc.add("dve", lambda e: e.tensor_tensor(out=t2[1][:], in0=B[2][:, :], in1=rs[s][:], op=ALU.mult),
                   reads=[("B", 2), ("rs", s)], writes=["t2k"])
            sc.add("pool", lambda e: e.tensor_tensor(out=kT[:, c0:c0 + TT], in0=t1[1][:], in1=t2[1][:], op=ALU.add),
                   reads=["t1k", "t2k"], writes=[("kT", tt)])

        def scores(qt, kb, slot):
            j = kb - 4 * qt
            f0 = 128 * j if j > 0 else 0
            q0 = qt * TT
            for c in range(2):
                bi = c * 2 + slot
                mm(B[bi][:, f0:TT], kT[c * 64:(c + 1) * 64, kb * 128:(kb + 1) * 128],
                   qT[c * 64:(c + 1) * 64, q0 + f0:q0 + TT], True, True,
                   [("kT", kb // 4), ("qT", qt)], [("B", bi)])

        def expo(qt, kb, slot):
            j = kb - 4 * qt
            f0 = 128 * j if j > 0 else 0
            for c in range(2):
                bi = c * 2 + slot
                sc.add("act", lambda e, c=c, bi=bi: e.activation(out=pT[c][slot][:, f0:TT], in_=B[bi][:, f0:TT],
                                                                 func=AF.Exp, scale=0.125),
                       reads=[("B", bi)], writes=[("pT", c, slot)])
                if j >= 0:
                    sc.add("pool", lambda e, c=c: e.memset(pT[c][slot][64:128, f0:f0 + 64], 0.0),
                           writes=[("pT", c, slot)])

        def pv(qt, kb, slot, last):
            j = kb - 4 * qt
            f0 = 128 * j if j > 0 else 0
            for c in range(2):
                mm(B[4 + c][:, f0:TT], vS[:, kb, :], pT[c][slot][:, f0:TT], kb == 0, kb == last,
                   [("vS", kb // 4), ("pT", c, slot)], [("B", 4 + c)])
                mm(B[6 + c][:, f0:TT], ones[:, :], pT[c][slot][:, f0:TT], kb == 0, kb == last,
                   ["ones", ("pT", c, slot)], [("B", 6 + c)])

        def epilogue(qt):
            q0 = qt * TT
            r = res[qt % 2]
            for c in range(2):
                sc.add("dve", lambda e, c=c: e.reciprocal(out=rz[c][:], in_=B[6 + c][:, :]),
                       reads=[("B", 6 + c)], writes=[("rz", c)])
                sc.add("dve", lambda e, c=c: e.tensor_tensor(out=oo[c][:], in0=B[4 + c][:, :], in1=rz[c][:],
                                                             op=ALU.mult),
                       reads=[("B", 4 + c), ("rz", c)], writes=[("oo", c)])
            sc.add("dve", lambda e: e.scalar_tensor_tensor(out=od[:], in0=oo[1][:], scalar=neglam[:, 0:1],
                                                           in1=oo[0][:], op0=ALU.mult, op1=ALU.add),
                   reads=[("oo", 0), ("oo", 1), "neglam"], writes=["od"])
            sc.add("pool", lambda e: e.tensor_tensor(out=osq[:], in0=od[:], in1=od[:], op=ALU.mult),
                   reads=["od"], writes=["osq"])
            mm(B[0][:, :], ones[:, :], osq[:, :], True, True, ["ones", "osq"], [("B", 0)])
            sc.add("dve", lambda e: e.tensor_scalar(out=lnv[:], in0=B[0][:, :], scalar1=1.0 / 128.0,
                                                    scalar2=LN_EPS, op0=ALU.mult, op1=ALU.add),
                   reads=[("B", 0)], writes=["lnv"])
            sc.add("act", lambda e: e.activation(out=lnv[:], in_=lnv[:], func=AF.Ln),
                   reads=["lnv"], writes=["lnv"])
            sc.add("act", lambda e: e.activation(out=rstd[:], in_=lnv[:], func=AF.Exp, scale=-0.5),
                   reads=["lnv"], writes=["rstd"])
            sc.add("dve", lambda e: e.scalar_tensor_tensor(out=r[:], in0=od[:], scalar=gsc[:, 0:1], in1=rstd[:],
                                                           op0=ALU.mult, op1=ALU.mult),
                   reads=["od", "rstd", "gsc"], writes=[("res", qt % 2)])
            sc.add("sp", lambda e: e.dma_start(out=out[:, q0:q0 + TT], in_=r[:]),
                   reads=[("res", qt % 2)], dma=True)

        load_x(0)
        if stage == 1:
            sc.add("sp", lambda e: e.dma_start(out=out[:, 0:TT], in_=rc[0][:]), reads=[("rc", 0), "neglam", "gsc", "wsb", "pm", "ones", ("xb", 0)], dma=True)
            sc.emit()
            return nc
        for tt in range(nt):
            if tt + 1 < nt:
                load_x(tt + 1)
            qkv(tt)
            if stage == 2:
                sc.add("act", lambda e: e.copy(out=res[0][:], in_=qT[:, 0:TT]), reads=[("qT", 0), ("kT", 0), ("vS", 0)], writes=[("res", 0)])
                sc.add("sp", lambda e: e.dma_start(out=out[:, 0:TT], in_=res[0][:]), reads=[("res", 0)], dma=True)
                sc.emit()
                return nc
            nkb = 4 * tt + 4
            last = nkb - 1
            scores(tt, 0, 0)
            for kb in range(nkb):
                if kb + 1 < nkb:
                    scores(tt, kb + 1, (kb + 1) % 2)
                expo(tt, kb, kb % 2)
                pv(tt, kb, kb % 2, last)
            epilogue(tt)
        sc.emit()
    return nc


def rope_tables(n):
    half = 8
    pos = np.arange(n, dtype=np.float32)
    inv = (ROPE_THETA ** (-np.arange(0, 16, 2, dtype=np.float32) / np.float32(16))).astype(np.float32)
    ang = (pos[:, None] * inv[None, :]).astype(np.float32)
    cos = np.cos(ang).astype(np.float32).T
    sin = np.sin(ang).astype(np.float32).T
    C = np.ones((128, n), np.float32)
    Sg = np.zeros((128, n), np.float32)
    for c in range(2):
        b = c * 64
        C[b:b + 8] = cos
        C[b + 8:b + 16] = cos
        Sg[b:b + 8] = -sin
        Sg[b + 8:b + 16] = sin
    P = np.zeros((128, 128), np.float32)
    for c in range(2):
        b = c * 64
        for i in range(half):
            P[b + i + 8, b + i] = 1.0
            P[b + i, b + i + 8] = 1.0
    return C, Sg, P


def run_A(x, attn_w_qkv, lq1, lk1, lq2, lk2, subln_g, nt=S // 512):
    SS = nt * 512
    xT = np.ascontiguousarray(x.reshape(S, D)[:SS].T)
    C, Sg, P = rope_tables(SS)
    lam4 = np.concatenate([lq1.reshape(-1), lk1.reshape(-1), lq2.reshape(-1), lk2.reshape(-1)]).reshape(1, 256)
    lam4 = np.ascontiguousarray(lam4.astype(np.float32))
    subg = np.ascontiguousarray(subln_g.reshape(128, 1).astype(np.float32))
    w = attn_w_qkv.reshape(D, 3 * D)
    in_maps = []
    for h in range(NCORES):
        wh = np.concatenate([w[:, h * 128:(h + 1) * 128], w[:, D + h * 128:D + (h + 1) * 128],
                             w[:, 2 * D + h * 128:2 * D + (h + 1) * 128]], axis=1)
        in_maps.append({"xT": xT, "wqkv": np.ascontiguousarray(wh), "ropeC": C, "ropeS": Sg, "perm": P,
                        "lam4": lam4, "subg": subg})
    nc = build_A(nt)
    r = run_bass_kernel_spmd(nc, in_maps, core_ids=list(range(NCORES)))
    return np.concatenate([r.results[h]["attnT"] for h in range(NCORES)], axis=0)


TOK = S // NCORES
TG = 1024
TT = 512
WCAP = 6144


def build_BC(mode, tok=TOK):
    ngroups = tok // TG
    ntile = TG // TT
    nc = bass.Bass("TRN2", target_bir_lowering=False)
    hin = nc.dram_tensor("hin", [D, tok], F32, kind="ExternalInput").ap()
    if mode == "B":
        ain = nc.dram_tensor("ain", [D, tok], F32, kind="ExternalInput").ap()
        wmix = nc.dram_tensor("wmix", [D, D], F32, kind="ExternalInput").ap()
    else:
        halo = nc.dram_tensor("halo", [D, 2], F32, kind="ExternalInput").ap()
        w_in = nc.dram_tensor("w_in", [D, 3 * D], F32, kind="ExternalInput").ap()
        cw = nc.dram_tensor("cw", [128, 24], F32, kind="ExternalInput").ap()
        wmix = nc.dram_tensor("wmix", [D, D], F32, kind="ExternalInput").ap()
    wg = nc.dram_tensor("wg", [D, DFF], F32, kind="ExternalInput").ap()
    wu = nc.dram_tensor("wu", [D, DFF], F32, kind="ExternalInput").ap()
    wd = nc.dram_tensor("wd", [DFF, D], F32, kind="ExternalInput").ap()
    lnp = nc.dram_tensor("lnp", [128, 32], F32, kind="ExternalInput").ap()
    hout = nc.dram_tensor("hout", [D, tok], F32, kind="ExternalOutput").ap()

    hin_r = hin.rearrange("(k p) t -> p k t", p=128)
    hout_r = hout.rearrange("(k p) t -> p k t", p=128)
    wmix_r = wmix.rearrange("(k p) n -> p k n", p=128)
    wg_r = wg.rearrange("(k p) n -> p k n", p=128)
    wu_r = wu.rearrange("(k p) n -> p k n", p=128)
    wd_r = wd.rearrange("(k p) n -> p k n", p=128)
    NF = DFF // 128

    with ExitStack() as st:
        sc = Sched(nc, st)
        hres = _sb(nc, st, "hres", [128, 8, TG], F32)
        hb = _sb(nc, st, "hb", [128, 8, TG], BF16)
        ab = _sb(nc, st, "ab", [128, 8, TG], BF16)
        hid = _sb(nc, st, "hid", [128, NF, TG], BF16)
        wsl = [_sb(nc, st, "wsl%d" % i, [128, WCAP], BF16) for i in range(3)]
        zb = _sb(nc, st, "zb", [128, 8, TT], BF16)
        zsq = _sb(nc, st, "zsq", [128, 8, TT], BF16)
        mean = _sb(nc, st, "mean", [128, TT], F32)
        msq = _sb(nc, st, "msq", [128, TT], F32)
        var = _sb(nc, st, "var", [128, TT], F32)
        rstd = _sb(nc, st, "rstd", [128, TT], F32)
        nmr = _sb(nc, st, "nmr", [128, TT], F32)
        tmp = [_sb(nc, st, "tmp%d" % i, [128, TT], F32) for i in range(2)]
        tmp2 = [_sb(nc, st, "tmq%d" % i, [128, TT], F32) for i in range(2)]
        sg = [_sb(nc, st, "sg%d" % i, [128, TT], F32) for i in range(2)]
        ones = _sb(nc, st, "ones", [128, 128], BF16)
        lnt = _sb(nc, st, "lnt", [128, 32], F32)
        B = [_ps(nc, st, "B%d" % i) for i in range(8)]
        if mode == "C":
            cwt = _sb(nc, st, "cwt", [128, 24], F32)
            hbh = _sb(nc, st, "hbh", [128, 8, 2], BF16)
            uprev = _sb(nc, st, "uprev", [128, 8, 2], F32)
            ubuf = [_sb(nc, st, "ubuf%d" % i, [128, TT + 2], F32) for i in range(2)]
            cs = [_sb(nc, st, "cs%d" % i, [128, TT], F32) for i in range(2)]
            ybuf = [_sb(nc, st, "ybuf%d" % i, [128, TT], F32) for i in range(2)]
            hcs = _sb(nc, st, "hcs", [128, 2], F32)

        state = {"bank": 0, "job": 0, "tmp": 0}

        def bank():
            b = state["bank"]
            state["bank"] = (b + 1) % 8
            return b

        def mm(outp, lhsT, rhs, start, stop, reads, writes):
            sc.add("pe", lambda e: e.matmul(outp, lhsT, rhs, start=start, stop=stop),
                   reads=reads, writes=writes)

        sc.add("pool", lambda e: e.memset(ones[:], 1.0), writes=["ones"])
        sc.add("sp", lambda e: e.dma_start(out=lnt[:], in_=lnp), writes=["lnt"], dma=True)
        if mode == "C":
            sc.add("sp", lambda e: e.dma_start(out=cwt[:], in_=cw), writes=["cwt"], dma=True)

        jobs = []

        def run_jobs():
            n = len(jobs)
            base = state["job"]
            if n:
                jobs[0][0]((base) % 3)
            for i in range(n):
                if i + 1 < n:
                    jobs[i + 1][0]((base + i + 1) % 3)
                jobs[i][1]((base + i) % 3)
            state["job"] = base + n
            del jobs[:]

        def wload(slot, view, src):
            sc.add("pool", lambda e: e.dma_start(out=view, in_=src), writes=[("wsl", slot)], dma=True)

        def tcols(t):
            return slice(t * TT, (t + 1) * TT)

        def add_proj_jobs(w_r, src, srckey):
            for nb in range(2):
                def load(slot, nb=nb):
                    v = wsl[slot][:, 0:4096].rearrange("p (k n) -> p k n", k=8)
                    wload(slot, v, w_r[:, :, nb * 512:(nb + 1) * 512])

                def comp(slot, nb=nb):
                    v = wsl[slot][:, 0:4096].rearrange("p (k n) -> p k n", k=8)
                    for t in range(ntile):
                        for j in range(4):
                            n = nb * 4 + j
                            b = bank()
                            for k in range(8):
                                mm(B[b][:, :], v[:, k, j * 128:(j + 1) * 128], src[:, k, tcols(t)], k == 0, k == 7,
                                   [("wsl", slot), (srckey, k, t)], [("B", b)])
                            sc.add("dve", lambda e, n=n, t=t, b=b: e.scalar_tensor_tensor(
                                out=hres[:, n, tcols(t)], in0=hres[:, n, tcols(t)], scalar=ALPHA, in1=B[b][:, :],
                                op0=ALU.mult, op1=ALU.add),
                                reads=[("B", b), ("hres", n, t)], writes=[("hres", n, t)])
                jobs.append((load, comp))

        def layer_norm(q, want_hb):
            for t in range(ntile):
                for k in range(8):
                    sc.add("act", lambda e, k=k, t=t: e.copy(out=zb[:, k, :], in_=hres[:, k, tcols(t)]),
                           reads=[("hres", k, t)], writes=[("zb", k)])
                    sc.add("pool", lambda e, k=k, t=t: e.tensor_tensor(out=zsq[:, k, :], in0=hres[:, k, tcols(t)],
                                                                         in1=hres[:, k, tcols(t)], op=ALU.mult),
                           reads=[("hres", k, t)], writes=[("zsq", k)])
                ba = bank()
                for k in range(8):
                    mm(B[ba][:, :], ones[:, :], zb[:, k, :], k == 0, k == 7, ["ones", ("zb", k)], [("B", ba)])
                bb = bank()
                for k in range(8):
                    mm(B[bb][:, :], ones[:, :], zsq[:, k, :], k == 0, k == 7, ["ones", ("zsq", k)], [("B", bb)])
                sc.add("dve", lambda e, ba=ba: e.tensor_scalar(out=mean[:], in0=B[ba][:, :], scalar1=1.0 / D,
                                                               scalar2=None, op0=ALU.mult),
                       reads=[("B", ba)], writes=["mean"])
                sc.add("dve", lambda e: e.tensor_tensor(out=msq[:], in0=mean[:], in1=mean[:], op=ALU.mult),
                       reads=["mean"], writes=["msq"])
                sc.add("dve", lambda e, bb=bb: e.scalar_tensor_tensor(out=var[:], in0=B[bb][:, :], scalar=1.0 / D,
                                                                      in1=msq[:], op0=ALU.mult, op1=ALU.subtract),
                       reads=[("B", bb), "msq"], writes=["var"])
                sc.add("dve", lambda e: e.tensor_scalar(out=var[:], in0=var[:], scalar1=LN_EPS, scalar2=None,
                                                        op0=ALU.add), reads=["var"], writes=["var"])
                sc.add("act", lambda e: e.activation(out=var[:], in_=var[:], func=AF.Ln),
                       reads=["var"], writes=["var"])
                sc.add("act", lambda e: e.activation(out=rstd[:], in_=var[:], func=AF.Exp, scale=-0.5),
                       reads=["var"], writes=["rstd"])
                sc.add("dve", lambda e: e.scalar_tensor_tensor(out=nmr[:], in0=mean[:], scalar=-1.0, in1=rstd[:],
                                                               op0=ALU.mult, op1=ALU.mult),
                       reads=["mean", "rstd"], writes=["nmr"])
                for k in range(8):
                    i = state["tmp"]
                    state["tmp"] = 1 - i
                    sc.add("dve", lambda e, k=k, t=t, i=i: e.tensor_tensor(out=tmp[i][:], in0=hres[:, k, tcols(t)],
                                                                           in1=rstd[:], op=ALU.mult),
                           reads=[("hres", k, t), "rstd"], writes=[("tmp", i)])
                    sc.add("pool", lambda e, i=i: e.tensor_tensor(out=tmp2[i][:], in0=tmp[i][:], in1=nmr[:],
                                                                  op=ALU.add),
                           reads=[("tmp", i), "nmr"], writes=[("tmp2", i)])
                    gcol = q * 16 + k
                    bcol = q * 16 + 8 + k
                    sc.add("act", lambda e, k=k, t=t, i=i, gcol=gcol, bcol=bcol: e.activation(
                        out=hres[:, k, tcols(t)], in_=tmp2[i][:], func=AF.Identity,
                        scale=lnt[:, gcol:gcol + 1], bias=lnt[:, bcol:bcol + 1]),
                        reads=[("tmp2", i), "lnt"], writes=[("hres", k, t)])
                    if want_hb:
                        sc.add("dve", lambda e, k=k, t=t, i=i, gcol=gcol, bcol=bcol: e.tensor_scalar(
                            out=hb[:, k, tcols(t)], in0=tmp2[i][:], scalar1=lnt[:, gcol:gcol + 1],
                            scalar2=lnt[:, bcol:bcol + 1], op0=ALU.mult, op1=ALU.add),
                            reads=[("tmp2", i), "lnt"], writes=[("hb", k, t)])

        def add_ffn_jobs():
            f0 = 0
            while f0 < NF:
                fb = min(3, NF - f0)

                def load(slot, f0=f0, fb=fb):
                    v = wsl[slot][:, 0:2 * 8 * 384].rearrange("p (g k n) -> p g k n", g=2, k=8)
                    wload(slot, v[:, 0, :, 0:fb * 128], wg_r[:, :, f0 * 128:(f0 + fb) * 128])
                    wload(slot, v[:, 1, :, 0:fb * 128], wu_r[:, :, f0 * 128:(f0 + fb) * 128])

                def comp(slot, f0=f0, fb=fb):
                    v = wsl[slot][:, 0:2 * 8 * 384].rearrange("p (g k n) -> p g k n", g=2, k=8)
                    for t in range(ntile):
                        for j in range(fb):
                            fc = f0 + j
                            bg = bank()
                            for k in range(8):
                                mm(B[bg][:, :], v[:, 0, k, j * 128:(j + 1) * 128], hb[:, k, tcols(t)], k == 0, k == 7,
                                   [("wsl", slot), ("hb", k, t)], [("B", bg)])
                            bu = bank()
                            for k in range(8):
                                mm(B[bu][:, :], v[:, 1, k, j * 128:(j + 1) * 128], hb[:, k, tcols(t)], k == 0, k == 7,
                                   [("wsl", slot), ("hb", k, t)], [("B", bu)])
                            i = state["tmp"]
                            state["tmp"] = 1 - i
                            sc.add("act", lambda e, i=i, bg=bg: e.activation(out=sg[i][:], in_=B[bg][:, :],
                                                                             func=AF.Silu),
                                   reads=[("B", bg)], writes=[("sg", i)])
                            sc.add("dve", lambda e, i=i, bu=bu, fc=fc, t=t: e.tensor_tensor(
                                out=hid[:, fc, tcols(t)], in0=B[bu][:, :], in1=sg[i][:], op=ALU.mult),
                                reads=[("B", bu), ("sg", i)], writes=[("hid", fc, t)])
                jobs.append((load, comp))
                f0 += fb
            for nb in range(4):
                def load(slot, nb=nb):
                    v = wsl[slot][:, 0:NF * 256].rearrange("p (k n) -> p k n", k=NF)
                    wload(slot, v, wd_r[:, :, nb * 256:(nb + 1) * 256])

                def comp(slot, nb=nb):
                    v = wsl[slot][:, 0:NF * 256].rearrange("p (k n) -> p k n", k=NF)
                    for t in range(ntile):
                        for j in range(2):
                            n = nb * 2 + j
                            b = bank()
                            for fc in range(NF):
                                mm(B[b][:, :], v[:, fc, j * 128:(j + 1) * 128], hid[:, fc, tcols(t)], fc == 0,
                                   fc == NF - 1, [("wsl", slot), ("hid", fc, t)], [("B", b)])
                            sc.add("dve", lambda e, n=n, t=t, b=b: e.scalar_tensor_tensor(
                                out=hres[:, n, tcols(t)], in0=hres[:, n, tcols(t)], scalar=ALPHA, in1=B[b][:, :],
                                op0=ALU.mult, op1=ALU.add),
                                reads=[("B", b), ("hres", n, t)], writes=[("hres", n, t)])
                jobs.append((load, comp))

        def add_conv_jobs(g):
            w_in_r = w_in.rearrange("(k p) (g n) -> p k g n", p=128, g=3)
            for jb in range(4):
                def load(slot, jb=jb):
                    v = wsl[slot][:, 0:WCAP].rearrange("p (k g n) -> p k g n", k=8, g=3)
                    for gg in range(3):
                        wload(slot, v[:, :, gg, :], w_in_r[:, :, gg, jb * 256:(jb + 1) * 256])

                def comp(slot, jb=jb):
                    v = wsl[slot][:, 0:WCAP].rearrange("p (k g n) -> p k g n", k=8, g=3)
                    for jj in range(2):
                        ch = jb * 2 + jj
                        cols = slice(jj * 128, (jj + 1) * 128)
                        if g == 0:
                            bc = bank()
                            for k in range(8):
                                mm(B[bc][:, 0:2], v[:, k, 1, cols], hbh[:, k, :], k == 0, k == 7,
                                   [("wsl", slot), "hbh"], [("B", bc)])
                            bx = bank()
                            for k in range(8):
                                mm(B[bx][:, 0:2], v[:, k, 2, cols], hbh[:, k, :], k == 0, k == 7,
                                   [("wsl", slot), "hbh"], [("B", bx)])
                            sc.add("act", lambda e, bc=bc: e.copy(out=hcs[:], in_=B[bc][:, 0:2]),
                                   reads=[("B", bc)], writes=["hcs"])
                            sc.add("dve", lambda e, bx=bx, ch=ch: e.tensor_tensor(out=uprev[:, ch, :], in0=B[bx][:, 0:2],
                                                                                  in1=hcs[:], op=ALU.mult),
                                   reads=[("B", bx), "hcs"], writes=[("uprev", ch)])
                        for t in range(ntile):
                            bbk = bank()
                            for k in range(8):
                                mm(B[bbk][:, :], v[:, k, 0, cols], hb[:, k, tcols(t)], k == 0, k == 7,
                                   [("wsl", slot), ("hb", k, t)], [("B", bbk)])
                            bc = bank()
                            for k in range(8):
                                mm(B[bc][:, :], v[:, k, 1, cols], hb[:, k, tcols(t)], k == 0, k == 7,
                                   [("wsl", slot), ("hb", k, t)], [("B", bc)])
                            bx = bank()
                            for k in range(8):
                                mm(B[bx][:, :], v[:, k, 2, cols], hb[:, k, tcols(t)], k == 0, k == 7,
                                   [("wsl", slot), ("hb", k, t)], [("B", bx)])
                            i = state["tmp"]
                            state["tmp"] = 1 - i
                            u = ubuf[i]
                            sc.add("act", lambda e, i=i, bc=bc: e.copy(out=cs[i][:], in_=B[bc][:, :]),
                                   reads=[("B", bc)], writes=[("cs", i)])
                            sc.add("pool", lambda e, u=u, ch=ch: e.tensor_copy(out=u[:, 0:2], in_=uprev[:, ch, :]),
                                   reads=[("uprev", ch)], writes=[("u", i)])
                            sc.add("dve", lambda e, u=u, i=i, bx=bx: e.tensor_tensor(out=u[:, 2:TT + 2], in0=B[bx][:, :],
                                                                                     in1=cs[i][:], op=ALU.mult),
                                   reads=[("B", bx), ("cs", i)], writes=[("u", i)])
                            sc.add("pool", lambda e, u=u, ch=ch: e.tensor_copy(out=uprev[:, ch, :], in_=u[:, TT:TT + 2]),
                                   reads=[("u", i)], writes=[("uprev", ch)])
                            y = ybuf[i]
                            c3 = ch * 3
                            sc.add("dve", lambda e, u=u, y=y, c3=c3: e.tensor_scalar(
                                out=y[:], in0=u[:, 2:TT + 2], scalar1=cwt[:, c3 + 2:c3 + 3], scalar2=None, op0=ALU.mult),
                                reads=[("u", i), "cwt"], writes=[("y", i)])
                            sc.add("dve", lambda e, u=u, y=y, c3=c3: e.scalar_tensor_tensor(
                                out=y[:], in0=u[:, 1:TT + 1], scalar=cwt[:, c3 + 1:c3 + 2], in1=y[:],
                                op0=ALU.mult, op1=ALU.add), reads=[("u", i), "cwt", ("y", i)], writes=[("y", i)])
                            sc.add("dve", lambda e, u=u, y=y, c3=c3: e.scalar_tensor_tensor(
                                out=y[:], in0=u[:, 0:TT], scalar=cwt[:, c3:c3 + 1], in1=y[:],
                                op0=ALU.mult, op1=ALU.add), reads=[("u", i), "cwt", ("y", i)], writes=[("y", i)])
                            sc.add("dve", lambda e, y=y, bbk=bbk, ch=ch, t=t: e.tensor_tensor(
                                out=ab[:, ch, tcols(t)], in0=B[bbk][:, :], in1=y[:], op=ALU.mult),
                                reads=[("B", bbk), ("y", i)], writes=[("ab", ch, t)])
                jobs.append((load, comp))

        for g in range(ngroups):
            g0 = g * TG
            for t in range(ntile):
                c0 = g0 + t * TT
                sc.add("sp", lambda e, t=t, c0=c0: e.dma_start(out=hres[:, :, tcols(t)], in_=hin_r[:, :, c0:c0 + TT]),
                       writes=[("hres", k, t) for k in range(8)], dma=True)
            if mode == "B":
                ain_r = ain.rearrange("(k p) t -> p k t", p=128)
                for t in range(ntile):
                    c0 = g0 + t * TT
                    sc.add("pool", lambda e, t=t, c0=c0: e.dma_start(out=ab[:, :, tcols(t)], in_=ain_r[:, :, c0:c0 + TT]),
                           writes=[("ab", k, t) for k in range(8)], dma=True)
                add_proj_jobs(wmix_r, ab, "ab")
                run_jobs()
            else:
                for t in range(ntile):
                    c0 = g0 + t * TT
                    sc.add("pool", lambda e, t=t, c0=c0: e.dma_start(out=hb[:, :, tcols(t)], in_=hin_r[:, :, c0:c0 + TT]),
                           writes=[("hb", k, t) for k in range(8)], dma=True)
                if g == 0:
                    halo_r = halo.rearrange("(k p) t -> p k t", p=128)
                    sc.add("pool", lambda e: e.dma_start(out=hbh[:], in_=halo_r), writes=["hbh"], dma=True)
                add_conv_jobs(g)
                add_proj_jobs(wmix_r, ab, "ab")
                run_jobs()
            layer_norm(0, True)
            add_ffn_jobs()
            run_jobs()
            layer_norm(1, False)
            for t in range(ntile):
                c0 = g0 + t * TT
                sc.add("sp", lambda e, t=t, c0=c0: e.dma_start(out=hout_r[:, :, c0:c0 + TT], in_=hres[:, :, tcols(t)]),
                       reads=[("hres", k, t) for k in range(8)], dma=True)
        sc.emit()
    return nc


def _lnp(ln1_g, ln1_b, ln2_g, ln2_b, i):
    cols = []
    for v in (ln1_g[i], ln1_b[i], ln2_g[i], ln2_b[i]):
        cols.append(np.asarray(v, np.float32).reshape(8, 128).T)
    return np.ascontiguousarray(np.concatenate(cols, axis=1))


def run_B(xT, attnT, w_o, wg, wu, wd, lnp):
    in_maps = []
    for c in range(NCORES):
        sl = slice(c * TOK, (c + 1) * TOK)
        in_maps.append({"hin": np.ascontiguousarray(xT[:, sl]), "ain": np.ascontiguousarray(attnT[:, sl]),
                        "wmix": w_o, "wg": wg, "wu": wu, "wd": wd, "lnp": lnp})
    nc = build_BC("B")
    r = run_bass_kernel_spmd(nc, in_maps, core_ids=list(range(NCORES)))
    return np.concatenate([r.results[c]["hout"] for c in range(NCORES)], axis=1)


def run_C(hT, w_in, cw, w_out, wg, wu, wd, lnp):
    in_maps = []
    for c in range(NCORES):
        sl = slice(c * TOK, (c + 1) * TOK)
        halo = np.zeros((D, 2), np.float32) if c == 0 else np.ascontiguousarray(hT[:, c * TOK - 2:c * TOK])
        in_maps.append({"hin": np.ascontiguousarray(hT[:, sl]), "halo": halo, "w_in": w_in, "cw": cw,
                        "wmix": w_out, "wg": wg, "wu": wu, "wd": wd, "lnp": lnp})
    nc = build_BC("C")
    r = run_bass_kernel_spmd(nc, in_maps, core_ids=list(range(NCORES)))
    return np.concatenate([r.results[c]["hout"] for c in range(NCORES)], axis=1)


def kernel3(x, attn_w_qkv, attn_w_o, attn_lambda_q1, attn_lambda_k1, attn_lambda_q2, attn_lambda_k2,
           attn_subln_g, conv_w_in, conv_w, conv_w_out, ffn_w_gate, ffn_w_up, ffn_w_down,
           ln1_g, ln1_b, ln2_g, ln2_b):
    f = lambda a: np.ascontiguousarray(np.asarray(a, np.float32))
    x = f(x)
    attnT = run_A(x, f(attn_w_qkv), f(attn_lambda_q1), f(attn_lambda_k1), f(attn_lambda_q2), f(attn_lambda_k2),
                  f(attn_subln_g))
    xT = np.ascontiguousarray(x.reshape(S, D).T)
    h1T = run_B(xT, attnT, f(attn_w_o)[0], f(ffn_w_gate)[0], f(ffn_w_up)[0], f(ffn_w_down)[0],
                _lnp(ln1_g, ln1_b, ln2_g, ln2_b, 0))
    cw = np.ascontiguousarray(f(conv_w)[0].reshape(3, 8, 128).transpose(2, 1, 0).reshape(128, 24))
    h2T = run_C(h1T, f(conv_w_in)[0], cw, f(conv_w_out)[0], f(ffn_w_gate)[1], f(ffn_w_up)[1], f(ffn_w_down)[1],
                _lnp(ln1_g, ln1_b, ln2_g, ln2_b, 1))
    return np.ascontiguousarray(h2T.T).reshape(1, S, D).astype(np.float32)


def build_T(tok=TOK):
    ngroups = tok // TG
    TGW = TG + 2
    nc = bass.Bass("TRN2", target_bir_lowering=False)
    hin = nc.dram_tensor("hin", [D, tok], F32, kind="ExternalInput").ap()
    xhalo = nc.dram_tensor("xhalo", [D, 2], F32, kind="ExternalInput").ap()
    ain = nc.dram_tensor("ain", [D, tok], F32, kind="ExternalInput").ap()
    ahalo = nc.dram_tensor("ahalo", [D, 2], F32, kind="ExternalInput").ap()
    uflag = nc.dram_tensor("uflag", [128, 1], F32, kind="ExternalInput").ap()
    w_o = nc.dram_tensor("w_o", [D, D], F32, kind="ExternalInput").ap()
    w_in = nc.dram_tensor("w_in", [D, 3 * D], F32, kind="ExternalInput").ap()
    cw = nc.dram_tensor("cw", [128, 24], F32, kind="ExternalInput").ap()
    w_out = nc.dram_tensor("w_out", [D, D], F32, kind="ExternalInput").ap()
    wgs = [nc.dram_tensor("wg%d" % i, [D, DFF], F32, kind="ExternalInput").ap() for i in range(2)]
    wus = [nc.dram_tensor("wu%d" % i, [D, DFF], F32, kind="ExternalInput").ap() for i in range(2)]
    wds = [nc.dram_tensor("wd%d" % i, [DFF, D], F32, kind="ExternalInput").ap() for i in range(2)]
    lnp = nc.dram_tensor("lnp", [128, 64], F32, kind="ExternalInput").ap()
    hout = nc.dram_tensor("hout", [D, tok], F32, kind="ExternalOutput").ap()

    kp = lambda a: a.rearrange("(k p) t -> p k t", p=128)
    hin_r, ain_r, hout_r = kp(hin), kp(ain), kp(hout)
    xhalo_r, ahalo_r = kp(xhalo), kp(ahalo)
    w_o_r, w_out_r = kp(w_o), kp(w_out)
    wg_r = [kp(a) for a in wgs]
    wu_r = [kp(a) for a in wus]
    wd_r = [kp(a) for a in wds]
    w_in_r = w_in.rearrange("(k p) (g n) -> p k g n", p=128, g=3)
    NF = DFF // 128

    with ExitStack() as st:
        sc = Sched(nc, st)
        hres = _sb(nc, st, "hres", [128, 8, TGW], F32)
        hb = _sb(nc, st, "hb", [128, 8, TGW], BF16)
        ab = _sb(nc, st, "ab", [128, 8, TGW], BF16)
        hid = _sb(nc, st, "hid", [128, NF, TGW], BF16)
        wsl = [_sb(nc, st, "wsl%d" % i, [128, WCAP], BF16) for i in range(3)]
        zb = _sb(nc, st, "zb", [128, 8, TT], BF16)
        zsq = _sb(nc, st, "zsq", [128, 8, TT], BF16)
        mean = _sb(nc, st, "mean", [128, TT], F32)
        msq = _sb(nc, st, "msq", [128, TT], F32)
        var = _sb(nc, st, "var", [128, TT], F32)
        rstd = _sb(nc, st, "rstd", [128, TT], F32)
        nmr = _sb(nc, st, "nmr", [128, TT], F32)
        tmp = [_sb(nc, st, "tmp%d" % i, [128, TT], F32) for i in range(2)]
        tmp2 = [_sb(nc, st, "tmq%d" % i, [128, TT], F32) for i in range(2)]
        sg = [_sb(nc, st, "sg%d" % i, [128, TT], F32) for i in range(2)]
        ones = _sb(nc, st, "ones", [128, 128], BF16)
        lnt = _sb(nc, st, "lnt", [128, 64], F32)
        cwt = _sb(nc, st, "cwt", [128, 24], F32)
        ufl = _sb(nc, st, "ufl", [128, 1], F32)
        uprev = _sb(nc, st, "uprev", [128, 8, 2], F32)
        ubuf = [_sb(nc, st, "ubuf%d" % i, [128, TT + 2], F32) for i in range(2)]
        cs = [_sb(nc, st, "cs%d" % i, [128, TT], F32) for i in range(2)]
        ybuf = [_sb(nc, st, "ybuf%d" % i, [128, TT], F32) for i in range(2)]
        hcs = _sb(nc, st, "hcs", [128, 2], F32)
        B = [_ps(nc, st, "B%d" % i) for i in range(8)]

        state = {"bank": 0, "job": 0, "tmp": 0}

        def bank():
            b = state["bank"]
            state["bank"] = (b + 1) % 8
            return b

        def flip():
            i = state["tmp"]
            state["tmp"] = 1 - i
            return i

        def mm(outp, lhsT, rhs, start, stop, reads, writes):
            sc.add("pe", lambda e: e.matmul(outp, lhsT, rhs, start=start, stop=stop),
                   reads=reads, writes=writes)

        sc.add("pool", lambda e: e.memset(ones[:], 1.0), writes=["ones"])
        sc.add("sp", lambda e: e.dma_start(out=lnt[:], in_=lnp), writes=["lnt"], dma=True)
        sc.add("sp", lambda e: e.dma_start(out=cwt[:], in_=cw), writes=["cwt"], dma=True)
        sc.add("sp", lambda e: e.dma_start(out=ufl[:], in_=uflag), writes=["ufl"], dma=True)

        def tiles(g, layer):
            ts = [(0, 0, TT), (1, TT, TT)]
            if g == 0 and layer == 0:
                ts.append((2, TG, 2))
            return ts

        jobs = []

        def run_jobs():
            n = len(jobs)
            base = state["job"]
            if n:
                jobs[0][0](base % 3)
            for i in range(n):
                if i + 1 < n:
                    jobs[i + 1][0]((base + i + 1) % 3)
                jobs[i][1]((base + i) % 3)
            state["job"] = base + n
            del jobs[:]

        def wload(slot, view, src):
            sc.add("pool", lambda e: e.dma_start(out=view, in_=src), writes=[("wsl", slot)], dma=True)

        def add_proj_jobs(w_r, tl):
            for nb in range(2):
                def load(slot, nb=nb):
                    v = wsl[slot][:, 0:4096].rearrange("p (k n) -> p k n", k=8)
                    wload(slot, v, w_r[:, :, nb * 512:(nb + 1) * 512])

                def comp(slot, nb=nb):
                    v = wsl[slot][:, 0:4096].rearrange("p (k n) -> p k n", k=8)
                    for (t, c0, w) in tl:
                        for j in range(4):
                            n = nb * 4 + j
                            b = bank()
                            for k in range(8):
                                mm(B[b][:, 0:w], v[:, k, j * 128:(j + 1) * 128], ab[:, k, c0:c0 + w], k == 0, k == 7,
                                   [("wsl", slot), ("ab", k, t)], [("B", b)])
                            sc.add("dve", lambda e, n=n, c0=c0, w=w, b=b: e.scalar_tensor_tensor(
                                out=hres[:, n, c0:c0 + w], in0=hres[:, n, c0:c0 + w], scalar=ALPHA, in1=B[b][:, 0:w],
                                op0=ALU.mult, op1=ALU.add),
                                reads=[("B", b), ("hres", n, t)], writes=[("hres", n, t)])
                jobs.append((load, comp))

        def layer_norm(q, tl):
            for (t, c0, w) in tl:
                for k in range(8):
                    sc.add("act", lambda e, k=k, c0=c0, w=w: e.copy(out=zb[:, k, 0:w], in_=hres[:, k, c0:c0 + w]),
                           reads=[("hres", k, t)], writes=[("zb", k)])
                    sc.add("pool", lambda e, k=k, c0=c0, w=w: e.tensor_tensor(
                        out=zsq[:, k, 0:w], in0=hres[:, k, c0:c0 + w], in1=hres[:, k, c0:c0 + w], op=ALU.mult),
                        reads=[("hres", k, t)], writes=[("zsq", k)])
                ba = bank()
                for k in range(8):
                    mm(B[ba][:, 0:w], ones[:, :], zb[:, k, 0:w], k == 0, k == 7, ["ones", ("zb", k)], [("B", ba)])
                bb = bank()
                for k in range(8):
                    mm(B[bb][:, 0:w], ones[:, :], zsq[:, k, 0:w], k == 0, k == 7, ["ones", ("zsq", k)], [("B", bb)])
                sc.add("dve", lambda e, ba=ba, w=w: e.tensor_scalar(out=mean[:, 0:w], in0=B[ba][:, 0:w], scalar1=1.0 / D,
                                                                    scalar2=None, op0=ALU.mult),
                       reads=[("B", ba)], writes=["mean"])
                sc.add("dve", lambda e, w=w: e.tensor_tensor(out=msq[:, 0:w], in0=mean[:, 0:w], in1=mean[:, 0:w],
                                                             op=ALU.mult), reads=["mean"], writes=["msq"])
                sc.add("dve", lambda e, bb=bb, w=w: e.scalar_tensor_tensor(
                    out=var[:, 0:w], in0=B[bb][:, 0:w], scalar=1.0 / D, in1=msq[:, 0:w], op0=ALU.mult,
                    op1=ALU.subtract), reads=[("B", bb), "msq"], writes=["var"])
                sc.add("dve", lambda e, w=w: e.tensor_scalar(out=var[:, 0:w], in0=var[:, 0:w], scalar1=LN_EPS,
                                                             scalar2=None, op0=ALU.add), reads=["var"], writes=["var"])
                sc.add("act", lambda e, w=w: e.activation(out=var[:, 0:w], in_=var[:, 0:w], func=AF.Ln),
                       reads=["var"], writes=["var"])
                sc.add("act", lambda e, w=w: e.activation(out=rstd[:, 0:w], in_=var[:, 0:w], func=AF.Exp, scale=-0.5),
                       reads=["var"], writes=["rstd"])
                sc.add("dve", lambda e, w=w: e.scalar_tensor_tensor(out=nmr[:, 0:w], in0=mean[:, 0:w], scalar=-1.0,
                                                                    in1=rstd[:, 0:w], op0=ALU.mult, op1=ALU.mult),
                       reads=["mean", "rstd"], writes=["nmr"])
                for k in range(8):
                    i = flip()
                    sc.add("dve", lambda e, k=k, c0=c0, w=w, i=i: e.tensor_tensor(
                        out=tmp[i][:, 0:w], in0=hres[:, k, c0:c0 + w], in1=rstd[:, 0:w], op=ALU.mult),
                        reads=[("hres", k, t), "rstd"], writes=[("tmp", i)])
                    sc.add("pool", lambda e, i=i, w=w: e.tensor_tensor(out=tmp2[i][:, 0:w], in0=tmp[i][:, 0:w],
                                                                       in1=nmr[:, 0:w], op=ALU.add),
                           reads=[("tmp", i), "nmr"], writes=[("tmp2", i)])
                    gcol = q * 16 + k
                    bcol = q * 16 + 8 + k
                    sc.add("act", lambda e, k=k, c0=c0, w=w, i=i, gcol=gcol, bcol=bcol: e.activation(
                        out=hres[:, k, c0:c0 + w], in_=tmp2[i][:, 0:w], func=AF.Identity,
                        scale=lnt[:, gcol:gcol + 1], bias=lnt[:, bcol:bcol + 1]),
                        reads=[("tmp2", i), "lnt"], writes=[("hres", k, t)])
                    sc.add("dve", lambda e, k=k, c0=c0, w=w, i=i, gcol=gcol, bcol=bcol: e.tensor_scalar(
                        out=hb[:, k, c0:c0 + w], in0=tmp2[i][:, 0:w], scalar1=lnt[:, gcol:gcol + 1],
                        scalar2=lnt[:, bcol:bcol + 1], op0=ALU.mult, op1=ALU.add),
                        reads=[("tmp2", i), "lnt"], writes=[("hb", k, t)])

        def add_ffn_jobs(layer, tl):
            f0 = 0
            while f0 < NF:
                fb = min(3, NF - f0)

                def load(slot, f0=f0, fb=fb):
                    v = wsl[slot][:, 0:2 * 8 * 384].rearrange("p (g k n) -> p g k n", g=2, k=8)
                    wload(slot, v[:, 0, :, 0:fb * 128], wg_r[layer][:, :, f0 * 128:(f0 + fb) * 128])
                    wload(slot, v[:, 1, :, 0:fb * 128], wu_r[layer][:, :, f0 * 128:(f0 + fb) * 128])

                def comp(slot, f0=f0, fb=fb):
                    v = wsl[slot][:, 0:2 * 8 * 384].rearrange("p (g k n) -> p g k n", g=2, k=8)
                    for (t, c0, w) in tl:
                        for j in range(fb):
                            fc = f0 + j
                            bg = bank()
                            for k in range(8):
                                mm(B[bg][:, 0:w], v[:, 0, k, j * 128:(j + 1) * 128], hb[:, k, c0:c0 + w], k == 0, k == 7,
                                   [("wsl", slot), ("hb", k, t)], [("B", bg)])
                            bu = bank()
                            for k in range(8):
                                mm(B[bu][:, 0:w], v[:, 1, k, j * 128:(j + 1) * 128], hb[:, k, c0:c0 + w], k == 0, k == 7,
                                   [("wsl", slot), ("hb", k, t)], [("B", bu)])
                            i = flip()
                            sc.add("act", lambda e, i=i, bg=bg, w=w: e.activation(out=sg[i][:, 0:w], in_=B[bg][:, 0:w],
                                                                                  func=AF.Silu),
                                   reads=[("B", bg)], writes=[("sg", i)])
                            sc.add("dve", lambda e, i=i, bu=bu, fc=fc, c0=c0, w=w: e.tensor_tensor(
                                out=hid[:, fc, c0:c0 + w], in0=B[bu][:, 0:w], in1=sg[i][:, 0:w], op=ALU.mult),
                                reads=[("B", bu), ("sg", i)], writes=[("hid", fc, t)])
                jobs.append((load, comp))
                f0 += fb
            for nb in range(4):
                def load(slot, nb=nb):
                    v = wsl[slot][:, 0:NF * 256].rearrange("p (k n) -> p k n", k=NF)
                    wload(slot, v, wd_r[layer][:, :, nb * 256:(nb + 1) * 256])

                def comp(slot, nb=nb):
                    v = wsl[slot][:, 0:NF * 256].rearrange("p (k n) -> p k n", k=NF)
                    for (t, c0, w) in tl:
                        for j in range(2):
                            n = nb * 2 + j
                            b = bank()
                            for fc in range(NF):
                                mm(B[b][:, 0:w], v[:, fc, j * 128:(j + 1) * 128], hid[:, fc, c0:c0 + w], fc == 0,
                                   fc == NF - 1, [("wsl", slot), ("hid", fc, t)], [("B", b)])
                            sc.add("dve", lambda e, n=n, c0=c0, w=w, b=b: e.scalar_tensor_tensor(
                                out=hres[:, n, c0:c0 + w], in0=hres[:, n, c0:c0 + w], scalar=ALPHA, in1=B[b][:, 0:w],
                                op0=ALU.mult, op1=ALU.add),
                                reads=[("B", b), ("hres", n, t)], writes=[("hres", n, t)])
                jobs.append((load, comp))

        def add_conv_jobs(g):
            for jb in range(4):
                def load(slot, jb=jb):
                    v = wsl[slot][:, 0:WCAP].rearrange("p (k g n) -> p k g n", k=8, g=3)
                    for gg in range(3):
                        wload(slot, v[:, :, gg, :], w_in_r[:, :, gg, jb * 256:(jb + 1) * 256])

                def comp(slot, jb=jb):
                    v = wsl[slot][:, 0:WCAP].rearrange("p (k g n) -> p k g n", k=8, g=3)
                    for jj in range(2):
                        ch = jb * 2 + jj
                        cols = slice(jj * 128, (jj + 1) * 128)
                        if g == 0:
                            bc = bank()
                            for k in range(8):
                                mm(B[bc][:, 0:2], v[:, k, 1, cols], hb[:, k, TG:TG + 2], k == 0, k == 7,
                                   [("wsl", slot), ("hb", k, 2)], [("B", bc)])
                            bx = bank()
                            for k in range(8):
                                mm(B[bx][:, 0:2], v[:, k, 2, cols], hb[:, k, TG:TG + 2], k == 0, k == 7,
                                   [("wsl", slot), ("hb", k, 2)], [("B", bx)])
                            sc.add("act", lambda e, bc=bc: e.copy(out=hcs[:], in_=B[bc][:, 0:2]),
                                   reads=[("B", bc)], writes=["hcs"])
                            sc.add("dve", lambda e, bx=bx, ch=ch: e.scalar_tensor_tensor(
                                out=uprev[:, ch, :], in0=B[bx][:, 0:2], scalar=ufl[:, 0:1], in1=hcs[:],
                                op0=ALU.mult, op1=ALU.mult),
                                reads=[("B", bx), "hcs", "ufl"], writes=[("uprev", ch)])
                        for (t, c0, w) in tiles(g, 1):
                            bbk = bank()
                            for k in range(8):
                                mm(B[bbk][:, :], v[:, k, 0, cols], hb[:, k, c0:c0 + w], k == 0, k == 7,
                                   [("wsl", slot), ("hb", k, t)], [("B", bbk)])
                            bc = bank()
                            for k in range(8):
                                mm(B[bc][:, :], v[:, k, 1, cols], hb[:, k, c0:c0 + w], k == 0, k == 7,
                                   [("wsl", slot), ("hb", k, t)], [("B", bc)])
                            bx = bank()
                            for k in range(8):
                                mm(B[bx][:, :], v[:, k, 2, cols], hb[:, k, c0:c0 + w], k == 0, k == 7,
                                   [("wsl", slot), ("hb", k, t)], [("B", bx)])
                            i = flip()
                            u = ubuf[i]
                            y = ybuf[i]
                            c3 = ch * 3
                            sc.add("act", lambda e, i=i, bc=bc: e.copy(out=cs[i][:], in_=B[bc][:, :]),
                                   reads=[("B", bc)], writes=[("cs", i)])
                            sc.add("pool", lambda e, u=u, ch=ch: e.tensor_copy(out=u[:, 0:2], in_=uprev[:, ch, :]),
                                   reads=[("uprev", ch)], writes=[("u", i)])
                            sc.add("dve", lambda e, u=u, i=i, bx=bx: e.tensor_tensor(out=u[:, 2:TT + 2], in0=B[bx][:, :],
                                                                                     in1=cs[i][:], op=ALU.mult),
                                   reads=[("B", bx), ("cs", i)], writes=[("u", i)])
                            sc.add("pool", lambda e, u=u, ch=ch: e.tensor_copy(out=uprev[:, ch, :], in_=u[:, TT:TT + 2]),
                                   reads=[("u", i)], writes=[("uprev", ch)])
                            sc.add("dve", lambda e, u=u, y=y, c3=c3: e.tensor_scalar(
                                out=y[:], in0=u[:, 2:TT + 2], scalar1=cwt[:, c3 + 2:c3 + 3], scalar2=None, op0=ALU.mult),
                                reads=[("u", i), "cwt"], writes=[("y", i)])
                            sc.add("dve", lambda e, u=u, y=y, c3=c3: e.scalar_tensor_tensor(
                                out=y[:], in0=u[:, 1:TT + 1], scalar=cwt[:, c3 + 1:c3 + 2], in1=y[:],
                                op0=ALU.mult, op1=ALU.add), reads=[("u", i), "cwt", ("y", i)], writes=[("y", i)])
                            sc.add("dve", lambda e, u=u, y=y, c3=c3: e.scalar_tensor_tensor(
                                out=y[:], in0=u[:, 0:TT], scalar=cwt[:, c3:c3 + 1], in1=y[:],
                                op0=ALU.mult, op1=ALU.add), reads=[("u", i), "cwt", ("y", i)], writes=[("y", i)])
                            sc.add("dve", lambda e, y=y, bbk=bbk, ch=ch, c0=c0, w=w: e.tensor_tensor(
                                out=ab[:, ch, c0:c0 + w], in0=B[bbk][:, :], in1=y[:], op=ALU.mult),
                                reads=[("B", bbk), ("y", i)], writes=[("ab", ch, t)])
                jobs.append((load, comp))

        for g in range(ngroups):
            g0 = g * TG
            tl0 = tiles(g, 0)
            tl1 = tiles(g, 1)
            for (t, c0, w) in tl0:
                if t < 2:
                    sc.add("sp", lambda e, c0=c0, w=w, g0=g0: e.dma_start(out=hres[:, :, c0:c0 + w],
                                                                   in_=hin_r[:, :, g0 + c0:g0 + c0 + w]),
                           writes=[("hres", k, t) for k in range(8)], dma=True)
                    sc.add("pool", lambda e, c0=c0, w=w, g0=g0: e.dma_start(out=ab[:, :, c0:c0 + w],
                                                                     in_=ain_r[:, :, g0 + c0:g0 + c0 + w]),
                           writes=[("ab", k, t) for k in range(8)], dma=True)
                else:
                    sc.add("sp", lambda e, c0=c0, w=w: e.dma_start(out=hres[:, :, c0:c0 + w], in_=xhalo_r),
                           writes=[("hres", k, t) for k in range(8)], dma=True)
                    sc.add("pool", lambda e, c0=c0, w=w: e.dma_start(out=ab[:, :, c0:c0 + w], in_=ahalo_r),
                           writes=[("ab", k, t) for k in range(8)], dma=True)
            add_proj_jobs(w_o_r, tl0)
            run_jobs()
            layer_norm(0, tl0)
            add_ffn_jobs(0, tl0)
            run_jobs()
            layer_norm(1, tl0)
            add_conv_jobs(g)
            add_proj_jobs(w_out_r, tl1)
            run_jobs()
            layer_norm(2, tl1)
            add_ffn_jobs(1, tl1)
            run_jobs()
            layer_norm(3, tl1)
            for (t, c0, w) in tl1:
                sc.add("sp", lambda e, c0=c0, w=w, g0=g0: e.dma_start(out=hout_r[:, :, g0 + c0:g0 + c0 + w],
                                                               in_=hres[:, :, c0:c0 + w]),
                       reads=[("hres", k, t) for k in range(8)], dma=True)
        sc.emit()
    return nc


def run_T(xT, attnT, inp):
    f = lambda a: np.ascontiguousarray(np.asarray(a, np.float32))
    lnp = np.ascontiguousarray(np.concatenate(
        [_lnp(inp["ln1_g"], inp["ln1_b"], inp["ln2_g"], inp["ln2_b"], i) for i in range(2)], axis=1))
    cw = np.ascontiguousarray(f(inp["conv_w"])[0].reshape(3, 8, 128).transpose(2, 1, 0).reshape(128, 24))
    shared = {"w_o": f(inp["attn_w_o"])[0], "w_in": f(inp["conv_w_in"])[0], "cw": cw, "w_out": f(inp["conv_w_out"])[0],
              "wg0": f(inp["ffn_w_gate"])[0], "wu0": f(inp["ffn_w_up"])[0], "wd0": f(inp["ffn_w_down"])[0],
              "wg1": f(inp["ffn_w_gate"])[1], "wu1": f(inp["ffn_w_up"])[1], "wd1": f(inp["ffn_w_down"])[1], "lnp": lnp}
    in_maps = []
    for c in range(NCORES):
        sl = slice(c * TOK, (c + 1) * TOK)
        if c == 0:
            xh = np.zeros((D, 2), np.float32)
            ah = np.zeros((D, 2), np.float32)
        else:
            xh = np.ascontiguousarray(xT[:, c * TOK - 2:c * TOK])
            ah = np.ascontiguousarray(attnT[:, c * TOK - 2:c * TOK])
        m = {"hin": np.ascontiguousarray(xT[:, sl]), "ain": np.ascontiguousarray(attnT[:, sl]), "xhalo": xh,
             "ahalo": ah, "uflag": np.full((128, 1), 0.0 if c == 0 else 1.0, np.float32)}
        m.update(shared)
        in_maps.append(m)
    nc = build_T()
    r = run_bass_kernel_spmd(nc, in_maps, core_ids=list(range(NCORES)))
    return np.concatenate([r.results[c]["hout"] for c in range(NCORES)], axis=1)


def kernel(**inp):
    f = lambda a: np.ascontiguousarray(np.asarray(a, np.float32))
    x = f(inp["x"])
    attnT = run_A(x, f(inp["attn_w_qkv"]), f(inp["attn_lambda_q1"]), f(inp["attn_lambda_k1"]),
                  f(inp["attn_lambda_q2"]), f(inp["attn_lambda_k2"]), f(inp["attn_subln_g"]))
    xT = np.ascontiguousarray(x.reshape(S, D).T)
    h2T = run_T(xT, attnT, inp)
    return np.ascontiguousarray(h2T.T).reshape(1, S, D).astype(np.float32)
```

```python
import math
from contextlib import ExitStack

import numpy as np
import concourse.bass as bass
import concourse.mybir as mybir
from concourse.bass_utils import run_bass_kernel_spmd

F32 = mybir.dt.float32
BF16 = mybir.dt.bfloat16
AF = mybir.ActivationFunctionType
ALU = mybir.AluOpType
AX = mybir.AxisListType

D = 1024
S = 16384
NCORES = 8
DFF = 2816
ALPHA = (2 * 2) ** 0.25
LN_EPS = 1e-5
ROPE_THETA = 500000.0
LAM_INIT0 = 0.8 - 0.6 * math.exp(-0.3 * 0)


class _Op:
    __slots__ = ("eng", "fn", "deps", "dma", "sig", "sem", "val", "prev")


class Sched:
    ENGS = ("sp", "pe", "act", "dve", "pool")
    NDS = 12

    def __init__(self, nc, stack):
        self.nc = nc
        self.stack = stack
        self.ops = []
        self.last_w = {}
        self.readers = {}

    def add(self, eng, fn, reads=(), writes=(), dma=False):
        op = _Op()
        op.eng, op.fn, op.dma, op.sig = eng, fn, dma, False
        op.sem = None
        op.val = 0
        op.prev = 0
        deps = []
        seen = set()

        def dep(o):
            if o is not None and id(o) not in seen:
                seen.add(id(o))
                deps.append(o)

        for r in reads:
            dep(self.last_w.get(r))
            if isinstance(r, tuple) and r[0] == "B":
                for rd in self.readers.get(r, ()):
                    if rd.eng != eng:
                        dep(rd)
        for w in writes:
            dep(self.last_w.get(w))
            for rd in self.readers.get(w, ()):
                dep(rd)
        op.deps = deps
        for r in reads:
            self.readers.setdefault(r, []).append(op)
        for w in writes:
            self.last_w[w] = op
            self.readers[w] = []
        self.ops.append(op)
        return op

    def emit(self):
        nc = self.nc
        st = self.stack
        esem = {e: st.enter_context(nc.semaphore("e_" + e)) for e in ("pe", "act", "dve", "pool")}
        dsem = {e: [st.enter_context(nc.semaphore("d_%s%d" % (e, i))) for i in range(self.NDS)]
                for e in ("sp", "act", "pool")}
        for op in self.ops:
            for d in op.deps:
                if d.dma:
                    continue
                if d.eng == "pe" and op.eng == "pe" and not op.dma:
                    continue
                d.sig = True
        cnt = {e: 0 for e in self.ENGS}
        dcnt = {e: 0 for e in self.ENGS}
        for op in self.ops:
            if op.dma:
                n = dcnt[op.eng]
                dcnt[op.eng] += 1
                op.sem = dsem[op.eng][n % self.NDS]
                op.val = 16 * (n // self.NDS + 1)
                op.prev = 16 * (n // self.NDS)
            elif op.sig:
                cnt[op.eng] += 1
                op.sem = esem[op.eng]
                op.val = cnt[op.eng]
        finals = []
        for e in dsem:
            n = dcnt[e]
            for i in range(min(n, self.NDS)):
                tot = (n - 1 - i) // self.NDS + 1
                finals.append((dsem[e][i], 16 * tot))

        block = st.enter_context(nc.Block())
        decos = {"sp": block.sync, "pe": block.tensor, "act": block.scalar,
                 "dve": block.vector, "pool": block.gpsimd}
        for ename in self.ENGS:
            ops_e = [op for op in self.ops if op.eng == ename]

            def body(eng, ops_e=ops_e, ename=ename):
                waited = {}

                def wait(sem, val):
                    if val > 0 and waited.get(id(sem), 0) < val:
                        eng.wait_ge(sem, val)
                        waited[id(sem)] = val

                for op in ops_e:
                    need = {}
                    for d in op.deps:
                        if (not d.dma) and d.eng == "pe" and ename == "pe" and not op.dma:
                            continue
                        k = id(d.sem)
                        if k not in need or need[k][1] < d.val:
                            need[k] = (d.sem, d.val)
                    if op.dma:
                        wait(op.sem, op.prev)
                    for sem, val in need.values():
                        wait(sem, val)
                    ins = op.fn(eng)
                    if op.dma:
                        ins.then_inc(op.sem, 16)
                    elif op.sig:
                        ins.then_inc(op.sem, 1)
                if ename == "sp":
                    for sem, val in finals:
                        wait(sem, val)

            decos[ename](body)


def _sb(nc, st, name, shape, dt):
    return st.enter_context(nc.sbuf_tensor(name, list(shape), dt))


def _ps(nc, st, name):
    return st.enter_context(nc.psum_tensor(name, [128, 512], F32))


def build_A(nt=S // 512, stage=99):
    TT = 512
    SS = nt * TT
    nc = bass.Bass("TRN2", target_bir_lowering=False)
    xT = nc.dram_tensor("xT", [D, SS], F32, kind="ExternalInput").ap()
    wqkv = nc.dram_tensor("wqkv", [D, 384], F32, kind="ExternalInput").ap()
    ropeC = nc.dram_tensor("ropeC", [128, SS], F32, kind="ExternalInput").ap()
    ropeS = nc.dram_tensor("ropeS", [128, SS], F32, kind="ExternalInput").ap()
    perm = nc.dram_tensor("perm", [128, 128], F32, kind="ExternalInput").ap()
    lam4 = nc.dram_tensor("lam4", [1, 256], F32, kind="ExternalInput").ap()
    subg = nc.dram_tensor("subg", [128, 1], F32, kind="ExternalInput").ap()
    out = nc.dram_tensor("attnT", [128, SS], F32, kind="ExternalOutput").ap()
    xTr = xT.rearrange("(k p) t -> p k t", p=128)
    wr = wqkv.rearrange("(k p) n -> p k n", p=128)

    with ExitStack() as st:
        sc = Sched(nc, st)
        qT = _sb(nc, st, "qT", [128, SS], BF16)
        kT = _sb(nc, st, "kT", [128, SS], BF16)
        vS = _sb(nc, st, "vS", [128, nt * 4, 128], BF16)
        xb = [_sb(nc, st, "xb%d" % i, [128, 8, TT], BF16) for i in range(2)]
        wsb = _sb(nc, st, "wsb", [128, 8, 384], BF16)
        pm = _sb(nc, st, "pm", [128, 128], BF16)
        ones = _sb(nc, st, "ones", [128, 128], BF16)
        rc = [_sb(nc, st, "rc%d" % i, [128, TT], F32) for i in range(2)]
        rs = [_sb(nc, st, "rs%d" % i, [128, TT], F32) for i in range(2)]
        qb = _sb(nc, st, "qb", [128, TT], BF16)
        kb_ = _sb(nc, st, "kb_", [128, TT], BF16)
        t1 = [_sb(nc, st, "t1%d" % i, [128, TT], F32) for i in range(2)]
        t2 = [_sb(nc, st, "t2%d" % i, [128, TT], F32) for i in range(2)]
        pT = [[_sb(nc, st, "pT%d%d" % (c, s), [128, TT], BF16) for s in range(2)] for c in range(2)]
        rz = [_sb(nc, st, "rz%d" % i, [128, TT], F32) for i in range(2)]
        oo = [_sb(nc, st, "oo%d" % i, [128, TT], F32) for i in range(2)]
        od = _sb(nc, st, "od", [128, TT], F32)
        osq = _sb(nc, st, "osq", [128, TT], BF16)
        lnv = _sb(nc, st, "lnv", [128, TT], F32)
        rstd = _sb(nc, st, "rstd", [128, TT], F32)
        res = [_sb(nc, st, "res%d" % i, [128, TT], F32) for i in range(2)]
        lamt = _sb(nc, st, "lamt", [128, 256], F32)
        lprod = _sb(nc, st, "lprod", [128, 128], F32)
        lsum = _sb(nc, st, "lsum", [128, 2], F32)
        lexp = _sb(nc, st, "lexp", [128, 2], F32)
        neglam = _sb(nc, st, "neglam", [128, 1], F32)
        gsc = _sb(nc, st, "gsc", [128, 1], F32)
        gld = _sb(nc, st, "gld", [128, 1], F32)
        B = [_ps(nc, st, "B%d" % i) for i in range(8)]

        sc.add("pool", lambda e: e.dma_start(out=wsb[:], in_=wr), writes=["wsb"], dma=True)
        sc.add("pool", lambda e: e.dma_start(out=pm[:], in_=perm), writes=["pm"], dma=True)
        sc.add("sp", lambda e: e.dma_start(out=lamt[:], in_=lam4.partition_broadcast(128)),
               writes=["lamt"], dma=True)
        sc.add("sp", lambda e: e.dma_start(out=gld[:], in_=subg), writes=["gld"], dma=True)
        sc.add("pool", lambda e: e.memset(ones[:], 1.0), writes=["ones"])
        sc.add("dve", lambda e: e.tensor_tensor(out=lprod[:, 0:64], in0=lamt[:, 0:64], in1=lamt[:, 64:128],
                                                op=ALU.mult), reads=["lamt"], writes=["lprod"])
        sc.add("dve", lambda e: e.tensor_tensor(out=lprod[:, 64:128], in0=lamt[:, 128:192], in1=lamt[:, 192:256],
                                                op=ALU.mult), reads=["lamt"], writes=["lprod"])
        sc.add("dve", lambda e: e.reduce_sum(out=lsum[:, 0:1], in_=lprod[:, 0:64], axis=AX.X),
               reads=["lprod"], writes=["lsum"])
        sc.add("dve", lambda e: e.reduce_sum(out=lsum[:, 1:2], in_=lprod[:, 64:128], axis=AX.X),
               reads=["lprod"], writes=["lsum"])
        sc.add("act", lambda e: e.activation(out=lexp[:], in_=lsum[:], func=AF.Exp),
               reads=["lsum"], writes=["lexp"])
        sc.add("dve", lambda e: e.tensor_tensor(out=neglam[:], in0=lexp[:, 1:2], in1=lexp[:, 0:1],
                                                op=ALU.subtract), reads=["lexp"], writes=["neglam"])
        sc.add("dve", lambda e: e.tensor_scalar(out=neglam[:], in0=neglam[:], scalar1=-LAM_INIT0, scalar2=None,
                                                op0=ALU.add), reads=["neglam"], writes=["neglam"])
        sc.add("dve", lambda e: e.tensor_scalar(out=gsc[:], in0=gld[:], scalar1=(1.0 - LAM_INIT0), scalar2=None,
                                                op0=ALU.mult), reads=["gld"], writes=["gsc"])

        def load_x(tt):
            s = tt % 2
            c0 = tt * TT
            sc.add("pool", lambda e: e.dma_start(out=xb[s][:], in_=xTr[:, :, c0:c0 + TT]),
                   writes=[("xb", s)], dma=True)
            sc.add("sp", lambda e: e.dma_start(out=rc[s][:], in_=ropeC[:, c0:c0 + TT]),
                   writes=[("rc", s)], dma=True)
            sc.add("sp", lambda e: e.dma_start(out=rs[s][:], in_=ropeS[:, c0:c0 + TT]),
                   writes=[("rs", s)], dma=True)

        def mm(outp, lhsT, rhs, start, stop, reads, writes):
            sc.add("pe", lambda e: e.matmul(outp, lhsT, rhs, start=start, stop=stop),
                   reads=reads, writes=writes)

        def qkv(tt):
            s = tt % 2
            c0 = tt * TT
            xs = xb[s]
            for k in range(8):
                mm(B[0][:, :], wsb[:, k, 0:128], xs[:, k, :], k == 0, k == 7, ["wsb", ("xb", s)], [("B", 0)])
            for k in range(8):
                mm(B[1][:, :], wsb[:, k, 128:256], xs[:, k, :], k == 0, k == 7, ["wsb", ("xb", s)], [("B", 1)])
            for sub in range(4):
                for k in range(8):
                    mm(B[2][:, sub * 128:(sub + 1) * 128], xs[:, k, sub * 128:(sub + 1) * 128],
                       wsb[:, k, 256:384], k == 0, k == 7, ["wsb", ("xb", s)], [("B", 2)])
            sc.add("act", lambda e: e.copy(out=qb[:], in_=B[0][:, :]), reads=[("B", 0)], writes=["qb"])
            sc.add("act", lambda e: e.copy(out=kb_[:], in_=B[1][:, :]), reads=[("B", 1)], writes=["kb_"])
            sc.add("act", lambda e: e.copy(out=vS[:, tt * 4:(tt + 1) * 4, :],
                                           in_=B[2][:, :].rearrange("p (a b) -> p a b", a=4)),
                   reads=[("B", 2)], writes=[("vS", tt)])
            mm(B[3][:, :], pm[:, :], qb[:, :], True, True, ["pm", "qb"], [("B", 3)])
            sc.add("dve", lambda e: e.tensor_tensor(out=t1[0][:], in0=B[0][:, :], in1=rc[s][:], op=ALU.mult),
                   reads=[("B", 0), ("rc", s)], writes=["t1q"])
            sc.add("dve", lambda e: e.tensor_tensor(out=t2[0][:], in0=B[3][:, :], in1=rs[s][:], op=ALU.mult),
                   reads=[("B", 3), ("rs", s)], writes=["t2q"])
            sc.add("pool", lambda e: e.tensor_tensor(out=qT[:, c0:c0 + TT], in0=t1[0][:], in1=t2[0][:], op=ALU.add),
                   reads=["t1q", "t2q"], writes=[("qT", tt)])
            mm(B[2][:, :], pm[:, :], kb_[:, :], True, True, ["pm", "kb_"], [("B", 2)])
            sc.add("dve", lambda e: e.tensor_tensor(out=t1[1][:], in0=B[1][:, :], in1=rc[s][:], op=ALU.mult),
                   reads=[("B", 1), ("rc", s)], writes=["t1k"])
            sc.add("dve", lambda e: e.tensor_tensor(out=t2[1][:], in0=B[2][:, :], in1=rs[s][:], op=ALU.mult),
                   reads=[("B", 2), ("rs", s)], writes=["t2k"])
            sc.add("pool", lambda e: e.tensor_tensor(out=kT[:, c0:c0 + TT], in0=t1[1][:], in1=t2[1][:], op=ALU.add),
                   reads=["t1k", "t2k"], writes=[("kT", tt)])

        def scores(qt, kb, slot):
            j = kb - 4 * qt
            f0 = 128 * j if j > 0 else 0
            q0 = qt * TT
            for c in range(2):
                bi = c * 2 + slot
                mm(B[bi][:, f0:TT], kT[c * 64:(c + 1) * 64, kb * 128:(kb + 1) * 128],
                   qT[c * 64:(c + 1) * 64, q0 + f0:q0 + TT], True, True,
                   [("kT", kb // 4), ("qT", qt)], [("B", bi)])

        def expo(qt, kb, slot):
            j = kb - 4 * qt
            f0 = 128 * j if j > 0 else 0
            for c in range(2):
                bi = c * 2 + slot
                sc.add("act", lambda e, c=c, bi=bi: e.activation(out=pT[c][slot][:, f0:TT], in_=B[bi][:, f0:TT],
                                                                 func=AF.Exp, scale=0.125),
                       reads=[("B", bi)], writes=[("pT", c, slot)])
                if j >= 0:
                    sc.add("pool", lambda e, c=c: e.memset(pT[c][slot][64:128, f0:f0 + 64], 0.0),
                           writes=[("pT", c, slot)])

        def pv(qt, kb, slot, last):
            j = kb - 4 * qt
            f0 = 128 * j if j > 0 else 0
            for c in range(2):
                mm(B[4 + c][:, f0:TT], vS[:, kb, :], pT[c][slot][:, f0:TT], kb == 0, kb == last,
                   [("vS", kb // 4), ("pT", c, slot)], [("B", 4 + c)])
                mm(B[6 + c][:, f0:TT], ones[:, :], pT[c][slot][:, f0:TT], kb == 0, kb == last,
                   ["ones", ("pT", c, slot)], [("B", 6 + c)])

        def epilogue(qt):
            q0 = qt * TT
            r = res[qt % 2]
            for c in range(2):
                sc.add("dve", lambda e, c=c: e.reciprocal(out=rz[c][:], in_=B[6 + c][:, :]),
                       reads=[("B", 6 + c)], writes=[("rz", c)])
                sc.add("dve", lambda e, c=c: e.tensor_tensor(out=oo[c][:], in0=B[4 + c][:, :], in1=rz[c][:],
                                                             op=ALU.mult),
                       reads=[("B", 4 + c), ("rz", c)], writes=[("oo", c)])
            sc.add("dve", lambda e: e.scalar_tensor_tensor(out=od[:], in0=oo[1][:], scalar=neglam[:, 0:1],
                                                           in1=oo[0][:], op0=ALU.mult, op1=ALU.add),
                   reads=[("oo", 0), ("oo", 1), "neglam"], writes=["od"])
            sc.add("pool", lambda e: e.tensor_tensor(out=osq[:], in0=od[:], in1=od[:], op=ALU.mult),
                   reads=["od"], writes=["osq"])
            mm(B[0][:, :], ones[:, :], osq[:, :], True, True, ["ones", "osq"], [("B", 0)])
            sc.add("dve", lambda e: e.tensor_scalar(out=lnv[:], in0=B[0][:, :], scalar1=1.0 / 128.0,
                                                    scalar2=LN_EPS, op0=ALU.mult, op1=ALU.add),
                   reads=[("B", 0)], writes=["lnv"])
            sc.add("act", lambda e: e.activation(out=lnv[:], in_=lnv[:], func=AF.Ln),
                   reads=["lnv"], writes=["lnv"])
            sc.add("act", lambda e: e.activation(out=rstd[:], in_=lnv[:], func=AF.Exp, scale=-0.5),
                   reads=["lnv"], writes=["rstd"])
            sc.add("dve", lambda e: e.scalar_tensor_tensor(out=r[:], in0=od[:], scalar=gsc[:, 0:1], in1=rstd[:],
                                                           op0=ALU.mult, op1=ALU.mult),
                   reads=["od", "rstd", "gsc"], writes=[("res", qt % 2)])
            sc.add("sp", lambda e: e.dma_start(out=out[:, q0:q0 + TT], in_=r[:]),
                   reads=[("res", qt % 2)], dma=True)

        load_x(0)
        if stage == 1:
            sc.add("sp", lambda e: e.dma_start(out=out[:, 0:TT], in_=rc[0][:]), reads=[("rc", 0), "neglam", "gsc", "wsb", "pm", "ones", ("xb", 0)], dma=True)
            sc.emit()
            return nc
        for tt in range(nt):
            if tt + 1 < nt:
                load_x(tt + 1)
            qkv(tt)
            if stage == 2:
                sc.add("act", lambda e: e.copy(out=res[0][:], in_=qT[:, 0:TT]), reads=[("qT", 0), ("kT", 0), ("vS", 0)], writes=[("res", 0)])
                sc.add("sp", lambda e: e.dma_start(out=out[:, 0:TT], in_=res[0][:]), reads=[("res", 0)], dma=True)
                sc.emit()
                return nc
            nkb = 4 * tt + 4
            last = nkb - 1
            scores(tt, 0, 0)
            for kb in range(nkb):
                if kb + 1 < nkb:
                    scores(tt, kb + 1, (kb + 1) % 2)
                expo(tt, kb, kb % 2)
                pv(tt, kb, kb % 2, last)
            epilogue(tt)
        sc.emit()
    return nc


def rope_tables(n):
    half = 8
    pos = np.arange(n, dtype=np.float32)
    inv = (ROPE_THETA ** (-np.arange(0, 16, 2, dtype=np.float32) / np.float32(16))).astype(np.float32)
    ang = (pos[:, None] * inv[None, :]).astype(np.float32)
    cos = np.cos(ang).astype(np.float32).T
    sin = np.sin(ang).astype(np.float32).T
    C = np.ones((128, n), np.float32)
    Sg = np.zeros((128, n), np.float32)
    for c in range(2):
        b = c * 64
        C[b:b + 8] = cos
        C[b + 8:b + 16] = cos
        Sg[b:b + 8] = -sin
        Sg[b + 8:b + 16] = sin
    P = np.zeros((128, 128), np.float32)
    for c in range(2):
        b = c * 64
        for i in range(half):
            P[b + i + 8, b + i] = 1.0
            P[b + i, b + i + 8] = 1.0
    return C, Sg, P


def run_A(x, attn_w_qkv, lq1, lk1, lq2, lk2, subln_g, nt=S // 512):
    SS = nt * 512
    xT = np.ascontiguousarray(x.reshape(S, D)[:SS].T)
    C, Sg, P = rope_tables(SS)
    lam4 = np.concatenate([lq1.reshape(-1), lk1.reshape(-1), lq2.reshape(-1), lk2.reshape(-1)]).reshape(1, 256)
    lam4 = np.ascontiguousarray(lam4.astype(np.float32))
    subg = np.ascontiguousarray(subln_g.reshape(128, 1).astype(np.float32))
    w = attn_w_qkv.reshape(D, 3 * D)
    in_maps = []
    for h in range(NCORES):
        wh = np.concatenate([w[:, h * 128:(h + 1) * 128], w[:, D + h * 128:D + (h + 1) * 128],
                             w[:, 2 * D + h * 128:2 * D + (h + 1) * 128]], axis=1)
        in_maps.append({"xT": xT, "wqkv": np.ascontiguousarray(wh), "ropeC": C, "ropeS": Sg, "perm": P,
                        "lam4": lam4, "subg": subg})
    nc = build_A(nt)
    r = run_bass_kernel_spmd(nc, in_maps, core_ids=list(range(NCORES)))
    return np.concatenate([r.results[h]["attnT"] for h in range(NCORES)], axis=0)


TOK = S // NCORES
TG = 1024
TT = 512
WCAP = 6144


def build_BC(mode, tok=TOK):
    ngroups = tok // TG
    ntile = TG // TT
    nc = bass.Bass("TRN2", target_bir_lowering=False)
    hin = nc.dram_tensor("hin", [D, tok], F32, kind="ExternalInput").ap()
    if mode == "B":
        ain = nc.dram_tensor("ain", [D, tok], F32, kind="ExternalInput").ap()
        wmix = nc.dram_tensor("wmix", [D, D], F32, kind="ExternalInput").ap()
    else:
        halo = nc.dram_tensor("halo", [D, 2], F32, kind="ExternalInput").ap()
        w_in = nc.dram_tensor("w_in", [D, 3 * D], F32, kind="ExternalInput").ap()
        cw = nc.dram_tensor("cw", [128, 24], F32, kind="ExternalInput").ap()
        wmix = nc.dram_tensor("wmix", [D, D], F32, kind="ExternalInput").ap()
    wg = nc.dram_tensor("wg", [D, DFF], F32, kind="ExternalInput").ap()
    wu = nc.dram_tensor("wu", [D, DFF], F32, kind="ExternalInput").ap()
    wd = nc.dram_tensor("wd", [DFF, D], F32, kind="ExternalInput").ap()
    lnp = nc.dram_tensor("lnp", [128, 32], F32, kind="ExternalInput").ap()
    hout = nc.dram_tensor("hout", [D, tok], F32, kind="ExternalOutput").ap()

    hin_r = hin.rearrange("(k p) t -> p k t", p=128)
    hout_r = hout.rearrange("(k p) t -> p k t", p=128)
    wmix_r = wmix.rearrange("(k p) n -> p k n", p=128)
    wg_r = wg.rearrange("(k p) n -> p k n", p=128)
    wu_r = wu.rearrange("(k p) n -> p k n", p=128)
    wd_r = wd.rearrange("(k p) n -> p k n", p=128)
    NF = DFF // 128

    with ExitStack() as st:
        sc = Sched(nc, st)
        hres = _sb(nc, st, "hres", [128, 8, TG], F32)
        hb = _sb(nc, st, "hb", [128, 8, TG], BF16)
        ab = _sb(nc, st, "ab", [128, 8, TG], BF16)
        hid = _sb(nc, st, "hid", [128, NF, TG], BF16)
        wsl = [_sb(nc, st, "wsl%d" % i, [128, WCAP], BF16) for i in range(3)]
        zb = _sb(nc, st, "zb", [128, 8, TT], BF16)
        zsq = _sb(nc, st, "zsq", [128, 8, TT], BF16)
        mean = _sb(nc, st, "mean", [128, TT], F32)
        msq = _sb(nc, st, "msq", [128, TT], F32)
        var = _sb(nc, st, "var", [128, TT], F32)
        rstd = _sb(nc, st, "rstd", [128, TT], F32)
        nmr = _sb(nc, st, "nmr", [128, TT], F32)
        tmp = [_sb(nc, st, "tmp%d" % i, [128, TT], F32) for i in range(2)]
        tmp2 = [_sb(nc, st, "tmq%d" % i, [128, TT], F32) for i in range(2)]
        sg = [_sb(nc, st, "sg%d" % i, [128, TT], F32) for i in range(2)]
        ones = _sb(nc, st, "ones", [128, 128], BF16)
        lnt = _sb(nc, st, "lnt", [128, 32], F32)
        B = [_ps(nc, st, "B%d" % i) for i in range(8)]
        if mode == "C":
            cwt = _sb(nc, st, "cwt", [128, 24], F32)
            hbh = _sb(nc, st, "hbh", [128, 8, 2], BF16)
            uprev = _sb(nc, st, "uprev", [128, 8, 2], F32)
            ubuf = [_sb(nc, st, "ubuf%d" % i, [128, TT + 2], F32) for i in range(2)]
            cs = [_sb(nc, st, "cs%d" % i, [128, TT], F32) for i in range(2)]
            ybuf = [_sb(nc, st, "ybuf%d" % i, [128, TT], F32) for i in range(2)]
            hcs = _sb(nc, st, "hcs", [128, 2], F32)

        state = {"bank": 0, "job": 0, "tmp": 0}

        def bank():
            b = state["bank"]
            state["bank"] = (b + 1) % 8
            return b

        def mm(outp, lhsT, rhs, start, stop, reads, writes):
            sc.add("pe", lambda e: e.matmul(outp, lhsT, rhs, start=start, stop=stop),
                   reads=reads, writes=writes)

        sc.add("pool", lambda e: e.memset(ones[:], 1.0), writes=["ones"])
        sc.add("sp", lambda e: e.dma_start(out=lnt[:], in_=lnp), writes=["lnt"], dma=True)
        if mode == "C":
            sc.add("sp", lambda e: e.dma_start(out=cwt[:], in_=cw), writes=["cwt"], dma=True)

        jobs = []

        def run_jobs():
            n = len(jobs)
            base = state["job"]
            if n:
                jobs[0][0]((base) % 3)
            for i in range(n):
                if i + 1 < n:
                    jobs[i + 1][0]((base + i + 1) % 3)
                jobs[i][1]((base + i) % 3)
            state["job"] = base + n
            del jobs[:]

        def wload(slot, view, src):
            sc.add("pool", lambda e: e.dma_start(out=view, in_=src), writes=[("wsl", slot)], dma=True)

        def tcols(t):
            return slice(t * TT, (t + 1) * TT)

        def add_proj_jobs(w_r, src, srckey):
            for nb in range(2):
                def load(slot, nb=nb):
                    v = wsl[slot][:, 0:4096].rearrange("p (k n) -> p k n", k=8)
                    wload(slot, v, w_r[:, :, nb * 512:(nb + 1) * 512])

                def comp(slot, nb=nb):
                    v = wsl[slot][:, 0:4096].rearrange("p (k n) -> p k n", k=8)
                    for t in range(ntile):
                        for j in range(4):
                            n = nb * 4 + j
                            b = bank()
                            for k in range(8):
                                mm(B[b][:, :], v[:, k, j * 128:(j + 1) * 128], src[:, k, tcols(t)], k == 0, k == 7,
                                   [("wsl", slot), (srckey, k, t)], [("B", b)])
                            sc.add("dve", lambda e, n=n, t=t, b=b: e.scalar_tensor_tensor(
                                out=hres[:, n, tcols(t)], in0=hres[:, n, tcols(t)], scalar=ALPHA, in1=B[b][:, :],
                                op0=ALU.mult, op1=ALU.add),
                                reads=[("B", b), ("hres", n, t)], writes=[("hres", n, t)])
                jobs.append((load, comp))

        def layer_norm(q, want_hb):
            for t in range(ntile):
                for k in range(8):
                    sc.add("act", lambda e, k=k, t=t: e.copy(out=zb[:, k, :], in_=hres[:, k, tcols(t)]),
                           reads=[("hres", k, t)], writes=[("zb", k)])
                    sc.add("pool", lambda e, k=k, t=t: e.tensor_tensor(out=zsq[:, k, :], in0=hres[:, k, tcols(t)],
                                                                         in1=hres[:, k, tcols(t)], op=ALU.mult),
                           reads=[("hres", k, t)], writes=[("zsq", k)])
                ba = bank()
                for k in range(8):
                    mm(B[ba][:, :], ones[:, :], zb[:, k, :], k == 0, k == 7, ["ones", ("zb", k)], [("B", ba)])
                bb = bank()
                for k in range(8):
                    mm(B[bb][:, :], ones[:, :], zsq[:, k, :], k == 0, k == 7, ["ones", ("zsq", k)], [("B", bb)])
                sc.add("dve", lambda e, ba=ba: e.tensor_scalar(out=mean[:], in0=B[ba][:, :], scalar1=1.0 / D,
                                                               scalar2=None, op0=ALU.mult),
                       reads=[("B", ba)], writes=["mean"])
                sc.add("dve", lambda e: e.tensor_tensor(out=msq[:], in0=mean[:], in1=mean[:], op=ALU.mult),
                       reads=["mean"], writes=["msq"])
                sc.add("dve", lambda e, bb=bb: e.scalar_tensor_tensor(out=var[:], in0=B[bb][:, :], scalar=1.0 / D,
                                                                      in1=msq[:], op0=ALU.mult, op1=ALU.subtract),
                       reads=[("B", bb), "msq"], writes=["var"])
                sc.add("dve", lambda e: e.tensor_scalar(out=var[:], in0=var[:], scalar1=LN_EPS, scalar2=None,
                                                        op0=ALU.add), reads=["var"], writes=["var"])
                sc.add("act", lambda e: e.activation(out=var[:], in_=var[:], func=AF.Ln),
                       reads=["var"], writes=["var"])
                sc.add("act", lambda e: e.activation(out=rstd[:], in_=var[:], func=AF.Exp, scale=-0.5),
                       reads=["var"], writes=["rstd"])
                sc.add("dve", lambda e: e.scalar_tensor_tensor(out=nmr[:], in0=mean[:], scalar=-1.0, in1=rstd[:],
                                                               op0=ALU.mult, op1=ALU.mult),
                       reads=["mean", "rstd"], writes=["nmr"])
                for k in range(8):
                    i = state["tmp"]
                    state["tmp"] = 1 - i
                    sc.add("dve", lambda e, k=k, t=t, i=i: e.tensor_tensor(out=tmp[i][:], in0=hres[:, k, tcols(t)],
                                                                           in1=rstd[:], op=ALU.mult),
                           reads=[("hres", k, t), "rstd"], writes=[("tmp", i)])
                    sc.add("pool", lambda e, i=i: e.tensor_tensor(out=tmp2[i][:], in0=tmp[i][:], in1=nmr[:],
                                                                  op=ALU.add),
                           reads=[("tmp", i), "nmr"], writes=[("tmp2", i)])
                    gcol = q * 16 + k
                    bcol = q * 16 + 8 + k
                    sc.add("act", lambda e, k=k, t=t, i=i, gcol=gcol, bcol=bcol: e.activation(
                        out=hres[:, k, tcols(t)], in_=tmp2[i][:], func=AF.Identity,
                        scale=lnt[:, gcol:gcol + 1], bias=lnt[:, bcol:bcol + 1]),
                        reads=[("tmp2", i), "lnt"], writes=[("hres", k, t)])
                    if want_hb:
                        sc.add("dve", lambda e, k=k, t=t, i=i, gcol=gcol, bcol=bcol: e.tensor_scalar(
                            out=hb[:, k, tcols(t)], in0=tmp2[i][:], scalar1=lnt[:, gcol:gcol + 1],
                            scalar2=lnt[:, bcol:bcol + 1], op0=ALU.mult, op1=ALU.add),
                            reads=[("tmp2", i), "lnt"], writes=[("hb", k, t)])

        def add_ffn_jobs():
            f0 = 0
            while f0 < NF:
                fb = min(3, NF - f0)

                def load(slot, f0=f0, fb=fb):
                    v = wsl[slot][:, 0:2 * 8 * 384].rearrange("p (g k n) -> p g k n", g=2, k=8)
                    wload(slot, v[:, 0, :, 0:fb * 128], wg_r[:, :, f0 * 128:(f0 + fb) * 128])
                    wload(slot, v[:, 1, :, 0:fb * 128], wu_r[:, :, f0 * 128:(f0 + fb) * 128])

                def comp(slot, f0=f0, fb=fb):
                    v = wsl[slot][:, 0:2 * 8 * 384].rearrange("p (g k n) -> p g k n", g=2, k=8)
                    for t in range(ntile):
                        for j in range(fb):
                            fc = f0 + j
                            bg = bank()
                            for k in range(8):
                                mm(B[bg][:, :], v[:, 0, k, j * 128:(j + 1) * 128], hb[:, k, tcols(t)], k == 0, k == 7,
                                   [("wsl", slot), ("hb", k, t)], [("B", bg)])
                            bu = bank()
                            for k in range(8):
                                mm(B[bu][:, :], v[:, 1, k, j * 128:(j + 1) * 128], hb[:, k, tcols(t)], k == 0, k == 7,
                                   [("wsl", slot), ("hb", k, t)], [("B", bu)])
                            i = state["tmp"]
                            state["tmp"] = 1 - i
                            sc.add("act", lambda e, i=i, bg=bg: e.activation(out=sg[i][:], in_=B[bg][:, :],
                                                                             func=AF.Silu),
                                   reads=[("B", bg)], writes=[("sg", i)])
                            sc.add("dve", lambda e, i=i, bu=bu, fc=fc, t=t: e.tensor_tensor(
                                out=hid[:, fc, tcols(t)], in0=B[bu][:, :], in1=sg[i][:], op=ALU.mult),
                                reads=[("B", bu), ("sg", i)], writes=[("hid", fc, t)])
                jobs.append((load, comp))
                f0 += fb
            for nb in range(4):
                def load(slot, nb=nb):
                    v = wsl[slot][:, 0:NF * 256].rearrange("p (k n) -> p k n", k=NF)
                    wload(slot, v, wd_r[:, :, nb * 256:(nb + 1) * 256])

                def comp(slot, nb=nb):
                    v = wsl[slot][:, 0:NF * 256].rearrange("p (k n) -> p k n", k=NF)
                    for t in range(ntile):
                        for j in range(2):
                            n = nb * 2 + j
                            b = bank()
                            for fc in range(NF):
                                mm(B[b][:, :], v[:, fc, j * 128:(j + 1) * 128], hid[:, fc, tcols(t)], fc == 0,
                                   fc == NF - 1, [("wsl", slot), ("hid", fc, t)], [("B", b)])
                            sc.add("dve", lambda e, n=n, t=t, b=b: e.scalar_tensor_tensor(
                                out=hres[:, n, tcols(t)], in0=hres[:, n, tcols(t)], scalar=ALPHA, in1=B[b][:, :],
                                op0=ALU.mult, op1=ALU.add),
                                reads=[("B", b), ("hres", n, t)], writes=[("hres", n, t)])
                jobs.append((load, comp))

        def add_conv_jobs(g):
            w_in_r = w_in.rearrange("(k p) (g n) -> p k g n", p=128, g=3)
            for jb in range(4):
                def load(slot, jb=jb):
                    v = wsl[slot][:, 0:WCAP].rearrange("p (k g n) -> p k g n", k=8, g=3)
                    for gg in range(3):
                        wload(slot, v[:, :, gg, :], w_in_r[:, :, gg, jb * 256:(jb + 1) * 256])

                def comp(slot, jb=jb):
                    v = wsl[slot][:, 0:WCAP].rearrange("p (k g n) -> p k g n", k=8, g=3)
                    for jj in range(2):
                        ch = jb * 2 + jj
                        cols = slice(jj * 128, (jj + 1) * 128)
                        if g == 0:
                            bc = bank()
                            for k in range(8):
                                mm(B[bc][:, 0:2], v[:, k, 1, cols], hbh[:, k, :], k == 0, k == 7,
                                   [("wsl", slot), "hbh"], [("B", bc)])
                            bx = bank()
                            for k in range(8):
                                mm(B[bx][:, 0:2], v[:, k, 2, cols], hbh[:, k, :], k == 0, k == 7,
                                   [("wsl", slot), "hbh"], [("B", bx)])
                            sc.add("act", lambda e, bc=bc: e.copy(out=hcs[:], in_=B[bc][:, 0:2]),
                                   reads=[("B", bc)], writes=["hcs"])
                            sc.add("dve", lambda e, bx=bx, ch=ch: e.tensor_tensor(out=uprev[:, ch, :], in0=B[bx][:, 0:2],
                                                                                  in1=hcs[:], op=ALU.mult),
                                   reads=[("B", bx), "hcs"], writes=[("uprev", ch)])
                        for t in range(ntile):
                            bbk = bank()
                            for k in range(8):
                                mm(B[bbk][:, :], v[:, k, 0, cols], hb[:, k, tcols(t)], k == 0, k == 7,
                                   [("wsl", slot), ("hb", k, t)], [("B", bbk)])
                            bc = bank()
                            for k in range(8):
                                mm(B[bc][:, :], v[:, k, 1, cols], hb[:, k, tcols(t)], k == 0, k == 7,
                                   [("wsl", slot), ("hb", k, t)], [("B", bc)])
                            bx = bank()
                            for k in range(8):
                                mm(B[bx][:, :], v[:, k, 2, cols], hb[:, k, tcols(t)], k == 0, k == 7,
                                   [("wsl", slot), ("hb", k, t)], [("B", bx)])
                            i = state["tmp"]
                            state["tmp"] = 1 - i
                            u = ubuf[i]
                            sc.add("act", lambda e, i=i, bc=bc: e.copy(out=cs[i][:], in_=B[bc][:, :]),
                                   reads=[("B", bc)], writes=[("cs", i)])
                            sc.add("pool", lambda e, u=u, ch=ch: e.tensor_copy(out=u[:, 0:2], in_=uprev[:, ch, :]),
                                   reads=[("uprev", ch)], writes=[("u", i)])
                            sc.add("dve", lambda e, u=u, i=i, bx=bx: e.tensor_tensor(out=u[:, 2:TT + 2], in0=B[bx][:, :],
                                                                                     in1=cs[i][:], op=ALU.mult),
                                   reads=[("B", bx), ("cs", i)], writes=[("u", i)])
                            sc.add("pool", lambda e, u=u, ch=ch: e.tensor_copy(out=uprev[:, ch, :], in_=u[:, TT:TT + 2]),
                                   reads=[("u", i)], writes=[("uprev", ch)])
                            y = ybuf[i]
                            c3 = ch * 3
                            sc.add("dve", lambda e, u=u, y=y, c3=c3: e.tensor_scalar(
                                out=y[:], in0=u[:, 2:TT + 2], scalar1=cwt[:, c3 + 2:c3 + 3], scalar2=None, op0=ALU.mult),
                                reads=[("u", i), "cwt"], writes=[("y", i)])
                            sc.add("dve", lambda e, u=u, y=y, c3=c3: e.scalar_tensor_tensor(
                                out=y[:], in0=u[:, 1:TT + 1], scalar=cwt[:, c3 + 1:c3 + 2], in1=y[:],
                                op0=ALU.mult, op1=ALU.add), reads=[("u", i), "cwt", ("y", i)], writes=[("y", i)])
                            sc.add("dve", lambda e, u=u, y=y, c3=c3: e.scalar_tensor_tensor(
                                out=y[:], in0=u[:, 0:TT], scalar=cwt[:, c3:c3 + 1], in1=y[:],
                                op0=ALU.mult, op1=ALU.add), reads=[("u", i), "cwt", ("y", i)], writes=[("y", i)])
                            sc.add("dve", lambda e, y=y, bbk=bbk, ch=ch, t=t: e.tensor_tensor(
                                out=ab[:, ch, tcols(t)], in0=B[bbk][:, :], in1=y[:], op=ALU.mult),
                                reads=[("B", bbk), ("y", i)], writes=[("ab", ch, t)])
                jobs.append((load, comp))

        for g in range(ngroups):
            g0 = g * TG
            for t in range(ntile):
                c0 = g0 + t * TT
                sc.add("sp", lambda e, t=t, c0=c0: e.dma_start(out=hres[:, :, tcols(t)], in_=hin_r[:, :, c0:c0 + TT]),
                       writes=[("hres", k, t) for k in range(8)], dma=True)
            if mode == "B":
                ain_r = ain.rearrange("(k p) t -> p k t", p=128)
                for t in range(ntile):
                    c0 = g0 + t * TT
                    sc.add("pool", lambda e, t=t, c0=c0: e.dma_start(out=ab[:, :, tcols(t)], in_=ain_r[:, :, c0:c0 + TT]),
                           writes=[("ab", k, t) for k in range(8)], dma=True)
                add_proj_jobs(wmix_r, ab, "ab")
                run_jobs()
            else:
                for t in range(ntile):
                    c0 = g0 + t * TT
                    sc.add("pool", lambda e, t=t, c0=c0: e.dma_start(out=hb[:, :, tcols(t)], in_=hin_r[:, :, c0:c0 + TT]),
                           writes=[("hb", k, t) for k in range(8)], dma=True)
                if g == 0:
                    halo_r = halo.rearrange("(k p) t -> p k t", p=128)
                    sc.add("pool", lambda e: e.dma_start(out=hbh[:], in_=halo_r), writes=["hbh"], dma=True)
                add_conv_jobs(g)
                add_proj_jobs(wmix_r, ab, "ab")
                run_jobs()
            layer_norm(0, True)
            add_ffn_jobs()
            run_jobs()
            layer_norm(1, False)
            for t in range(ntile):
                c0 = g0 + t * TT
                sc.add("sp", lambda e, t=t, c0=c0: e.dma_start(out=hout_r[:, :, c0:c0 + TT], in_=hres[:, :, tcols(t)]),
                       reads=[("hres", k, t) for k in range(8)], dma=True)
        sc.emit()
    return nc


def _lnp(ln1_g, ln1_b, ln2_g, ln2_b, i):
    cols = []
    for v in (ln1_g[i], ln1_b[i], ln2_g[i], ln2_b[i]):
        cols.append(np.asarray(v, np.float32).reshape(8, 128).T)
    return np.ascontiguousarray(np.concatenate(cols, axis=1))


def run_B(xT, attnT, w_o, wg, wu, wd, lnp):
    in_maps = []
    for c in range(NCORES):
        sl = slice(c * TOK, (c + 1) * TOK)
        in_maps.append({"hin": np.ascontiguousarray(xT[:, sl]), "ain": np.ascontiguousarray(attnT[:, sl]),
                        "wmix": w_o, "wg": wg, "wu": wu, "wd": wd, "lnp": lnp})
    nc = build_BC("B")
    r = run_bass_kernel_spmd(nc, in_maps, core_ids=list(range(NCORES)))
    return np.concatenate([r.results[c]["hout"] for c in range(NCORES)], axis=1)


def run_C(hT, w_in, cw, w_out, wg, wu, wd, lnp):
    in_maps = []
    for c in range(NCORES):
        sl = slice(c * TOK, (c + 1) * TOK)
        halo = np.zeros((D, 2), np.float32) if c == 0 else np.ascontiguousarray(hT[:, c * TOK - 2:c * TOK])
        in_maps.append({"hin": np.ascontiguousarray(hT[:, sl]), "halo": halo, "w_in": w_in, "cw": cw,
                        "wmix": w_out, "wg": wg, "wu": wu, "wd": wd, "lnp": lnp})
    nc = build_BC("C")
    r = run_bass_kernel_spmd(nc, in_maps, core_ids=list(range(NCORES)))
    return np.concatenate([r.results[c]["hout"] for c in range(NCORES)], axis=1)


def kernel3(x, attn_w_qkv, attn_w_o, attn_lambda_q1, attn_lambda_k1, attn_lambda_q2, attn_lambda_k2,
           attn_subln_g, conv_w_in, conv_w, conv_w_out, ffn_w_gate, ffn_w_up, ffn_w_down,
           ln1_g, ln1_b, ln2_g, ln2_b):
    f = lambda a: np.ascontiguousarray(np.asarray(a, np.float32))
    x = f(x)
    attnT = run_A(x, f(attn_w_qkv), f(attn_lambda_q1), f(attn_lambda_k1), f(attn_lambda_q2), f(attn_lambda_k2),
                  f(attn_subln_g))
    xT = np.ascontiguousarray(x.reshape(S, D).T)
    h1T = run_B(xT, attnT, f(attn_w_o)[0], f(ffn_w_gate)[0], f(ffn_w_up)[0], f(ffn_w_down)[0],
                _lnp(ln1_g, ln1_b, ln2_g, ln2_b, 0))
    cw = np.ascontiguousarray(f(conv_w)[0].reshape(3, 8, 128).transpose(2, 1, 0).reshape(128, 24))
    h2T = run_C(h1T, f(conv_w_in)[0], cw, f(conv_w_out)[0], f(ffn_w_gate)[1], f(ffn_w_up)[1], f(ffn_w_down)[1],
                _lnp(ln1_g, ln1_b, ln2_g, ln2_b, 1))
    return np.ascontiguousarray(h2T.T).reshape(1, S, D).astype(np.float32)


def build_T(tok=TOK):
    ngroups = tok // TG
    TGW = TG + 2
    nc = bass.Bass("TRN2", target_bir_lowering=False)
    hin = nc.dram_tensor("hin", [D, tok], F32, kind="ExternalInput").ap()
    xhalo = nc.dram_tensor("xhalo", [D, 2], F32, kind="ExternalInput").ap()
    ain = nc.dram_tensor("ain", [D, tok], F32, kind="ExternalInput").ap()
    ahalo = nc.dram_tensor("ahalo", [D, 2], F32, kind="ExternalInput").ap()
    uflag = nc.dram_tensor("uflag", [128, 1], F32, kind="ExternalInput").ap()
    w_o = nc.dram_tensor("w_o", [D, D], F32, kind="ExternalInput").ap()
    w_in = nc.dram_tensor("w_in", [D, 3 * D], F32, kind="ExternalInput").ap()
    cw = nc.dram_tensor("cw", [128, 24], F32, kind="ExternalInput").ap()
    w_out = nc.dram_tensor("w_out", [D, D], F32, kind="ExternalInput").ap()
    wgs = [nc.dram_tensor("wg%d" % i, [D, DFF], F32, kind="ExternalInput").ap() for i in range(2)]
    wus = [nc.dram_tensor("wu%d" % i, [D, DFF], F32, kind="ExternalInput").ap() for i in range(2)]
    wds = [nc.dram_tensor("wd%d" % i, [DFF, D], F32, kind="ExternalInput").ap() for i in range(2)]
    lnp = nc.dram_tensor("lnp", [128, 64], F32, kind="ExternalInput").ap()
    hout = nc.dram_tensor("hout", [D, tok], F32, kind="ExternalOutput").ap()

    kp = lambda a: a.rearrange("(k p) t -> p k t", p=128)
    hin_r, ain_r, hout_r = kp(hin), kp(ain), kp(hout)
    xhalo_r, ahalo_r = kp(xhalo), kp(ahalo)
    w_o_r, w_out_r = kp(w_o), kp(w_out)
    wg_r = [kp(a) for a in wgs]
    wu_r = [kp(a) for a in wus]
    wd_r = [kp(a) for a in wds]
    w_in_r = w_in.rearrange("(k p) (g n) -> p k g n", p=128, g=3)
    NF = DFF // 128

    with ExitStack() as st:
        sc = Sched(nc, st)
        hres = _sb(nc, st, "hres", [128, 8, TGW], F32)
        hb = _sb(nc, st, "hb", [128, 8, TGW], BF16)
        ab = _sb(nc, st, "ab", [128, 8, TGW], BF16)
        hid = _sb(nc, st, "hid", [128, NF, TGW], BF16)
        wsl = [_sb(nc, st, "wsl%d" % i, [128, WCAP], BF16) for i in range(3)]
        zb = _sb(nc, st, "zb", [128, 8, TT], BF16)
        zsq = _sb(nc, st, "zsq", [128, 8, TT], BF16)
        mean = _sb(nc, st, "mean", [128, TT], F32)
        msq = _sb(nc, st, "msq", [128, TT], F32)
        var = _sb(nc, st, "var", [128, TT], F32)
        rstd = _sb(nc, st, "rstd", [128, TT], F32)
        nmr = _sb(nc, st, "nmr", [128, TT], F32)
        tmp = [_sb(nc, st, "tmp%d" % i, [128, TT], F32) for i in range(2)]
        tmp2 = [_sb(nc, st, "tmq%d" % i, [128, TT], F32) for i in range(2)]
        sg = [_sb(nc, st, "sg%d" % i, [128, TT], F32) for i in range(2)]
        ones = _sb(nc, st, "ones", [128, 128], BF16)
        lnt = _sb(nc, st, "lnt", [128, 64], F32)
        cwt = _sb(nc, st, "cwt", [128, 24], F32)
        ufl = _sb(nc, st, "ufl", [128, 1], F32)
        uprev = _sb(nc, st, "uprev", [128, 8, 2], F32)
        ubuf = [_sb(nc, st, "ubuf%d" % i, [128, TT + 2], F32) for i in range(2)]
        cs = [_sb(nc, st, "cs%d" % i, [128, TT], F32) for i in range(2)]
        ybuf = [_sb(nc, st, "ybuf%d" % i, [128, TT], F32) for i in range(2)]
        hcs = _sb(nc, st, "hcs", [128, 2], F32)
        B = [_ps(nc, st, "B%d" % i) for i in range(8)]

        state = {"bank": 0, "job": 0, "tmp": 0}

        def bank():
            b = state["bank"]
            state["bank"] = (b + 1) % 8
            return b

        def flip():
            i = state["tmp"]
            state["tmp"] = 1 - i
            return i

        def mm(outp, lhsT, rhs, start, stop, reads, writes):
            sc.add("pe", lambda e: e.matmul(outp, lhsT, rhs, start=start, stop=stop),
                   reads=reads, writes=writes)

        sc.add("pool", lambda e: e.memset(ones[:], 1.0), writes=["ones"])
        sc.add("sp", lambda e: e.dma_start(out=lnt[:], in_=lnp), writes=["lnt"], dma=True)
        sc.add("sp", lambda e: e.dma_start(out=cwt[:], in_=cw), writes=["cwt"], dma=True)
        sc.add("sp", lambda e: e.dma_start(out=ufl[:], in_=uflag), writes=["ufl"], dma=True)

        def tiles(g, layer):
            ts = [(0, 0, TT), (1, TT, TT)]
            if g == 0 and layer == 0:
                ts.append((2, TG, 2))
            return ts

        jobs = []

        def run_jobs():
            n = len(jobs)
            cur = state["job"]
            slots = []
            for (ns, _, _) in jobs:
                slots.append([(cur + j) % 3 for j in range(ns)])
                cur += ns
            for i in range(n - 1):
                assert jobs[i][0] + jobs[i + 1][0] <= 3
            if n:
                jobs[0][1](slots[0])
            for i in range(n):
                if i + 1 < n:
                    jobs[i + 1][1](slots[i + 1])
                jobs[i][2](slots[i])
            state["job"] = cur
            del jobs[:]

        def wload(slot, view, src):
            sc.add("pool", lambda e: e.dma_start(out=view, in_=src), writes=[("wsl", slot)], dma=True)

        def add_proj_jobs(w_r, tl):
            def load(slots):
                for nb in range(2):
                    v = wsl[slots[nb]][:, 0:4096].rearrange("p (k n) -> p k n", k=8)
                    wload(slots[nb], v, w_r[:, :, nb * 512:(nb + 1) * 512])

            def comp(slots):
                for (t, c0, w) in tl:
                    for nb in range(2):
                        v = wsl[slots[nb]][:, 0:4096].rearrange("p (k n) -> p k n", k=8)
                        for j in range(4):
                            n = nb * 4 + j
                            b = bank()
                            for k in range(8):
                                mm(B[b][:, 0:w], v[:, k, j * 128:(j + 1) * 128], ab[:, k, c0:c0 + w], k == 0, k == 7,
                                   [("wsl", slots[nb]), ("ab", k, t)], [("B", b)])
                            sc.add("dve", lambda e, n=n, c0=c0, w=w, b=b: e.scalar_tensor_tensor(
                                out=hres[:, n, c0:c0 + w], in0=hres[:, n, c0:c0 + w], scalar=ALPHA, in1=B[b][:, 0:w],
                                op0=ALU.mult, op1=ALU.add),
                                reads=[("B", b), ("hres", n, t)], writes=[("hres", n, t)])
            jobs.append((2, load, comp))

        def layer_norm(q, tl):
            for (t, c0, w) in tl:
                for k in range(8):
                    sc.add("act", lambda e, k=k, c0=c0, w=w: e.activation(out=zsq[:, k, 0:w], in_=hres[:, k, c0:c0 + w],
                                                                          func=AF.Square),
                           reads=[("hres", k, t)], writes=[("zsq", k)])
                    ceng = "dve" if k % 2 == 0 else "pool"
                    sc.add(ceng, lambda e, k=k, c0=c0, w=w: e.tensor_copy(out=zb[:, k, 0:w], in_=hres[:, k, c0:c0 + w]),
                           reads=[("hres", k, t)], writes=[("zb", k)])
                ba = bank()
                for k in range(8):
                    mm(B[ba][:, 0:w], ones[:, :], zb[:, k, 0:w], k == 0, k == 7, ["ones", ("zb", k)], [("B", ba)])
                bb = bank()
                for k in range(8):
                    mm(B[bb][:, 0:w], ones[:, :], zsq[:, k, 0:w], k == 0, k == 7, ["ones", ("zsq", k)], [("B", bb)])
                sc.add("dve", lambda e, ba=ba, w=w: e.tensor_scalar(out=mean[:, 0:w], in0=B[ba][:, 0:w], scalar1=1.0 / D,
                                                                    scalar2=None, op0=ALU.mult),
                       reads=[("B", ba)], writes=["mean"])
                sc.add("dve", lambda e, w=w: e.tensor_tensor(out=msq[:, 0:w], in0=mean[:, 0:w], in1=mean[:, 0:w],
                                                             op=ALU.mult), reads=["mean"], writes=["msq"])
                sc.add("dve", lambda e, bb=bb, w=w: e.scalar_tensor_tensor(
                    out=var[:, 0:w], in0=B[bb][:, 0:w], scalar=1.0 / D, in1=msq[:, 0:w], op0=ALU.mult,
                    op1=ALU.subtract), reads=[("B", bb), "msq"], writes=["var"])
                sc.add("dve", lambda e, w=w: e.tensor_scalar(out=var[:, 0:w], in0=var[:, 0:w], scalar1=LN_EPS,
                                                             scalar2=None, op0=ALU.add), reads=["var"], writes=["var"])
                sc.add("act", lambda e, w=w: e.activation(out=var[:, 0:w], in_=var[:, 0:w], func=AF.Ln),
                       reads=["var"], writes=["var"])
                br = bank()
                sc.add("act", lambda e, w=w, br=br: e.activation(out=B[br][:, 0:w], in_=var[:, 0:w], func=AF.Exp,
                                                                 scale=-0.5),
                       reads=["var"], writes=[("B", br)])
                sc.add("dve", lambda e, w=w, br=br: e.scalar_tensor_tensor(out=nmr[:, 0:w], in0=mean[:, 0:w], scalar=-1.0,
                                                                           in1=B[br][:, 0:w], op0=ALU.mult, op1=ALU.mult),
                       reads=["mean", ("B", br)], writes=["nmr"])
                for k in range(8):
                    i = flip()
                    sc.add("dve", lambda e, k=k, c0=c0, w=w, i=i, br=br: e.tensor_tensor(
                        out=tmp[i][:, 0:w], in0=hres[:, k, c0:c0 + w], in1=B[br][:, 0:w], op=ALU.mult),
                        reads=[("hres", k, t), ("B", br)], writes=[("tmp", i)])
                    sc.add("pool", lambda e, i=i, w=w: e.tensor_tensor(out=tmp2[i][:, 0:w], in0=tmp[i][:, 0:w],
                                                                       in1=nmr[:, 0:w], op=ALU.add),
                           reads=[("tmp", i), "nmr"], writes=[("tmp2", i)])
                    gcol = q * 16 + k
                    bcol = q * 16 + 8 + k
                    sc.add("act", lambda e, k=k, c0=c0, w=w, i=i, gcol=gcol, bcol=bcol: e.activation(
                        out=hres[:, k, c0:c0 + w], in_=tmp2[i][:, 0:w], func=AF.Identity,
                        scale=lnt[:, gcol:gcol + 1], bias=lnt[:, bcol:bcol + 1]),
                        reads=[("tmp2", i), "lnt"], writes=[("hres", k, t)])
                    if k % 2 == 0:
                        sc.add("dve", lambda e, k=k, c0=c0, w=w, i=i, gcol=gcol, bcol=bcol: e.tensor_scalar(
                            out=hb[:, k, c0:c0 + w], in0=tmp2[i][:, 0:w], scalar1=lnt[:, gcol:gcol + 1],
                            scalar2=lnt[:, bcol:bcol + 1], op0=ALU.mult, op1=ALU.add),
                            reads=[("tmp2", i), "lnt"], writes=[("hb", k, t)])
                    else:
                        sc.add("act", lambda e, k=k, c0=c0, w=w, i=i, gcol=gcol, bcol=bcol: e.activation(
                            out=hb[:, k, c0:c0 + w], in_=tmp2[i][:, 0:w], func=AF.Identity,
                            scale=lnt[:, gcol:gcol + 1], bias=lnt[:, bcol:bcol + 1]),
                            reads=[("tmp2", i), "lnt"], writes=[("hb", k, t)])

        def add_ffn_jobs(layer, tl):
            f0 = 0
            while f0 < NF:
                fb = min(3, NF - f0)

                def load(slots, f0=f0, fb=fb):
                    slot = slots[0]
                    v = wsl[slot][:, 0:2 * 8 * 384].rearrange("p (g k n) -> p g k n", g=2, k=8)
                    wload(slot, v[:, 0, :, 0:fb * 128], wg_r[layer][:, :, f0 * 128:(f0 + fb) * 128])
                    wload(slot, v[:, 1, :, 0:fb * 128], wu_r[layer][:, :, f0 * 128:(f0 + fb) * 128])

                def comp(slots, f0=f0, fb=fb):
                    slot = slots[0]
                    v = wsl[slot][:, 0:2 * 8 * 384].rearrange("p (g k n) -> p g k n", g=2, k=8)
                    for (t, c0, w) in tl:
                        for j in range(fb):
                            fc = f0 + j
                            bg = bank()
                            for k in range(8):
                                mm(B[bg][:, 0:w], v[:, 0, k, j * 128:(j + 1) * 128], hb[:, k, c0:c0 + w], k == 0, k == 7,
                                   [("wsl", slot), ("hb", k, t)], [("B", bg)])
                            bu = bank()
                            for k in range(8):
                                mm(B[bu][:, 0:w], v[:, 1, k, j * 128:(j + 1) * 128], hb[:, k, c0:c0 + w], k == 0, k == 7,
                                   [("wsl", slot), ("hb", k, t)], [("B", bu)])
                            i = flip()
                            sc.add("act", lambda e, i=i, bg=bg, w=w: e.activation(out=sg[i][:, 0:w], in_=B[bg][:, 0:w],
                                                                                  func=AF.Silu),
                                   reads=[("B", bg)], writes=[("sg", i)])
                            sc.add("dve", lambda e, i=i, bu=bu, fc=fc, c0=c0, w=w: e.tensor_tensor(
                                out=hid[:, fc, c0:c0 + w], in0=B[bu][:, 0:w], in1=sg[i][:, 0:w], op=ALU.mult),
                                reads=[("B", bu), ("sg", i)], writes=[("hid", fc, t)])
                jobs.append((1, load, comp))
                f0 += fb
            for nb in range(4):
                def load(slots, nb=nb):
                    slot = slots[0]
                    v = wsl[slot][:, 0:NF * 256].rearrange("p (k n) -> p k n", k=NF)
                    wload(slot, v, wd_r[layer][:, :, nb * 256:(nb + 1) * 256])

                def comp(slots, nb=nb):
                    slot = slots[0]
                    v = wsl[slot][:, 0:NF * 256].rearrange("p (k n) -> p k n", k=NF)
                    for (t, c0, w) in tl:
                        for j in range(2):
                            n = nb * 2 + j
                            b = bank()
                            for fc in range(NF):
                                mm(B[b][:, 0:w], v[:, fc, j * 128:(j + 1) * 128], hid[:, fc, c0:c0 + w], fc == 0,
                                   fc == NF - 1, [("wsl", slot), ("hid", fc, t)], [("B", b)])
                            sc.add("dve", lambda e, n=n, c0=c0, w=w, b=b: e.scalar_tensor_tensor(
                                out=hres[:, n, c0:c0 + w], in0=hres[:, n, c0:c0 + w], scalar=ALPHA, in1=B[b][:, 0:w],
                                op0=ALU.mult, op1=ALU.add),
                                reads=[("B", b), ("hres", n, t)], writes=[("hres", n, t)])
                jobs.append((1, load, comp))

        def add_conv_jobs(g):
            for jb in range(4):
                def load(slots, jb=jb):
                    slot = slots[0]
                    v = wsl[slot][:, 0:WCAP].rearrange("p (k g n) -> p k g n", k=8, g=3)
                    for gg in range(3):
                        wload(slot, v[:, :, gg, :], w_in_r[:, :, gg, jb * 256:(jb + 1) * 256])

                def comp(slots, jb=jb):
                    slot = slots[0]
                    v = wsl[slot][:, 0:WCAP].rearrange("p (k g n) -> p k g n", k=8, g=3)
                    for jj in range(2):
                        ch = jb * 2 + jj
                        cols = slice(jj * 128, (jj + 1) * 128)
                        if g == 0:
                            bc = bank()
                            for k in range(8):
                                mm(B[bc][:, 0:2], v[:, k, 1, cols], hb[:, k, TG:TG + 2], k == 0, k == 7,
                                   [("wsl", slot), ("hb", k, 2)], [("B", bc)])
                            bx = bank()
                            for k in range(8):
                                mm(B[bx][:, 0:2], v[:, k, 2, cols], hb[:, k, TG:TG + 2], k == 0, k == 7,
                                   [("wsl", slot), ("hb", k, 2)], [("B", bx)])
                            sc.add("act", lambda e, bc=bc: e.copy(out=hcs[:], in_=B[bc][:, 0:2]),
                                   reads=[("B", bc)], writes=["hcs"])
                            sc.add("dve", lambda e, bx=bx, ch=ch: e.scalar_tensor_tensor(
                                out=uprev[:, ch, :], in0=B[bx][:, 0:2], scalar=ufl[:, 0:1], in1=hcs[:],
                                op0=ALU.mult, op1=ALU.mult),
                                reads=[("B", bx), "hcs", "ufl"], writes=[("uprev", ch)])
                        for (t, c0, w) in tiles(g, 1):
                            bbk = bank()
                            for k in range(8):
                                mm(B[bbk][:, :], v[:, k, 0, cols], hb[:, k, c0:c0 + w], k == 0, k == 7,
                                   [("wsl", slot), ("hb", k, t)], [("B", bbk)])
                            bc = bank()
                            for k in range(8):
                                mm(B[bc][:, :], v[:, k, 1, cols], hb[:, k, c0:c0 + w], k == 0, k == 7,
                                   [("wsl", slot), ("hb", k, t)], [("B", bc)])
                            bx = bank()
                            for k in range(8):
                                mm(B[bx][:, :], v[:, k, 2, cols], hb[:, k, c0:c0 + w], k == 0, k == 7,
                                   [("wsl", slot), ("hb", k, t)], [("B", bx)])
                            i = flip()
                            u = ubuf[i]
                            y = ybuf[i]
                            c3 = ch * 3
                            sc.add("act", lambda e, i=i, bc=bc: e.copy(out=cs[i][:], in_=B[bc][:, :]),
                                   reads=[("B", bc)], writes=[("cs", i)])
                            sc.add("pool", lambda e, u=u, ch=ch: e.tensor_copy(out=u[:, 0:2], in_=uprev[:, ch, :]),
                                   reads=[("uprev", ch)], writes=[("u", i)])
                            sc.add("dve", lambda e, u=u, i=i, bx=bx: e.tensor_tensor(out=u[:, 2:TT + 2], in0=B[bx][:, :],
                                                                                     in1=cs[i][:], op=ALU.mult),
                                   reads=[("B", bx), ("cs", i)], writes=[("u", i)])
                            sc.add("pool", lambda e, u=u, ch=ch: e.tensor_copy(out=uprev[:, ch, :], in_=u[:, TT:TT + 2]),
                                   reads=[("u", i)], writes=[("uprev", ch)])
                            sc.add("dve", lambda e, u=u, y=y, c3=c3: e.tensor_scalar(
                                out=y[:], in0=u[:, 2:TT + 2], scalar1=cwt[:, c3 + 2:c3 + 3], scalar2=None, op0=ALU.mult),
                                reads=[("u", i), "cwt"], writes=[("y", i)])
                            sc.add("dve", lambda e, u=u, y=y, c3=c3: e.scalar_tensor_tensor(
                                out=y[:], in0=u[:, 1:TT + 1], scalar=cwt[:, c3 + 1:c3 + 2], in1=y[:],
                                op0=ALU.mult, op1=ALU.add), reads=[("u", i), "cwt", ("y", i)], writes=[("y", i)])
                            sc.add("dve", lambda e, u=u, y=y, c3=c3: e.scalar_tensor_tensor(
                                out=y[:], in0=u[:, 0:TT], scalar=cwt[:, c3:c3 + 1], in1=y[:],
                                op0=ALU.mult, op1=ALU.add), reads=[("u", i), "cwt", ("y", i)], writes=[("y", i)])
                            sc.add("dve", lambda e, y=y, bbk=bbk, ch=ch, c0=c0, w=w: e.tensor_tensor(
                                out=ab[:, ch, c0:c0 + w], in0=B[bbk][:, :], in1=y[:], op=ALU.mult),
                                reads=[("B", bbk), ("y", i)], writes=[("ab", ch, t)])
                jobs.append((1, load, comp))

        for g in range(ngroups):
            g0 = g * TG
            tl0 = tiles(g, 0)
            tl1 = tiles(g, 1)
            for (t, c0, w) in tl0:
                if t < 2:
                    sc.add("sp", lambda e, c0=c0, w=w, g0=g0: e.dma_start(out=hres[:, :, c0:c0 + w],
                                                                   in_=hin_r[:, :, g0 + c0:g0 + c0 + w]),
                           writes=[("hres", k, t) for k in range(8)], dma=True)
                    sc.add("pool", lambda e, c0=c0, w=w, g0=g0: e.dma_start(out=ab[:, :, c0:c0 + w],
                                                                     in_=ain_r[:, :, g0 + c0:g0 + c0 + w]),
                           writes=[("ab", k, t) for k in range(8)], dma=True)
                else:
                    sc.add("sp", lambda e, c0=c0, w=w: e.dma_start(out=hres[:, :, c0:c0 + w], in_=xhalo_r),
                           writes=[("hres", k, t) for k in range(8)], dma=True)
                    sc.add("pool", lambda e, c0=c0, w=w: e.dma_start(out=ab[:, :, c0:c0 + w], in_=ahalo_r),
                           writes=[("ab", k, t) for k in range(8)], dma=True)
            add_proj_jobs(w_o_r, tl0)
            run_jobs()
            layer_norm(0, tl0)
            add_ffn_jobs(0, tl0)
            run_jobs()
            layer_norm(1, tl0)
            add_conv_jobs(g)
            add_proj_jobs(w_out_r, tl1)
            run_jobs()
            layer_norm(2, tl1)
            add_ffn_jobs(1, tl1)
            run_jobs()
            layer_norm(3, tl1)
            for (t, c0, w) in tl1:
                sc.add("sp", lambda e, c0=c0, w=w, g0=g0: e.dma_start(out=hout_r[:, :, g0 + c0:g0 + c0 + w],
                                                               in_=hres[:, :, c0:c0 + w]),
                       reads=[("hres", k, t) for k in range(8)], dma=True)
        sc.emit()
    return nc


def run_T(xT, attnT, inp):
    f = lambda a: np.ascontiguousarray(np.asarray(a, np.float32))
    lnp = np.ascontiguousarray(np.concatenate(
        [_lnp(inp["ln1_g"], inp["ln1_b"], inp["ln2_g"], inp["ln2_b"], i) for i in range(2)], axis=1))
    cw = np.ascontiguousarray(f(inp["conv_w"])[0].reshape(3, 8, 128).transpose(2, 1, 0).reshape(128, 24))
    shared = {"w_o": f(inp["attn_w_o"])[0], "w_in": f(inp["conv_w_in"])[0], "cw": cw, "w_out": f(inp["conv_w_out"])[0],
              "wg0": f(inp["ffn_w_gate"])[0], "wu0": f(inp["ffn_w_up"])[0], "wd0": f(inp["ffn_w_down"])[0],
              "wg1": f(inp["ffn_w_gate"])[1], "wu1": f(inp["ffn_w_up"])[1], "wd1": f(inp["ffn_w_down"])[1], "lnp": lnp}
    in_maps = []
    for c in range(NCORES):
        sl = slice(c * TOK, (c + 1) * TOK)
        if c == 0:
            xh = np.zeros((D, 2), np.float32)
            ah = np.zeros((D, 2), np.float32)
        else:
            xh = np.ascontiguousarray(xT[:, c * TOK - 2:c * TOK])
            ah = np.ascontiguousarray(attnT[:, c * TOK - 2:c * TOK])
        m = {"hin": np.ascontiguousarray(xT[:, sl]), "ain": np.ascontiguousarray(attnT[:, sl]), "xhalo": xh,
             "ahalo": ah, "uflag": np.full((128, 1), 0.0 if c == 0 else 1.0, np.float32)}
        m.update(shared)
        in_maps.append(m)
    nc = build_T()
    r = run_bass_kernel_spmd(nc, in_maps, core_ids=list(range(NCORES)))
    return np.concatenate([r.results[c]["hout"] for c in range(NCORES)], axis=1)


def kernel(**inp):
    f = lambda a: np.ascontiguousarray(np.asarray(a, np.float32))
    x = f(inp["x"])
    attnT = run_A(x, f(inp["attn_w_qkv"]), f(inp["attn_lambda_q1"]), f(inp["attn_lambda_k1"]),
                  f(inp["attn_lambda_q2"]), f(inp["attn_lambda_k2"]), f(inp["attn_subln_g"]))
    xT = np.ascontiguousarray(x.reshape(S, D).T)
    h2T = run_T(xT, attnT, inp)
    return np.ascontiguousarray(h2T.T).reshape(1, S, D).astype(np.float32)
```

```python
import math
import os
from contextlib import ExitStack

import numpy as np
import concourse.bass as bass
import concourse.mybir as mybir
from concourse.bass_utils import run_bass_kernel_spmd

F32 = mybir.dt.float32
BF16 = mybir.dt.bfloat16
AF = mybir.ActivationFunctionType
ALU = mybir.AluOpType
AX = mybir.AxisListType

D = 1024
S = 16384
NCORES = 8
DFF = 2816
ALPHA = (2 * 2) ** 0.25
LN_EPS = 1e-5
ROPE_THETA = 500000.0
LAM_INIT0 = 0.8 - 0.6 * math.exp(-0.3 * 0)


class _Op:
    __slots__ = ("eng", "fn", "deps", "dma", "sig", "sem", "val", "prev")


class Sched:
    ENGS = ("sp", "pe", "act", "dve", "pool")
    NDS = 12

    def __init__(self, nc, stack):
        self.nc = nc
        self.stack = stack
        self.ops = []
        self.last_w = {}
        self.readers = {}

    def add(self, eng, fn, reads=(), writes=(), dma=False):
        op = _Op()
        op.eng, op.fn, op.dma, op.sig = eng, fn, dma, False
        op.sem = None
        op.val = 0
        op.prev = 0
        deps = []
        seen = set()

        def dep(o):
            if o is not None and id(o) not in seen:
                seen.add(id(o))
                deps.append(o)

        for r in reads:
            dep(self.last_w.get(r))
            if isinstance(r, tuple) and r[0] == "B":
                for rd in self.readers.get(r, ()):
                    if rd.eng != eng:
                        dep(rd)
        for w in writes:
            dep(self.last_w.get(w))
            for rd in self.readers.get(w, ()):
                dep(rd)
        op.deps = deps
        for r in reads:
            self.readers.setdefault(r, []).append(op)
        for w in writes:
            self.last_w[w] = op
            self.readers[w] = []
        self.ops.append(op)
        return op

    def emit(self):
        nc = self.nc
        st = self.stack
        esem = {e: st.enter_context(nc.semaphore("e_" + e)) for e in ("pe", "act", "dve", "pool")}
        dsem = {e: [st.enter_context(nc.semaphore("d_%s%d" % (e, i))) for i in range(self.NDS)]
                for e in ("sp", "act", "pool")}
        for op in self.ops:
            for d in op.deps:
                if d.dma:
                    continue
                if d.eng == "pe" and op.eng == "pe" and not op.dma:
                    continue
                d.sig = True
        cnt = {e: 0 for e in self.ENGS}
        dcnt = {e: 0 for e in self.ENGS}
        for op in self.ops:
            if op.dma:
                n = dcnt[op.eng]
                dcnt[op.eng] += 1
                op.sem = dsem[op.eng][n % self.NDS]
                op.val = 16 * (n // self.NDS + 1)
                op.prev = 16 * (n // self.NDS)
            elif op.sig:
                cnt[op.eng] += 1
                op.sem = esem[op.eng]
                op.val = cnt[op.eng]
        finals = []
        for e in dsem:
            n = dcnt[e]
            for i in range(min(n, self.NDS)):
                tot = (n - 1 - i) // self.NDS + 1
                finals.append((dsem[e][i], 16 * tot))

        block = st.enter_context(nc.Block())
        decos = {"sp": block.sync, "pe": block.tensor, "act": block.scalar,
                 "dve": block.vector, "pool": block.gpsimd}
        for ename in self.ENGS:
            ops_e = [op for op in self.ops if op.eng == ename]

            def body(eng, ops_e=ops_e, ename=ename):
                waited = {}

                def wait(sem, val):
                    if val > 0 and waited.get(id(sem), 0) < val:
                        eng.wait_ge(sem, val)
                        waited[id(sem)] = val

                for op in ops_e:
                    need = {}
                    for d in op.deps:
                        if (not d.dma) and d.eng == "pe" and ename == "pe" and not op.dma:
                            continue
                        k = id(d.sem)
                        if k not in need or need[k][1] < d.val:
                            need[k] = (d.sem, d.val)
                    if op.dma:
                        wait(op.sem, op.prev)
                    for sem, val in need.values():
                        wait(sem, val)
                    ins = op.fn(eng)
                    if op.dma:
                        ins.then_inc(op.sem, 16)
                    elif op.sig:
                        ins.then_inc(op.sem, 1)
                if ename == "sp":
                    for sem, val in finals:
                        wait(sem, val)

            decos[ename](body)


def _sb(nc, st, name, shape, dt):
    return st.enter_context(nc.sbuf_tensor(name, list(shape), dt))


def _ps(nc, st, name):
    return st.enter_context(nc.psum_tensor(name, [128, 512], F32))


def build_A(nt=S // 512, stage=99):
    TT = 512
    SS = nt * TT
    nc = bass.Bass("TRN2", target_bir_lowering=False)
    xT = nc.dram_tensor("xT", [D, SS], F32, kind="ExternalInput").ap()
    wqkv = nc.dram_tensor("wqkv", [D, 384], F32, kind="ExternalInput").ap()
    ropeC = nc.dram_tensor("ropeC", [128, SS], F32, kind="ExternalInput").ap()
    ropeS = nc.dram_tensor("ropeS", [128, SS], F32, kind="ExternalInput").ap()
    perm = nc.dram_tensor("perm", [128, 128], F32, kind="ExternalInput").ap()
    lam4 = nc.dram_tensor("lam4", [1, 256], F32, kind="ExternalInput").ap()
    subg = nc.dram_tensor("subg", [128, 1], F32, kind="ExternalInput").ap()
    out = nc.dram_tensor("attnT", [128, SS], F32, kind="ExternalOutput").ap()
    xTr = xT.rearrange("(k p) t -> p k t", p=128)
    wr = wqkv.rearrange("(k p) n -> p k n", p=128)

    with ExitStack() as st:
        sc = Sched(nc, st)
        qT = _sb(nc, st, "qT", [128, SS], BF16)
        kT = _sb(nc, st, "kT", [128, SS], BF16)
        vS = _sb(nc, st, "vS", [128, nt * 4, 128], BF16)
        xb = [_sb(nc, st, "xb%d" % i, [128, 8, TT], BF16) for i in range(2)]
        wsb = _sb(nc, st, "wsb", [128, 8, 384], BF16)
        pm = _sb(nc, st, "pm", [128, 128], BF16)
        ones = _sb(nc, st, "ones", [128, 128], BF16)
        rc = [_sb(nc, st, "rc%d" % i, [128, TT], F32) for i in range(2)]
        rs = [_sb(nc, st, "rs%d" % i, [128, TT], F32) for i in range(2)]
        qb = _sb(nc, st, "qb", [128, TT], BF16)
        kb_ = _sb(nc, st, "kb_", [128, TT], BF16)
        t1 = [_sb(nc, st, "t1%d" % i, [128, TT], F32) for i in range(2)]
        t2 = [_sb(nc, st, "t2%d" % i, [128, TT], F32) for i in range(2)]
        pT = [[_sb(nc, st, "pT%d%d" % (c, s), [128, TT], BF16) for s in range(2)] for c in range(2)]
        rz = [_sb(nc, st, "rz%d" % i, [128, TT], F32) for i in range(2)]
        oo = [_sb(nc, st, "oo%d" % i, [128, TT], F32) for i in range(2)]
        od = _sb(nc, st, "od", [128, TT], F32)
        osq = _sb(nc, st, "osq", [128, TT], BF16)
        lnv = _sb(nc, st, "lnv", [128, TT], F32)
        rstd = _sb(nc, st, "rstd", [128, TT], F32)
        res = [_sb(nc, st, "res%d" % i, [128, TT], F32) for i in range(2)]
        lamt = _sb(nc, st, "lamt", [128, 256], F32)
        lprod = _sb(nc, st, "lprod", [128, 128], F32)
        lsum = _sb(nc, st, "lsum", [128, 2], F32)
        lexp = _sb(nc, st, "lexp", [128, 2], F32)
        neglam = _sb(nc, st, "neglam", [128, 1], F32)
        gsc = _sb(nc, st, "gsc", [128, 1], F32)
        gld = _sb(nc, st, "gld", [128, 1], F32)
        B = [_ps(nc, st, "B%d" % i) for i in range(8)]

        sc.add("pool", lambda e: e.dma_start(out=wsb[:], in_=wr), writes=["wsb"], dma=True)
        sc.add("pool", lambda e: e.dma_start(out=pm[:], in_=perm), writes=["pm"], dma=True)
        sc.add("sp", lambda e: e.dma_start(out=lamt[:], in_=lam4.partition_broadcast(128)),
               writes=["lamt"], dma=True)
        sc.add("sp", lambda e: e.dma_start(out=gld[:], in_=subg), writes=["gld"], dma=True)
        sc.add("pool", lambda e: e.memset(ones[:], 1.0), writes=["ones"])
        sc.add("dve", lambda e: e.tensor_tensor(out=lprod[:, 0:64], in0=lamt[:, 0:64], in1=lamt[:, 64:128],
                                                op=ALU.mult), reads=["lamt"], writes=["lprod"])
        sc.add("dve", lambda e: e.tensor_tensor(out=lprod[:, 64:128], in0=lamt[:, 128:192], in1=lamt[:, 192:256],
                                                op=ALU.mult), reads=["lamt"], writes=["lprod"])
        sc.add("dve", lambda e: e.reduce_sum(out=lsum[:, 0:1], in_=lprod[:, 0:64], axis=AX.X),
               reads=["lprod"], writes=["lsum"])
        sc.add("dve", lambda e: e.reduce_sum(out=lsum[:, 1:2], in_=lprod[:, 64:128], axis=AX.X),
               reads=["lprod"], writes=["lsum"])
        sc.add("act", lambda e: e.activation(out=lexp[:], in_=lsum[:], func=AF.Exp),
               reads=["lsum"], writes=["lexp"])
        sc.add("dve", lambda e: e.tensor_tensor(out=neglam[:], in0=lexp[:, 1:2], in1=lexp[:, 0:1],
                                                op=ALU.subtract), reads=["lexp"], writes=["neglam"])
        sc.add("dve", lambda e: e.tensor_scalar(out=neglam[:], in0=neglam[:], scalar1=-LAM_INIT0, scalar2=None,
                                                op0=ALU.add), reads=["neglam"], writes=["neglam"])
        sc.add("dve", lambda e: e.tensor_scalar(out=gsc[:], in0=gld[:], scalar1=(1.0 - LAM_INIT0), scalar2=None,
                                                op0=ALU.mult), reads=["gld"], writes=["gsc"])

        def load_x(tt):
            s = tt % 2
            c0 = tt * TT
            sc.add("pool", lambda e: e.dma_start(out=xb[s][:], in_=xTr[:, :, c0:c0 + TT]),
                   writes=[("xb", s)], dma=True)
            sc.add("sp", lambda e: e.dma_start(out=rc[s][:], in_=ropeC[:, c0:c0 + TT]),
                   writes=[("rc", s)], dma=True)
            sc.add("sp", lambda e: e.dma_start(out=rs[s][:], in_=ropeS[:, c0:c0 + TT]),
                   writes=[("rs", s)], dma=True)

        def mm(outp, lhsT, rhs, start, stop, reads, writes):
            sc.add("pe", lambda e: e.matmul(outp, lhsT, rhs, start=start, stop=stop),
                   reads=reads, writes=writes)

        def qkv_mm(tt):
            s = tt % 2
            c0 = tt * TT
            xs = xb[s]
            for k in range(8):
                mm(B[0][:, :], wsb[:, k, 0:128], xs[:, k, :], k == 0, k == 7, ["wsb", ("xb", s)], [("B", 0)])
            for k in range(8):
                mm(B[1][:, :], wsb[:, k, 128:256], xs[:, k, :], k == 0, k == 7, ["wsb", ("xb", s)], [("B", 1)])
            for sub in range(4):
                for k in range(8):
                    mm(B[2][:, sub * 128:(sub + 1) * 128], xs[:, k, sub * 128:(sub + 1) * 128],
                       wsb[:, k, 256:384], k == 0, k == 7, ["wsb", ("xb", s)], [("B", 2)])

        def qkv_post(tt):
            s = tt % 2
            c0 = tt * TT
            sc.add("act", lambda e: e.copy(out=qb[:], in_=B[0][:, :]), reads=[("B", 0)], writes=["qb"])
            sc.add("act", lambda e: e.copy(out=kb_[:], in_=B[1][:, :]), reads=[("B", 1)], writes=["kb_"])
            sc.add("act", lambda e: e.copy(out=vS[:, tt * 4:(tt + 1) * 4, :],
                                           in_=B[2][:, :].rearrange("p (a b) -> p a b", a=4)),
                   reads=[("B", 2)], writes=[("vS", tt)])
            mm(B[3][:, :], pm[:, :], qb[:, :], True, True, ["pm", "qb"], [("B", 3)])
            sc.add("dve", lambda e: e.tensor_tensor(out=t1[0][:], in0=B[0][:, :], in1=rc[s][:], op=ALU.mult),
                   reads=[("B", 0), ("rc", s)], writes=["t1q"])
            sc.add("dve", lambda e: e.tensor_tensor(out=t2[0][:], in0=B[3][:, :], in1=rs[s][:], op=ALU.mult),
                   reads=[("B", 3), ("rs", s)], writes=["t2q"])
            sc.add("pool", lambda e: e.tensor_tensor(out=qT[:, c0:c0 + TT], in0=t1[0][:], in1=t2[0][:], op=ALU.add),
                   reads=["t1q", "t2q"], writes=[("qT", tt)])
            mm(B[2][:, :], pm[:, :], kb_[:, :], True, True, ["pm", "kb_"], [("B", 2)])
            sc.add("dve", lambda e: e.tensor_tensor(out=t1[1][:], in0=B[1][:, :], in1=rc[s][:], op=ALU.mult),
                   reads=[("B", 1), ("rc", s)], writes=["t1k"])
            sc.add("dve", lambda e: e.tensor_tensor(out=t2[1][:], in0=B[2][:, :], in1=rs[s][:], op=ALU.mult),
                   reads=[("B", 2), ("rs", s)], writes=["t2k"])
            sc.add("pool", lambda e: e.tensor_tensor(out=kT[:, c0:c0 + TT], in0=t1[1][:], in1=t2[1][:], op=ALU.add),
                   reads=["t1k", "t2k"], writes=[("kT", tt)])

        def scores(qt, kb, slot):
            j = kb - 4 * qt
            f0 = 128 * j if j > 0 else 0
            q0 = qt * TT
            for c in range(2):
                bi = c * 2 + slot
                mm(B[bi][:, f0:TT], kT[c * 64:(c + 1) * 64, kb * 128:(kb + 1) * 128],
                   qT[c * 64:(c + 1) * 64, q0 + f0:q0 + TT], True, True,
                   [("kT", kb // 4), ("qT", qt)], [("B", bi)])

        def expo(qt, kb, slot):
            j = kb - 4 * qt
            f0 = 128 * j if j > 0 else 0
            for c in range(2):
                bi = c * 2 + slot
                sc.add("act", lambda e, c=c, bi=bi: e.activation(out=pT[c][slot][:, f0:TT], in_=B[bi][:, f0:TT],
                                                                 func=AF.Exp, scale=0.125),
                       reads=[("B", bi)], writes=[("pT", c, slot)])
                if j >= 0:
                    sc.add("pool", lambda e, c=c: e.memset(pT[c][slot][64:128, f0:f0 + 64], 0.0),
                           writes=[("pT", c, slot)])

        def pv(qt, kb, slot, last):
            j = kb - 4 * qt
            f0 = 128 * j if j > 0 else 0
            for c in range(2):
                mm(B[4 + c][:, f0:TT], vS[:, kb, :], pT[c][slot][:, f0:TT], kb == 0, kb == last,
                   [("vS", kb // 4), ("pT", c, slot)], [("B", 4 + c)])
                mm(B[6 + c][:, f0:TT], ones[:, :], pT[c][slot][:, f0:TT], kb == 0, kb == last,
                   ["ones", ("pT", c, slot)], [("B", 6 + c)])

        def epilogue_pre(qt):
            q0 = qt * TT
            r = res[qt % 2]
            for c in range(2):
                sc.add("dve", lambda e, c=c: e.reciprocal(out=rz[c][:], in_=B[6 + c][:, :]),
                       reads=[("B", 6 + c)], writes=[("rz", c)])
                sc.add("dve", lambda e, c=c: e.tensor_tensor(out=oo[c][:], in0=B[4 + c][:, :], in1=rz[c][:],
                                                             op=ALU.mult),
                       reads=[("B", 4 + c), ("rz", c)], writes=[("oo", c)])
            sc.add("dve", lambda e: e.scalar_tensor_tensor(out=od[:], in0=oo[1][:], scalar=neglam[:, 0:1],
                                                           in1=oo[0][:], op0=ALU.mult, op1=ALU.add),
                   reads=[("oo", 0), ("oo", 1), "neglam"], writes=["od"])
            sc.add("pool", lambda e: e.tensor_tensor(out=osq[:], in0=od[:], in1=od[:], op=ALU.mult),
                   reads=["od"], writes=["osq"])

        def epilogue_post(qt):
            q0 = qt * TT
            r = res[qt % 2]
            mm(B[3][:, :], ones[:, :], osq[:, :], True, True, ["ones", "osq"], [("B", 3)])
            sc.add("dve", lambda e: e.tensor_scalar(out=lnv[:], in0=B[3][:, :], scalar1=1.0 / 128.0,
                                                    scalar2=LN_EPS, op0=ALU.mult, op1=ALU.add),
                   reads=[("B", 3)], writes=["lnv"])
            sc.add("act", lambda e: e.activation(out=lnv[:], in_=lnv[:], func=AF.Ln),
                   reads=["lnv"], writes=["lnv"])
            sc.add("act", lambda e: e.activation(out=rstd[:], in_=lnv[:], func=AF.Exp, scale=-0.5),
                   reads=["lnv"], writes=["rstd"])
            sc.add("dve", lambda e: e.scalar_tensor_tensor(out=r[:], in0=od[:], scalar=gsc[:, 0:1], in1=rstd[:],
                                                           op0=ALU.mult, op1=ALU.mult),
                   reads=["od", "rstd", "gsc"], writes=[("res", qt % 2)])
            sc.add("sp", lambda e: e.dma_start(out=out[:, q0:q0 + TT], in_=r[:]),
                   reads=[("res", qt % 2)], dma=True)

        load_x(0)
        if stage == 1:
            sc.add("sp", lambda e: e.dma_start(out=out[:, 0:TT], in_=rc[0][:]), reads=[("rc", 0), "neglam", "gsc", "wsb", "pm", "ones", ("xb", 0)], dma=True)
            sc.emit()
            return nc
        qkv_mm(0)
        qkv_post(0)
        for tt in range(nt):
            if tt + 1 < nt:
                load_x(tt + 1)
            if stage == 2:
                sc.add("act", lambda e: e.copy(out=res[0][:], in_=qT[:, 0:TT]), reads=[("qT", 0), ("kT", 0), ("vS", 0)], writes=[("res", 0)])
                sc.add("sp", lambda e: e.dma_start(out=out[:, 0:TT], in_=res[0][:]), reads=[("res", 0)], dma=True)
                sc.emit()
                return nc
            nkb = 4 * tt + 4
            last = nkb - 1
            scores(tt, 0, 0)
            for kb in range(nkb):
                if kb + 1 < nkb:
                    scores(tt, kb + 1, (kb + 1) % 2)
                expo(tt, kb, kb % 2)
                pv(tt, kb, kb % 2, last)
            epilogue_pre(tt)
            if tt + 1 < nt:
                qkv_mm(tt + 1)
            epilogue_post(tt)
            if tt + 1 < nt:
                qkv_post(tt + 1)
        sc.emit()
    return nc


def rope_tables(n):
    half = 8
    pos = np.arange(n, dtype=np.float32)
    inv = (ROPE_THETA ** (-np.arange(0, 16, 2, dtype=np.float32) / np.float32(16))).astype(np.float32)
    ang = (pos[:, None] * inv[None, :]).astype(np.float32)
    cos = np.cos(ang).astype(np.float32).T
    sin = np.sin(ang).astype(np.float32).T
    C = np.ones((128, n), np.float32)
    Sg = np.zeros((128, n), np.float32)
    for c in range(2):
        b = c * 64
        C[b:b + 8] = cos
        C[b + 8:b + 16] = cos
        Sg[b:b + 8] = -sin
        Sg[b + 8:b + 16] = sin
    P = np.zeros((128, 128), np.float32)
    for c in range(2):
        b = c * 64
        for i in range(half):
            P[b + i + 8, b + i] = 1.0
            P[b + i, b + i + 8] = 1.0
    return C, Sg, P


def run_A(x, attn_w_qkv, lq1, lk1, lq2, lk2, subln_g, nt=S // 512):
    SS = nt * 512
    xT = np.ascontiguousarray(x.reshape(S, D)[:SS].T)
    C, Sg, P = rope_tables(SS)
    lam4 = np.concatenate([lq1.reshape(-1), lk1.reshape(-1), lq2.reshape(-1), lk2.reshape(-1)]).reshape(1, 256)
    lam4 = np.ascontiguousarray(lam4.astype(np.float32))
    subg = np.ascontiguousarray(subln_g.reshape(128, 1).astype(np.float32))
    w = attn_w_qkv.reshape(D, 3 * D)
    in_maps = []
    for h in range(NCORES):
        wh = np.concatenate([w[:, h * 128:(h + 1) * 128], w[:, D + h * 128:D + (h + 1) * 128],
                             w[:, 2 * D + h * 128:2 * D + (h + 1) * 128]], axis=1)
        in_maps.append({"xT": xT, "wqkv": np.ascontiguousarray(wh), "ropeC": C, "ropeS": Sg, "perm": P,
                        "lam4": lam4, "subg": subg})
    nc = build_A(nt)
    r = run_bass_kernel_spmd(nc, in_maps, core_ids=list(range(NCORES)))
    return np.concatenate([r.results[h]["attnT"] for h in range(NCORES)], axis=0)


TOK = S // NCORES
TG = 1024
TT = 512
WCAP = 6144


def build_BC(mode, tok=TOK):
    ngroups = tok // TG
    ntile = TG // TT
    nc = bass.Bass("TRN2", target_bir_lowering=False)
    hin = nc.dram_tensor("hin", [D, tok], F32, kind="ExternalInput").ap()
    if mode == "B":
        ain = nc.dram_tensor("ain", [D, tok], F32, kind="ExternalInput").ap()
        wmix = nc.dram_tensor("wmix", [D, D], F32, kind="ExternalInput").ap()
    else:
        halo = nc.dram_tensor("halo", [D, 2], F32, kind="ExternalInput").ap()
        w_in = nc.dram_tensor("w_in", [D, 3 * D], F32, kind="ExternalInput").ap()
        cw = nc.dram_tensor("cw", [128, 24], F32, kind="ExternalInput").ap()
        wmix = nc.dram_tensor("wmix", [D, D], F32, kind="ExternalInput").ap()
    wg = nc.dram_tensor("wg", [D, DFF], F32, kind="ExternalInput").ap()
    wu = nc.dram_tensor("wu", [D, DFF], F32, kind="ExternalInput").ap()
    wd = nc.dram_tensor("wd", [DFF, D], F32, kind="ExternalInput").ap()
    lnp = nc.dram_tensor("lnp", [128, 32], F32, kind="ExternalInput").ap()
    hout = nc.dram_tensor("hout", [D, tok], F32, kind="ExternalOutput").ap()

    hin_r = hin.rearrange("(k p) t -> p k t", p=128)
    hout_r = hout.rearrange("(k p) t -> p k t", p=128)
    wmix_r = wmix.rearrange("(k p) n -> p k n", p=128)
    wg_r = wg.rearrange("(k p) n -> p k n", p=128)
    wu_r = wu.rearrange("(k p) n -> p k n", p=128)
    wd_r = wd.rearrange("(k p) n -> p k n", p=128)
    NF = DFF // 128

    with ExitStack() as st:
        sc = Sched(nc, st)
        hres = _sb(nc, st, "hres", [128, 8, TG], F32)
        hb = _sb(nc, st, "hb", [128, 8, TG], BF16)
        ab = _sb(nc, st, "ab", [128, 8, TG], BF16)
        hid = _sb(nc, st, "hid", [128, NF, TG], BF16)
        wsl = [_sb(nc, st, "wsl%d" % i, [128, WCAP], BF16) for i in range(3)]
        zb = _sb(nc, st, "zb", [128, 8, TT], BF16)
        zsq = _sb(nc, st, "zsq", [128, 8, TT], BF16)
        mean = _sb(nc, st, "mean", [128, TT], F32)
        msq = _sb(nc, st, "msq", [128, TT], F32)
        var = _sb(nc, st, "var", [128, TT], F32)
        rstd = _sb(nc, st, "rstd", [128, TT], F32)
        nmr = _sb(nc, st, "nmr", [128, TT], F32)
        tmp = [_sb(nc, st, "tmp%d" % i, [128, TT], F32) for i in range(2)]
        tmp2 = [_sb(nc, st, "tmq%d" % i, [128, TT], F32) for i in range(2)]
        sg = [_sb(nc, st, "sg%d" % i, [128, TT], F32) for i in range(2)]
        ones = _sb(nc, st, "ones", [128, 128], BF16)
        lnt = _sb(nc, st, "lnt", [128, 32], F32)
        B = [_ps(nc, st, "B%d" % i) for i in range(8)]
        if mode == "C":
            cwt = _sb(nc, st, "cwt", [128, 24], F32)
            hbh = _sb(nc, st, "hbh", [128, 8, 2], BF16)
            uprev = _sb(nc, st, "uprev", [128, 8, 2], F32)
            ubuf = [_sb(nc, st, "ubuf%d" % i, [128, TT + 2], F32) for i in range(2)]
            cs = [_sb(nc, st, "cs%d" % i, [128, TT], F32) for i in range(2)]
            ybuf = [_sb(nc, st, "ybuf%d" % i, [128, TT], F32) for i in range(2)]
            hcs = _sb(nc, st, "hcs", [128, 2], F32)

        state = {"bank": 0, "job": 0, "tmp": 0}

        def bank():
            b = state["bank"]
            state["bank"] = (b + 1) % 8
            return b

        def mm(outp, lhsT, rhs, start, stop, reads, writes):
            sc.add("pe", lambda e: e.matmul(outp, lhsT, rhs, start=start, stop=stop),
                   reads=reads, writes=writes)

        sc.add("pool", lambda e: e.memset(ones[:], 1.0), writes=["ones"])
        sc.add("sp", lambda e: e.dma_start(out=lnt[:], in_=lnp), writes=["lnt"], dma=True)
        if mode == "C":
            sc.add("sp", lambda e: e.dma_start(out=cwt[:], in_=cw), writes=["cwt"], dma=True)

        jobs = []

        def run_jobs():
            n = len(jobs)
            base = state["job"]
            if n:
                jobs[0][0]((base) % 3)
            for i in range(n):
                if i + 1 < n:
                    jobs[i + 1][0]((base + i + 1) % 3)
                jobs[i][1]((base + i) % 3)
            state["job"] = base + n
            del jobs[:]

        def wload(slot, view, src):
            sc.add("pool", lambda e: e.dma_start(out=view, in_=src), writes=[("wsl", slot)], dma=True)

        def tcols(t):
            return slice(t * TT, (t + 1) * TT)

        def add_proj_jobs(w_r, src, srckey):
            for nb in range(2):
                def load(slot, nb=nb):
                    v = wsl[slot][:, 0:4096].rearrange("p (k n) -> p k n", k=8)
                    wload(slot, v, w_r[:, :, nb * 512:(nb + 1) * 512])

                def comp(slot, nb=nb):
                    v = wsl[slot][:, 0:4096].rearrange("p (k n) -> p k n", k=8)
                    for t in range(ntile):
                        for j in range(4):
                            n = nb * 4 + j
                            b = bank()
                            for k in range(8):
                                mm(B[b][:, :], v[:, k, j * 128:(j + 1) * 128], src[:, k, tcols(t)], k == 0, k == 7,
                                   [("wsl", slot), (srckey, k, t)], [("B", b)])
                            sc.add("dve", lambda e, n=n, t=t, b=b: e.scalar_tensor_tensor(
                                out=hres[:, n, tcols(t)], in0=hres[:, n, tcols(t)], scalar=ALPHA, in1=B[b][:, :],
                                op0=ALU.mult, op1=ALU.add),
                                reads=[("B", b), ("hres", n, t)], writes=[("hres", n, t)])
                jobs.append((load, comp))

        def layer_norm(q, want_hb):
            for t in range(ntile):
                for k in range(8):
                    sc.add("act", lambda e, k=k, t=t: e.copy(out=zb[:, k, :], in_=hres[:, k, tcols(t)]),
                           reads=[("hres", k, t)], writes=[("zb", k)])
                    sc.add("pool", lambda e, k=k, t=t: e.tensor_tensor(out=zsq[:, k, :], in0=hres[:, k, tcols(t)],
                                                                         in1=hres[:, k, tcols(t)], op=ALU.mult),
                           reads=[("hres", k, t)], writes=[("zsq", k)])
                ba = bank()
                for k in range(8):
                    mm(B[ba][:, :], ones[:, :], zb[:, k, :], k == 0, k == 7, ["ones", ("zb", k)], [("B", ba)])
                bb = bank()
                for k in range(8):
                    mm(B[bb][:, :], ones[:, :], zsq[:, k, :], k == 0, k == 7, ["ones", ("zsq", k)], [("B", bb)])
                sc.add("dve", lambda e, ba=ba: e.tensor_scalar(out=mean[:], in0=B[ba][:, :], scalar1=1.0 / D,
                                                               scalar2=None, op0=ALU.mult),
                       reads=[("B", ba)], writes=["mean"])
                sc.add("dve", lambda e: e.tensor_tensor(out=msq[:], in0=mean[:], in1=mean[:], op=ALU.mult),
                       reads=["mean"], writes=["msq"])
                sc.add("dve", lambda e, bb=bb: e.scalar_tensor_tensor(out=var[:], in0=B[bb][:, :], scalar=1.0 / D,
                                                                      in1=msq[:], op0=ALU.mult, op1=ALU.subtract),
                       reads=[("B", bb), "msq"], writes=["var"])
                sc.add("dve", lambda e: e.tensor_scalar(out=var[:], in0=var[:], scalar1=LN_EPS, scalar2=None,
                                                        op0=ALU.add), reads=["var"], writes=["var"])
                sc.add("act", lambda e: e.activation(out=var[:], in_=var[:], func=AF.Ln),
                       reads=["var"], writes=["var"])
                sc.add("act", lambda e: e.activation(out=rstd[:], in_=var[:], func=AF.Exp, scale=-0.5),
                       reads=["var"], writes=["rstd"])
                sc.add("dve", lambda e: e.scalar_tensor_tensor(out=nmr[:], in0=mean[:], scalar=-1.0, in1=rstd[:],
                                                               op0=ALU.mult, op1=ALU.mult),
                       reads=["mean", "rstd"], writes=["nmr"])
                for k in range(8):
                    i = state["tmp"]
                    state["tmp"] = 1 - i
                    sc.add("dve", lambda e, k=k, t=t, i=i: e.tensor_tensor(out=tmp[i][:], in0=hres[:, k, tcols(t)],
                                                                           in1=rstd[:], op=ALU.mult),
                           reads=[("hres", k, t), "rstd"], writes=[("tmp", i)])
                    sc.add("pool", lambda e, i=i: e.tensor_tensor(out=tmp2[i][:], in0=tmp[i][:], in1=nmr[:],
                                                                  op=ALU.add),
                           reads=[("tmp", i), "nmr"], writes=[("tmp2", i)])
                    gcol = q * 16 + k
                    bcol = q * 16 + 8 + k
                    sc.add("act", lambda e, k=k, t=t, i=i, gcol=gcol, bcol=bcol: e.activation(
                        out=hres[:, k, tcols(t)], in_=tmp2[i][:], func=AF.Identity,
                        scale=lnt[:, gcol:gcol + 1], bias=lnt[:, bcol:bcol + 1]),
                        reads=[("tmp2", i), "lnt"], writes=[("hres", k, t)])
                    if want_hb:
                        sc.add("dve", lambda e, k=k, t=t, i=i, gcol=gcol, bcol=bcol: e.tensor_scalar(
                            out=hb[:, k, tcols(t)], in0=tmp2[i][:], scalar1=lnt[:, gcol:gcol + 1],
                            scalar2=lnt[:, bcol:bcol + 1], op0=ALU.mult, op1=ALU.add),
                            reads=[("tmp2", i), "lnt"], writes=[("hb", k, t)])

        def add_ffn_jobs():
            f0 = 0
            while f0 < NF:
                fb = min(3, NF - f0)

                def load(slot, f0=f0, fb=fb):
                    v = wsl[slot][:, 0:2 * 8 * 384].rearrange("p (g k n) -> p g k n", g=2, k=8)
                    wload(slot, v[:, 0, :, 0:fb * 128], wg_r[:, :, f0 * 128:(f0 + fb) * 128])
                    wload(slot, v[:, 1, :, 0:fb * 128], wu_r[:, :, f0 * 128:(f0 + fb) * 128])

                def comp(slot, f0=f0, fb=fb):
                    v = wsl[slot][:, 0:2 * 8 * 384].rearrange("p (g k n) -> p g k n", g=2, k=8)
                    for t in range(ntile):
                        for j in range(fb):
                            fc = f0 + j
                            bg = bank()
                            for k in range(8):
                                mm(B[bg][:, :], v[:, 0, k, j * 128:(j + 1) * 128], hb[:, k, tcols(t)], k == 0, k == 7,
                                   [("wsl", slot), ("hb", k, t)], [("B", bg)])
                            bu = bank()
                            for k in range(8):
                                mm(B[bu][:, :], v[:, 1, k, j * 128:(j + 1) * 128], hb[:, k, tcols(t)], k == 0, k == 7,
                                   [("wsl", slot), ("hb", k, t)], [("B", bu)])
                            i = state["tmp"]
                            state["tmp"] = 1 - i
                            sc.add("act", lambda e, i=i, bg=bg: e.activation(out=sg[i][:], in_=B[bg][:, :],
                                                                             func=AF.Silu),
                                   reads=[("B", bg)], writes=[("sg", i)])
                            sc.add("dve", lambda e, i=i, bu=bu, fc=fc, t=t: e.tensor_tensor(
                                out=hid[:, fc, tcols(t)], in0=B[bu][:, :], in1=sg[i][:], op=ALU.mult),
                                reads=[("B", bu), ("sg", i)], writes=[("hid", fc, t)])
                jobs.append((load, comp))
                f0 += fb
            for nb in range(4):
                def load(slot, nb=nb):
                    v = wsl[slot][:, 0:NF * 256].rearrange("p (k n) -> p k n", k=NF)
                    wload(slot, v, wd_r[:, :, nb * 256:(nb + 1) * 256])

                def comp(slot, nb=nb):
                    v = wsl[slot][:, 0:NF * 256].rearrange("p (k n) -> p k n", k=NF)
                    for t in range(ntile):
                        for j in range(2):
                            n = nb * 2 + j
                            b = bank()
                            for fc in range(NF):
                                mm(B[b][:, :], v[:, fc, j * 128:(j + 1) * 128], hid[:, fc, tcols(t)], fc == 0,
                                   fc == NF - 1, [("wsl", slot), ("hid", fc, t)], [("B", b)])
                            sc.add("dve", lambda e, n=n, t=t, b=b: e.scalar_tensor_tensor(
                                out=hres[:, n, tcols(t)], in0=hres[:, n, tcols(t)], scalar=ALPHA, in1=B[b][:, :],
                                op0=ALU.mult, op1=ALU.add),
                                reads=[("B", b), ("hres", n, t)], writes=[("hres", n, t)])
                jobs.append((load, comp))

        def add_conv_jobs(g):
            w_in_r = w_in.rearrange("(k p) (g n) -> p k g n", p=128, g=3)
            for jb in range(4):
                def load(slot, jb=jb):
                    v = wsl[slot][:, 0:WCAP].rearrange("p (k g n) -> p k g n", k=8, g=3)
                    for gg in range(3):
                        wload(slot, v[:, :, gg, :], w_in_r[:, :, gg, jb * 256:(jb + 1) * 256])

                def comp(slot, jb=jb):
                    v = wsl[slot][:, 0:WCAP].rearrange("p (k g n) -> p k g n", k=8, g=3)
                    for jj in range(2):
                        ch = jb * 2 + jj
                        cols = slice(jj * 128, (jj + 1) * 128)
                        if g == 0:
                            bc = bank()
                            for k in range(8):
                                mm(B[bc][:, 0:2], v[:, k, 1, cols], hbh[:, k, :], k == 0, k == 7,
                                   [("wsl", slot), "hbh"], [("B", bc)])
                            bx = bank()
                            for k in range(8):
                                mm(B[bx][:, 0:2], v[:, k, 2, cols], hbh[:, k, :], k == 0, k == 7,
                                   [("wsl", slot), "hbh"], [("B", bx)])
                            sc.add("act", lambda e, bc=bc: e.copy(out=hcs[:], in_=B[bc][:, 0:2]),
                                   reads=[("B", bc)], writes=["hcs"])
                            sc.add("dve", lambda e, bx=bx, ch=ch: e.tensor_tensor(out=uprev[:, ch, :], in0=B[bx][:, 0:2],
                                                                                  in1=hcs[:], op=ALU.mult),
                                   reads=[("B", bx), "hcs"], writes=[("uprev", ch)])
                        for t in range(ntile):
                            bbk = bank()
                            for k in range(8):
                                mm(B[bbk][:, :], v[:, k, 0, cols], hb[:, k, tcols(t)], k == 0, k == 7,
                                   [("wsl", slot), ("hb", k, t)], [("B", bbk)])
                            bc = bank()
                            for k in range(8):
                                mm(B[bc][:, :], v[:, k, 1, cols], hb[:, k, tcols(t)], k == 0, k == 7,
                                   [("wsl", slot), ("hb", k, t)], [("B", bc)])
                            bx = bank()
                            for k in range(8):
                                mm(B[bx][:, :], v[:, k, 2, cols], hb[:, k, tcols(t)], k == 0, k == 7,
                                   [("wsl", slot), ("hb", k, t)], [("B", bx)])
                            i = state["tmp"]
                            state["tmp"] = 1 - i
                            u = ubuf[i]
                            sc.add("act", lambda e, i=i, bc=bc: e.copy(out=cs[i][:], in_=B[bc][:, :]),
                                   reads=[("B", bc)], writes=[("cs", i)])
                            sc.add("pool", lambda e, u=u, ch=ch: e.tensor_copy(out=u[:, 0:2], in_=uprev[:, ch, :]),
                                   reads=[("uprev", ch)], writes=[("u", i)])
                            sc.add("dve", lambda e, u=u, i=i, bx=bx: e.tensor_tensor(out=u[:, 2:TT + 2], in0=B[bx][:, :],
                                                                                     in1=cs[i][:], op=ALU.mult),
                                   reads=[("B", bx), ("cs", i)], writes=[("u", i)])
                            sc.add("pool", lambda e, u=u, ch=ch: e.tensor_copy(out=uprev[:, ch, :], in_=u[:, TT:TT + 2]),
                                   reads=[("u", i)], writes=[("uprev", ch)])
                            y = ybuf[i]
                            c3 = ch * 3
                            sc.add("dve", lambda e, u=u, y=y, c3=c3: e.tensor_scalar(
                                out=y[:], in0=u[:, 2:TT + 2], scalar1=cwt[:, c3 + 2:c3 + 3], scalar2=None, op0=ALU.mult),
                                reads=[("u", i), "cwt"], writes=[("y", i)])
                            sc.add("dve", lambda e, u=u, y=y, c3=c3: e.scalar_tensor_tensor(
                                out=y[:], in0=u[:, 1:TT + 1], scalar=cwt[:, c3 + 1:c3 + 2], in1=y[:],
                                op0=ALU.mult, op1=ALU.add), reads=[("u", i), "cwt", ("y", i)], writes=[("y", i)])
                            sc.add("dve", lambda e, u=u, y=y, c3=c3: e.scalar_tensor_tensor(
                                out=y[:], in0=u[:, 0:TT], scalar=cwt[:, c3:c3 + 1], in1=y[:],
                                op0=ALU.mult, op1=ALU.add), reads=[("u", i), "cwt", ("y", i)], writes=[("y", i)])
                            sc.add("dve", lambda e, y=y, bbk=bbk, ch=ch, t=t: e.tensor_tensor(
                                out=ab[:, ch, tcols(t)], in0=B[bbk][:, :], in1=y[:], op=ALU.mult),
                                reads=[("B", bbk), ("y", i)], writes=[("ab", ch, t)])
                jobs.append((load, comp))

        for g in range(ngroups):
            g0 = g * TG
            for t in range(ntile):
                c0 = g0 + t * TT
                sc.add("sp", lambda e, t=t, c0=c0: e.dma_start(out=hres[:, :, tcols(t)], in_=hin_r[:, :, c0:c0 + TT]),
                       writes=[("hres", k, t) for k in range(8)], dma=True)
            if mode == "B":
                ain_r = ain.rearrange("(k p) t -> p k t", p=128)
                for t in range(ntile):
                    c0 = g0 + t * TT
                    sc.add("pool", lambda e, t=t, c0=c0: e.dma_start(out=ab[:, :, tcols(t)], in_=ain_r[:, :, c0:c0 + TT]),
                           writes=[("ab", k, t) for k in range(8)], dma=True)
                add_proj_jobs(wmix_r, ab, "ab")
                run_jobs()
            else:
                for t in range(ntile):
                    c0 = g0 + t * TT
                    sc.add("pool", lambda e, t=t, c0=c0: e.dma_start(out=hb[:, :, tcols(t)], in_=hin_r[:, :, c0:c0 + TT]),
                           writes=[("hb", k, t) for k in range(8)], dma=True)
                if g == 0:
                    halo_r = halo.rearrange("(k p) t -> p k t", p=128)
                    sc.add("pool", lambda e: e.dma_start(out=hbh[:], in_=halo_r), writes=["hbh"], dma=True)
                add_conv_jobs(g)
                add_proj_jobs(wmix_r, ab, "ab")
                run_jobs()
            layer_norm(0, True)
            add_ffn_jobs()
            run_jobs()
            layer_norm(1, False)
            for t in range(ntile):
                c0 = g0 + t * TT
                sc.add("sp", lambda e, t=t, c0=c0: e.dma_start(out=hout_r[:, :, c0:c0 + TT], in_=hres[:, :, tcols(t)]),
                       reads=[("hres", k, t) for k in range(8)], dma=True)
        sc.emit()
    return nc


def _lnp(ln1_g, ln1_b, ln2_g, ln2_b, i):
    cols = []
    for v in (ln1_g[i], ln1_b[i], ln2_g[i], ln2_b[i]):
        cols.append(np.asarray(v, np.float32).reshape(8, 128).T)
    return np.ascontiguousarray(np.concatenate(cols, axis=1))


def run_B(xT, attnT, w_o, wg, wu, wd, lnp):
    in_maps = []
    for c in range(NCORES):
        sl = slice(c * TOK, (c + 1) * TOK)
        in_maps.append({"hin": np.ascontiguousarray(xT[:, sl]), "ain": np.ascontiguousarray(attnT[:, sl]),
                        "wmix": w_o, "wg": wg, "wu": wu, "wd": wd, "lnp": lnp})
    nc = build_BC("B")
    r = run_bass_kernel_spmd(nc, in_maps, core_ids=list(range(NCORES)))
    return np.concatenate([r.results[c]["hout"] for c in range(NCORES)], axis=1)


def run_C(hT, w_in, cw, w_out, wg, wu, wd, lnp):
    in_maps = []
    for c in range(NCORES):
        sl = slice(c * TOK, (c + 1) * TOK)
        halo = np.zeros((D, 2), np.float32) if c == 0 else np.ascontiguousarray(hT[:, c * TOK - 2:c * TOK])
        in_maps.append({"hin": np.ascontiguousarray(hT[:, sl]), "halo": halo, "w_in": w_in, "cw": cw,
                        "wmix": w_out, "wg": wg, "wu": wu, "wd": wd, "lnp": lnp})
    nc = build_BC("C")
    r = run_bass_kernel_spmd(nc, in_maps, core_ids=list(range(NCORES)))
    return np.concatenate([r.results[c]["hout"] for c in range(NCORES)], axis=1)


def kernel3(x, attn_w_qkv, attn_w_o, attn_lambda_q1, attn_lambda_k1, attn_lambda_q2, attn_lambda_k2,
           attn_subln_g, conv_w_in, conv_w, conv_w_out, ffn_w_gate, ffn_w_up, ffn_w_down,
           ln1_g, ln1_b, ln2_g, ln2_b):
    f = lambda a: np.ascontiguousarray(np.asarray(a, np.float32))
    x = f(x)
    attnT = run_A(x, f(attn_w_qkv), f(attn_lambda_q1), f(attn_lambda_k1), f(attn_lambda_q2), f(attn_lambda_k2),
                  f(attn_subln_g))
    xT = np.ascontiguousarray(x.reshape(S, D).T)
    h1T = run_B(xT, attnT, f(attn_w_o)[0], f(ffn_w_gate)[0], f(ffn_w_up)[0], f(ffn_w_down)[0],
                _lnp(ln1_g, ln1_b, ln2_g, ln2_b, 0))
    cw = np.ascontiguousarray(f(conv_w)[0].reshape(3, 8, 128).transpose(2, 1, 0).reshape(128, 24))
    h2T = run_C(h1T, f(conv_w_in)[0], cw, f(conv_w_out)[0], f(ffn_w_gate)[1], f(ffn_w_up)[1], f(ffn_w_down)[1],
                _lnp(ln1_g, ln1_b, ln2_g, ln2_b, 1))
    return np.ascontiguousarray(h2T.T).reshape(1, S, D).astype(np.float32)


def build_T(tok=TOK):
    ngroups = tok // TG
    TGW = TG + 2
    nc = bass.Bass("TRN2", target_bir_lowering=False)
    hin = nc.dram_tensor("hin", [D, tok], F32, kind="ExternalInput").ap()
    xhalo = nc.dram_tensor("xhalo", [D, 2], F32, kind="ExternalInput").ap()
    ain = nc.dram_tensor("ain", [D, tok], F32, kind="ExternalInput").ap()
    ahalo = nc.dram_tensor("ahalo", [D, 2], F32, kind="ExternalInput").ap()
    uflag = nc.dram_tensor("uflag", [128, 1], F32, kind="ExternalInput").ap()
    w_o = nc.dram_tensor("w_o", [D, D], F32, kind="ExternalInput").ap()
    w_in = nc.dram_tensor("w_in", [D, 3 * D], F32, kind="ExternalInput").ap()
    cw = nc.dram_tensor("cw", [128, 24], F32, kind="ExternalInput").ap()
    w_out = nc.dram_tensor("w_out", [D, D], F32, kind="ExternalInput").ap()
    wgs = [nc.dram_tensor("wg%d" % i, [D, DFF], F32, kind="ExternalInput").ap() for i in range(2)]
    wus = [nc.dram_tensor("wu%d" % i, [D, DFF], F32, kind="ExternalInput").ap() for i in range(2)]
    wds = [nc.dram_tensor("wd%d" % i, [DFF, D], F32, kind="ExternalInput").ap() for i in range(2)]
    lnp = nc.dram_tensor("lnp", [128, 64], F32, kind="ExternalInput").ap()
    hout = nc.dram_tensor("hout", [D, tok], F32, kind="ExternalOutput").ap()

    kp = lambda a: a.rearrange("(k p) t -> p k t", p=128)
    hin_r, ain_r, hout_r = kp(hin), kp(ain), kp(hout)
    xhalo_r, ahalo_r = kp(xhalo), kp(ahalo)
    w_o_r, w_out_r = kp(w_o), kp(w_out)
    wg_r = [kp(a) for a in wgs]
    wu_r = [kp(a) for a in wus]
    wd_r = [kp(a) for a in wds]
    w_in_r = w_in.rearrange("(k p) (g n) -> p k g n", p=128, g=3)
    NF = DFF // 128

    with ExitStack() as st:
        sc = Sched(nc, st)
        hres = _sb(nc, st, "hres", [128, 8, TGW], F32)
        hb = _sb(nc, st, "hb", [128, 8, TGW], BF16)
        ab = _sb(nc, st, "ab", [128, 8, TGW], BF16)
        hid = _sb(nc, st, "hid", [128, NF, TGW], BF16)
        wsl = [_sb(nc, st, "wsl%d" % i, [128, WCAP], BF16) for i in range(3)]
        zb = _sb(nc, st, "zb", [128, 8, TT], BF16)
        zsq = _sb(nc, st, "zsq", [128, 8, TT], BF16)
        mean = _sb(nc, st, "mean", [128, TT], F32)
        msq = _sb(nc, st, "msq", [128, TT], F32)
        var = _sb(nc, st, "var", [128, TT], F32)
        rstd = _sb(nc, st, "rstd", [128, TT], F32)
        nmr = _sb(nc, st, "nmr", [128, TT], F32)
        tmp = [_sb(nc, st, "tmp%d" % i, [128, TT], F32) for i in range(2)]
        tmp2 = [_sb(nc, st, "tmq%d" % i, [128, TT], F32) for i in range(2)]
        sg = [_sb(nc, st, "sg%d" % i, [128, TT], F32) for i in range(2)]
        ones = _sb(nc, st, "ones", [128, 128], BF16)
        lnt = _sb(nc, st, "lnt", [128, 64], F32)
        cwt = _sb(nc, st, "cwt", [128, 24], F32)
        ufl = _sb(nc, st, "ufl", [128, 1], F32)
        uprev = _sb(nc, st, "uprev", [128, 8, 2], F32)
        ubuf = [_sb(nc, st, "ubuf%d" % i, [128, TT + 2], F32) for i in range(2)]
        cs = [_sb(nc, st, "cs%d" % i, [128, TT], F32) for i in range(2)]
        ybuf = [_sb(nc, st, "ybuf%d" % i, [128, TT], F32) for i in range(2)]
        hcs = _sb(nc, st, "hcs", [128, 2], F32)
        B = [_ps(nc, st, "B%d" % i) for i in range(8)]

        state = {"bank": 0, "job": 0, "tmp": 0}

        def bank():
            b = state["bank"]
            state["bank"] = (b + 1) % 7
            return b

        def flip():
            i = state["tmp"]
            state["tmp"] = 1 - i
            return i

        def mm(outp, lhsT, rhs, start, stop, reads, writes):
            sc.add("pe", lambda e: e.matmul(outp, lhsT, rhs, start=start, stop=stop),
                   reads=reads, writes=writes)

        sc.add("pool", lambda e: e.memset(ones[:], 1.0), writes=["ones"])
        sc.add("sp", lambda e: e.dma_start(out=lnt[:], in_=lnp), writes=["lnt"], dma=True)
        sc.add("sp", lambda e: e.dma_start(out=cwt[:], in_=cw), writes=["cwt"], dma=True)
        sc.add("sp", lambda e: e.dma_start(out=ufl[:], in_=uflag), writes=["ufl"], dma=True)

        def tiles(g, layer):
            ts = [(0, 0, TT), (1, TT, TT)]
            if g == 0 and layer == 0:
                ts.append((2, TG, 2))
            return ts

        jobs = []
        side = []

        def hook():
            for _ in range(state.get("rate", 0)):
                if side:
                    side.pop(0)()

        def flush():
            while side:
                side.pop(0)()
            state["rate"] = 0

        def enqueue(pieces, nsteps):
            side.extend(pieces)
            state["rate"] = max(1, -(-len(side) // max(1, nsteps)))

        def split(tl):
            first = [x for x in tl if x[0] != 1]
            second = [x for x in tl if x[0] == 1]
            return first, second

        def ensure_loaded(idx):
            if idx < len(jobs) and not jobs[idx]["loaded"]:
                jobs[idx]["loaded"] = True
                jobs[idx]["load"]((state["job"] + idx) % 3)

        def run_segment(lo, hi, ln_q, ln_tl, pre_side):
            npairs = (hi - lo) // 2
            lf, ls = split(ln_tl)
            for p in range(npairs):
                ia, ib = lo + 2 * p, lo + 2 * p + 1
                ja, jb = jobs[ia], jobs[ib]
                sa, sb = (state["job"] + ia) % 3, (state["job"] + ib) % 3
                ensure_loaded(ia)
                ensure_loaded(ib)
                ensure_loaded(ib + 1)
                fa, sea = split(ja["tl"])
                fb_, seb = split(jb["tl"])
                if p == 0 and pre_side:
                    enqueue(pre_side, ja["spt"] * len(fa) + jb["spt"] * len(fb_))
                ja["comp"](sa, fa)
                jb["comp"](sb, fb_)
                if p == 0:
                    flush()
                if p == npairs - 1:
                    pcs = []
                    for tile in lf:
                        pcs += ln_pieces(ln_q, tile)
                    enqueue(pcs, ja["spt"] * len(sea) + jb["spt"] * len(seb))
                ja["comp"](sa, sea)
                ensure_loaded(ib + 2)
                jb["comp"](sb, seb)
                if p == npairs - 1:
                    flush()
            pcs = []
            for tile in ls:
                pcs += ln_pieces(ln_q, tile)
            return pcs

        def wload(slot, view, src):
            sc.add("pool", lambda e: e.dma_start(out=view, in_=src), writes=[("wsl", slot)], dma=True)

        def add_proj_jobs(w_r, tl):
            for nb in range(2):
                def load(slot, nb=nb):
                    v = wsl[slot][:, 0:4096].rearrange("p (k n) -> p k n", k=8)
                    wload(slot, v, w_r[:, :, nb * 512:(nb + 1) * 512])

                def comp(slot, tls, nb=nb):
                    v = wsl[slot][:, 0:4096].rearrange("p (k n) -> p k n", k=8)
                    for (t, c0, w) in tls:
                        for j in range(4):
                            n = nb * 4 + j
                            b = bank()
                            for k in range(8):
                                mm(B[b][:, 0:w], v[:, k, j * 128:(j + 1) * 128], ab[:, k, c0:c0 + w], k == 0, k == 7,
                                   [("wsl", slot), ("ab", k, t)], [("B", b)])
                            sc.add("dve", lambda e, n=n, c0=c0, w=w, b=b: e.scalar_tensor_tensor(
                                out=hres[:, n, c0:c0 + w], in0=hres[:, n, c0:c0 + w], scalar=ALPHA, in1=B[b][:, 0:w],
                                op0=ALU.mult, op1=ALU.add),
                                reads=[("B", b), ("hres", n, t)], writes=[("hres", n, t)])
                            hook()
                jobs.append({"load": load, "comp": comp, "tl": tl, "spt": 4, "loaded": False})

        def ln_pieces(q, tile):
            (t, c0, w) = tile
            pcs = []
            shared = {}

            def prep(k):
                sc.add("act", lambda e: e.activation(out=zsq[:, k, 0:w], in_=hres[:, k, c0:c0 + w], func=AF.Square),
                       reads=[("hres", k, t)], writes=[("zsq", k)])
                ceng = "dve" if k % 2 == 0 else "pool"
                sc.add(ceng, lambda e: e.tensor_copy(out=zb[:, k, 0:w], in_=hres[:, k, c0:c0 + w]),
                       reads=[("hres", k, t)], writes=[("zb", k)])

            def stats():
                ba = bank()
                for k in range(8):
                    mm(B[ba][:, 0:w], ones[:, :], zb[:, k, 0:w], k == 0, k == 7, ["ones", ("zb", k)], [("B", ba)])
                bb = bank()
                for k in range(8):
                    mm(B[bb][:, 0:w], ones[:, :], zsq[:, k, 0:w], k == 0, k == 7, ["ones", ("zsq", k)], [("B", bb)])
                sc.add("dve", lambda e: e.tensor_scalar(out=mean[:, 0:w], in0=B[ba][:, 0:w], scalar1=1.0 / D,
                                                        scalar2=None, op0=ALU.mult),
                       reads=[("B", ba)], writes=["mean"])
                sc.add("dve", lambda e: e.tensor_tensor(out=msq[:, 0:w], in0=mean[:, 0:w], in1=mean[:, 0:w],
                                                        op=ALU.mult), reads=["mean"], writes=["msq"])
                sc.add("dve", lambda e: e.scalar_tensor_tensor(
                    out=var[:, 0:w], in0=B[bb][:, 0:w], scalar=1.0 / D, in1=msq[:, 0:w], op0=ALU.mult,
                    op1=ALU.subtract), reads=[("B", bb), "msq"], writes=["var"])
                sc.add("dve", lambda e: e.tensor_scalar(out=var[:, 0:w], in0=var[:, 0:w], scalar1=LN_EPS,
                                                        scalar2=None, op0=ALU.add), reads=["var"], writes=["var"])
                sc.add("act", lambda e: e.activation(out=var[:, 0:w], in_=var[:, 0:w], func=AF.Ln),
                       reads=["var"], writes=["var"])
                br = 7
                shared["br"] = br
                sc.add("act", lambda e: e.activation(out=B[br][:, 0:w], in_=var[:, 0:w], func=AF.Exp, scale=-0.5),
                       reads=["var"], writes=[("B", br)])
                sc.add("dve", lambda e: e.scalar_tensor_tensor(out=nmr[:, 0:w], in0=mean[:, 0:w], scalar=-1.0,
                                                               in1=B[br][:, 0:w], op0=ALU.mult, op1=ALU.mult),
                       reads=["mean", ("B", br)], writes=["nmr"])

            def norm(k):
                br = shared["br"]
                i = flip()
                sc.add("dve", lambda e: e.tensor_tensor(out=tmp[i][:, 0:w], in0=hres[:, k, c0:c0 + w], in1=B[br][:, 0:w],
                                                        op=ALU.mult),
                       reads=[("hres", k, t), ("B", br)], writes=[("tmp", i)])
                sc.add("pool", lambda e: e.tensor_tensor(out=tmp2[i][:, 0:w], in0=tmp[i][:, 0:w], in1=nmr[:, 0:w],
                                                         op=ALU.add),
                       reads=[("tmp", i), "nmr"], writes=[("tmp2", i)])
                gcol = q * 16 + k
                bcol = q * 16 + 8 + k
                sc.add("act", lambda e: e.activation(out=hres[:, k, c0:c0 + w], in_=tmp2[i][:, 0:w], func=AF.Identity,
                                                     scale=lnt[:, gcol:gcol + 1], bias=lnt[:, bcol:bcol + 1]),
                       reads=[("tmp2", i), "lnt"], writes=[("hres", k, t)])
                if k % 2 == 0:
                    sc.add("dve", lambda e: e.tensor_scalar(out=hb[:, k, c0:c0 + w], in0=tmp2[i][:, 0:w],
                                                            scalar1=lnt[:, gcol:gcol + 1], scalar2=lnt[:, bcol:bcol + 1],
                                                            op0=ALU.mult, op1=ALU.add),
                           reads=[("tmp2", i), "lnt"], writes=[("hb", k, t)])
                else:
                    sc.add("act", lambda e: e.activation(out=hb[:, k, c0:c0 + w], in_=tmp2[i][:, 0:w], func=AF.Identity,
                                                         scale=lnt[:, gcol:gcol + 1], bias=lnt[:, bcol:bcol + 1]),
                           reads=[("tmp2", i), "lnt"], writes=[("hb", k, t)])

            for k in range(8):
                pcs.append(lambda k=k: prep(k))
            pcs.append(stats)
            for k in range(8):
                pcs.append(lambda k=k: norm(k))
            return pcs

        def add_ffn_jobs(layer, tl):
            f0 = 0
            while f0 < NF:
                fb = min(3, NF - f0)

                def load(slot, f0=f0, fb=fb):
                    v = wsl[slot][:, 0:2 * 8 * 384].rearrange("p (g k n) -> p g k n", g=2, k=8)
                    wload(slot, v[:, 0, :, 0:fb * 128], wg_r[layer][:, :, f0 * 128:(f0 + fb) * 128])
                    wload(slot, v[:, 1, :, 0:fb * 128], wu_r[layer][:, :, f0 * 128:(f0 + fb) * 128])

                def comp(slot, tls, f0=f0, fb=fb):
                    v = wsl[slot][:, 0:2 * 8 * 384].rearrange("p (g k n) -> p g k n", g=2, k=8)
                    for (t, c0, w) in tls:
                        for j in range(fb):
                            fc = f0 + j
                            bg = bank()
                            for k in range(8):
                                mm(B[bg][:, 0:w], v[:, 0, k, j * 128:(j + 1) * 128], hb[:, k, c0:c0 + w], k == 0, k == 7,
                                   [("wsl", slot), ("hb", k, t)], [("B", bg)])
                            bu = bank()
                            for k in range(8):
                                mm(B[bu][:, 0:w], v[:, 1, k, j * 128:(j + 1) * 128], hb[:, k, c0:c0 + w], k == 0, k == 7,
                                   [("wsl", slot), ("hb", k, t)], [("B", bu)])
                            i = flip()
                            sc.add("act", lambda e, i=i, bg=bg, w=w: e.activation(out=sg[i][:, 0:w], in_=B[bg][:, 0:w],
                                                                                  func=AF.Silu),
                                   reads=[("B", bg)], writes=[("sg", i)])
                            sc.add("dve", lambda e, i=i, bu=bu, fc=fc, c0=c0, w=w: e.tensor_tensor(
                                out=hid[:, fc, c0:c0 + w], in0=B[bu][:, 0:w], in1=sg[i][:, 0:w], op=ALU.mult),
                                reads=[("B", bu), ("sg", i)], writes=[("hid", fc, t)])
                            hook()
                jobs.append({"load": load, "comp": comp, "tl": tl, "spt": fb, "loaded": False})
                f0 += fb
            for nb in range(4):
                def load(slot, nb=nb):
                    v = wsl[slot][:, 0:NF * 256].rearrange("p (k n) -> p k n", k=NF)
                    wload(slot, v, wd_r[layer][:, :, nb * 256:(nb + 1) * 256])

                def comp(slot, tls, nb=nb):
                    v = wsl[slot][:, 0:NF * 256].rearrange("p (k n) -> p k n", k=NF)
                    for (t, c0, w) in tls:
                        for j in range(2):
                            n = nb * 2 + j
                            b = bank()
                            for fc in range(NF):
                                mm(B[b][:, 0:w], v[:, fc, j * 128:(j + 1) * 128], hid[:, fc, c0:c0 + w], fc == 0,
                                   fc == NF - 1, [("wsl", slot), ("hid", fc, t)], [("B", b)])
                            sc.add("dve", lambda e, n=n, c0=c0, w=w, b=b: e.scalar_tensor_tensor(
                                out=hres[:, n, c0:c0 + w], in0=hres[:, n, c0:c0 + w], scalar=ALPHA, in1=B[b][:, 0:w],
                                op0=ALU.mult, op1=ALU.add),
                                reads=[("B", b), ("hres", n, t)], writes=[("hres", n, t)])
                            hook()
                jobs.append({"load": load, "comp": comp, "tl": tl, "spt": 2, "loaded": False})

        def add_conv_jobs(g):
            for jb in range(4):
                def load(slot, jb=jb):
                    v = wsl[slot][:, 0:WCAP].rearrange("p (k g n) -> p k g n", k=8, g=3)
                    for gg in range(3):
                        wload(slot, v[:, :, gg, :], w_in_r[:, :, gg, jb * 256:(jb + 1) * 256])

                def comp(slot, tls, jb=jb):
                    v = wsl[slot][:, 0:WCAP].rearrange("p (k g n) -> p k g n", k=8, g=3)
                    for jj in range(2):
                        ch = jb * 2 + jj
                        cols = slice(jj * 128, (jj + 1) * 128)
                        if g == 0 and any(x[0] == 0 for x in tls):
                            bc = bank()
                            for k in range(8):
                                mm(B[bc][:, 0:2], v[:, k, 1, cols], hb[:, k, TG:TG + 2], k == 0, k == 7,
                                   [("wsl", slot), ("hb", k, 2)], [("B", bc)])
                            bx = bank()
                            for k in range(8):
                                mm(B[bx][:, 0:2], v[:, k, 2, cols], hb[:, k, TG:TG + 2], k == 0, k == 7,
                                   [("wsl", slot), ("hb", k, 2)], [("B", bx)])
                            sc.add("act", lambda e, bc=bc: e.copy(out=hcs[:], in_=B[bc][:, 0:2]),
                                   reads=[("B", bc)], writes=["hcs"])
                            sc.add("dve", lambda e, bx=bx, ch=ch: e.scalar_tensor_tensor(
                                out=uprev[:, ch, :], in0=B[bx][:, 0:2], scalar=ufl[:, 0:1], in1=hcs[:],
                                op0=ALU.mult, op1=ALU.mult),
                                reads=[("B", bx), "hcs", "ufl"], writes=[("uprev", ch)])
                        for (t, c0, w) in tls:
                            bbk = bank()
                            for k in range(8):
                                mm(B[bbk][:, :], v[:, k, 0, cols], hb[:, k, c0:c0 + w], k == 0, k == 7,
                                   [("wsl", slot), ("hb", k, t)], [("B", bbk)])
                            bc = bank()
                            for k in range(8):
                                mm(B[bc][:, :], v[:, k, 1, cols], hb[:, k, c0:c0 + w], k == 0, k == 7,
                                   [("wsl", slot), ("hb", k, t)], [("B", bc)])
                            bx = bank()
                            for k in range(8):
                                mm(B[bx][:, :], v[:, k, 2, cols], hb[:, k, c0:c0 + w], k == 0, k == 7,
                                   [("wsl", slot), ("hb", k, t)], [("B", bx)])
                            i = flip()
                            u = ubuf[i]
                            y = ybuf[i]
                            c3 = ch * 3
                            sc.add("act", lambda e, i=i, bc=bc: e.copy(out=cs[i][:], in_=B[bc][:, :]),
                                   reads=[("B", bc)], writes=[("cs", i)])
                            sc.add("pool", lambda e, u=u, ch=ch: e.tensor_copy(out=u[:, 0:2], in_=uprev[:, ch, :]),
                                   reads=[("uprev", ch)], writes=[("u", i)])
                            sc.add("dve", lambda e, u=u, i=i, bx=bx: e.tensor_tensor(out=u[:, 2:TT + 2], in0=B[bx][:, :],
                                                                                     in1=cs[i][:], op=ALU.mult),
                                   reads=[("B", bx), ("cs", i)], writes=[("u", i)])
                            sc.add("pool", lambda e, u=u, ch=ch: e.tensor_copy(out=uprev[:, ch, :], in_=u[:, TT:TT + 2]),
                                   reads=[("u", i)], writes=[("uprev", ch)])
                            sc.add("dve", lambda e, u=u, y=y, c3=c3: e.tensor_scalar(
                                out=y[:], in0=u[:, 2:TT + 2], scalar1=cwt[:, c3 + 2:c3 + 3], scalar2=None, op0=ALU.mult),
                                reads=[("u", i), "cwt"], writes=[("y", i)])
                            sc.add("dve", lambda e, u=u, y=y, c3=c3: e.scalar_tensor_tensor(
                                out=y[:], in0=u[:, 1:TT + 1], scalar=cwt[:, c3 + 1:c3 + 2], in1=y[:],
                                op0=ALU.mult, op1=ALU.add), reads=[("u", i), "cwt", ("y", i)], writes=[("y", i)])
                            sc.add("dve", lambda e, u=u, y=y, c3=c3: e.scalar_tensor_tensor(
                                out=y[:], in0=u[:, 0:TT], scalar=cwt[:, c3:c3 + 1], in1=y[:],
                                op0=ALU.mult, op1=ALU.add), reads=[("u", i), "cwt", ("y", i)], writes=[("y", i)])
                            sc.add("dve", lambda e, y=y, bbk=bbk, ch=ch, c0=c0, w=w: e.tensor_tensor(
                                out=ab[:, ch, c0:c0 + w], in0=B[bbk][:, :], in1=y[:], op=ALU.mult),
                                reads=[("B", bbk), ("y", i)], writes=[("ab", ch, t)])
                            hook()
                jobs.append({"load": load, "comp": comp, "tl": tiles(g, 1), "spt": 2, "loaded": False})

        for g in range(ngroups):
            g0 = g * TG
            tl0 = tiles(g, 0)
            tl1 = tiles(g, 1)
            for (t, c0, w) in tl0:
                if t < 2:
                    sc.add("sp", lambda e, c0=c0, w=w, g0=g0: e.dma_start(out=hres[:, :, c0:c0 + w],
                                                                   in_=hin_r[:, :, g0 + c0:g0 + c0 + w]),
                           writes=[("hres", k, t) for k in range(8)], dma=True)
                    sc.add("pool", lambda e, c0=c0, w=w, g0=g0: e.dma_start(out=ab[:, :, c0:c0 + w],
                                                                     in_=ain_r[:, :, g0 + c0:g0 + c0 + w]),
                           writes=[("ab", k, t) for k in range(8)], dma=True)
                else:
                    sc.add("sp", lambda e, c0=c0, w=w: e.dma_start(out=hres[:, :, c0:c0 + w], in_=xhalo_r),
                           writes=[("hres", k, t) for k in range(8)], dma=True)
                    sc.add("pool", lambda e, c0=c0, w=w: e.dma_start(out=ab[:, :, c0:c0 + w], in_=ahalo_r),
                           writes=[("ab", k, t) for k in range(8)], dma=True)
            base = len(jobs)
            add_proj_jobs(w_o_r, tl0)
            n1 = len(jobs)
            add_ffn_jobs(0, tl0)
            n2 = len(jobs)
            add_conv_jobs(g)
            add_proj_jobs(w_out_r, tl1)
            n3 = len(jobs)
            add_ffn_jobs(1, tl1)
            n4 = len(jobs)
            pre = run_segment(base, n1, 0, tl0, [])
            pre = run_segment(n1, n2, 1, tl0, pre)
            pre = run_segment(n2, n3, 2, tl1, pre)
            pre = run_segment(n3, n4, 3, tl1, pre)
            enqueue(pre, 1)
            flush()
            for (t, c0, w) in tl1:
                sc.add("sp", lambda e, c0=c0, w=w, g0=g0: e.dma_start(out=hout_r[:, :, g0 + c0:g0 + c0 + w],
                                                               in_=hres[:, :, c0:c0 + w]),
                       reads=[("hres", k, t) for k in range(8)], dma=True)
        sc.emit()
    return nc


def run_T(xT, attnT, inp):
    f = lambda a: np.ascontiguousarray(np.asarray(a, np.float32))
    lnp = np.ascontiguousarray(np.concatenate(
        [_lnp(inp["ln1_g"], inp["ln1_b"], inp["ln2_g"], inp["ln2_b"], i) for i in range(2)], axis=1))
    cw = np.ascontiguousarray(f(inp["conv_w"])[0].reshape(3, 8, 128).transpose(2, 1, 0).reshape(128, 24))
    shared = {"w_o": f(inp["attn_w_o"])[0], "w_in": f(inp["conv_w_in"])[0], "cw": cw, "w_out": f(inp["conv_w_out"])[0],
              "wg0": f(inp["ffn_w_gate"])[0], "wu0": f(inp["ffn_w_up"])[0], "wd0": f(inp["ffn_w_down"])[0],
              "wg1": f(inp["ffn_w_gate"])[1], "wu1": f(inp["ffn_w_up"])[1], "wd1": f(inp["ffn_w_down"])[1], "lnp": lnp}
    in_maps = []
    for c in range(NCORES):
        sl = slice(c * TOK, (c + 1) * TOK)
        if c == 0:
            xh = np.zeros((D, 2), np.float32)
            ah = np.zeros((D, 2), np.float32)
        else:
            xh = np.ascontiguousarray(xT[:, c * TOK - 2:c * TOK])
            ah = np.ascontiguousarray(attnT[:, c * TOK - 2:c * TOK])
        m = {"hin": np.ascontiguousarray(xT[:, sl]), "ain": np.ascontiguousarray(attnT[:, sl]), "xhalo": xh,
             "ahalo": ah, "uflag": np.full((128, 1), 0.0 if c == 0 else 1.0, np.float32)}
        m.update(shared)
        in_maps.append(m)
    nc = build_T()
    r = run_bass_kernel_spmd(nc, in_maps, core_ids=list(range(NCORES)))
    return np.concatenate([r.results[c]["hout"] for c in range(NCORES)], axis=1)


def kernel(**inp):
    f = lambda a: np.ascontiguousarray(np.asarray(a, np.float32))
    x = f(inp["x"])
    attnT = run_A(x, f(inp["attn_w_qkv"]), f(inp["attn_lambda_q1"]), f(inp["attn_lambda_k1"]),
                  f(inp["attn_lambda_q2"]), f(inp["attn_lambda_k2"]), f(inp["attn_subln_g"]))
    if os.environ.get("K_PROBE_ONLY_A"):
        return np.zeros((1, S, D), np.float32)
    xT = np.ascontiguousarray(x.reshape(S, D).T)
    h2T = run_T(xT, attnT, inp)
    return np.ascontiguousarray(h2T.T).reshape(1, S, D).astype(np.float32)
```

```python
import math
import os
from contextlib import ExitStack

import numpy as np
import concourse.bass as bass
import concourse.mybir as mybir
from concourse.bass_utils import run_bass_kernel_spmd

F32 = mybir.dt.float32
BF16 = mybir.dt.bfloat16
AF = mybir.ActivationFunctionType
ALU = mybir.AluOpType
AX = mybir.AxisListType

D = 1024
S = 16384
NCORES = 8
DFF = 2816
ALPHA = (2 * 2) ** 0.25
LN_EPS = 1e-5
ROPE_THETA = 500000.0
LAM_INIT0 = 0.8 - 0.6 * math.exp(-0.3 * 0)


class _Op:
    __slots__ = ("eng", "fn", "deps", "dma", "sig", "sem", "val", "prev")


class Sched:
    ENGS = ("sp", "pe", "act", "dve", "pool")
    NDS = 12

    def __init__(self, nc, stack):
        self.nc = nc
        self.stack = stack
        self.ops = []
        self.last_w = {}
        self.readers = {}

    def add(self, eng, fn, reads=(), writes=(), dma=False):
        op = _Op()
        op.eng, op.fn, op.dma, op.sig = eng, fn, dma, False
        op.sem = None
        op.val = 0
        op.prev = 0
        deps = []
        seen = set()

        def dep(o):
            if o is not None and id(o) not in seen:
                seen.add(id(o))
                deps.append(o)

        for r in reads:
            dep(self.last_w.get(r))
            if isinstance(r, tuple) and r[0] == "B":
                for rd in self.readers.get(r, ()):
                    if rd.eng != eng:
                        dep(rd)
        for w in writes:
            dep(self.last_w.get(w))
            for rd in self.readers.get(w, ()):
                dep(rd)
        op.deps = deps
        for r in reads:
            self.readers.setdefault(r, []).append(op)
        for w in writes:
            self.last_w[w] = op
            self.readers[w] = []
        self.ops.append(op)
        return op

    def emit(self):
        nc = self.nc
        st = self.stack
        esem = {e: st.enter_context(nc.semaphore("e_" + e)) for e in ("pe", "act", "dve", "pool")}
        dsem = {e: [st.enter_context(nc.semaphore("d_%s%d" % (e, i))) for i in range(self.NDS)]
                for e in ("sp", "act", "pool")}
        for op in self.ops:
            for d in op.deps:
                if d.dma:
                    continue
                if d.eng == "pe" and op.eng == "pe" and not op.dma:
                    continue
                d.sig = True
        cnt = {e: 0 for e in self.ENGS}
        dcnt = {e: 0 for e in self.ENGS}
        for op in self.ops:
            if op.dma:
                n = dcnt[op.eng]
                dcnt[op.eng] += 1
                op.sem = dsem[op.eng][n % self.NDS]
                op.val = 16 * (n // self.NDS + 1)
                op.prev = 16 * (n // self.NDS)
            elif op.sig:
                cnt[op.eng] += 1
                op.sem = esem[op.eng]
                op.val = cnt[op.eng]
        finals = []
        for e in dsem:
            n = dcnt[e]
            for i in range(min(n, self.NDS)):
                tot = (n - 1 - i) // self.NDS + 1
                finals.append((dsem[e][i], 16 * tot))

        block = st.enter_context(nc.Block())
        decos = {"sp": block.sync, "pe": block.tensor, "act": block.scalar,
                 "dve": block.vector, "pool": block.gpsimd}
        for ename in self.ENGS:
            ops_e = [op for op in self.ops if op.eng == ename]

            def body(eng, ops_e=ops_e, ename=ename):
                waited = {}

                def wait(sem, val):
                    if val > 0 and waited.get(id(sem), 0) < val:
                        eng.wait_ge(sem, val)
                        waited[id(sem)] = val

                for op in ops_e:
                    need = {}
                    for d in op.deps:
                        if (not d.dma) and d.eng == "pe" and ename == "pe" and not op.dma:
                            continue
                        k = id(d.sem)
                        if k not in need or need[k][1] < d.val:
                            need[k] = (d.sem, d.val)
                    if op.dma:
                        wait(op.sem, op.prev)
                    for sem, val in need.values():
                        wait(sem, val)
                    ins = op.fn(eng)
                    if op.dma:
                        ins.then_inc(op.sem, 16)
                    elif op.sig:
                        ins.then_inc(op.sem, 1)
                if ename == "sp":
                    for sem, val in finals:
                        wait(sem, val)

            decos[ename](body)


def _sb(nc, st, name, shape, dt):
    return st.enter_context(nc.sbuf_tensor(name, list(shape), dt))


def _ps(nc, st, name):
    return st.enter_context(nc.psum_tensor(name, [128, 512], F32))


def build_A(nt=S // 512, stage=99):
    TT = 512
    SS = nt * TT
    nc = bass.Bass("TRN2", target_bir_lowering=False)
    xT = nc.dram_tensor("xT", [D, SS], F32, kind="ExternalInput").ap()
    wqkv = nc.dram_tensor("wqkv", [D, 384], F32, kind="ExternalInput").ap()
    ropeC = nc.dram_tensor("ropeC", [128, SS], F32, kind="ExternalInput").ap()
    ropeS = nc.dram_tensor("ropeS", [128, SS], F32, kind="ExternalInput").ap()
    perm = nc.dram_tensor("perm", [128, 128], F32, kind="ExternalInput").ap()
    lam4 = nc.dram_tensor("lam4", [1, 256], F32, kind="ExternalInput").ap()
    subg = nc.dram_tensor("subg", [128, 1], F32, kind="ExternalInput").ap()
    out = nc.dram_tensor("attnT", [128, SS], F32, kind="ExternalOutput").ap()
    xTr = xT.rearrange("(k p) t -> p k t", p=128)
    wr = wqkv.rearrange("(k p) n -> p k n", p=128)

    with ExitStack() as st:
        sc = Sched(nc, st)
        qT = _sb(nc, st, "qT", [128, SS], BF16)
        kT = _sb(nc, st, "kT", [128, SS], BF16)
        vS = _sb(nc, st, "vS", [128, nt * 4, 128], BF16)
        xb = [_sb(nc, st, "xb%d" % i, [128, 8, TT], BF16) for i in range(2)]
        wsb = _sb(nc, st, "wsb", [128, 8, 384], BF16)
        pm = _sb(nc, st, "pm", [128, 128], BF16)
        ones = _sb(nc, st, "ones", [128, 128], BF16)
        rc = [_sb(nc, st, "rc%d" % i, [128, TT], F32) for i in range(2)]
        rs = [_sb(nc, st, "rs%d" % i, [128, TT], F32) for i in range(2)]
        qb = _sb(nc, st, "qb", [128, TT], BF16)
        kb_ = _sb(nc, st, "kb_", [128, TT], BF16)
        t1 = [_sb(nc, st, "t1%d" % i, [128, TT], F32) for i in range(2)]
        t2 = [_sb(nc, st, "t2%d" % i, [128, TT], F32) for i in range(2)]
        pT = [[_sb(nc, st, "pT%d%d" % (c, s), [128, TT], BF16) for s in range(2)] for c in range(2)]
        rz = [_sb(nc, st, "rz%d" % i, [128, TT], F32) for i in range(2)]
        oo = [_sb(nc, st, "oo%d" % i, [128, TT], F32) for i in range(2)]
        od = _sb(nc, st, "od", [128, TT], F32)
        osq = _sb(nc, st, "osq", [128, TT], BF16)
        lnv = _sb(nc, st, "lnv", [128, TT], F32)
        rstd = _sb(nc, st, "rstd", [128, TT], F32)
        res = [_sb(nc, st, "res%d" % i, [128, TT], F32) for i in range(2)]
        lamt = _sb(nc, st, "lamt", [128, 256], F32)
        lprod = _sb(nc, st, "lprod", [128, 128], F32)
        lsum = _sb(nc, st, "lsum", [128, 2], F32)
        lexp = _sb(nc, st, "lexp", [128, 2], F32)
        neglam = _sb(nc, st, "neglam", [128, 1], F32)
        gsc = _sb(nc, st, "gsc", [128, 1], F32)
        gld = _sb(nc, st, "gld", [128, 1], F32)
        B = [_ps(nc, st, "B%d" % i) for i in range(8)]

        sc.add("pool", lambda e: e.dma_start(out=wsb[:], in_=wr), writes=["wsb"], dma=True)
        sc.add("pool", lambda e: e.dma_start(out=pm[:], in_=perm), writes=["pm"], dma=True)
        sc.add("sp", lambda e: e.dma_start(out=lamt[:], in_=lam4.partition_broadcast(128)),
               writes=["lamt"], dma=True)
        sc.add("sp", lambda e: e.dma_start(out=gld[:], in_=subg), writes=["gld"], dma=True)
        sc.add("pool", lambda e: e.memset(ones[:], 1.0), writes=["ones"])
        sc.add("dve", lambda e: e.tensor_tensor(out=lprod[:, 0:64], in0=lamt[:, 0:64], in1=lamt[:, 64:128],
                                                op=ALU.mult), reads=["lamt"], writes=["lprod"])
        sc.add("dve", lambda e: e.tensor_tensor(out=lprod[:, 64:128], in0=lamt[:, 128:192], in1=lamt[:, 192:256],
                                                op=ALU.mult), reads=["lamt"], writes=["lprod"])
        sc.add("dve", lambda e: e.reduce_sum(out=lsum[:, 0:1], in_=lprod[:, 0:64], axis=AX.X),
               reads=["lprod"], writes=["lsum"])
        sc.add("dve", lambda e: e.reduce_sum(out=lsum[:, 1:2], in_=lprod[:, 64:128], axis=AX.X),
               reads=["lprod"], writes=["lsum"])
        sc.add("act", lambda e: e.activation(out=lexp[:], in_=lsum[:], func=AF.Exp),
               reads=["lsum"], writes=["lexp"])
        sc.add("dve", lambda e: e.tensor_tensor(out=neglam[:], in0=lexp[:, 1:2], in1=lexp[:, 0:1],
                                                op=ALU.subtract), reads=["lexp"], writes=["neglam"])
        sc.add("dve", lambda e: e.tensor_scalar(out=neglam[:], in0=neglam[:], scalar1=-LAM_INIT0, scalar2=None,
                                                op0=ALU.add), reads=["neglam"], writes=["neglam"])
        sc.add("dve", lambda e: e.tensor_scalar(out=gsc[:], in0=gld[:], scalar1=(1.0 - LAM_INIT0), scalar2=None,
                                                op0=ALU.mult), reads=["gld"], writes=["gsc"])

        def load_x(tt):
            s = tt % 2
            c0 = tt * TT
            sc.add("pool", lambda e: e.dma_start(out=xb[s][:], in_=xTr[:, :, c0:c0 + TT]),
                   writes=[("xb", s)], dma=True)
            sc.add("sp", lambda e: e.dma_start(out=rc[s][:], in_=ropeC[:, c0:c0 + TT]),
                   writes=[("rc", s)], dma=True)
            sc.add("sp", lambda e: e.dma_start(out=rs[s][:], in_=ropeS[:, c0:c0 + TT]),
                   writes=[("rs", s)], dma=True)

        def mm(outp, lhsT, rhs, start, stop, reads, writes):
            sc.add("pe", lambda e: e.matmul(outp, lhsT, rhs, start=start, stop=stop),
                   reads=reads, writes=writes)

        def qkv_mm(tt):
            s = tt % 2
            c0 = tt * TT
            xs = xb[s]
            for k in range(8):
                mm(B[0][:, :], wsb[:, k, 0:128], xs[:, k, :], k == 0, k == 7, ["wsb", ("xb", s)], [("B", 0)])
            for k in range(8):
                mm(B[1][:, :], wsb[:, k, 128:256], xs[:, k, :], k == 0, k == 7, ["wsb", ("xb", s)], [("B", 1)])
            for sub in range(4):
                for k in range(8):
                    mm(B[2][:, sub * 128:(sub + 1) * 128], xs[:, k, sub * 128:(sub + 1) * 128],
                       wsb[:, k, 256:384], k == 0, k == 7, ["wsb", ("xb", s)], [("B", 2)])

        def qkv_post(tt):
            s = tt % 2
            c0 = tt * TT
            sc.add("act", lambda e: e.copy(out=qb[:], in_=B[0][:, :]), reads=[("B", 0)], writes=["qb"])
            sc.add("act", lambda e: e.copy(out=kb_[:], in_=B[1][:, :]), reads=[("B", 1)], writes=["kb_"])
            sc.add("act", lambda e: e.copy(out=vS[:, tt * 4:(tt + 1) * 4, :],
                                           in_=B[2][:, :].rearrange("p (a b) -> p a b", a=4)),
                   reads=[("B", 2)], writes=[("vS", tt)])
            mm(B[3][:, :], pm[:, :], qb[:, :], True, True, ["pm", "qb"], [("B", 3)])
            sc.add("dve", lambda e: e.tensor_tensor(out=t1[0][:], in0=B[0][:, :], in1=rc[s][:], op=ALU.mult),
                   reads=[("B", 0), ("rc", s)], writes=["t1q"])
            sc.add("dve", lambda e: e.tensor_tensor(out=t2[0][:], in0=B[3][:, :], in1=rs[s][:], op=ALU.mult),
                   reads=[("B", 3), ("rs", s)], writes=["t2q"])
            sc.add("pool", lambda e: e.tensor_tensor(out=qT[:, c0:c0 + TT], in0=t1[0][:], in1=t2[0][:], op=ALU.add),
                   reads=["t1q", "t2q"], writes=[("qT", tt)])
            mm(B[2][:, :], pm[:, :], kb_[:, :], True, True, ["pm", "kb_"], [("B", 2)])
            sc.add("dve", lambda e: e.tensor_tensor(out=t1[1][:], in0=B[1][:, :], in1=rc[s][:], op=ALU.mult),
                   reads=[("B", 1), ("rc", s)], writes=["t1k"])
            sc.add("dve", lambda e: e.tensor_tensor(out=t2[1][:], in0=B[2][:, :], in1=rs[s][:], op=ALU.mult),
                   reads=[("B", 2), ("rs", s)], writes=["t2k"])
            sc.add("pool", lambda e: e.tensor_tensor(out=kT[:, c0:c0 + TT], in0=t1[1][:], in1=t2[1][:], op=ALU.add),
                   reads=["t1k", "t2k"], writes=[("kT", tt)])

        def scores(qt, kb, slot):
            j = kb - 4 * qt
            f0 = 128 * j if j > 0 else 0
            q0 = qt * TT
            for c in range(2):
                bi = c * 2 + slot
                mm(B[bi][:, f0:TT], kT[c * 64:(c + 1) * 64, kb * 128:(kb + 1) * 128],
                   qT[c * 64:(c + 1) * 64, q0 + f0:q0 + TT], True, True,
                   [("kT", kb // 4), ("qT", qt)], [("B", bi)])

        def expo(qt, kb, slot):
            j = kb - 4 * qt
            f0 = 128 * j if j > 0 else 0
            for c in range(2):
                bi = c * 2 + slot
                sc.add("act", lambda e, c=c, bi=bi: e.activation(out=pT[c][slot][:, f0:TT], in_=B[bi][:, f0:TT],
                                                                 func=AF.Exp, scale=0.125),
                       reads=[("B", bi)], writes=[("pT", c, slot)])
                if j >= 0:
                    sc.add("pool", lambda e, c=c: e.memset(pT[c][slot][64:128, f0:f0 + 64], 0.0),
                           writes=[("pT", c, slot)])

        def pv(qt, kb, slot, last):
            j = kb - 4 * qt
            f0 = 128 * j if j > 0 else 0
            for c in range(2):
                mm(B[4 + c][:, f0:TT], vS[:, kb, :], pT[c][slot][:, f0:TT], kb == 0, kb == last,
                   [("vS", kb // 4), ("pT", c, slot)], [("B", 4 + c)])
                mm(B[6 + c][:, f0:TT], ones[:, :], pT[c][slot][:, f0:TT], kb == 0, kb == last,
                   ["ones", ("pT", c, slot)], [("B", 6 + c)])

        def epilogue_pre(qt):
            q0 = qt * TT
            r = res[qt % 2]
            for c in range(2):
                sc.add("dve", lambda e, c=c: e.reciprocal(out=rz[c][:], in_=B[6 + c][:, :]),
                       reads=[("B", 6 + c)], writes=[("rz", c)])
                sc.add("dve", lambda e, c=c: e.tensor_tensor(out=oo[c][:], in0=B[4 + c][:, :], in1=rz[c][:],
                                                             op=ALU.mult),
                       reads=[("B", 4 + c), ("rz", c)], writes=[("oo", c)])
            sc.add("dve", lambda e: e.scalar_tensor_tensor(out=od[:], in0=oo[1][:], scalar=neglam[:, 0:1],
                                                           in1=oo[0][:], op0=ALU.mult, op1=ALU.add),
                   reads=[("oo", 0), ("oo", 1), "neglam"], writes=["od"])
            sc.add("pool", lambda e: e.tensor_tensor(out=osq[:], in0=od[:], in1=od[:], op=ALU.mult),
                   reads=["od"], writes=["osq"])

        def epilogue_post(qt):
            q0 = qt * TT
            r = res[qt % 2]
            mm(B[3][:, :], ones[:, :], osq[:, :], True, True, ["ones", "osq"], [("B", 3)])
            sc.add("dve", lambda e: e.tensor_scalar(out=lnv[:], in0=B[3][:, :], scalar1=1.0 / 128.0,
                                                    scalar2=LN_EPS, op0=ALU.mult, op1=ALU.add),
                   reads=[("B", 3)], writes=["lnv"])
            sc.add("act", lambda e: e.activation(out=lnv[:], in_=lnv[:], func=AF.Ln),
                   reads=["lnv"], writes=["lnv"])
            sc.add("act", lambda e: e.activation(out=rstd[:], in_=lnv[:], func=AF.Exp, scale=-0.5),
                   reads=["lnv"], writes=["rstd"])
            sc.add("dve", lambda e: e.scalar_tensor_tensor(out=r[:], in0=od[:], scalar=gsc[:, 0:1], in1=rstd[:],
                                                           op0=ALU.mult, op1=ALU.mult),
                   reads=["od", "rstd", "gsc"], writes=[("res", qt % 2)])
            sc.add("sp", lambda e: e.dma_start(out=out[:, q0:q0 + TT], in_=r[:]),
                   reads=[("res", qt % 2)], dma=True)

        load_x(0)
        if stage == 1:
            sc.add("sp", lambda e: e.dma_start(out=out[:, 0:TT], in_=rc[0][:]), reads=[("rc", 0), "neglam", "gsc", "wsb", "pm", "ones", ("xb", 0)], dma=True)
            sc.emit()
            return nc
        qkv_mm(0)
        qkv_post(0)
        for tt in range(nt):
            if tt + 1 < nt:
                load_x(tt + 1)
            if stage == 2:
                sc.add("act", lambda e: e.copy(out=res[0][:], in_=qT[:, 0:TT]), reads=[("qT", 0), ("kT", 0), ("vS", 0)], writes=[("res", 0)])
                sc.add("sp", lambda e: e.dma_start(out=out[:, 0:TT], in_=res[0][:]), reads=[("res", 0)], dma=True)
                sc.emit()
                return nc
            nkb = 4 * tt + 4
            last = nkb - 1
            scores(tt, 0, 0)
            for kb in range(nkb):
                if kb + 1 < nkb:
                    scores(tt, kb + 1, (kb + 1) % 2)
                expo(tt, kb, kb % 2)
                pv(tt, kb, kb % 2, last)
            epilogue_pre(tt)
            if tt + 1 < nt:
                qkv_mm(tt + 1)
            epilogue_post(tt)
            if tt + 1 < nt:
                qkv_post(tt + 1)
        sc.emit()
    return nc


def rope_tables(n):
    half = 8
    pos = np.arange(n, dtype=np.float32)
    inv = (ROPE_THETA ** (-np.arange(0, 16, 2, dtype=np.float32) / np.float32(16))).astype(np.float32)
    ang = (pos[:, None] * inv[None, :]).astype(np.float32)
    cos = np.cos(ang).astype(np.float32).T
    sin = np.sin(ang).astype(np.float32).T
    C = np.ones((128, n), np.float32)
    Sg = np.zeros((128, n), np.float32)
    for c in range(2):
        b = c * 64
        C[b:b + 8] = cos
        C[b + 8:b + 16] = cos
        Sg[b:b + 8] = -sin
        Sg[b + 8:b + 16] = sin
    P = np.zeros((128, 128), np.float32)
    for c in range(2):
        b = c * 64
        for i in range(half):
            P[b + i + 8, b + i] = 1.0
            P[b + i, b + i + 8] = 1.0
    return C, Sg, P


def run_A(x, attn_w_qkv, lq1, lk1, lq2, lk2, subln_g, nt=S // 512):
    SS = nt * 512
    xT = np.ascontiguousarray(x.reshape(S, D)[:SS].T)
    C, Sg, P = rope_tables(SS)
    lam4 = np.concatenate([lq1.reshape(-1), lk1.reshape(-1), lq2.reshape(-1), lk2.reshape(-1)]).reshape(1, 256)
    lam4 = np.ascontiguousarray(lam4.astype(np.float32))
    subg = np.ascontiguousarray(subln_g.reshape(128, 1).astype(np.float32))
    w = attn_w_qkv.reshape(D, 3 * D)
    in_maps = []
    for h in range(NCORES):
        wh = np.concatenate([w[:, h * 128:(h + 1) * 128], w[:, D + h * 128:D + (h + 1) * 128],
                             w[:, 2 * D + h * 128:2 * D + (h + 1) * 128]], axis=1)
        in_maps.append({"xT": xT, "wqkv": np.ascontiguousarray(wh), "ropeC": C, "ropeS": Sg, "perm": P,
                        "lam4": lam4, "subg": subg})
    nc = build_A(nt)
    r = run_bass_kernel_spmd(nc, in_maps, core_ids=list(range(NCORES)))
    return np.concatenate([r.results[h]["attnT"] for h in range(NCORES)], axis=0)


TOK = S // NCORES
TG = 1024
TT = 512
WCAP = 6144


def build_BC(mode, tok=TOK):
    ngroups = tok // TG
    ntile = TG // TT
    nc = bass.Bass("TRN2", target_bir_lowering=False)
    hin = nc.dram_tensor("hin", [D, tok], F32, kind="ExternalInput").ap()
    if mode == "B":
        ain = nc.dram_tensor("ain", [D, tok], F32, kind="ExternalInput").ap()
        wmix = nc.dram_tensor("wmix", [D, D], F32, kind="ExternalInput").ap()
    else:
        halo = nc.dram_tensor("halo", [D, 2], F32, kind="ExternalInput").ap()
        w_in = nc.dram_tensor("w_in", [D, 3 * D], F32, kind="ExternalInput").ap()
        cw = nc.dram_tensor("cw", [128, 24], F32, kind="ExternalInput").ap()
        wmix = nc.dram_tensor("wmix", [D, D], F32, kind="ExternalInput").ap()
    wg = nc.dram_tensor("wg", [D, DFF], F32, kind="ExternalInput").ap()
    wu = nc.dram_tensor("wu", [D, DFF], F32, kind="ExternalInput").ap()
    wd = nc.dram_tensor("wd", [DFF, D], F32, kind="ExternalInput").ap()
    lnp = nc.dram_tensor("lnp", [128, 32], F32, kind="ExternalInput").ap()
    hout = nc.dram_tensor("hout", [D, tok], F32, kind="ExternalOutput").ap()

    hin_r = hin.rearrange("(k p) t -> p k t", p=128)
    hout_r = hout.rearrange("(k p) t -> p k t", p=128)
    wmix_r = wmix.rearrange("(k p) n -> p k n", p=128)
    wg_r = wg.rearrange("(k p) n -> p k n", p=128)
    wu_r = wu.rearrange("(k p) n -> p k n", p=128)
    wd_r = wd.rearrange("(k p) n -> p k n", p=128)
    NF = DFF // 128

    with ExitStack() as st:
        sc = Sched(nc, st)
        hres = _sb(nc, st, "hres", [128, 8, TG], F32)
        hb = _sb(nc, st, "hb", [128, 8, TG], BF16)
        ab = _sb(nc, st, "ab", [128, 8, TG], BF16)
        hid = _sb(nc, st, "hid", [128, NF, TG], BF16)
        wsl = [_sb(nc, st, "wsl%d" % i, [128, WCAP], BF16) for i in range(3)]
        zb = _sb(nc, st, "zb", [128, 8, TT], BF16)
        zsq = _sb(nc, st, "zsq", [128, 8, TT], BF16)
        mean = _sb(nc, st, "mean", [128, TT], F32)
        msq = _sb(nc, st, "msq", [128, TT], F32)
        var = _sb(nc, st, "var", [128, TT], F32)
        rstd = _sb(nc, st, "rstd", [128, TT], F32)
        nmr = _sb(nc, st, "nmr", [128, TT], F32)
        tmp = [_sb(nc, st, "tmp%d" % i, [128, TT], F32) for i in range(2)]
        tmp2 = [_sb(nc, st, "tmq%d" % i, [128, TT], F32) for i in range(2)]
        sg = [_sb(nc, st, "sg%d" % i, [128, TT], F32) for i in range(2)]
        ones = _sb(nc, st, "ones", [128, 128], BF16)
        lnt = _sb(nc, st, "lnt", [128, 32], F32)
        B = [_ps(nc, st, "B%d" % i) for i in range(8)]
        if mode == "C":
            cwt = _sb(nc, st, "cwt", [128, 24], F32)
            hbh = _sb(nc, st, "hbh", [128, 8, 2], BF16)
            uprev = _sb(nc, st, "uprev", [128, 8, 2], F32)
            ubuf = [_sb(nc, st, "ubuf%d" % i, [128, TT + 2], F32) for i in range(2)]
            cs = [_sb(nc, st, "cs%d" % i, [128, TT], F32) for i in range(2)]
            ybuf = [_sb(nc, st, "ybuf%d" % i, [128, TT], F32) for i in range(2)]
            hcs = _sb(nc, st, "hcs", [128, 2], F32)

        state = {"bank": 0, "job": 0, "tmp": 0}

        def bank():
            b = state["bank"]
            state["bank"] = (b + 1) % 8
            return b

        def mm(outp, lhsT, rhs, start, stop, reads, writes):
            sc.add("pe", lambda e: e.matmul(outp, lhsT, rhs, start=start, stop=stop),
                   reads=reads, writes=writes)

        sc.add("pool", lambda e: e.memset(ones[:], 1.0), writes=["ones"])
        sc.add("sp", lambda e: e.dma_start(out=lnt[:], in_=lnp), writes=["lnt"], dma=True)
        if mode == "C":
            sc.add("sp", lambda e: e.dma_start(out=cwt[:], in_=cw), writes=["cwt"], dma=True)

        jobs = []

        def run_jobs():
            n = len(jobs)
            base = state["job"]
            if n:
                jobs[0][0]((base) % 3)
            for i in range(n):
                if i + 1 < n:
                    jobs[i + 1][0]((base + i + 1) % 3)
                jobs[i][1]((base + i) % 3)
            state["job"] = base + n
            del jobs[:]

        def wload(slot, view, src):
            sc.add("pool", lambda e: e.dma_start(out=view, in_=src), writes=[("wsl", slot)], dma=True)

        def tcols(t):
            return slice(t * TT, (t + 1) * TT)

        def add_proj_jobs(w_r, src, srckey):
            for nb in range(2):
                def load(slot, nb=nb):
                    v = wsl[slot][:, 0:4096].rearrange("p (k n) -> p k n", k=8)
                    wload(slot, v, w_r[:, :, nb * 512:(nb + 1) * 512])

                def comp(slot, nb=nb):
                    v = wsl[slot][:, 0:4096].rearrange("p (k n) -> p k n", k=8)
                    for t in range(ntile):
                        for j in range(4):
                            n = nb * 4 + j
                            b = bank()
                            for k in range(8):
                                mm(B[b][:, :], v[:, k, j * 128:(j + 1) * 128], src[:, k, tcols(t)], k == 0, k == 7,
                                   [("wsl", slot), (srckey, k, t)], [("B", b)])
                            sc.add("dve", lambda e, n=n, t=t, b=b: e.scalar_tensor_tensor(
                                out=hres[:, n, tcols(t)], in0=hres[:, n, tcols(t)], scalar=ALPHA, in1=B[b][:, :],
                                op0=ALU.mult, op1=ALU.add),
                                reads=[("B", b), ("hres", n, t)], writes=[("hres", n, t)])
                jobs.append((load, comp))

        def layer_norm(q, want_hb):
            for t in range(ntile):
                for k in range(8):
                    sc.add("act", lambda e, k=k, t=t: e.copy(out=zb[:, k, :], in_=hres[:, k, tcols(t)]),
                           reads=[("hres", k, t)], writes=[("zb", k)])
                    sc.add("pool", lambda e, k=k, t=t: e.tensor_tensor(out=zsq[:, k, :], in0=hres[:, k, tcols(t)],
                                                                         in1=hres[:, k, tcols(t)], op=ALU.mult),
                           reads=[("hres", k, t)], writes=[("zsq", k)])
                ba = bank()
                for k in range(8):
                    mm(B[ba][:, :], ones[:, :], zb[:, k, :], k == 0, k == 7, ["ones", ("zb", k)], [("B", ba)])
                bb = bank()
                for k in range(8):
                    mm(B[bb][:, :], ones[:, :], zsq[:, k, :], k == 0, k == 7, ["ones", ("zsq", k)], [("B", bb)])
                sc.add("dve", lambda e, ba=ba: e.tensor_scalar(out=mean[:], in0=B[ba][:, :], scalar1=1.0 / D,
                                                               scalar2=None, op0=ALU.mult),
                       reads=[("B", ba)], writes=["mean"])
                sc.add("dve", lambda e: e.tensor_tensor(out=msq[:], in0=mean[:], in1=mean[:], op=ALU.mult),
                       reads=["mean"], writes=["msq"])
                sc.add("dve", lambda e, bb=bb: e.scalar_tensor_tensor(out=var[:], in0=B[bb][:, :], scalar=1.0 / D,
                                                                      in1=msq[:], op0=ALU.mult, op1=ALU.subtract),
                       reads=[("B", bb), "msq"], writes=["var"])
                sc.add("dve", lambda e: e.tensor_scalar(out=var[:], in0=var[:], scalar1=LN_EPS, scalar2=None,
                                                        op0=ALU.add), reads=["var"], writes=["var"])
                sc.add("act", lambda e: e.activation(out=var[:], in_=var[:], func=AF.Ln),
                       reads=["var"], writes=["var"])
                sc.add("act", lambda e: e.activation(out=rstd[:], in_=var[:], func=AF.Exp, scale=-0.5),
                       reads=["var"], writes=["rstd"])
                sc.add("dve", lambda e: e.scalar_tensor_tensor(out=nmr[:], in0=mean[:], scalar=-1.0, in1=rstd[:],
                                                               op0=ALU.mult, op1=ALU.mult),
                       reads=["mean", "rstd"], writes=["nmr"])
                for k in range(8):
                    i = state["tmp"]
                    state["tmp"] = 1 - i
                    sc.add("dve", lambda e, k=k, t=t, i=i: e.tensor_tensor(out=tmp[i][:], in0=hres[:, k, tcols(t)],
                                                                           in1=rstd[:], op=ALU.mult),
                           reads=[("hres", k, t), "rstd"], writes=[("tmp", i)])
                    sc.add("pool", lambda e, i=i: e.tensor_tensor(out=tmp2[i][:], in0=tmp[i][:], in1=nmr[:],
                                                                  op=ALU.add),
                           reads=[("tmp", i), "nmr"], writes=[("tmp2", i)])
                    gcol = q * 16 + k
                    bcol = q * 16 + 8 + k
                    sc.add("act", lambda e, k=k, t=t, i=i, gcol=gcol, bcol=bcol: e.activation(
                        out=hres[:, k, tcols(t)], in_=tmp2[i][:], func=AF.Identity,
                        scale=lnt[:, gcol:gcol + 1], bias=lnt[:, bcol:bcol + 1]),
                        reads=[("tmp2", i), "lnt"], writes=[("hres", k, t)])
                    if want_hb:
                        sc.add("dve", lambda e, k=k, t=t, i=i, gcol=gcol, bcol=bcol: e.tensor_scalar(
                            out=hb[:, k, tcols(t)], in0=tmp2[i][:], scalar1=lnt[:, gcol:gcol + 1],
                            scalar2=lnt[:, bcol:bcol + 1], op0=ALU.mult, op1=ALU.add),
                            reads=[("tmp2", i), "lnt"], writes=[("hb", k, t)])

        def add_ffn_jobs():
            f0 = 0
            while f0 < NF:
                fb = min(3, NF - f0)

                def load(slot, f0=f0, fb=fb):
                    v = wsl[slot][:, 0:2 * 8 * 384].rearrange("p (g k n) -> p g k n", g=2, k=8)
                    wload(slot, v[:, 0, :, 0:fb * 128], wg_r[:, :, f0 * 128:(f0 + fb) * 128])
                    wload(slot, v[:, 1, :, 0:fb * 128], wu_r[:, :, f0 * 128:(f0 + fb) * 128])

                def comp(slot, f0=f0, fb=fb):
                    v = wsl[slot][:, 0:2 * 8 * 384].rearrange("p (g k n) -> p g k n", g=2, k=8)
                    for t in range(ntile):
                        for j in range(fb):
                            fc = f0 + j
                            bg = bank()
                            for k in range(8):
                                mm(B[bg][:, :], v[:, 0, k, j * 128:(j + 1) * 128], hb[:, k, tcols(t)], k == 0, k == 7,
                                   [("wsl", slot), ("hb", k, t)], [("B", bg)])
                            bu = bank()
                            for k in range(8):
                                mm(B[bu][:, :], v[:, 1, k, j * 128:(j + 1) * 128], hb[:, k, tcols(t)], k == 0, k == 7,
                                   [("wsl", slot), ("hb", k, t)], [("B", bu)])
                            i = state["tmp"]
                            state["tmp"] = 1 - i
                            sc.add("act", lambda e, i=i, bg=bg: e.activation(out=sg[i][:], in_=B[bg][:, :],
                                                                             func=AF.Silu),
                                   reads=[("B", bg)], writes=[("sg", i)])
                            sc.add("dve", lambda e, i=i, bu=bu, fc=fc, t=t: e.tensor_tensor(
                                out=hid[:, fc, tcols(t)], in0=B[bu][:, :], in1=sg[i][:], op=ALU.mult),
                                reads=[("B", bu), ("sg", i)], writes=[("hid", fc, t)])
                jobs.append((load, comp))
                f0 += fb
            for nb in range(4):
                def load(slot, nb=nb):
                    v = wsl[slot][:, 0:NF * 256].rearrange("p (k n) -> p k n", k=NF)
                    wload(slot, v, wd_r[:, :, nb * 256:(nb + 1) * 256])

                def comp(slot, nb=nb):
                    v = wsl[slot][:, 0:NF * 256].rearrange("p (k n) -> p k n", k=NF)
                    for t in range(ntile):
                        for j in range(2):
                            n = nb * 2 + j
                            b = bank()
                            for fc in range(NF):
                                mm(B[b][:, :], v[:, fc, j * 128:(j + 1) * 128], hid[:, fc, tcols(t)], fc == 0,
                                   fc == NF - 1, [("wsl", slot), ("hid", fc, t)], [("B", b)])
                            sc.add("dve", lambda e, n=n, t=t, b=b: e.scalar_tensor_tensor(
                                out=hres[:, n, tcols(t)], in0=hres[:, n, tcols(t)], scalar=ALPHA, in1=B[b][:, :],
                                op0=ALU.mult, op1=ALU.add),
                                reads=[("B", b), ("hres", n, t)], writes=[("hres", n, t)])
                jobs.append((load, comp))

        def add_conv_jobs(g):
            w_in_r = w_in.rearrange("(k p) (g n) -> p k g n", p=128, g=3)
            for jb in range(4):
                def load(slot, jb=jb):
                    v = wsl[slot][:, 0:WCAP].rearrange("p (k g n) -> p k g n", k=8, g=3)
                    for gg in range(3):
                        wload(slot, v[:, :, gg, :], w_in_r[:, :, gg, jb * 256:(jb + 1) * 256])

                def comp(slot, jb=jb):
                    v = wsl[slot][:, 0:WCAP].rearrange("p (k g n) -> p k g n", k=8, g=3)
                    for jj in range(2):
                        ch = jb * 2 + jj
                        cols = slice(jj * 128, (jj + 1) * 128)
                        if g == 0:
                            bc = bank()
                            for k in range(8):
                                mm(B[bc][:, 0:2], v[:, k, 1, cols], hbh[:, k, :], k == 0, k == 7,
                                   [("wsl", slot), "hbh"], [("B", bc)])
                            bx = bank()
                            for k in range(8):
                                mm(B[bx][:, 0:2], v[:, k, 2, cols], hbh[:, k, :], k == 0, k == 7,
                                   [("wsl", slot), "hbh"], [("B", bx)])
                            sc.add("act", lambda e, bc=bc: e.copy(out=hcs[:], in_=B[bc][:, 0:2]),
                                   reads=[("B", bc)], writes=["hcs"])
                            sc.add("dve", lambda e, bx=bx, ch=ch: e.tensor_tensor(out=uprev[:, ch, :], in0=B[bx][:, 0:2],
                                                                                  in1=hcs[:], op=ALU.mult),
                                   reads=[("B", bx), "hcs"], writes=[("uprev", ch)])
                        for t in range(ntile):
                            bbk = bank()
                            for k in range(8):
                                mm(B[bbk][:, :], v[:, k, 0, cols], hb[:, k, tcols(t)], k == 0, k == 7,
                                   [("wsl", slot), ("hb", k, t)], [("B", bbk)])
                            bc = bank()
                            for k in range(8):
                                mm(B[bc][:, :], v[:, k, 1, cols], hb[:, k, tcols(t)], k == 0, k == 7,
                                   [("wsl", slot), ("hb", k, t)], [("B", bc)])
                            bx = bank()
                            for k in range(8):
                                mm(B[bx][:, :], v[:, k, 2, cols], hb[:, k, tcols(t)], k == 0, k == 7,
                                   [("wsl", slot), ("hb", k, t)], [("B", bx)])
                            i = state["tmp"]
                            state["tmp"] = 1 - i
                            u = ubuf[i]
                            sc.add("act", lambda e, i=i, bc=bc: e.copy(out=cs[i][:], in_=B[bc][:, :]),
                                   reads=[("B", bc)], writes=[("cs", i)])
                            sc.add("pool", lambda e, u=u, ch=ch: e.tensor_copy(out=u[:, 0:2], in_=uprev[:, ch, :]),
                                   reads=[("uprev", ch)], writes=[("u", i)])
                            sc.add("dve", lambda e, u=u, i=i, bx=bx: e.tensor_tensor(out=u[:, 2:TT + 2], in0=B[bx][:, :],
                                                                                     in1=cs[i][:], op=ALU.mult),
                                   reads=[("B", bx), ("cs", i)], writes=[("u", i)])
                            sc.add("pool", lambda e, u=u, ch=ch: e.tensor_copy(out=uprev[:, ch, :], in_=u[:, TT:TT + 2]),
                                   reads=[("u", i)], writes=[("uprev", ch)])
                            y = ybuf[i]
                            c3 = ch * 3
                            sc.add("dve", lambda e, u=u, y=y, c3=c3: e.tensor_scalar(
                                out=y[:], in0=u[:, 2:TT + 2], scalar1=cwt[:, c3 + 2:c3 + 3], scalar2=None, op0=ALU.mult),
                                reads=[("u", i), "cwt"], writes=[("y", i)])
                            sc.add("dve", lambda e, u=u, y=y, c3=c3: e.scalar_tensor_tensor(
                                out=y[:], in0=u[:, 1:TT + 1], scalar=cwt[:, c3 + 1:c3 + 2], in1=y[:],
                                op0=ALU.mult, op1=ALU.add), reads=[("u", i), "cwt", ("y", i)], writes=[("y", i)])
                            sc.add("dve", lambda e, u=u, y=y, c3=c3: e.scalar_tensor_tensor(
                                out=y[:], in0=u[:, 0:TT], scalar=cwt[:, c3:c3 + 1], in1=y[:],
                                op0=ALU.mult, op1=ALU.add), reads=[("u", i), "cwt", ("y", i)], writes=[("y", i)])
                            sc.add("dve", lambda e, y=y, bbk=bbk, ch=ch, t=t: e.tensor_tensor(
                                out=ab[:, ch, tcols(t)], in0=B[bbk][:, :], in1=y[:], op=ALU.mult),
                                reads=[("B", bbk), ("y", i)], writes=[("ab", ch, t)])
                jobs.append((load, comp))

        for g in range(ngroups):
            g0 = g * TG
            for t in range(ntile):
                c0 = g0 + t * TT
                sc.add("sp", lambda e, t=t, c0=c0: e.dma_start(out=hres[:, :, tcols(t)], in_=hin_r[:, :, c0:c0 + TT]),
                       writes=[("hres", k, t) for k in range(8)], dma=True)
            if mode == "B":
                ain_r = ain.rearrange("(k p) t -> p k t", p=128)
                for t in range(ntile):
                    c0 = g0 + t * TT
                    sc.add("pool", lambda e, t=t, c0=c0: e.dma_start(out=ab[:, :, tcols(t)], in_=ain_r[:, :, c0:c0 + TT]),
                           writes=[("ab", k, t) for k in range(8)], dma=True)
                add_proj_jobs(wmix_r, ab, "ab")
                run_jobs()
            else:
                for t in range(ntile):
                    c0 = g0 + t * TT
                    sc.add("pool", lambda e, t=t, c0=c0: e.dma_start(out=hb[:, :, tcols(t)], in_=hin_r[:, :, c0:c0 + TT]),
                           writes=[("hb", k, t) for k in range(8)], dma=True)
                if g == 0:
                    halo_r = halo.rearrange("(k p) t -> p k t", p=128)
                    sc.add("pool", lambda e: e.dma_start(out=hbh[:], in_=halo_r), writes=["hbh"], dma=True)
                add_conv_jobs(g)
                add_proj_jobs(wmix_r, ab, "ab")
                run_jobs()
            layer_norm(0, True)
            add_ffn_jobs()
            run_jobs()
            layer_norm(1, False)
            for t in range(ntile):
                c0 = g0 + t * TT
                sc.add("sp", lambda e, t=t, c0=c0: e.dma_start(out=hout_r[:, :, c0:c0 + TT], in_=hres[:, :, tcols(t)]),
                       reads=[("hres", k, t) for k in range(8)], dma=True)
        sc.emit()
    return nc


def _lnp(ln1_g, ln1_b, ln2_g, ln2_b, i):
    cols = []
    for v in (ln1_g[i], ln1_b[i], ln2_g[i], ln2_b[i]):
        cols.append(np.asarray(v, np.float32).reshape(8, 128).T)
    return np.ascontiguousarray(np.concatenate(cols, axis=1))


def run_B(xT, attnT, w_o, wg, wu, wd, lnp):
    in_maps = []
    for c in range(NCORES):
        sl = slice(c * TOK, (c + 1) * TOK)
        in_maps.append({"hin": np.ascontiguousarray(xT[:, sl]), "ain": np.ascontiguousarray(attnT[:, sl]),
                        "wmix": w_o, "wg": wg, "wu": wu, "wd": wd, "lnp": lnp})
    nc = build_BC("B")
    r = run_bass_kernel_spmd(nc, in_maps, core_ids=list(range(NCORES)))
    return np.concatenate([r.results[c]["hout"] for c in range(NCORES)], axis=1)


def run_C(hT, w_in, cw, w_out, wg, wu, wd, lnp):
    in_maps = []
    for c in range(NCORES):
        sl = slice(c * TOK, (c + 1) * TOK)
        halo = np.zeros((D, 2), np.float32) if c == 0 else np.ascontiguousarray(hT[:, c * TOK - 2:c * TOK])
        in_maps.append({"hin": np.ascontiguousarray(hT[:, sl]), "halo": halo, "w_in": w_in, "cw": cw,
                        "wmix": w_out, "wg": wg, "wu": wu, "wd": wd, "lnp": lnp})
    nc = build_BC("C")
    r = run_bass_kernel_spmd(nc, in_maps, core_ids=list(range(NCORES)))
    return np.concatenate([r.results[c]["hout"] for c in range(NCORES)], axis=1)


def kernel3(x, attn_w_qkv, attn_w_o, attn_lambda_q1, attn_lambda_k1, attn_lambda_q2, attn_lambda_k2,
           attn_subln_g, conv_w_in, conv_w, conv_w_out, ffn_w_gate, ffn_w_up, ffn_w_down,
           ln1_g, ln1_b, ln2_g, ln2_b):
    f = lambda a: np.ascontiguousarray(np.asarray(a, np.float32))
    x = f(x)
    attnT = run_A(x, f(attn_w_qkv), f(attn_lambda_q1), f(attn_lambda_k1), f(attn_lambda_q2), f(attn_lambda_k2),
                  f(attn_subln_g))
    xT = np.ascontiguousarray(x.reshape(S, D).T)
    h1T = run_B(xT, attnT, f(attn_w_o)[0], f(ffn_w_gate)[0], f(ffn_w_up)[0], f(ffn_w_down)[0],
                _lnp(ln1_g, ln1_b, ln2_g, ln2_b, 0))
    cw = np.ascontiguousarray(f(conv_w)[0].reshape(3, 8, 128).transpose(2, 1, 0).reshape(128, 24))
    h2T = run_C(h1T, f(conv_w_in)[0], cw, f(conv_w_out)[0], f(ffn_w_gate)[1], f(ffn_w_up)[1], f(ffn_w_down)[1],
                _lnp(ln1_g, ln1_b, ln2_g, ln2_b, 1))
    return np.ascontiguousarray(h2T.T).reshape(1, S, D).astype(np.float32)


def build_T(tok=TOK):
    ngroups = tok // TG
    TGW = TG + 2
    nc = bass.Bass("TRN2", target_bir_lowering=False)
    hin = nc.dram_tensor("hin", [D, tok], F32, kind="ExternalInput").ap()
    xhalo = nc.dram_tensor("xhalo", [D, 2], F32, kind="ExternalInput").ap()
    ain = nc.dram_tensor("ain", [D, tok], F32, kind="ExternalInput").ap()
    ahalo = nc.dram_tensor("ahalo", [D, 2], F32, kind="ExternalInput").ap()
    uflag = nc.dram_tensor("uflag", [128, 1], F32, kind="ExternalInput").ap()
    w_o = nc.dram_tensor("w_o", [D, D], F32, kind="ExternalInput").ap()
    w_in = nc.dram_tensor("w_in", [D, 3 * D], F32, kind="ExternalInput").ap()
    cw = nc.dram_tensor("cw", [128, 24], F32, kind="ExternalInput").ap()
    w_out = nc.dram_tensor("w_out", [D, D], F32, kind="ExternalInput").ap()
    wgs = [nc.dram_tensor("wg%d" % i, [D, DFF], F32, kind="ExternalInput").ap() for i in range(2)]
    wus = [nc.dram_tensor("wu%d" % i, [D, DFF], F32, kind="ExternalInput").ap() for i in range(2)]
    wds = [nc.dram_tensor("wd%d" % i, [DFF, D], F32, kind="ExternalInput").ap() for i in range(2)]
    lnp = nc.dram_tensor("lnp", [128, 64], F32, kind="ExternalInput").ap()
    hout = nc.dram_tensor("hout", [D, tok], F32, kind="ExternalOutput").ap()

    kp = lambda a: a.rearrange("(k p) t -> p k t", p=128)
    hin_r, ain_r, hout_r = kp(hin), kp(ain), kp(hout)
    xhalo_r, ahalo_r = kp(xhalo), kp(ahalo)
    w_o_r, w_out_r = kp(w_o), kp(w_out)
    wg_r = [kp(a) for a in wgs]
    wu_r = [kp(a) for a in wus]
    wd_r = [kp(a) for a in wds]
    w_in_r = w_in.rearrange("(k p) (g n) -> p k g n", p=128, g=3)
    NF = DFF // 128

    with ExitStack() as st:
        sc = Sched(nc, st)
        hres = _sb(nc, st, "hres", [128, 8, TGW], F32)
        hb = _sb(nc, st, "hb", [128, 8, TGW], BF16)
        ab = _sb(nc, st, "ab", [128, 8, TGW], BF16)
        hid = _sb(nc, st, "hid", [128, NF, TGW], BF16)
        wsl = [_sb(nc, st, "wsl%d" % i, [128, WCAP], BF16) for i in range(3)]
        zb = _sb(nc, st, "zb", [128, 8, TT], BF16)
        zsq = _sb(nc, st, "zsq", [128, 8, TT], BF16)
        mean = _sb(nc, st, "mean", [128, TT], F32)
        msq = _sb(nc, st, "msq", [128, TT], F32)
        var = _sb(nc, st, "var", [128, TT], F32)
        rstd = _sb(nc, st, "rstd", [128, TT], F32)
        nmr = _sb(nc, st, "nmr", [128, TT], F32)
        tmp = [_sb(nc, st, "tmp%d" % i, [128, TT], F32) for i in range(2)]
        tmp2 = [_sb(nc, st, "tmq%d" % i, [128, TT], F32) for i in range(2)]
        sg = [_sb(nc, st, "sg%d" % i, [128, TT], F32) for i in range(2)]
        ones = _sb(nc, st, "ones", [128, 128], BF16)
        lnt = _sb(nc, st, "lnt", [128, 64], F32)
        cwt = _sb(nc, st, "cwt", [128, 24], F32)
        ufl = _sb(nc, st, "ufl", [128, 1], F32)
        uprev = _sb(nc, st, "uprev", [128, 8, 2], F32)
        ubuf = [_sb(nc, st, "ubuf%d" % i, [128, TT + 2], F32) for i in range(2)]
        cs = [_sb(nc, st, "cs%d" % i, [128, TT], F32) for i in range(2)]
        ybuf = [_sb(nc, st, "ybuf%d" % i, [128, TT], F32) for i in range(2)]
        hcs = _sb(nc, st, "hcs", [128, 2], F32)
        B = [_ps(nc, st, "B%d" % i) for i in range(8)]

        state = {"bank": 0, "job": 0, "tmp": 0}

        def bank():
            b = state["bank"]
            state["bank"] = (b + 1) % 6
            return b

        def flip():
            i = state["tmp"]
            state["tmp"] = 1 - i
            return i

        def mm(outp, lhsT, rhs, start, stop, reads, writes):
            sc.add("pe", lambda e: e.matmul(outp, lhsT, rhs, start=start, stop=stop),
                   reads=reads, writes=writes)

        sc.add("pool", lambda e: e.memset(ones[:], 1.0), writes=["ones"])
        sc.add("sp", lambda e: e.dma_start(out=lnt[:], in_=lnp), writes=["lnt"], dma=True)
        sc.add("sp", lambda e: e.dma_start(out=cwt[:], in_=cw), writes=["cwt"], dma=True)
        sc.add("sp", lambda e: e.dma_start(out=ufl[:], in_=uflag), writes=["ufl"], dma=True)

        def tiles(g, layer):
            ts = [(0, 0, TT), (1, TT, TT)]
            if g == 0 and layer == 0:
                ts.append((2, TG, 2))
            return ts

        jobs = []
        side = []

        def hook():
            for _ in range(state.get("rate", 0)):
                if side:
                    side.pop(0)()

        def flush():
            while side:
                side.pop(0)()
            state["rate"] = 0

        def enqueue(pieces, nsteps):
            side.extend(pieces)
            state["rate"] = max(1, -(-2 * len(side) // max(1, nsteps)))

        def split(tl):
            first = [x for x in tl if x[0] != 1]
            second = [x for x in tl if x[0] == 1]
            return first, second

        def ensure_loaded(idx):
            if idx < len(jobs) and not jobs[idx]["loaded"]:
                jobs[idx]["loaded"] = True
                jobs[idx]["load"]((state["job"] + idx) % 3)

        def run_segment(lo, hi, ln_q, ln_tl, pre_side, after_first=None):
            npairs = (hi - lo) // 2
            lf, ls = split(ln_tl)
            for p in range(npairs):
                ia, ib = lo + 2 * p, lo + 2 * p + 1
                ja, jb = jobs[ia], jobs[ib]
                sa, sb = (state["job"] + ia) % 3, (state["job"] + ib) % 3
                ensure_loaded(ia)
                ensure_loaded(ib)
                ensure_loaded(ib + 1)
                fa, sea = split(ja["tl"])
                fb_, seb = split(jb["tl"])
                if p == 0 and pre_side:
                    enqueue(pre_side, ja["spt"] * len(fa) + jb["spt"] * len(fb_))
                ja["comp"](sa, fa)
                jb["comp"](sb, fb_)
                if p == 0:
                    flush()
                    if after_first is not None:
                        after_first()
                if p == npairs - 1:
                    pcs = []
                    for tile in lf:
                        pcs += ln_pieces(ln_q, tile)
                    enqueue(pcs, ja["spt"] * len(sea) + jb["spt"] * len(seb))
                ja["comp"](sa, sea)
                ensure_loaded(ib + 2)
                jb["comp"](sb, seb)
                if p == npairs - 1:
                    flush()
            pcs = []
            for tile in ls:
                pcs += ln_pieces(ln_q, tile)
            return pcs

        def wload(slot, view, src):
            sc.add("pool", lambda e: e.dma_start(out=view, in_=src), writes=[("wsl", slot)], dma=True)

        def add_proj_jobs(w_r, tl):
            for nb in range(2):
                def load(slot, nb=nb):
                    v = wsl[slot][:, 0:4096].rearrange("p (k n) -> p k n", k=8)
                    wload(slot, v, w_r[:, :, nb * 512:(nb + 1) * 512])

                def comp(slot, tls, nb=nb):
                    v = wsl[slot][:, 0:4096].rearrange("p (k n) -> p k n", k=8)
                    for (t, c0, w) in tls:
                        for j in range(4):
                            n = nb * 4 + j
                            b = bank()
                            for k in range(8):
                                mm(B[b][:, 0:w], v[:, k, j * 128:(j + 1) * 128], ab[:, k, c0:c0 + w], k == 0, k == 7,
                                   [("wsl", slot), ("ab", k, t)], [("B", b)])
                            sc.add("dve", lambda e, n=n, c0=c0, w=w, b=b: e.scalar_tensor_tensor(
                                out=hres[:, n, c0:c0 + w], in0=hres[:, n, c0:c0 + w], scalar=ALPHA, in1=B[b][:, 0:w],
                                op0=ALU.mult, op1=ALU.add),
                                reads=[("B", b), ("hres", n, t)], writes=[("hres", n, t)])
                            hook()
                jobs.append({"load": load, "comp": comp, "tl": tl, "spt": 4, "loaded": False})

        def ln_pieces(q, tile):
            (t, c0, w) = tile
            pcs = []
            shared = {}

            def prep(k):
                sc.add("act", lambda e: e.activation(out=zsq[:, k, 0:w], in_=hres[:, k, c0:c0 + w], func=AF.Square),
                       reads=[("hres", k, t)], writes=[("zsq", k)])
                sc.add("dve", lambda e: e.tensor_copy(out=zb[:, k, 0:w], in_=hres[:, k, c0:c0 + w]),
                       reads=[("hres", k, t)], writes=[("zb", k)])

            def stats():
                ba = bank()
                for k in range(8):
                    mm(B[ba][:, 0:w], ones[:, :], zb[:, k, 0:w], k == 0, k == 7, ["ones", ("zb", k)], [("B", ba)])
                bb = bank()
                for k in range(8):
                    mm(B[bb][:, 0:w], ones[:, :], zsq[:, k, 0:w], k == 0, k == 7, ["ones", ("zsq", k)], [("B", bb)])
                sc.add("dve", lambda e: e.tensor_scalar(out=mean[:, 0:w], in0=B[ba][:, 0:w], scalar1=1.0 / D,
                                                        scalar2=None, op0=ALU.mult),
                       reads=[("B", ba)], writes=["mean"])
                sc.add("dve", lambda e: e.tensor_tensor(out=msq[:, 0:w], in0=mean[:, 0:w], in1=mean[:, 0:w],
                                                        op=ALU.mult), reads=["mean"], writes=["msq"])
                sc.add("dve", lambda e: e.scalar_tensor_tensor(
                    out=var[:, 0:w], in0=B[bb][:, 0:w], scalar=1.0 / D, in1=msq[:, 0:w], op0=ALU.mult,
                    op1=ALU.subtract), reads=[("B", bb), "msq"], writes=["var"])
                sc.add("dve", lambda e: e.tensor_scalar(out=var[:, 0:w], in0=var[:, 0:w], scalar1=LN_EPS,
                                                        scalar2=None, op0=ALU.add), reads=["var"], writes=["var"])
                sc.add("act", lambda e: e.activation(out=var[:, 0:w], in_=var[:, 0:w], func=AF.Ln),
                       reads=["var"], writes=["var"])
                br = 7
                shared["br"] = br
                sc.add("act", lambda e: e.activation(out=B[br][:, 0:w], in_=var[:, 0:w], func=AF.Exp, scale=-0.5),
                       reads=["var"], writes=[("B", br)])
                sc.add("dve", lambda e: e.scalar_tensor_tensor(out=B[6][:, 0:w], in0=mean[:, 0:w], scalar=-1.0,
                                                               in1=B[br][:, 0:w], op0=ALU.mult, op1=ALU.mult),
                       reads=["mean", ("B", br)], writes=[("B", 6)])

            def norm(k):
                br = shared["br"]
                i = flip()
                sc.add("dve", lambda e: e.tensor_tensor(out=tmp[i][:, 0:w], in0=hres[:, k, c0:c0 + w], in1=B[br][:, 0:w],
                                                        op=ALU.mult),
                       reads=[("hres", k, t), ("B", br)], writes=[("tmp", i)])
                sc.add("dve", lambda e: e.tensor_tensor(out=tmp2[i][:, 0:w], in0=tmp[i][:, 0:w], in1=B[6][:, 0:w],
                                                        op=ALU.add),
                       reads=[("tmp", i), ("B", 6)], writes=[("tmp2", i)])
                gcol = q * 16 + k
                bcol = q * 16 + 8 + k
                sc.add("act", lambda e: e.activation(out=hres[:, k, c0:c0 + w], in_=tmp2[i][:, 0:w], func=AF.Identity,
                                                     scale=lnt[:, gcol:gcol + 1], bias=lnt[:, bcol:bcol + 1]),
                       reads=[("tmp2", i), "lnt"], writes=[("hres", k, t)])
                if True:
                    sc.add("act", lambda e: e.activation(out=hb[:, k, c0:c0 + w], in_=tmp2[i][:, 0:w], func=AF.Identity,
                                                         scale=lnt[:, gcol:gcol + 1], bias=lnt[:, bcol:bcol + 1]),
                           reads=[("tmp2", i), "lnt"], writes=[("hb", k, t)])

            for k in range(8):
                pcs.append(lambda k=k: prep(k))
            pcs.append(stats)
            for k in range(8):
                pcs.append(lambda k=k: norm(k))
            return pcs

        def add_ffn_jobs(layer, tl):
            f0 = 0
            while f0 < NF:
                fb = min(3, NF - f0)

                def load(slot, f0=f0, fb=fb):
                    v = wsl[slot][:, 0:2 * 8 * 384].rearrange("p (g k n) -> p g k n", g=2, k=8)
                    wload(slot, v[:, 0, :, 0:fb * 128], wg_r[layer][:, :, f0 * 128:(f0 + fb) * 128])
                    wload(slot, v[:, 1, :, 0:fb * 128], wu_r[layer][:, :, f0 * 128:(f0 + fb) * 128])

                def comp(slot, tls, f0=f0, fb=fb):
                    v = wsl[slot][:, 0:2 * 8 * 384].rearrange("p (g k n) -> p g k n", g=2, k=8)
                    for (t, c0, w) in tls:
                        for j in range(fb):
                            fc = f0 + j
                            bg = bank()
                            for k in range(8):
                                mm(B[bg][:, 0:w], v[:, 0, k, j * 128:(j + 1) * 128], hb[:, k, c0:c0 + w], k == 0, k == 7,
                                   [("wsl", slot), ("hb", k, t)], [("B", bg)])
                            bu = bank()
                            for k in range(8):
                                mm(B[bu][:, 0:w], v[:, 1, k, j * 128:(j + 1) * 128], hb[:, k, c0:c0 + w], k == 0, k == 7,
                                   [("wsl", slot), ("hb", k, t)], [("B", bu)])
                            i = flip()
                            sc.add("act", lambda e, i=i, bg=bg, w=w: e.activation(out=sg[i][:, 0:w], in_=B[bg][:, 0:w],
                                                                                  func=AF.Silu),
                                   reads=[("B", bg)], writes=[("sg", i)])
                            sc.add("dve", lambda e, i=i, bu=bu, fc=fc, c0=c0, w=w: e.tensor_tensor(
                                out=hid[:, fc, c0:c0 + w], in0=B[bu][:, 0:w], in1=sg[i][:, 0:w], op=ALU.mult),
                                reads=[("B", bu), ("sg", i)], writes=[("hid", fc, t)])
                            hook()
                jobs.append({"load": load, "comp": comp, "tl": tl, "spt": fb, "loaded": False})
                f0 += fb
            for nb in range(4):
                def load(slot, nb=nb):
                    v = wsl[slot][:, 0:NF * 256].rearrange("p (k n) -> p k n", k=NF)
                    wload(slot, v, wd_r[layer][:, :, nb * 256:(nb + 1) * 256])

                def comp(slot, tls, nb=nb):
                    v = wsl[slot][:, 0:NF * 256].rearrange("p (k n) -> p k n", k=NF)
                    for (t, c0, w) in tls:
                        for j in range(2):
                            n = nb * 2 + j
                            b = bank()
                            for fc in range(NF):
                                mm(B[b][:, 0:w], v[:, fc, j * 128:(j + 1) * 128], hid[:, fc, c0:c0 + w], fc == 0,
                                   fc == NF - 1, [("wsl", slot), ("hid", fc, t)], [("B", b)])
                            sc.add("dve", lambda e, n=n, c0=c0, w=w, b=b: e.scalar_tensor_tensor(
                                out=hres[:, n, c0:c0 + w], in0=hres[:, n, c0:c0 + w], scalar=ALPHA, in1=B[b][:, 0:w],
                                op0=ALU.mult, op1=ALU.add),
                                reads=[("B", b), ("hres", n, t)], writes=[("hres", n, t)])
                            hook()
                jobs.append({"load": load, "comp": comp, "tl": tl, "spt": 2, "loaded": False})

        def add_conv_jobs(g):
            for jb in range(4):
                def load(slot, jb=jb):
                    v = wsl[slot][:, 0:WCAP].rearrange("p (k g n) -> p k g n", k=8, g=3)
                    for gg in range(3):
                        wload(slot, v[:, :, gg, :], w_in_r[:, :, gg, jb * 256:(jb + 1) * 256])

                def comp(slot, tls, jb=jb):
                    v = wsl[slot][:, 0:WCAP].rearrange("p (k g n) -> p k g n", k=8, g=3)
                    for jj in range(2):
                        ch = jb * 2 + jj
                        cols = slice(jj * 128, (jj + 1) * 128)
                        if g == 0 and any(x[0] == 0 for x in tls):
                            bc = bank()
                            for k in range(8):
                                mm(B[bc][:, 0:2], v[:, k, 1, cols], hb[:, k, TG:TG + 2], k == 0, k == 7,
                                   [("wsl", slot), ("hb", k, 2)], [("B", bc)])
                            bx = bank()
                            for k in range(8):
                                mm(B[bx][:, 0:2], v[:, k, 2, cols], hb[:, k, TG:TG + 2], k == 0, k == 7,
                                   [("wsl", slot), ("hb", k, 2)], [("B", bx)])
                            sc.add("act", lambda e, bc=bc: e.copy(out=hcs[:], in_=B[bc][:, 0:2]),
                                   reads=[("B", bc)], writes=["hcs"])
                            sc.add("dve", lambda e, bx=bx, ch=ch: e.scalar_tensor_tensor(
                                out=uprev[:, ch, :], in0=B[bx][:, 0:2], scalar=ufl[:, 0:1], in1=hcs[:],
                                op0=ALU.mult, op1=ALU.mult),
                                reads=[("B", bx), "hcs", "ufl"], writes=[("uprev", ch)])
                        for (t, c0, w) in tls:
                            bbk = bank()
                            for k in range(8):
                                mm(B[bbk][:, :], v[:, k, 0, cols], hb[:, k, c0:c0 + w], k == 0, k == 7,
                                   [("wsl", slot), ("hb", k, t)], [("B", bbk)])
                            bc = bank()
                            for k in range(8):
                                mm(B[bc][:, :], v[:, k, 1, cols], hb[:, k, c0:c0 + w], k == 0, k == 7,
                                   [("wsl", slot), ("hb", k, t)], [("B", bc)])
                            bx = bank()
                            for k in range(8):
                                mm(B[bx][:, :], v[:, k, 2, cols], hb[:, k, c0:c0 + w], k == 0, k == 7,
                                   [("wsl", slot), ("hb", k, t)], [("B", bx)])
                            i = flip()
                            u = ubuf[i]
                            y = ybuf[i]
                            c3 = ch * 3
                            sc.add("act", lambda e, i=i, bc=bc: e.copy(out=cs[i][:], in_=B[bc][:, :]),
                                   reads=[("B", bc)], writes=[("cs", i)])
                            sc.add("act", lambda e, u=u, ch=ch: e.copy(out=u[:, 0:2], in_=uprev[:, ch, :]),
                                   reads=[("uprev", ch)], writes=[("u", i)])
                            sc.add("dve", lambda e, u=u, i=i, bx=bx: e.tensor_tensor(out=u[:, 2:TT + 2], in0=B[bx][:, :],
                                                                                     in1=cs[i][:], op=ALU.mult),
                                   reads=[("B", bx), ("cs", i)], writes=[("u", i)])
                            sc.add("act", lambda e, u=u, ch=ch: e.copy(out=uprev[:, ch, :], in_=u[:, TT:TT + 2]),
                                   reads=[("u", i)], writes=[("uprev", ch)])
                            sc.add("dve", lambda e, u=u, y=y, c3=c3: e.tensor_scalar(
                                out=y[:], in0=u[:, 2:TT + 2], scalar1=cwt[:, c3 + 2:c3 + 3], scalar2=None, op0=ALU.mult),
                                reads=[("u", i), "cwt"], writes=[("y", i)])
                            sc.add("dve", lambda e, u=u, y=y, c3=c3: e.scalar_tensor_tensor(
                                out=y[:], in0=u[:, 1:TT + 1], scalar=cwt[:, c3 + 1:c3 + 2], in1=y[:],
                                op0=ALU.mult, op1=ALU.add), reads=[("u", i), "cwt", ("y", i)], writes=[("y", i)])
                            sc.add("dve", lambda e, u=u, y=y, c3=c3: e.scalar_tensor_tensor(
                                out=y[:], in0=u[:, 0:TT], scalar=cwt[:, c3:c3 + 1], in1=y[:],
                                op0=ALU.mult, op1=ALU.add), reads=[("u", i), "cwt", ("y", i)], writes=[("y", i)])
                            sc.add("dve", lambda e, y=y, bbk=bbk, ch=ch, c0=c0, w=w: e.tensor_tensor(
                                out=ab[:, ch, c0:c0 + w], in0=B[bbk][:, :], in1=y[:], op=ALU.mult),
                                reads=[("B", bbk), ("y", i)], writes=[("ab", ch, t)])
                            hook()
                jobs.append({"load": load, "comp": comp, "tl": tiles(g, 1), "spt": 2, "loaded": False})

        def load_tile(g, tile):
            (t, c0, w) = tile
            g0 = g * TG
            if t < 2:
                sc.add("sp", lambda e: e.dma_start(out=hres[:, :, c0:c0 + w], in_=hin_r[:, :, g0 + c0:g0 + c0 + w]),
                       writes=[("hres", k, t) for k in range(8)], dma=True)
                sc.add("pool", lambda e: e.dma_start(out=ab[:, :, c0:c0 + w], in_=ain_r[:, :, g0 + c0:g0 + c0 + w]),
                       writes=[("ab", k, t) for k in range(8)], dma=True)
            else:
                sc.add("sp", lambda e: e.dma_start(out=hres[:, :, c0:c0 + w], in_=xhalo_r),
                       writes=[("hres", k, t) for k in range(8)], dma=True)
                sc.add("pool", lambda e: e.dma_start(out=ab[:, :, c0:c0 + w], in_=ahalo_r),
                       writes=[("ab", k, t) for k in range(8)], dma=True)

        def store_tile(g, tile):
            (t, c0, w) = tile
            g0 = g * TG
            sc.add("sp", lambda e: e.dma_start(out=hout_r[:, :, g0 + c0:g0 + c0 + w], in_=hres[:, :, c0:c0 + w]),
                   reads=[("hres", k, t) for k in range(8)], dma=True)

        pre = []
        for g in range(ngroups):
            tl0 = tiles(g, 0)
            tl1 = tiles(g, 1)
            f0_, s0_ = split(tl0)
            if g == 0:
                f0_, s0_ = tl0, []
            for tile in f0_:
                load_tile(g, tile)

            def after_first(g=g, s0_=s0_):
                if g > 0:
                    for tile in split(tiles(g - 1, 1))[1]:
                        store_tile(g - 1, tile)
                for tile in s0_:
                    load_tile(g, tile)

            base = len(jobs)
            add_proj_jobs(w_o_r, tl0)
            n1 = len(jobs)
            add_ffn_jobs(0, tl0)
            n2 = len(jobs)
            add_conv_jobs(g)
            add_proj_jobs(w_out_r, tl1)
            n3 = len(jobs)
            add_ffn_jobs(1, tl1)
            n4 = len(jobs)
            pre = run_segment(base, n1, 0, tl0, pre, after_first)
            pre = run_segment(n1, n2, 1, tl0, pre)
            pre = run_segment(n2, n3, 2, tl1, pre)
            pre = run_segment(n3, n4, 3, tl1, pre)
            for tile in split(tl1)[0]:
                store_tile(g, tile)
        enqueue(pre, 1)
        flush()
        for tile in split(tiles(ngroups - 1, 1))[1]:
            store_tile(ngroups - 1, tile)
        sc.emit()
    return nc


def run_T(xT, attnT, inp):
    f = lambda a: np.ascontiguousarray(np.asarray(a, np.float32))
    lnp = np.ascontiguousarray(np.concatenate(
        [_lnp(inp["ln1_g"], inp["ln1_b"], inp["ln2_g"], inp["ln2_b"], i) for i in range(2)], axis=1))
    cw = np.ascontiguousarray(f(inp["conv_w"])[0].reshape(3, 8, 128).transpose(2, 1, 0).reshape(128, 24))
    shared = {"w_o": f(inp["attn_w_o"])[0], "w_in": f(inp["conv_w_in"])[0], "cw": cw, "w_out": f(inp["conv_w_out"])[0],
              "wg0": f(inp["ffn_w_gate"])[0], "wu0": f(inp["ffn_w_up"])[0], "wd0": f(inp["ffn_w_down"])[0],
              "wg1": f(inp["ffn_w_gate"])[1], "wu1": f(inp["ffn_w_up"])[1], "wd1": f(inp["ffn_w_down"])[1], "lnp": lnp}
    in_maps = []
    for c in range(NCORES):
        sl = slice(c * TOK, (c + 1) * TOK)
        if c == 0:
            xh = np.zeros((D, 2), np.float32)
            ah = np.zeros((D, 2), np.float32)
        else:
            xh = np.ascontiguousarray(xT[:, c * TOK - 2:c * TOK])
            ah = np.ascontiguousarray(attnT[:, c * TOK - 2:c * TOK])
        m = {"hin": np.ascontiguousarray(xT[:, sl]), "ain": np.ascontiguousarray(attnT[:, sl]), "xhalo": xh,
             "ahalo": ah, "uflag": np.full((128, 1), 0.0 if c == 0 else 1.0, np.float32)}
        m.update(shared)
        in_maps.append(m)
    nc = build_T()
    r = run_bass_kernel_spmd(nc, in_maps, core_ids=list(range(NCORES)))
    return np.concatenate([r.results[c]["hout"] for c in range(NCORES)], axis=1)


def kernel(**inp):
    f = lambda a: np.ascontiguousarray(np.asarray(a, np.float32))
    x = f(inp["x"])
    attnT = run_A(x, f(inp["attn_w_qkv"]), f(inp["attn_lambda_q1"]), f(inp["attn_lambda_k1"]),
                  f(inp["attn_lambda_q2"]), f(inp["attn_lambda_k2"]), f(inp["attn_subln_g"]))
    if os.environ.get("K_PROBE_ONLY_A"):
        return np.zeros((1, S, D), np.float32)
    xT = np.ascontiguousarray(x.reshape(S, D).T)
    h2T = run_T(xT, attnT, inp)
    return np.ascontiguousarray(h2T.T).reshape(1, S, D).astype(np.float32)
```

```python
import math
import os
from contextlib import ExitStack

import numpy as np
import concourse.bass as bass
import concourse.mybir as mybir
from concourse.bass_utils import run_bass_kernel_spmd

F32 = mybir.dt.float32
BF16 = mybir.dt.bfloat16
AF = mybir.ActivationFunctionType
ALU = mybir.AluOpType
AX = mybir.AxisListType

D = 1024
S = 16384
NCORES = 8
DFF = 2816
ALPHA = (2 * 2) ** 0.25
LN_EPS = 1e-5
ROPE_THETA = 500000.0
LAM_INIT0 = 0.8 - 0.6 * math.exp(-0.3 * 0)


class _Op:
    __slots__ = ("eng", "fn", "deps", "dma", "sig", "sem", "val", "prev")


class Sched:
    ENGS = ("sp", "pe", "act", "dve", "pool")
    NDS = 12

    def __init__(self, nc, stack):
        self.nc = nc
        self.stack = stack
        self.ops = []
        self.last_w = {}
        self.readers = {}

    def add(self, eng, fn, reads=(), writes=(), dma=False):
        op = _Op()
        op.eng, op.fn, op.dma, op.sig = eng, fn, dma, False
        op.sem = None
        op.val = 0
        op.prev = 0
        deps = []
        seen = set()

        def dep(o):
            if o is not None and id(o) not in seen:
                seen.add(id(o))
                deps.append(o)

        for r in reads:
            dep(self.last_w.get(r))
            if isinstance(r, tuple) and r[0] == "B":
                for rd in self.readers.get(r, ()):
                    if rd.eng != eng:
                        dep(rd)
        for w in writes:
            dep(self.last_w.get(w))
            for rd in self.readers.get(w, ()):
                dep(rd)
        op.deps = deps
        for r in reads:
            self.readers.setdefault(r, []).append(op)
        for w in writes:
            self.last_w[w] = op
            self.readers[w] = []
        self.ops.append(op)
        return op

    def emit(self):
        nc = self.nc
        st = self.stack
        esem = {e: st.enter_context(nc.semaphore("e_" + e)) for e in ("pe", "act", "dve", "pool")}
        dsem = {e: [st.enter_context(nc.semaphore("d_%s%d" % (e, i))) for i in range(self.NDS)]
                for e in ("sp", "act", "pool")}
        for op in self.ops:
            for d in op.deps:
                if d.dma:
                    continue
                if d.eng == "pe" and op.eng == "pe" and not op.dma:
                    continue
                d.sig = True
        cnt = {e: 0 for e in self.ENGS}
        dcnt = {e: 0 for e in self.ENGS}
        for op in self.ops:
            if op.dma:
                n = dcnt[op.eng]
                dcnt[op.eng] += 1
                op.sem = dsem[op.eng][n % self.NDS]
                op.val = 16 * (n // self.NDS + 1)
                op.prev = 16 * (n // self.NDS)
            elif op.sig:
                cnt[op.eng] += 1
                op.sem = esem[op.eng]
                op.val = cnt[op.eng]
        finals = []
        for e in dsem:
            n = dcnt[e]
            for i in range(min(n, self.NDS)):
                tot = (n - 1 - i) // self.NDS + 1
                finals.append((dsem[e][i], 16 * tot))

        block = st.enter_context(nc.Block())
        decos = {"sp": block.sync, "pe": block.tensor, "act": block.scalar,
                 "dve": block.vector, "pool": block.gpsimd}
        for ename in self.ENGS:
            ops_e = [op for op in self.ops if op.eng == ename]

            def body(eng, ops_e=ops_e, ename=ename):
                waited = {}

                def wait(sem, val):
                    if val > 0 and waited.get(id(sem), 0) < val:
                        eng.wait_ge(sem, val)
                        waited[id(sem)] = val

                for op in ops_e:
                    need = {}
                    for d in op.deps:
                        if (not d.dma) and d.eng == "pe" and ename == "pe" and not op.dma:
                            continue
                        k = id(d.sem)
                        if k not in need or need[k][1] < d.val:
                            need[k] = (d.sem, d.val)
                    if op.dma:
                        wait(op.sem, op.prev)
                    for sem, val in need.values():
                        wait(sem, val)
                    ins = op.fn(eng)
                    if op.dma:
                        ins.then_inc(op.sem, 16)
                    elif op.sig:
                        ins.then_inc(op.sem, 1)
                if ename == "sp":
                    for sem, val in finals:
                        wait(sem, val)

            decos[ename](body)


def _sb(nc, st, name, shape, dt):
    return st.enter_context(nc.sbuf_tensor(name, list(shape), dt))


def _ps(nc, st, name):
    return st.enter_context(nc.psum_tensor(name, [128, 512], F32))


def build_A(nt=S // 512, stage=99):
    TT = 512
    SS = nt * TT
    nc = bass.Bass("TRN2", target_bir_lowering=False)
    xT = nc.dram_tensor("xT", [D, SS], F32, kind="ExternalInput").ap()
    wqkv = nc.dram_tensor("wqkv", [D, 384], F32, kind="ExternalInput").ap()
    ropeC = nc.dram_tensor("ropeC", [128, SS], F32, kind="ExternalInput").ap()
    ropeS = nc.dram_tensor("ropeS", [128, SS], F32, kind="ExternalInput").ap()
    perm = nc.dram_tensor("perm", [128, 128], F32, kind="ExternalInput").ap()
    lam4 = nc.dram_tensor("lam4", [1, 256], F32, kind="ExternalInput").ap()
    subg = nc.dram_tensor("subg", [128, 1], F32, kind="ExternalInput").ap()
    out = nc.dram_tensor("attnT", [128, SS], F32, kind="ExternalOutput").ap()
    xTr = xT.rearrange("(k p) t -> p k t", p=128)
    wr = wqkv.rearrange("(k p) n -> p k n", p=128)

    with ExitStack() as st:
        sc = Sched(nc, st)
        qT = _sb(nc, st, "qT", [128, SS], BF16)
        kT = _sb(nc, st, "kT", [128, SS], BF16)
        vS = _sb(nc, st, "vS", [128, nt * 4, 128], BF16)
        xb = [_sb(nc, st, "xb%d" % i, [128, 8, TT], BF16) for i in range(2)]
        wsb = _sb(nc, st, "wsb", [128, 8, 384], BF16)
        pm = _sb(nc, st, "pm", [128, 128], BF16)
        ones = _sb(nc, st, "ones", [128, 128], BF16)
        ones32 = _sb(nc, st, "ones32", [128, 128], F32)
        zs = _sb(nc, st, "zs", [128, TT], F32)
        rc = [_sb(nc, st, "rc%d" % i, [128, TT], F32) for i in range(2)]
        rs = [_sb(nc, st, "rs%d" % i, [128, TT], F32) for i in range(2)]
        qb = _sb(nc, st, "qb", [128, TT], BF16)
        kb_ = _sb(nc, st, "kb_", [128, TT], BF16)
        t1 = [_sb(nc, st, "t1%d" % i, [128, TT], F32) for i in range(2)]
        t2 = [_sb(nc, st, "t2%d" % i, [128, TT], F32) for i in range(2)]
        pT = [[_sb(nc, st, "pT%d%d" % (c, s), [128, TT], BF16) for s in range(2)] for c in range(2)]
        rz = [_sb(nc, st, "rz%d" % i, [128, TT], F32) for i in range(2)]
        oo = [_sb(nc, st, "oo%d" % i, [128, TT], F32) for i in range(2)]
        od = _sb(nc, st, "od", [128, TT], F32)
        osq = _sb(nc, st, "osq", [128, TT], BF16)
        lnv = _sb(nc, st, "lnv", [128, TT], F32)
        rstd = _sb(nc, st, "rstd", [128, TT], F32)
        res = [_sb(nc, st, "res%d" % i, [128, TT], F32) for i in range(2)]
        lamt = _sb(nc, st, "lamt", [128, 256], F32)
        lprod = _sb(nc, st, "lprod", [128, 128], F32)
        lsum = _sb(nc, st, "lsum", [128, 2], F32)
        lexp = _sb(nc, st, "lexp", [128, 2], F32)
        neglam = _sb(nc, st, "neglam", [128, 1], F32)
        gsc = _sb(nc, st, "gsc", [128, 1], F32)
        gld = _sb(nc, st, "gld", [128, 1], F32)
        B = [_ps(nc, st, "B%d" % i) for i in range(8)]

        sc.add("pool", lambda e: e.dma_start(out=wsb[:], in_=wr), writes=["wsb"], dma=True)
        sc.add("pool", lambda e: e.dma_start(out=pm[:], in_=perm), writes=["pm"], dma=True)
        sc.add("sp", lambda e: e.dma_start(out=lamt[:], in_=lam4.partition_broadcast(128)),
               writes=["lamt"], dma=True)
        sc.add("sp", lambda e: e.dma_start(out=gld[:], in_=subg), writes=["gld"], dma=True)
        sc.add("pool", lambda e: e.memset(ones[:], 1.0), writes=["ones"])
        sc.add("pool", lambda e: e.memset(ones32[:], 1.0), writes=["ones32"])
        sc.add("dve", lambda e: e.tensor_tensor(out=lprod[:, 0:64], in0=lamt[:, 0:64], in1=lamt[:, 64:128],
                                                op=ALU.mult), reads=["lamt"], writes=["lprod"])
        sc.add("dve", lambda e: e.tensor_tensor(out=lprod[:, 64:128], in0=lamt[:, 128:192], in1=lamt[:, 192:256],
                                                op=ALU.mult), reads=["lamt"], writes=["lprod"])
        sc.add("dve", lambda e: e.reduce_sum(out=lsum[:, 0:1], in_=lprod[:, 0:64], axis=AX.X),
               reads=["lprod"], writes=["lsum"])
        sc.add("dve", lambda e: e.reduce_sum(out=lsum[:, 1:2], in_=lprod[:, 64:128], axis=AX.X),
               reads=["lprod"], writes=["lsum"])
        sc.add("act", lambda e: e.activation(out=lexp[:], in_=lsum[:], func=AF.Exp),
               reads=["lsum"], writes=["lexp"])
        sc.add("dve", lambda e: e.tensor_tensor(out=neglam[:], in0=lexp[:, 1:2], in1=lexp[:, 0:1],
                                                op=ALU.subtract), reads=["lexp"], writes=["neglam"])
        sc.add("dve", lambda e: e.tensor_scalar(out=neglam[:], in0=neglam[:], scalar1=-LAM_INIT0, scalar2=None,
                                                op0=ALU.add), reads=["neglam"], writes=["neglam"])
        sc.add("dve", lambda e: e.tensor_scalar(out=gsc[:], in0=gld[:], scalar1=(1.0 - LAM_INIT0), scalar2=None,
                                                op0=ALU.mult), reads=["gld"], writes=["gsc"])

        def load_x(tt):
            s = tt % 2
            c0 = tt * TT
            sc.add("pool", lambda e: e.dma_start(out=xb[s][:], in_=xTr[:, :, c0:c0 + TT]),
                   writes=[("xb", s)], dma=True)
            sc.add("sp", lambda e: e.dma_start(out=rc[s][:], in_=ropeC[:, c0:c0 + TT]),
                   writes=[("rc", s)], dma=True)
            sc.add("sp", lambda e: e.dma_start(out=rs[s][:], in_=ropeS[:, c0:c0 + TT]),
                   writes=[("rs", s)], dma=True)

        def mm(outp, lhsT, rhs, start, stop, reads, writes):
            sc.add("pe", lambda e: e.matmul(outp, lhsT, rhs, start=start, stop=stop),
                   reads=reads, writes=writes)

        def qkv_mm(tt):
            s = tt % 2
            c0 = tt * TT
            xs = xb[s]
            for k in range(8):
                mm(B[0][:, :], wsb[:, k, 0:128], xs[:, k, :], k == 0, k == 7, ["wsb", ("xb", s)], [("B", 0)])
            for k in range(8):
                mm(B[1][:, :], wsb[:, k, 128:256], xs[:, k, :], k == 0, k == 7, ["wsb", ("xb", s)], [("B", 1)])
            for sub in range(4):
                for k in range(8):
                    mm(B[2][:, sub * 128:(sub + 1) * 128], xs[:, k, sub * 128:(sub + 1) * 128],
                       wsb[:, k, 256:384], k == 0, k == 7, ["wsb", ("xb", s)], [("B", 2)])

        def qkv_post(tt):
            s = tt % 2
            c0 = tt * TT
            sc.add("act", lambda e: e.copy(out=qb[:], in_=B[0][:, :]), reads=[("B", 0)], writes=["qb"])
            sc.add("act", lambda e: e.copy(out=kb_[:], in_=B[1][:, :]), reads=[("B", 1)], writes=["kb_"])
            sc.add("act", lambda e: e.copy(out=vS[:, tt * 4:(tt + 1) * 4, :],
                                           in_=B[2][:, :].rearrange("p (a b) -> p a b", a=4)),
                   reads=[("B", 2)], writes=[("vS", tt)])
            mm(B[3][:, :], pm[:, :], qb[:, :], True, True, ["pm", "qb"], [("B", 3)])
            sc.add("dve", lambda e: e.tensor_tensor(out=t1[0][:], in0=B[0][:, :], in1=rc[s][:], op=ALU.mult),
                   reads=[("B", 0), ("rc", s)], writes=["t1q"])
            sc.add("dve", lambda e: e.tensor_tensor(out=t2[0][:], in0=B[3][:, :], in1=rs[s][:], op=ALU.mult),
                   reads=[("B", 3), ("rs", s)], writes=["t2q"])
            sc.add("pool", lambda e: e.tensor_tensor(out=qT[:, c0:c0 + TT], in0=t1[0][:], in1=t2[0][:], op=ALU.add),
                   reads=["t1q", "t2q"], writes=[("qT", tt)])
            mm(B[2][:, :], pm[:, :], kb_[:, :], True, True, ["pm", "kb_"], [("B", 2)])
            sc.add("dve", lambda e: e.tensor_tensor(out=t1[1][:], in0=B[1][:, :], in1=rc[s][:], op=ALU.mult),
                   reads=[("B", 1), ("rc", s)], writes=["t1k"])
            sc.add("dve", lambda e: e.tensor_tensor(out=t2[1][:], in0=B[2][:, :], in1=rs[s][:], op=ALU.mult),
                   reads=[("B", 2), ("rs", s)], writes=["t2k"])
            sc.add("pool", lambda e: e.tensor_tensor(out=kT[:, c0:c0 + TT], in0=t1[1][:], in1=t2[1][:], op=ALU.add),
                   reads=["t1k", "t2k"], writes=[("kT", tt)])

        def scores(qt, kb, slot):
            j = kb - 4 * qt
            f0 = 128 * j if j > 0 else 0
            q0 = qt * TT
            for c in range(2):
                bi = c * 2 + slot
                mm(B[bi][:, f0:TT], kT[c * 64:(c + 1) * 64, kb * 128:(kb + 1) * 128],
                   qT[c * 64:(c + 1) * 64, q0 + f0:q0 + TT], True, True,
                   [("kT", kb // 4), ("qT", qt)], [("B", bi)])

        def expo(qt, kb, slot):
            j = kb - 4 * qt
            f0 = 128 * j if j > 0 else 0
            for c in range(2):
                bi = c * 2 + slot
                sc.add("act", lambda e, c=c, bi=bi: e.activation(out=pT[c][slot][:, f0:TT], in_=B[bi][:, f0:TT],
                                                                 func=AF.Exp, scale=0.125),
                       reads=[("B", bi)], writes=[("pT", c, slot)])
                if j >= 0:
                    sc.add("pool", lambda e, c=c: e.memset(pT[c][slot][64:128, f0:f0 + 64], 0.0),
                           writes=[("pT", c, slot)])

        def pv(qt, kb, slot, last):
            j = kb - 4 * qt
            f0 = 128 * j if j > 0 else 0
            for c in range(2):
                mm(B[4 + c][:, f0:TT], vS[:, kb, :], pT[c][slot][:, f0:TT], kb == 0, kb == last,
                   [("vS", kb // 4), ("pT", c, slot)], [("B", 4 + c)])
                if c == 0:
                    mm(B[6][:, f0:TT], ones[:, :], pT[0][slot][:, f0:TT], kb == 0, kb == last,
                       ["ones", ("pT", 0, slot)], [("B", 6)])
                elif kb == 0:
                    sc.add("dve", lambda e: e.tensor_copy(out=B[7][:, :], in_=pT[1][slot][:, :]),
                           reads=[("pT", 1, slot)], writes=[("B", 7)])
                else:
                    sc.add("dve", lambda e: e.tensor_tensor(out=B[7][:, f0:TT], in0=B[7][:, f0:TT],
                                                            in1=pT[1][slot][:, f0:TT], op=ALU.add),
                           reads=[("pT", 1, slot), ("B", 7)], writes=[("B", 7)])

        def epilogue_pre(qt):
            q0 = qt * TT
            r = res[qt % 2]
            sc.add("dve", lambda e: e.tensor_copy(out=zs[:], in_=B[7][:, :]), reads=[("B", 7)], writes=["zs"])
            mm(B[7][:, :], ones32[:, :], zs[:, :], True, True, ["ones32", "zs"], [("B", 7)])
            for c in range(2):
                sc.add("dve", lambda e, c=c: e.reciprocal(out=rz[c][:], in_=B[6 + c][:, :]),
                       reads=[("B", 6 + c)], writes=[("rz", c)])
                sc.add("dve", lambda e, c=c: e.tensor_tensor(out=oo[c][:], in0=B[4 + c][:, :], in1=rz[c][:],
                                                             op=ALU.mult),
                       reads=[("B", 4 + c), ("rz", c)], writes=[("oo", c)])
            sc.add("dve", lambda e: e.scalar_tensor_tensor(out=od[:], in0=oo[1][:], scalar=neglam[:, 0:1],
                                                           in1=oo[0][:], op0=ALU.mult, op1=ALU.add),
                   reads=[("oo", 0), ("oo", 1), "neglam"], writes=["od"])
            sc.add("pool", lambda e: e.tensor_tensor(out=osq[:], in0=od[:], in1=od[:], op=ALU.mult),
                   reads=["od"], writes=["osq"])

        def epilogue_post(qt):
            q0 = qt * TT
            r = res[qt % 2]
            mm(B[3][:, :], ones[:, :], osq[:, :], True, True, ["ones", "osq"], [("B", 3)])
            sc.add("dve", lambda e: e.tensor_scalar(out=lnv[:], in0=B[3][:, :], scalar1=1.0 / 128.0,
                                                    scalar2=LN_EPS, op0=ALU.mult, op1=ALU.add),
                   reads=[("B", 3)], writes=["lnv"])
            sc.add("act", lambda e: e.activation(out=lnv[:], in_=lnv[:], func=AF.Ln),
                   reads=["lnv"], writes=["lnv"])
            sc.add("act", lambda e: e.activation(out=rstd[:], in_=lnv[:], func=AF.Exp, scale=-0.5),
                   reads=["lnv"], writes=["rstd"])
            sc.add("dve", lambda e: e.scalar_tensor_tensor(out=r[:], in0=od[:], scalar=gsc[:, 0:1], in1=rstd[:],
                                                           op0=ALU.mult, op1=ALU.mult),
                   reads=["od", "rstd", "gsc"], writes=[("res", qt % 2)])
            sc.add("sp", lambda e: e.dma_start(out=out[:, q0:q0 + TT], in_=r[:]),
                   reads=[("res", qt % 2)], dma=True)

        load_x(0)
        if stage == 1:
            sc.add("sp", lambda e: e.dma_start(out=out[:, 0:TT], in_=rc[0][:]), reads=[("rc", 0), "neglam", "gsc", "wsb", "pm", "ones", ("xb", 0)], dma=True)
            sc.emit()
            return nc
        qkv_mm(0)
        qkv_post(0)
        for tt in range(nt):
            if tt + 1 < nt:
                load_x(tt + 1)
            if stage == 2:
                sc.add("act", lambda e: e.copy(out=res[0][:], in_=qT[:, 0:TT]), reads=[("qT", 0), ("kT", 0), ("vS", 0)], writes=[("res", 0)])
                sc.add("sp", lambda e: e.dma_start(out=out[:, 0:TT], in_=res[0][:]), reads=[("res", 0)], dma=True)
                sc.emit()
                return nc
            nkb = 4 * tt + 4
            last = nkb - 1
            scores(tt, 0, 0)
            for kb in range(nkb):
                if kb + 1 < nkb:
                    scores(tt, kb + 1, (kb + 1) % 2)
                expo(tt, kb, kb % 2)
                pv(tt, kb, kb % 2, last)
            epilogue_pre(tt)
            if tt + 1 < nt:
                qkv_mm(tt + 1)
            epilogue_post(tt)
            if tt + 1 < nt:
                qkv_post(tt + 1)
        sc.emit()
    return nc


def rope_tables(n):
    half = 8
    pos = np.arange(n, dtype=np.float32)
    inv = (ROPE_THETA ** (-np.arange(0, 16, 2, dtype=np.float32) / np.float32(16))).astype(np.float32)
    ang = (pos[:, None] * inv[None, :]).astype(np.float32)
    cos = np.cos(ang).astype(np.float32).T
    sin = np.sin(ang).astype(np.float32).T
    C = np.ones((128, n), np.float32)
    Sg = np.zeros((128, n), np.float32)
    for c in range(2):
        b = c * 64
        C[b:b + 8] = cos
        C[b + 8:b + 16] = cos
        Sg[b:b + 8] = -sin
        Sg[b + 8:b + 16] = sin
    P = np.zeros((128, 128), np.float32)
    for c in range(2):
        b = c * 64
        for i in range(half):
            P[b + i + 8, b + i] = 1.0
            P[b + i, b + i + 8] = 1.0
    return C, Sg, P


def run_A(x, attn_w_qkv, lq1, lk1, lq2, lk2, subln_g, nt=S // 512):
    SS = nt * 512
    xT = np.ascontiguousarray(x.reshape(S, D)[:SS].T)
    C, Sg, P = rope_tables(SS)
    lam4 = np.concatenate([lq1.reshape(-1), lk1.reshape(-1), lq2.reshape(-1), lk2.reshape(-1)]).reshape(1, 256)
    lam4 = np.ascontiguousarray(lam4.astype(np.float32))
    subg = np.ascontiguousarray(subln_g.reshape(128, 1).astype(np.float32))
    w = attn_w_qkv.reshape(D, 3 * D)
    in_maps = []
    for h in range(NCORES):
        wh = np.concatenate([w[:, h * 128:(h + 1) * 128], w[:, D + h * 128:D + (h + 1) * 128],
                             w[:, 2 * D + h * 128:2 * D + (h + 1) * 128]], axis=1)
        in_maps.append({"xT": xT, "wqkv": np.ascontiguousarray(wh), "ropeC": C, "ropeS": Sg, "perm": P,
                        "lam4": lam4, "subg": subg})
    nc = build_A(nt)
    r = run_bass_kernel_spmd(nc, in_maps, core_ids=list(range(NCORES)))
    return np.concatenate([r.results[h]["attnT"] for h in range(NCORES)], axis=0)


TOK = S // NCORES
TG = 1024
TT = 512
WCAP = 6144


def build_BC(mode, tok=TOK):
    ngroups = tok // TG
    ntile = TG // TT
    nc = bass.Bass("TRN2", target_bir_lowering=False)
    hin = nc.dram_tensor("hin", [D, tok], F32, kind="ExternalInput").ap()
    if mode == "B":
        ain = nc.dram_tensor("ain", [D, tok], F32, kind="ExternalInput").ap()
        wmix = nc.dram_tensor("wmix", [D, D], F32, kind="ExternalInput").ap()
    else:
        halo = nc.dram_tensor("halo", [D, 2], F32, kind="ExternalInput").ap()
        w_in = nc.dram_tensor("w_in", [D, 3 * D], F32, kind="ExternalInput").ap()
        cw = nc.dram_tensor("cw", [128, 24], F32, kind="ExternalInput").ap()
        wmix = nc.dram_tensor("wmix", [D, D], F32, kind="ExternalInput").ap()
    wg = nc.dram_tensor("wg", [D, DFF], F32, kind="ExternalInput").ap()
    wu = nc.dram_tensor("wu", [D, DFF], F32, kind="ExternalInput").ap()
    wd = nc.dram_tensor("wd", [DFF, D], F32, kind="ExternalInput").ap()
    lnp = nc.dram_tensor("lnp", [128, 32], F32, kind="ExternalInput").ap()
    hout = nc.dram_tensor("hout", [D, tok], F32, kind="ExternalOutput").ap()

    hin_r = hin.rearrange("(k p) t -> p k t", p=128)
    hout_r = hout.rearrange("(k p) t -> p k t", p=128)
    wmix_r = wmix.rearrange("(k p) n -> p k n", p=128)
    wg_r = wg.rearrange("(k p) n -> p k n", p=128)
    wu_r = wu.rearrange("(k p) n -> p k n", p=128)
    wd_r = wd.rearrange("(k p) n -> p k n", p=128)
    NF = DFF // 128

    with ExitStack() as st:
        sc = Sched(nc, st)
        hres = _sb(nc, st, "hres", [128, 8, TG], F32)
        hb = _sb(nc, st, "hb", [128, 8, TG], BF16)
        ab = _sb(nc, st, "ab", [128, 8, TG], BF16)
        hid = _sb(nc, st, "hid", [128, NF, TG], BF16)
        wsl = [_sb(nc, st, "wsl%d" % i, [128, WCAP], BF16) for i in range(3)]
        zb = _sb(nc, st, "zb", [128, 8, TT], BF16)
        zsq = _sb(nc, st, "zsq", [128, 8, TT], BF16)
        mean = _sb(nc, st, "mean", [128, TT], F32)
        msq = _sb(nc, st, "msq", [128, TT], F32)
        var = _sb(nc, st, "var", [128, TT], F32)
        rstd = _sb(nc, st, "rstd", [128, TT], F32)
        nmr = _sb(nc, st, "nmr", [128, TT], F32)
        tmp = [_sb(nc, st, "tmp%d" % i, [128, TT], F32) for i in range(2)]
        tmp2 = [_sb(nc, st, "tmq%d" % i, [128, TT], F32) for i in range(2)]
        sg = [_sb(nc, st, "sg%d" % i, [128, TT], F32) for i in range(2)]
        ones = _sb(nc, st, "ones", [128, 128], BF16)
        lnt = _sb(nc, st, "lnt", [128, 32], F32)
        B = [_ps(nc, st, "B%d" % i) for i in range(8)]
        if mode == "C":
            cwt = _sb(nc, st, "cwt", [128, 24], F32)
            hbh = _sb(nc, st, "hbh", [128, 8, 2], BF16)
            uprev = _sb(nc, st, "uprev", [128, 8, 2], F32)
            ubuf = [_sb(nc, st, "ubuf%d" % i, [128, TT + 2], F32) for i in range(2)]
            cs = [_sb(nc, st, "cs%d" % i, [128, TT], F32) for i in range(2)]
            ybuf = [_sb(nc, st, "ybuf%d" % i, [128, TT], F32) for i in range(2)]
            hcs = _sb(nc, st, "hcs", [128, 2], F32)

        state = {"bank": 0, "job": 0, "tmp": 0}

        def bank():
            b = state["bank"]
            state["bank"] = (b + 1) % 8
            return b

        def mm(outp, lhsT, rhs, start, stop, reads, writes):
            sc.add("pe", lambda e: e.matmul(outp, lhsT, rhs, start=start, stop=stop),
                   reads=reads, writes=writes)

        sc.add("pool", lambda e: e.memset(ones[:], 1.0), writes=["ones"])
        sc.add("sp", lambda e: e.dma_start(out=lnt[:], in_=lnp), writes=["lnt"], dma=True)
        if mode == "C":
            sc.add("sp", lambda e: e.dma_start(out=cwt[:], in_=cw), writes=["cwt"], dma=True)

        jobs = []

        def run_jobs():
            n = len(jobs)
            base = state["job"]
            if n:
                jobs[0][0]((base) % 3)
            for i in range(n):
                if i + 1 < n:
                    jobs[i + 1][0]((base + i + 1) % 3)
                jobs[i][1]((base + i) % 3)
            state["job"] = base + n
            del jobs[:]

        def wload(slot, view, src):
            sc.add("pool", lambda e: e.dma_start(out=view, in_=src), writes=[("wsl", slot)], dma=True)

        def tcols(t):
            return slice(t * TT, (t + 1) * TT)

        def add_proj_jobs(w_r, src, srckey):
            for nb in range(2):
                def load(slot, nb=nb):
                    v = wsl[slot][:, 0:4096].rearrange("p (k n) -> p k n", k=8)
                    wload(slot, v, w_r[:, :, nb * 512:(nb + 1) * 512])

                def comp(slot, nb=nb):
                    v = wsl[slot][:, 0:4096].rearrange("p (k n) -> p k n", k=8)
                    for t in range(ntile):
                        for j in range(4):
                            n = nb * 4 + j
                            b = bank()
                            for k in range(8):
                                mm(B[b][:, :], v[:, k, j * 128:(j + 1) * 128], src[:, k, tcols(t)], k == 0, k == 7,
                                   [("wsl", slot), (srckey, k, t)], [("B", b)])
                            sc.add("dve", lambda e, n=n, t=t, b=b: e.scalar_tensor_tensor(
                                out=hres[:, n, tcols(t)], in0=hres[:, n, tcols(t)], scalar=ALPHA, in1=B[b][:, :],
                                op0=ALU.mult, op1=ALU.add),
                                reads=[("B", b), ("hres", n, t)], writes=[("hres", n, t)])
                jobs.append((load, comp))

        def layer_norm(q, want_hb):
            for t in range(ntile):
                for k in range(8):
                    sc.add("act", lambda e, k=k, t=t: e.copy(out=zb[:, k, :], in_=hres[:, k, tcols(t)]),
                           reads=[("hres", k, t)], writes=[("zb", k)])
                    sc.add("pool", lambda e, k=k, t=t: e.tensor_tensor(out=zsq[:, k, :], in0=hres[:, k, tcols(t)],
                                                                         in1=hres[:, k, tcols(t)], op=ALU.mult),
                           reads=[("hres", k, t)], writes=[("zsq", k)])
                ba = bank()
                for k in range(8):
                    mm(B[ba][:, :], ones[:, :], zb[:, k, :], k == 0, k == 7, ["ones", ("zb", k)], [("B", ba)])
                bb = bank()
                for k in range(8):
                    mm(B[bb][:, :], ones[:, :], zsq[:, k, :], k == 0, k == 7, ["ones", ("zsq", k)], [("B", bb)])
                sc.add("dve", lambda e, ba=ba: e.tensor_scalar(out=mean[:], in0=B[ba][:, :], scalar1=1.0 / D,
                                                               scalar2=None, op0=ALU.mult),
                       reads=[("B", ba)], writes=["mean"])
                sc.add("dve", lambda e: e.tensor_tensor(out=msq[:], in0=mean[:], in1=mean[:], op=ALU.mult),
                       reads=["mean"], writes=["msq"])
                sc.add("dve", lambda e, bb=bb: e.scalar_tensor_tensor(out=var[:], in0=B[bb][:, :], scalar=1.0 / D,
                                                                      in1=msq[:], op0=ALU.mult, op1=ALU.subtract),
                       reads=[("B", bb), "msq"], writes=["var"])
                sc.add("dve", lambda e: e.tensor_scalar(out=var[:], in0=var[:], scalar1=LN_EPS, scalar2=None,
                                                        op0=ALU.add), reads=["var"], writes=["var"])
                sc.add("act", lambda e: e.activation(out=var[:], in_=var[:], func=AF.Ln),
                       reads=["var"], writes=["var"])
                sc.add("act", lambda e: e.activation(out=rstd[:], in_=var[:], func=AF.Exp, scale=-0.5),
                       reads=["var"], writes=["rstd"])
                sc.add("dve", lambda e: e.scalar_tensor_tensor(out=nmr[:], in0=mean[:], scalar=-1.0, in1=rstd[:],
                                                               op0=ALU.mult, op1=ALU.mult),
                       reads=["mean", "rstd"], writes=["nmr"])
                for k in range(8):
                    i = state["tmp"]
                    state["tmp"] = 1 - i
                    sc.add("dve", lambda e, k=k, t=t, i=i: e.tensor_tensor(out=tmp[i][:], in0=hres[:, k, tcols(t)],
                                                                           in1=rstd[:], op=ALU.mult),
                           reads=[("hres", k, t), "rstd"], writes=[("tmp", i)])
                    sc.add("pool", lambda e, i=i: e.tensor_tensor(out=tmp2[i][:], in0=tmp[i][:], in1=nmr[:],
                                                                  op=ALU.add),
                           reads=[("tmp", i), "nmr"], writes=[("tmp2", i)])
                    gcol = q * 16 + k
                    bcol = q * 16 + 8 + k
                    sc.add("act", lambda e, k=k, t=t, i=i, gcol=gcol, bcol=bcol: e.activation(
                        out=hres[:, k, tcols(t)], in_=tmp2[i][:], func=AF.Identity,
                        scale=lnt[:, gcol:gcol + 1], bias=lnt[:, bcol:bcol + 1]),
                        reads=[("tmp2", i), "lnt"], writes=[("hres", k, t)])
                    if want_hb:
                        sc.add("dve", lambda e, k=k, t=t, i=i, gcol=gcol, bcol=bcol: e.tensor_scalar(
                            out=hb[:, k, tcols(t)], in0=tmp2[i][:], scalar1=lnt[:, gcol:gcol + 1],
                            scalar2=lnt[:, bcol:bcol + 1], op0=ALU.mult, op1=ALU.add),
                            reads=[("tmp2", i), "lnt"], writes=[("hb", k, t)])

        def add_ffn_jobs():
            f0 = 0
            while f0 < NF:
                fb = min(3, NF - f0)

                def load(slot, f0=f0, fb=fb):
                    v = wsl[slot][:, 0:2 * 8 * 384].rearrange("p (g k n) -> p g k n", g=2, k=8)
                    wload(slot, v[:, 0, :, 0:fb * 128], wg_r[:, :, f0 * 128:(f0 + fb) * 128])
                    wload(slot, v[:, 1, :, 0:fb * 128], wu_r[:, :, f0 * 128:(f0 + fb) * 128])

                def comp(slot, f0=f0, fb=fb):
                    v = wsl[slot][:, 0:2 * 8 * 384].rearrange("p (g k n) -> p g k n", g=2, k=8)
                    for t in range(ntile):
                        for j in range(fb):
                            fc = f0 + j
                            bg = bank()
                            for k in range(8):
                                mm(B[bg][:, :], v[:, 0, k, j * 128:(j + 1) * 128], hb[:, k, tcols(t)], k == 0, k == 7,
                                   [("wsl", slot), ("hb", k, t)], [("B", bg)])
                            bu = bank()
                            for k in range(8):
                                mm(B[bu][:, :], v[:, 1, k, j * 128:(j + 1) * 128], hb[:, k, tcols(t)], k == 0, k == 7,
                                   [("wsl", slot), ("hb", k, t)], [("B", bu)])
                            i = state["tmp"]
                            state["tmp"] = 1 - i
                            sc.add("act", lambda e, i=i, bg=bg: e.activation(out=sg[i][:], in_=B[bg][:, :],
                                                                             func=AF.Silu),
                                   reads=[("B", bg)], writes=[("sg", i)])
                            sc.add("dve", lambda e, i=i, bu=bu, fc=fc, t=t: e.tensor_tensor(
                                out=hid[:, fc, tcols(t)], in0=B[bu][:, :], in1=sg[i][:], op=ALU.mult),
                                reads=[("B", bu), ("sg", i)], writes=[("hid", fc, t)])
                jobs.append((load, comp))
                f0 += fb
            for nb in range(4):
                def load(slot, nb=nb):
                    v = wsl[slot][:, 0:NF * 256].rearrange("p (k n) -> p k n", k=NF)
                    wload(slot, v, wd_r[:, :, nb * 256:(nb + 1) * 256])

                def comp(slot, nb=nb):
                    v = wsl[slot][:, 0:NF * 256].rearrange("p (k n) -> p k n", k=NF)
                    for t in range(ntile):
                        for j in range(2):
                            n = nb * 2 + j
                            b = bank()
                            for fc in range(NF):
                                mm(B[b][:, :], v[:, fc, j * 128:(j + 1) * 128], hid[:, fc, tcols(t)], fc == 0,
                                   fc == NF - 1, [("wsl", slot), ("hid", fc, t)], [("B", b)])
                            sc.add("dve", lambda e, n=n, t=t, b=b: e.scalar_tensor_tensor(
                                out=hres[:, n, tcols(t)], in0=hres[:, n, tcols(t)], scalar=ALPHA, in1=B[b][:, :],
                                op0=ALU.mult, op1=ALU.add),
                                reads=[("B", b), ("hres", n, t)], writes=[("hres", n, t)])
                jobs.append((load, comp))

        def add_conv_jobs(g):
            w_in_r = w_in.rearrange("(k p) (g n) -> p k g n", p=128, g=3)
            for jb in range(4):
                def load(slot, jb=jb):
                    v = wsl[slot][:, 0:WCAP].rearrange("p (k g n) -> p k g n", k=8, g=3)
                    for gg in range(3):
                        wload(slot, v[:, :, gg, :], w_in_r[:, :, gg, jb * 256:(jb + 1) * 256])

                def comp(slot, jb=jb):
                    v = wsl[slot][:, 0:WCAP].rearrange("p (k g n) -> p k g n", k=8, g=3)
                    for jj in range(2):
                        ch = jb * 2 + jj
                        cols = slice(jj * 128, (jj + 1) * 128)
                        if g == 0:
                            bc = bank()
                            for k in range(8):
                                mm(B[bc][:, 0:2], v[:, k, 1, cols], hbh[:, k, :], k == 0, k == 7,
                                   [("wsl", slot), "hbh"], [("B", bc)])
                            bx = bank()
                            for k in range(8):
                                mm(B[bx][:, 0:2], v[:, k, 2, cols], hbh[:, k, :], k == 0, k == 7,
                                   [("wsl", slot), "hbh"], [("B", bx)])
                            sc.add("act", lambda e, bc=bc: e.copy(out=hcs[:], in_=B[bc][:, 0:2]),
                                   reads=[("B", bc)], writes=["hcs"])
                            sc.add("dve", lambda e, bx=bx, ch=ch: e.tensor_tensor(out=uprev[:, ch, :], in0=B[bx][:, 0:2],
                                                                                  in1=hcs[:], op=ALU.mult),
                                   reads=[("B", bx), "hcs"], writes=[("uprev", ch)])
                        for t in range(ntile):
                            bbk = bank()
                            for k in range(8):
                                mm(B[bbk][:, :], v[:, k, 0, cols], hb[:, k, tcols(t)], k == 0, k == 7,
                                   [("wsl", slot), ("hb", k, t)], [("B", bbk)])
                            bc = bank()
                            for k in range(8):
                                mm(B[bc][:, :], v[:, k, 1, cols], hb[:, k, tcols(t)], k == 0, k == 7,
                                   [("wsl", slot), ("hb", k, t)], [("B", bc)])
                            bx = bank()
                            for k in range(8):
                                mm(B[bx][:, :], v[:, k, 2, cols], hb[:, k, tcols(t)], k == 0, k == 7,
                                   [("wsl", slot), ("hb", k, t)], [("B", bx)])
                            i = state["tmp"]
                            state["tmp"] = 1 - i
                            u = ubuf[i]
                            sc.add("act", lambda e, i=i, bc=bc: e.copy(out=cs[i][:], in_=B[bc][:, :]),
                                   reads=[("B", bc)], writes=[("cs", i)])
                            sc.add("pool", lambda e, u=u, ch=ch: e.tensor_copy(out=u[:, 0:2], in_=uprev[:, ch, :]),
                                   reads=[("uprev", ch)], writes=[("u", i)])
                            sc.add("dve", lambda e, u=u, i=i, bx=bx: e.tensor_tensor(out=u[:, 2:TT + 2], in0=B[bx][:, :],
                                                                                     in1=cs[i][:], op=ALU.mult),
                                   reads=[("B", bx), ("cs", i)], writes=[("u", i)])
                            sc.add("pool", lambda e, u=u, ch=ch: e.tensor_copy(out=uprev[:, ch, :], in_=u[:, TT:TT + 2]),
                                   reads=[("u", i)], writes=[("uprev", ch)])
                            y = ybuf[i]
                            c3 = ch * 3
                            sc.add("dve", lambda e, u=u, y=y, c3=c3: e.tensor_scalar(
                                out=y[:], in0=u[:, 2:TT + 2], scalar1=cwt[:, c3 + 2:c3 + 3], scalar2=None, op0=ALU.mult),
                                reads=[("u", i), "cwt"], writes=[("y", i)])
                            sc.add("dve", lambda e, u=u, y=y, c3=c3: e.scalar_tensor_tensor(
                                out=y[:], in0=u[:, 1:TT + 1], scalar=cwt[:, c3 + 1:c3 + 2], in1=y[:],
                                op0=ALU.mult, op1=ALU.add), reads=[("u", i), "cwt", ("y", i)], writes=[("y", i)])
                            sc.add("dve", lambda e, u=u, y=y, c3=c3: e.scalar_tensor_tensor(
                                out=y[:], in0=u[:, 0:TT], scalar=cwt[:, c3:c3 + 1], in1=y[:],
                                op0=ALU.mult, op1=ALU.add), reads=[("u", i), "cwt", ("y", i)], writes=[("y", i)])
                            sc.add("dve", lambda e, y=y, bbk=bbk, ch=ch, t=t: e.tensor_tensor(
                                out=ab[:, ch, tcols(t)], in0=B[bbk][:, :], in1=y[:], op=ALU.mult),
                                reads=[("B", bbk), ("y", i)], writes=[("ab", ch, t)])
                jobs.append((load, comp))

        for g in range(ngroups):
            g0 = g * TG
            for t in range(ntile):
                c0 = g0 + t * TT
                sc.add("sp", lambda e, t=t, c0=c0: e.dma_start(out=hres[:, :, tcols(t)], in_=hin_r[:, :, c0:c0 + TT]),
                       writes=[("hres", k, t) for k in range(8)], dma=True)
            if mode == "B":
                ain_r = ain.rearrange("(k p) t -> p k t", p=128)
                for t in range(ntile):
                    c0 = g0 + t * TT
                    sc.add("pool", lambda e, t=t, c0=c0: e.dma_start(out=ab[:, :, tcols(t)], in_=ain_r[:, :, c0:c0 + TT]),
                           writes=[("ab", k, t) for k in range(8)], dma=True)
                add_proj_jobs(wmix_r, ab, "ab")
                run_jobs()
            else:
                for t in range(ntile):
                    c0 = g0 + t * TT
                    sc.add("pool", lambda e, t=t, c0=c0: e.dma_start(out=hb[:, :, tcols(t)], in_=hin_r[:, :, c0:c0 + TT]),
                           writes=[("hb", k, t) for k in range(8)], dma=True)
                if g == 0:
                    halo_r = halo.rearrange("(k p) t -> p k t", p=128)
                    sc.add("pool", lambda e: e.dma_start(out=hbh[:], in_=halo_r), writes=["hbh"], dma=True)
                add_conv_jobs(g)
                add_proj_jobs(wmix_r, ab, "ab")
                run_jobs()
            layer_norm(0, True)
            add_ffn_jobs()
            run_jobs()
            layer_norm(1, False)
            for t in range(ntile):
                c0 = g0 + t * TT
                sc.add("sp", lambda e, t=t, c0=c0: e.dma_start(out=hout_r[:, :, c0:c0 + TT], in_=hres[:, :, tcols(t)]),
                       reads=[("hres", k, t) for k in range(8)], dma=True)
        sc.emit()
    return nc


def _lnp(ln1_g, ln1_b, ln2_g, ln2_b, i):
    cols = []
    for v in (ln1_g[i], ln1_b[i], ln2_g[i], ln2_b[i]):
        cols.append(np.asarray(v, np.float32).reshape(8, 128).T)
    return np.ascontiguousarray(np.concatenate(cols, axis=1))


def run_B(xT, attnT, w_o, wg, wu, wd, lnp):
    in_maps = []
    for c in range(NCORES):
        sl = slice(c * TOK, (c + 1) * TOK)
        in_maps.append({"hin": np.ascontiguousarray(xT[:, sl]), "ain": np.ascontiguousarray(attnT[:, sl]),
                        "wmix": w_o, "wg": wg, "wu": wu, "wd": wd, "lnp": lnp})
    nc = build_BC("B")
    r = run_bass_kernel_spmd(nc, in_maps, core_ids=list(range(NCORES)))
    return np.concatenate([r.results[c]["hout"] for c in range(NCORES)], axis=1)


def run_C(hT, w_in, cw, w_out, wg, wu, wd, lnp):
    in_maps = []
    for c in range(NCORES):
        sl = slice(c * TOK, (c + 1) * TOK)
        halo = np.zeros((D, 2), np.float32) if c == 0 else np.ascontiguousarray(hT[:, c * TOK - 2:c * TOK])
        in_maps.append({"hin": np.ascontiguousarray(hT[:, sl]), "halo": halo, "w_in": w_in, "cw": cw,
                        "wmix": w_out, "wg": wg, "wu": wu, "wd": wd, "lnp": lnp})
    nc = build_BC("C")
    r = run_bass_kernel_spmd(nc, in_maps, core_ids=list(range(NCORES)))
    return np.concatenate([r.results[c]["hout"] for c in range(NCORES)], axis=1)


def kernel3(x, attn_w_qkv, attn_w_o, attn_lambda_q1, attn_lambda_k1, attn_lambda_q2, attn_lambda_k2,
           attn_subln_g, conv_w_in, conv_w, conv_w_out, ffn_w_gate, ffn_w_up, ffn_w_down,
           ln1_g, ln1_b, ln2_g, ln2_b):
    f = lambda a: np.ascontiguousarray(np.asarray(a, np.float32))
    x = f(x)
    attnT = run_A(x, f(attn_w_qkv), f(attn_lambda_q1), f(attn_lambda_k1), f(attn_lambda_q2), f(attn_lambda_k2),
                  f(attn_subln_g))
    xT = np.ascontiguousarray(x.reshape(S, D).T)
    h1T = run_B(xT, attnT, f(attn_w_o)[0], f(ffn_w_gate)[0], f(ffn_w_up)[0], f(ffn_w_down)[0],
                _lnp(ln1_g, ln1_b, ln2_g, ln2_b, 0))
    cw = np.ascontiguousarray(f(conv_w)[0].reshape(3, 8, 128).transpose(2, 1, 0).reshape(128, 24))
    h2T = run_C(h1T, f(conv_w_in)[0], cw, f(conv_w_out)[0], f(ffn_w_gate)[1], f(ffn_w_up)[1], f(ffn_w_down)[1],
                _lnp(ln1_g, ln1_b, ln2_g, ln2_b, 1))
    return np.ascontiguousarray(h2T.T).reshape(1, S, D).astype(np.float32)


def build_T(tok=TOK):
    ngroups = tok // TG
    TGW = TG + 2
    nc = bass.Bass("TRN2", target_bir_lowering=False)
    hin = nc.dram_tensor("hin", [D, tok], F32, kind="ExternalInput").ap()
    xhalo = nc.dram_tensor("xhalo", [D, 2], F32, kind="ExternalInput").ap()
    ain = nc.dram_tensor("ain", [D, tok], F32, kind="ExternalInput").ap()
    ahalo = nc.dram_tensor("ahalo", [D, 2], F32, kind="ExternalInput").ap()
    uflag = nc.dram_tensor("uflag", [128, 1], F32, kind="ExternalInput").ap()
    w_o = nc.dram_tensor("w_o", [D, D], F32, kind="ExternalInput").ap()
    w_in = nc.dram_tensor("w_in", [D, 3 * D], F32, kind="ExternalInput").ap()
    cw = nc.dram_tensor("cw", [128, 24], F32, kind="ExternalInput").ap()
    w_out = nc.dram_tensor("w_out", [D, D], F32, kind="ExternalInput").ap()
    wgs = [nc.dram_tensor("wg%d" % i, [D, DFF], F32, kind="ExternalInput").ap() for i in range(2)]
    wus = [nc.dram_tensor("wu%d" % i, [D, DFF], F32, kind="ExternalInput").ap() for i in range(2)]
    wds = [nc.dram_tensor("wd%d" % i, [DFF, D], F32, kind="ExternalInput").ap() for i in range(2)]
    lnp = nc.dram_tensor("lnp", [128, 64], F32, kind="ExternalInput").ap()
    hout = nc.dram_tensor("hout", [D, tok], F32, kind="ExternalOutput").ap()

    kp = lambda a: a.rearrange("(k p) t -> p k t", p=128)
    hin_r, ain_r, hout_r = kp(hin), kp(ain), kp(hout)
    xhalo_r, ahalo_r = kp(xhalo), kp(ahalo)
    w_o_r, w_out_r = kp(w_o), kp(w_out)
    wg_r = [kp(a) for a in wgs]
    wu_r = [kp(a) for a in wus]
    wd_r = [kp(a) for a in wds]
    w_in_r = w_in.rearrange("(k p) (g n) -> p k g n", p=128, g=3)
    NF = DFF // 128

    with ExitStack() as st:
        sc = Sched(nc, st)
        hres = _sb(nc, st, "hres", [128, 8, TGW], F32)
        hb = _sb(nc, st, "hb", [128, 8, TGW], BF16)
        ab = _sb(nc, st, "ab", [128, 8, TGW], BF16)
        hid = _sb(nc, st, "hid", [128, NF, TGW], BF16)
        wsl = [_sb(nc, st, "wsl%d" % i, [128, WCAP], BF16) for i in range(3)]
        zb = _sb(nc, st, "zb", [128, 8, TT], BF16)
        zsq = _sb(nc, st, "zsq", [128, 8, TT], BF16)
        mean = _sb(nc, st, "mean", [128, TT], F32)
        msq = _sb(nc, st, "msq", [128, TT], F32)
        var = _sb(nc, st, "var", [128, TT], F32)
        rstd = _sb(nc, st, "rstd", [128, TT], F32)
        nmr = _sb(nc, st, "nmr", [128, TT], F32)
        tmp = [_sb(nc, st, "tmp%d" % i, [128, TT], F32) for i in range(2)]
        tmp2 = [_sb(nc, st, "tmq%d" % i, [128, TT], F32) for i in range(2)]
        sg = [_sb(nc, st, "sg%d" % i, [128, TT], F32) for i in range(2)]
        ones = _sb(nc, st, "ones", [128, 128], BF16)
        lnt = _sb(nc, st, "lnt", [128, 64], F32)
        cwt = _sb(nc, st, "cwt", [128, 24], F32)
        ufl = _sb(nc, st, "ufl", [128, 1], F32)
        uprev = _sb(nc, st, "uprev", [128, 8, 2], F32)
        ubuf = [_sb(nc, st, "ubuf%d" % i, [128, TT + 2], F32) for i in range(2)]
        cs = [_sb(nc, st, "cs%d" % i, [128, TT], F32) for i in range(2)]
        ybuf = [_sb(nc, st, "ybuf%d" % i, [128, TT], F32) for i in range(2)]
        hcs = _sb(nc, st, "hcs", [128, 2], F32)
        B = [_ps(nc, st, "B%d" % i) for i in range(8)]

        state = {"bank": 0, "job": 0, "tmp": 0}

        def bank():
            b = state["bank"]
            state["bank"] = (b + 1) % 6
            return b

        def flip():
            i = state["tmp"]
            state["tmp"] = 1 - i
            return i

        def mm(outp, lhsT, rhs, start, stop, reads, writes):
            sc.add("pe", lambda e: e.matmul(outp, lhsT, rhs, start=start, stop=stop),
                   reads=reads, writes=writes)

        sc.add("pool", lambda e: e.memset(ones[:], 1.0), writes=["ones"])
        sc.add("sp", lambda e: e.dma_start(out=lnt[:], in_=lnp), writes=["lnt"], dma=True)
        sc.add("sp", lambda e: e.dma_start(out=cwt[:], in_=cw), writes=["cwt"], dma=True)
        sc.add("sp", lambda e: e.dma_start(out=ufl[:], in_=uflag), writes=["ufl"], dma=True)

        def tiles(g, layer):
            ts = [(0, 0, TT), (1, TT, TT)]
            if g == 0 and layer == 0:
                ts.append((2, TG, 2))
            return ts

        jobs = []
        side = []

        def hook():
            for _ in range(state.get("rate", 0)):
                if side:
                    side.pop(0)()

        def flush():
            while side:
                side.pop(0)()
            state["rate"] = 0

        def enqueue(pieces, nsteps):
            side.extend(pieces)
            state["rate"] = max(1, -(-2 * len(side) // max(1, nsteps)))

        def split(tl):
            first = [x for x in tl if x[0] != 1]
            second = [x for x in tl if x[0] == 1]
            return first, second

        def ensure_loaded(idx):
            if idx < len(jobs) and not jobs[idx]["loaded"]:
                jobs[idx]["loaded"] = True
                jobs[idx]["load"]((state["job"] + idx) % 3)

        def run_segment(lo, hi, ln_q, ln_tl, pre_side, after_first=None):
            npairs = (hi - lo) // 2
            lf, ls = split(ln_tl)
            for p in range(npairs):
                ia, ib = lo + 2 * p, lo + 2 * p + 1
                ja, jb = jobs[ia], jobs[ib]
                sa, sb = (state["job"] + ia) % 3, (state["job"] + ib) % 3
                ensure_loaded(ia)
                ensure_loaded(ib)
                ensure_loaded(ib + 1)
                fa, sea = split(ja["tl"])
                fb_, seb = split(jb["tl"])
                if p == 0 and pre_side:
                    enqueue(pre_side, ja["spt"] * len(fa) + jb["spt"] * len(fb_))
                ja["comp"](sa, fa)
                jb["comp"](sb, fb_)
                if p == 0:
                    flush()
                    if after_first is not None:
                        after_first()
                if p == npairs - 1:
                    pcs = []
                    for tile in lf:
                        pcs += ln_pieces(ln_q, tile)
                    enqueue(pcs, ja["spt"] * len(sea) + jb["spt"] * len(seb))
                ja["comp"](sa, sea)
                ensure_loaded(ib + 2)
                jb["comp"](sb, seb)
                if p == npairs - 1:
                    flush()
            pcs = []
            for tile in ls:
                pcs += ln_pieces(ln_q, tile)
            return pcs

        def wload(slot, view, src):
            sc.add("pool", lambda e: e.dma_start(out=view, in_=src), writes=[("wsl", slot)], dma=True)

        def add_proj_jobs(w_r, tl):
            for nb in range(2):
                def load(slot, nb=nb):
                    v = wsl[slot][:, 0:4096].rearrange("p (k n) -> p k n", k=8)
                    wload(slot, v, w_r[:, :, nb * 512:(nb + 1) * 512])

                def comp(slot, tls, nb=nb):
                    v = wsl[slot][:, 0:4096].rearrange("p (k n) -> p k n", k=8)
                    for (t, c0, w) in tls:
                        for j in range(4):
                            n = nb * 4 + j
                            b = bank()
                            for k in range(8):
                                mm(B[b][:, 0:w], v[:, k, j * 128:(j + 1) * 128], ab[:, k, c0:c0 + w], k == 0, k == 7,
                                   [("wsl", slot), ("ab", k, t)], [("B", b)])
                            sc.add("dve", lambda e, n=n, c0=c0, w=w, b=b: e.scalar_tensor_tensor(
                                out=hres[:, n, c0:c0 + w], in0=hres[:, n, c0:c0 + w], scalar=ALPHA, in1=B[b][:, 0:w],
                                op0=ALU.mult, op1=ALU.add),
                                reads=[("B", b), ("hres", n, t)], writes=[("hres", n, t)])
                            hook()
                jobs.append({"load": load, "comp": comp, "tl": tl, "spt": 4, "loaded": False})

        def ln_pieces(q, tile):
            (t, c0, w) = tile
            pcs = []
            shared = {}

            def prep(k):
                sc.add("act", lambda e: e.activation(out=zsq[:, k, 0:w], in_=hres[:, k, c0:c0 + w], func=AF.Square),
                       reads=[("hres", k, t)], writes=[("zsq", k)])
                sc.add("dve", lambda e: e.tensor_copy(out=zb[:, k, 0:w], in_=hres[:, k, c0:c0 + w]),
                       reads=[("hres", k, t)], writes=[("zb", k)])

            def stats():
                ba = bank()
                for k in range(8):
                    mm(B[ba][:, 0:w], ones[:, :], zb[:, k, 0:w], k == 0, k == 7, ["ones", ("zb", k)], [("B", ba)])
                bb = bank()
                for k in range(8):
                    mm(B[bb][:, 0:w], ones[:, :], zsq[:, k, 0:w], k == 0, k == 7, ["ones", ("zsq", k)], [("B", bb)])
                sc.add("dve", lambda e: e.tensor_scalar(out=mean[:, 0:w], in0=B[ba][:, 0:w], scalar1=1.0 / D,
                                                        scalar2=None, op0=ALU.mult),
                       reads=[("B", ba)], writes=["mean"])
                sc.add("dve", lambda e: e.tensor_tensor(out=msq[:, 0:w], in0=mean[:, 0:w], in1=mean[:, 0:w],
                                                        op=ALU.mult), reads=["mean"], writes=["msq"])
                sc.add("dve", lambda e: e.scalar_tensor_tensor(
                    out=var[:, 0:w], in0=B[bb][:, 0:w], scalar=1.0 / D, in1=msq[:, 0:w], op0=ALU.mult,
                    op1=ALU.subtract), reads=[("B", bb), "msq"], writes=["var"])
                sc.add("dve", lambda e: e.tensor_scalar(out=var[:, 0:w], in0=var[:, 0:w], scalar1=LN_EPS,
                                                        scalar2=None, op0=ALU.add), reads=["var"], writes=["var"])
                sc.add("act", lambda e: e.activation(out=var[:, 0:w], in_=var[:, 0:w], func=AF.Ln),
                       reads=["var"], writes=["var"])
                br = 7
                shared["br"] = br
                sc.add("act", lambda e: e.activation(out=B[br][:, 0:w], in_=var[:, 0:w], func=AF.Exp, scale=-0.5),
                       reads=["var"], writes=[("B", br)])
                sc.add("dve", lambda e: e.scalar_tensor_tensor(out=B[6][:, 0:w], in0=mean[:, 0:w], scalar=-1.0,
                                                               in1=B[br][:, 0:w], op0=ALU.mult, op1=ALU.mult),
                       reads=["mean", ("B", br)], writes=[("B", 6)])

            def norm(k):
                br = shared["br"]
                i = flip()
                sc.add("dve", lambda e: e.tensor_tensor(out=tmp[i][:, 0:w], in0=hres[:, k, c0:c0 + w], in1=B[br][:, 0:w],
                                                        op=ALU.mult),
                       reads=[("hres", k, t), ("B", br)], writes=[("tmp", i)])
                sc.add("dve", lambda e: e.tensor_tensor(out=tmp2[i][:, 0:w], in0=tmp[i][:, 0:w], in1=B[6][:, 0:w],
                                                        op=ALU.add),
                       reads=[("tmp", i), ("B", 6)], writes=[("tmp2", i)])
                gcol = q * 16 + k
                bcol = q * 16 + 8 + k
                sc.add("act", lambda e: e.activation(out=hres[:, k, c0:c0 + w], in_=tmp2[i][:, 0:w], func=AF.Identity,
                                                     scale=lnt[:, gcol:gcol + 1], bias=lnt[:, bcol:bcol + 1]),
                       reads=[("tmp2", i), "lnt"], writes=[("hres", k, t)])
                if True:
                    sc.add("act", lambda e: e.activation(out=hb[:, k, c0:c0 + w], in_=tmp2[i][:, 0:w], func=AF.Identity,
                                                         scale=lnt[:, gcol:gcol + 1], bias=lnt[:, bcol:bcol + 1]),
                           reads=[("tmp2", i), "lnt"], writes=[("hb", k, t)])

            for k in range(8):
                pcs.append(lambda k=k: prep(k))
            pcs.append(stats)
            for k in range(8):
                pcs.append(lambda k=k: norm(k))
            return pcs

        def add_ffn_jobs(layer, tl):
            f0 = 0
            while f0 < NF:
                fb = min(3, NF - f0)

                def load(slot, f0=f0, fb=fb):
                    v = wsl[slot][:, 0:2 * 8 * 384].rearrange("p (g k n) -> p g k n", g=2, k=8)
                    wload(slot, v[:, 0, :, 0:fb * 128], wg_r[layer][:, :, f0 * 128:(f0 + fb) * 128])
                    wload(slot, v[:, 1, :, 0:fb * 128], wu_r[layer][:, :, f0 * 128:(f0 + fb) * 128])

                def comp(slot, tls, f0=f0, fb=fb):
                    v = wsl[slot][:, 0:2 * 8 * 384].rearrange("p (g k n) -> p g k n", g=2, k=8)
                    for (t, c0, w) in tls:
                        for j in range(fb):
                            fc = f0 + j
                            bg = bank()
                            for k in range(8):
                                mm(B[bg][:, 0:w], v[:, 0, k, j * 128:(j + 1) * 128], hb[:, k, c0:c0 + w], k == 0, k == 7,
                                   [("wsl", slot), ("hb", k, t)], [("B", bg)])
                            bu = bank()
                            for k in range(8):
                                mm(B[bu][:, 0:w], v[:, 1, k, j * 128:(j + 1) * 128], hb[:, k, c0:c0 + w], k == 0, k == 7,
                                   [("wsl", slot), ("hb", k, t)], [("B", bu)])
                            i = flip()
                            sc.add("act", lambda e, i=i, bg=bg, w=w: e.activation(out=sg[i][:, 0:w], in_=B[bg][:, 0:w],
                                                                                  func=AF.Silu),
                                   reads=[("B", bg)], writes=[("sg", i)])
                            sc.add("dve", lambda e, i=i, bu=bu, fc=fc, c0=c0, w=w: e.tensor_tensor(
                                out=hid[:, fc, c0:c0 + w], in0=B[bu][:, 0:w], in1=sg[i][:, 0:w], op=ALU.mult),
                                reads=[("B", bu), ("sg", i)], writes=[("hid", fc, t)])
                            hook()
                jobs.append({"load": load, "comp": comp, "tl": tl, "spt": fb, "loaded": False})
                f0 += fb
            for nb in range(4):
                def load(slot, nb=nb):
                    v = wsl[slot][:, 0:NF * 256].rearrange("p (k n) -> p k n", k=NF)
                    wload(slot, v, wd_r[layer][:, :, nb * 256:(nb + 1) * 256])

                def comp(slot, tls, nb=nb):
                    v = wsl[slot][:, 0:NF * 256].rearrange("p (k n) -> p k n", k=NF)
                    for (t, c0, w) in tls:
                        for j in range(2):
                            n = nb * 2 + j
                            b = bank()
                            for fc in range(NF):
                                mm(B[b][:, 0:w], v[:, fc, j * 128:(j + 1) * 128], hid[:, fc, c0:c0 + w], fc == 0,
                                   fc == NF - 1, [("wsl", slot), ("hid", fc, t)], [("B", b)])
                            sc.add("dve", lambda e, n=n, c0=c0, w=w, b=b: e.scalar_tensor_tensor(
                                out=hres[:, n, c0:c0 + w], in0=hres[:, n, c0:c0 + w], scalar=ALPHA, in1=B[b][:, 0:w],
                                op0=ALU.mult, op1=ALU.add),
                                reads=[("B", b), ("hres", n, t)], writes=[("hres", n, t)])
                            hook()
                jobs.append({"load": load, "comp": comp, "tl": tl, "spt": 2, "loaded": False})

        def add_conv_jobs(g):
            for jb in range(4):
                def load(slot, jb=jb):
                    v = wsl[slot][:, 0:WCAP].rearrange("p (k g n) -> p k g n", k=8, g=3)
                    for gg in range(3):
                        wload(slot, v[:, :, gg, :], w_in_r[:, :, gg, jb * 256:(jb + 1) * 256])

                def comp(slot, tls, jb=jb):
                    v = wsl[slot][:, 0:WCAP].rearrange("p (k g n) -> p k g n", k=8, g=3)
                    for jj in range(2):
                        ch = jb * 2 + jj
                        cols = slice(jj * 128, (jj + 1) * 128)
                        if g == 0 and any(x[0] == 0 for x in tls):
                            bc = bank()
                            for k in range(8):
                                mm(B[bc][:, 0:2], v[:, k, 1, cols], hb[:, k, TG:TG + 2], k == 0, k == 7,
                                   [("wsl", slot), ("hb", k, 2)], [("B", bc)])
                            bx = bank()
                            for k in range(8):
                                mm(B[bx][:, 0:2], v[:, k, 2, cols], hb[:, k, TG:TG + 2], k == 0, k == 7,
                                   [("wsl", slot), ("hb", k, 2)], [("B", bx)])
                            sc.add("act", lambda e, bc=bc: e.copy(out=hcs[:], in_=B[bc][:, 0:2]),
                                   reads=[("B", bc)], writes=["hcs"])
                            sc.add("dve", lambda e, bx=bx, ch=ch: e.scalar_tensor_tensor(
                                out=uprev[:, ch, :], in0=B[bx][:, 0:2], scalar=ufl[:, 0:1], in1=hcs[:],
                                op0=ALU.mult, op1=ALU.mult),
                                reads=[("B", bx), "hcs", "ufl"], writes=[("uprev", ch)])
                        for (t, c0, w) in tls:
                            bbk = bank()
                            for k in range(8):
                                mm(B[bbk][:, :], v[:, k, 0, cols], hb[:, k, c0:c0 + w], k == 0, k == 7,
                                   [("wsl", slot), ("hb", k, t)], [("B", bbk)])
                            bc = bank()
                            for k in range(8):
                                mm(B[bc][:, :], v[:, k, 1, cols], hb[:, k, c0:c0 + w], k == 0, k == 7,
                                   [("wsl", slot), ("hb", k, t)], [("B", bc)])
                            bx = bank()
                            for k in range(8):
                                mm(B[bx][:, :], v[:, k, 2, cols], hb[:, k, c0:c0 + w], k == 0, k == 7,
                                   [("wsl", slot), ("hb", k, t)], [("B", bx)])
                            i = flip()
                            u = ubuf[i]
                            y = ybuf[i]
                            c3 = ch * 3
                            sc.add("act", lambda e, i=i, bc=bc: e.copy(out=cs[i][:], in_=B[bc][:, :]),
                                   reads=[("B", bc)], writes=[("cs", i)])
                            sc.add("act", lambda e, u=u, ch=ch: e.copy(out=u[:, 0:2], in_=uprev[:, ch, :]),
                                   reads=[("uprev", ch)], writes=[("u", i)])
                            sc.add("dve", lambda e, u=u, i=i, bx=bx: e.tensor_tensor(out=u[:, 2:TT + 2], in0=B[bx][:, :],
                                                                                     in1=cs[i][:], op=ALU.mult),
                                   reads=[("B", bx), ("cs", i)], writes=[("u", i)])
                            sc.add("act", lambda e, u=u, ch=ch: e.copy(out=uprev[:, ch, :], in_=u[:, TT:TT + 2]),
                                   reads=[("u", i)], writes=[("uprev", ch)])
                            sc.add("dve", lambda e, u=u, y=y, c3=c3: e.tensor_scalar(
                                out=y[:], in0=u[:, 2:TT + 2], scalar1=cwt[:, c3 + 2:c3 + 3], scalar2=None, op0=ALU.mult),
                                reads=[("u", i), "cwt"], writes=[("y", i)])
                            sc.add("dve", lambda e, u=u, y=y, c3=c3: e.scalar_tensor_tensor(
                                out=y[:], in0=u[:, 1:TT + 1], scalar=cwt[:, c3 + 1:c3 + 2], in1=y[:],
                                op0=ALU.mult, op1=ALU.add), reads=[("u", i), "cwt", ("y", i)], writes=[("y", i)])
                            sc.add("dve", lambda e, u=u, y=y, c3=c3: e.scalar_tensor_tensor(
                                out=y[:], in0=u[:, 0:TT], scalar=cwt[:, c3:c3 + 1], in1=y[:],
                                op0=ALU.mult, op1=ALU.add), reads=[("u", i), "cwt", ("y", i)], writes=[("y", i)])
                            sc.add("dve", lambda e, y=y, bbk=bbk, ch=ch, c0=c0, w=w: e.tensor_tensor(
                                out=ab[:, ch, c0:c0 + w], in0=B[bbk][:, :], in1=y[:], op=ALU.mult),
                                reads=[("B", bbk), ("y", i)], writes=[("ab", ch, t)])
                            hook()
                jobs.append({"load": load, "comp": comp, "tl": tiles(g, 1), "spt": 2, "loaded": False})

        def load_tile(g, tile):
            (t, c0, w) = tile
            g0 = g * TG
            if t < 2:
                sc.add("sp", lambda e: e.dma_start(out=hres[:, :, c0:c0 + w], in_=hin_r[:, :, g0 + c0:g0 + c0 + w]),
                       writes=[("hres", k, t) for k in range(8)], dma=True)
                sc.add("pool", lambda e: e.dma_start(out=ab[:, :, c0:c0 + w], in_=ain_r[:, :, g0 + c0:g0 + c0 + w]),
                       writes=[("ab", k, t) for k in range(8)], dma=True)
            else:
                sc.add("sp", lambda e: e.dma_start(out=hres[:, :, c0:c0 + w], in_=xhalo_r),
                       writes=[("hres", k, t) for k in range(8)], dma=True)
                sc.add("pool", lambda e: e.dma_start(out=ab[:, :, c0:c0 + w], in_=ahalo_r),
                       writes=[("ab", k, t) for k in range(8)], dma=True)

        def store_tile(g, tile):
            (t, c0, w) = tile
            g0 = g * TG
            sc.add("sp", lambda e: e.dma_start(out=hout_r[:, :, g0 + c0:g0 + c0 + w], in_=hres[:, :, c0:c0 + w]),
                   reads=[("hres", k, t) for k in range(8)], dma=True)

        pre = []
        for g in range(ngroups):
            tl0 = tiles(g, 0)
            tl1 = tiles(g, 1)
            f0_, s0_ = split(tl0)
            if g == 0:
                f0_, s0_ = tl0, []
            for tile in f0_:
                load_tile(g, tile)

            def after_first(g=g, s0_=s0_):
                if g > 0:
                    for tile in split(tiles(g - 1, 1))[1]:
                        store_tile(g - 1, tile)
                for tile in s0_:
                    load_tile(g, tile)

            base = len(jobs)
            add_proj_jobs(w_o_r, tl0)
            n1 = len(jobs)
            add_ffn_jobs(0, tl0)
            n2 = len(jobs)
            add_conv_jobs(g)
            add_proj_jobs(w_out_r, tl1)
            n3 = len(jobs)
            add_ffn_jobs(1, tl1)
            n4 = len(jobs)
            pre = run_segment(base, n1, 0, tl0, pre, after_first)
            pre = run_segment(n1, n2, 1, tl0, pre)
            pre = run_segment(n2, n3, 2, tl1, pre)
            pre = run_segment(n3, n4, 3, tl1, pre)
            for tile in split(tl1)[0]:
                store_tile(g, tile)
        enqueue(pre, 1)
        flush()
        for tile in split(tiles(ngroups - 1, 1))[1]:
            store_tile(ngroups - 1, tile)
        sc.emit()
    return nc


def run_T(xT, attnT, inp):
    f = lambda a: np.ascontiguousarray(np.asarray(a, np.float32))
    lnp = np.ascontiguousarray(np.concatenate(
        [_lnp(inp["ln1_g"], inp["ln1_b"], inp["ln2_g"], inp["ln2_b"], i) for i in range(2)], axis=1))
    cw = np.ascontiguousarray(f(inp["conv_w"])[0].reshape(3, 8, 128).transpose(2, 1, 0).reshape(128, 24))
    shared = {"w_o": f(inp["attn_w_o"])[0], "w_in": f(inp["conv_w_in"])[0], "cw": cw, "w_out": f(inp["conv_w_out"])[0],
              "wg0": f(inp["ffn_w_gate"])[0], "wu0": f(inp["ffn_w_up"])[0], "wd0": f(inp["ffn_w_down"])[0],
              "wg1": f(inp["ffn_w_gate"])[1], "wu1": f(inp["ffn_w_up"])[1], "wd1": f(inp["ffn_w_down"])[1], "lnp": lnp}
    in_maps = []
    for c in range(NCORES):
        sl = slice(c * TOK, (c + 1) * TOK)
        if c == 0:
            xh = np.zeros((D, 2), np.float32)
            ah = np.zeros((D, 2), np.float32)
        else:
            xh = np.ascontiguousarray(xT[:, c * TOK - 2:c * TOK])
            ah = np.ascontiguousarray(attnT[:, c * TOK - 2:c * TOK])
        m = {"hin": np.ascontiguousarray(xT[:, sl]), "ain": np.ascontiguousarray(attnT[:, sl]), "xhalo": xh,
             "ahalo": ah, "uflag": np.full((128, 1), 0.0 if c == 0 else 1.0, np.float32)}
        m.update(shared)
        in_maps.append(m)
    nc = build_T()
    r = run_bass_kernel_spmd(nc, in_maps, core_ids=list(range(NCORES)))
    return np.concatenate([r.results[c]["hout"] for c in range(NCORES)], axis=1)


def kernel(**inp):
    f = lambda a: np.ascontiguousarray(np.asarray(a, np.float32))
    x = f(inp["x"])
    attnT = run_A(x, f(inp["attn_w_qkv"]), f(inp["attn_lambda_q1"]), f(inp["attn_lambda_k1"]),
                  f(inp["attn_lambda_q2"]), f(inp["attn_lambda_k2"]), f(inp["attn_subln_g"]))
    if os.environ.get("K_PROBE_ONLY_A"):
        return np.zeros((1, S, D), np.float32)
    xT = np.ascontiguousarray(x.reshape(S, D).T)
    h2T = run_T(xT, attnT, inp)
    return np.ascontiguousarray(h2T.T).reshape(1, S, D).astype(np.float32)
```

```python
import math
import os
from contextlib import ExitStack

import numpy as np
import concourse.bass as bass
import concourse.mybir as mybir
from concourse.bass_utils import run_bass_kernel_spmd

F32 = mybir.dt.float32
BF16 = mybir.dt.bfloat16
AF = mybir.ActivationFunctionType
ALU = mybir.AluOpType
AX = mybir.AxisListType

D = 1024
S = 16384
NCORES = 8
DFF = 2816
ALPHA = (2 * 2) ** 0.25
LN_EPS = 1e-5
ROPE_THETA = 500000.0
LAM_INIT0 = 0.8 - 0.6 * math.exp(-0.3 * 0)


class _Op:
    __slots__ = ("eng", "fn", "deps", "dma", "sig", "sem", "val", "prev")


class Sched:
    ENGS = ("sp", "pe", "act", "dve", "pool")
    NDS = 12

    def __init__(self, nc, stack):
        self.nc = nc
        self.stack = stack
        self.ops = []
        self.last_w = {}
        self.readers = {}

    def add(self, eng, fn, reads=(), writes=(), dma=False):
        op = _Op()
        op.eng, op.fn, op.dma, op.sig = eng, fn, dma, False
        op.sem = None
        op.val = 0
        op.prev = 0
        deps = []
        seen = set()

        def dep(o):
            if o is not None and id(o) not in seen:
                seen.add(id(o))
                deps.append(o)

        for r in reads:
            dep(self.last_w.get(r))
            if isinstance(r, tuple) and r[0] == "B":
                for rd in self.readers.get(r, ()):
                    if rd.eng != eng:
                        dep(rd)
        for w in writes:
            dep(self.last_w.get(w))
            for rd in self.readers.get(w, ()):
                dep(rd)
        op.deps = deps
        for r in reads:
            self.readers.setdefault(r, []).append(op)
        for w in writes:
            self.last_w[w] = op
            self.readers[w] = []
        self.ops.append(op)
        return op

    def emit(self):
        nc = self.nc
        st = self.stack
        esem = {e: st.enter_context(nc.semaphore("e_" + e)) for e in ("pe", "act", "dve", "pool")}
        dsem = {e: [st.enter_context(nc.semaphore("d_%s%d" % (e, i))) for i in range(self.NDS)]
                for e in ("sp", "act", "pool")}
        for op in self.ops:
            for d in op.deps:
                if d.dma:
                    continue
                if d.eng == "pe" and op.eng == "pe" and not op.dma:
                    continue
                d.sig = True
        cnt = {e: 0 for e in self.ENGS}
        dcnt = {e: 0 for e in self.ENGS}
        for op in self.ops:
            if op.dma:
                n = dcnt[op.eng]
                dcnt[op.eng] += 1
                op.sem = dsem[op.eng][n % self.NDS]
                op.val = 16 * (n // self.NDS + 1)
                op.prev = 16 * (n // self.NDS)
            elif op.sig:
                cnt[op.eng] += 1
                op.sem = esem[op.eng]
                op.val = cnt[op.eng]
        finals = []
        for e in dsem:
            n = dcnt[e]
            for i in range(min(n, self.NDS)):
                tot = (n - 1 - i) // self.NDS + 1
                finals.append((dsem[e][i], 16 * tot))

        block = st.enter_context(nc.Block())
        decos = {"sp": block.sync, "pe": block.tensor, "act": block.scalar,
                 "dve": block.vector, "pool": block.gpsimd}
        for ename in self.ENGS:
            ops_e = [op for op in self.ops if op.eng == ename]

            def body(eng, ops_e=ops_e, ename=ename):
                waited = {}

                def wait(sem, val):
                    if val > 0 and waited.get(id(sem), 0) < val:
                        eng.wait_ge(sem, val)
                        waited[id(sem)] = val

                for op in ops_e:
                    need = {}
                    for d in op.deps:
                        if (not d.dma) and d.eng == "pe" and ename == "pe" and not op.dma:
                            continue
                        k = id(d.sem)
                        if k not in need or need[k][1] < d.val:
                            need[k] = (d.sem, d.val)
                    if op.dma:
                        wait(op.sem, op.prev)
                    for sem, val in need.values():
                        wait(sem, val)
                    ins = op.fn(eng)
                    if op.dma:
                        ins.then_inc(op.sem, 16)
                    elif op.sig:
                        ins.then_inc(op.sem, 1)
                if ename == "sp":
                    for sem, val in finals:
                        wait(sem, val)

            decos[ename](body)


def _sb(nc, st, name, shape, dt):
    return st.enter_context(nc.sbuf_tensor(name, list(shape), dt))


def _ps(nc, st, name):
    return st.enter_context(nc.psum_tensor(name, [128, 512], F32))


def build_A(nt=S // 512, stage=99):
    TT = 512
    SS = nt * TT
    nc = bass.Bass("TRN2", target_bir_lowering=False)
    xT = nc.dram_tensor("xT", [D, SS], F32, kind="ExternalInput").ap()
    wqkv = nc.dram_tensor("wqkv", [D, 384], F32, kind="ExternalInput").ap()
    ropeC = nc.dram_tensor("ropeC", [128, SS], F32, kind="ExternalInput").ap()
    ropeS = nc.dram_tensor("ropeS", [128, SS], F32, kind="ExternalInput").ap()
    perm = nc.dram_tensor("perm", [128, 128], F32, kind="ExternalInput").ap()
    identd = nc.dram_tensor("ident", [128, 128], F32, kind="ExternalInput").ap()
    lam4 = nc.dram_tensor("lam4", [1, 256], F32, kind="ExternalInput").ap()
    subg = nc.dram_tensor("subg", [128, 1], F32, kind="ExternalInput").ap()
    out = nc.dram_tensor("attnT", [128, SS], F32, kind="ExternalOutput").ap()
    xTr = xT.rearrange("(k p) t -> p k t", p=128)
    wr = wqkv.rearrange("(k p) n -> p k n", p=128)

    with ExitStack() as st:
        sc = Sched(nc, st)
        qT = _sb(nc, st, "qT", [128, SS], BF16)
        kT = _sb(nc, st, "kT", [128, SS], BF16)
        vS = _sb(nc, st, "vS", [128, nt * 4, 128], BF16)
        xb = [_sb(nc, st, "xb%d" % i, [128, 8, TT], BF16) for i in range(2)]
        wsb = _sb(nc, st, "wsb", [128, 8, 384], BF16)
        pm = _sb(nc, st, "pm", [128, 128], BF16)
        ones = _sb(nc, st, "ones", [128, 128], BF16)
        ones32 = _sb(nc, st, "ones32", [128, 128], F32)
        zs = _sb(nc, st, "zs", [128, TT], F32)
        rc = [_sb(nc, st, "rc%d" % i, [128, TT], F32) for i in range(2)]
        rs = [_sb(nc, st, "rs%d" % i, [128, TT], F32) for i in range(2)]
        qb = _sb(nc, st, "qb", [128, TT], BF16)
        kb_ = _sb(nc, st, "kb_", [128, TT], BF16)
        qs_ = _sb(nc, st, "qs_", [128, TT], BF16)
        ks_ = _sb(nc, st, "ks_", [128, TT], BF16)
        idn = _sb(nc, st, "idn", [128, 128], BF16)
        pT = [[_sb(nc, st, "pT%d%d" % (c, s), [128, TT], BF16) for s in range(2)] for c in range(2)]
        rz = [_sb(nc, st, "rz%d" % i, [128, TT], F32) for i in range(2)]
        oo = [_sb(nc, st, "oo%d" % i, [128, TT], F32) for i in range(2)]
        od = _sb(nc, st, "od", [128, TT], F32)
        osq = _sb(nc, st, "osq", [128, TT], BF16)
        lnv = _sb(nc, st, "lnv", [128, TT], F32)
        rstd = _sb(nc, st, "rstd", [128, TT], F32)
        res = [_sb(nc, st, "res%d" % i, [128, TT], F32) for i in range(2)]
        lamt = _sb(nc, st, "lamt", [128, 256], F32)
        lprod = _sb(nc, st, "lprod", [128, 128], F32)
        lsum = _sb(nc, st, "lsum", [128, 2], F32)
        lexp = _sb(nc, st, "lexp", [128, 2], F32)
        neglam = _sb(nc, st, "neglam", [128, 1], F32)
        gsc = _sb(nc, st, "gsc", [128, 1], F32)
        gld = _sb(nc, st, "gld", [128, 1], F32)
        B = [_ps(nc, st, "B%d" % i) for i in range(8)]

        sc.add("pool", lambda e: e.dma_start(out=wsb[:], in_=wr), writes=["wsb"], dma=True)
        sc.add("pool", lambda e: e.dma_start(out=pm[:], in_=perm), writes=["pm"], dma=True)
        sc.add("pool", lambda e: e.dma_start(out=idn[:], in_=identd), writes=["idn"], dma=True)
        sc.add("sp", lambda e: e.dma_start(out=lamt[:], in_=lam4.partition_broadcast(128)),
               writes=["lamt"], dma=True)
        sc.add("sp", lambda e: e.dma_start(out=gld[:], in_=subg), writes=["gld"], dma=True)
        sc.add("pool", lambda e: e.memset(ones[:], 1.0), writes=["ones"])
        sc.add("pool", lambda e: e.memset(ones32[:], 1.0), writes=["ones32"])
        sc.add("dve", lambda e: e.tensor_tensor(out=lprod[:, 0:64], in0=lamt[:, 0:64], in1=lamt[:, 64:128],
                                                op=ALU.mult), reads=["lamt"], writes=["lprod"])
        sc.add("dve", lambda e: e.tensor_tensor(out=lprod[:, 64:128], in0=lamt[:, 128:192], in1=lamt[:, 192:256],
                                                op=ALU.mult), reads=["lamt"], writes=["lprod"])
        sc.add("dve", lambda e: e.reduce_sum(out=lsum[:, 0:1], in_=lprod[:, 0:64], axis=AX.X),
               reads=["lprod"], writes=["lsum"])
        sc.add("dve", lambda e: e.reduce_sum(out=lsum[:, 1:2], in_=lprod[:, 64:128], axis=AX.X),
               reads=["lprod"], writes=["lsum"])
        sc.add("act", lambda e: e.activation(out=lexp[:], in_=lsum[:], func=AF.Exp),
               reads=["lsum"], writes=["lexp"])
        sc.add("dve", lambda e: e.tensor_tensor(out=neglam[:], in0=lexp[:, 1:2], in1=lexp[:, 0:1],
                                                op=ALU.subtract), reads=["lexp"], writes=["neglam"])
        sc.add("dve", lambda e: e.tensor_scalar(out=neglam[:], in0=neglam[:], scalar1=-LAM_INIT0, scalar2=None,
                                                op0=ALU.add), reads=["neglam"], writes=["neglam"])
        sc.add("dve", lambda e: e.tensor_scalar(out=gsc[:], in0=gld[:], scalar1=(1.0 - LAM_INIT0), scalar2=None,
                                                op0=ALU.mult), reads=["gld"], writes=["gsc"])

        def load_x(tt):
            s = tt % 2
            c0 = tt * TT
            sc.add("pool", lambda e: e.dma_start(out=xb[s][:], in_=xTr[:, :, c0:c0 + TT]),
                   writes=[("xb", s)], dma=True)
            sc.add("sp", lambda e: e.dma_start(out=rc[s][:], in_=ropeC[:, c0:c0 + TT]),
                   writes=[("rc", s)], dma=True)
            sc.add("sp", lambda e: e.dma_start(out=rs[s][:], in_=ropeS[:, c0:c0 + TT]),
                   writes=[("rs", s)], dma=True)

        def mm(outp, lhsT, rhs, start, stop, reads, writes):
            sc.add("pe", lambda e: e.matmul(outp, lhsT, rhs, start=start, stop=stop),
                   reads=reads, writes=writes)

        def qkv_mm(tt):
            s = tt % 2
            c0 = tt * TT
            xs = xb[s]
            for k in range(8):
                mm(B[0][:, :], wsb[:, k, 0:128], xs[:, k, :], k == 0, k == 7, ["wsb", ("xb", s)], [("B", 0)])
            for k in range(8):
                mm(B[1][:, :], wsb[:, k, 128:256], xs[:, k, :], k == 0, k == 7, ["wsb", ("xb", s)], [("B", 1)])
            for sub in range(4):
                for k in range(8):
                    mm(B[2][:, sub * 128:(sub + 1) * 128], xs[:, k, sub * 128:(sub + 1) * 128],
                       wsb[:, k, 256:384], k == 0, k == 7, ["wsb", ("xb", s)], [("B", 2)])

        def qkv_post(tt):
            s = tt % 2
            c0 = tt * TT
            sc.add("dve", lambda e: e.tensor_tensor(out=qb[:], in0=B[0][:, :], in1=rc[s][:], op=ALU.mult),
                   reads=[("B", 0), ("rc", s)], writes=["qb"])
            sc.add("dve", lambda e: e.tensor_tensor(out=qs_[:], in0=B[0][:, :], in1=rs[s][:], op=ALU.mult),
                   reads=[("B", 0), ("rs", s)], writes=["qs_"])
            sc.add("dve", lambda e: e.tensor_tensor(out=kb_[:], in0=B[1][:, :], in1=rc[s][:], op=ALU.mult),
                   reads=[("B", 1), ("rc", s)], writes=["kb_"])
            sc.add("dve", lambda e: e.tensor_tensor(out=ks_[:], in0=B[1][:, :], in1=rs[s][:], op=ALU.mult),
                   reads=[("B", 1), ("rs", s)], writes=["ks_"])
            sc.add("act", lambda e: e.copy(out=vS[:, tt * 4:(tt + 1) * 4, :],
                                           in_=B[2][:, :].rearrange("p (a b) -> p a b", a=4)),
                   reads=[("B", 2)], writes=[("vS", tt)])
            mm(B[3][:, :], idn[:, :], qb[:, :], True, False, ["idn", "qb"], [("B", 3)])
            mm(B[3][:, :], pm[:, :], qs_[:, :], False, True, ["pm", "qs_"], [("B", 3)])
            sc.add("act", lambda e: e.copy(out=qT[:, c0:c0 + TT], in_=B[3][:, :]), reads=[("B", 3)], writes=[("qT", tt)])
            mm(B[2][:, :], idn[:, :], kb_[:, :], True, False, ["idn", "kb_"], [("B", 2)])
            mm(B[2][:, :], pm[:, :], ks_[:, :], False, True, ["pm", "ks_"], [("B", 2)])
            sc.add("act", lambda e: e.copy(out=kT[:, c0:c0 + TT], in_=B[2][:, :]), reads=[("B", 2)], writes=[("kT", tt)])

        def scores(qt, kb, slot):
            j = kb - 4 * qt
            f0 = 128 * j if j > 0 else 0
            q0 = qt * TT
            for c in range(2):
                bi = c * 2 + slot
                mm(B[bi][:, f0:TT], kT[c * 64:(c + 1) * 64, kb * 128:(kb + 1) * 128],
                   qT[c * 64:(c + 1) * 64, q0 + f0:q0 + TT], True, True,
                   [("kT", kb // 4), ("qT", qt)], [("B", bi)])

        def expo(qt, kb, slot):
            j = kb - 4 * qt
            f0 = 128 * j if j > 0 else 0
            for c in range(2):
                bi = c * 2 + slot
                sc.add("act", lambda e, c=c, bi=bi: e.activation(out=pT[c][slot][:, f0:TT], in_=B[bi][:, f0:TT],
                                                                 func=AF.Exp, scale=0.125),
                       reads=[("B", bi)], writes=[("pT", c, slot)])
                if j >= 0:
                    sc.add("pool", lambda e, c=c: e.memset(pT[c][slot][64:128, f0:f0 + 64], 0.0),
                           writes=[("pT", c, slot)])

        def pv(qt, kb, slot, last):
            j = kb - 4 * qt
            f0 = 128 * j if j > 0 else 0
            for c in range(2):
                mm(B[4 + c][:, f0:TT], vS[:, kb, :], pT[c][slot][:, f0:TT], kb == 0, kb == last,
                   [("vS", kb // 4), ("pT", c, slot)], [("B", 4 + c)])
                if c == 0:
                    mm(B[6][:, f0:TT], ones[:, :], pT[0][slot][:, f0:TT], kb == 0, kb == last,
                       ["ones", ("pT", 0, slot)], [("B", 6)])
                elif kb == 0:
                    sc.add("dve", lambda e: e.tensor_copy(out=B[7][:, :], in_=pT[1][slot][:, :]),
                           reads=[("pT", 1, slot)], writes=[("B", 7)])
                else:
                    sc.add("dve", lambda e: e.tensor_tensor(out=B[7][:, f0:TT], in0=B[7][:, f0:TT],
                                                            in1=pT[1][slot][:, f0:TT], op=ALU.add),
                           reads=[("pT", 1, slot), ("B", 7)], writes=[("B", 7)])

        def epilogue_a(qt):
            sc.add("dve", lambda e: e.tensor_copy(out=zs[:], in_=B[7][:, :]), reads=[("B", 7)], writes=["zs"])
            mm(B[7][:, :], ones32[:, :], zs[:, :], True, True, ["ones32", "zs"], [("B", 7)])
            for c in range(2):
                sc.add("act", lambda e, c=c: e.activation(out=rz[c][:], in_=B[6 + c][:, :], func=AF.Ln),
                       reads=[("B", 6 + c)], writes=[("rz", c)])
                sc.add("act", lambda e, c=c: e.activation(out=rz[c][:], in_=rz[c][:], func=AF.Exp, scale=-1.0),
                       reads=[("rz", c)], writes=[("rz", c)])

        def epilogue_b(qt):
            for c in range(2):
                sc.add("dve", lambda e, c=c: e.tensor_tensor(out=oo[c][:], in0=B[4 + c][:, :], in1=rz[c][:],
                                                             op=ALU.mult),
                       reads=[("B", 4 + c), ("rz", c)], writes=[("oo", c)])
            sc.add("dve", lambda e: e.scalar_tensor_tensor(out=od[:], in0=oo[1][:], scalar=neglam[:, 0:1],
                                                           in1=oo[0][:], op0=ALU.mult, op1=ALU.add),
                   reads=[("oo", 0), ("oo", 1), "neglam"], writes=["od"])
            sc.add("act", lambda e: e.activation(out=osq[:], in_=od[:], func=AF.Square),
                   reads=["od"], writes=["osq"])

        def epilogue_post(qt):
            q0 = qt * TT
            r = res[qt % 2]
            mm(B[3][:, :], ones[:, :], osq[:, :], True, True, ["ones", "osq"], [("B", 3)])
            sc.add("dve", lambda e: e.tensor_scalar(out=lnv[:], in0=B[3][:, :], scalar1=1.0 / 128.0,
                                                    scalar2=LN_EPS, op0=ALU.mult, op1=ALU.add),
                   reads=[("B", 3)], writes=["lnv"])
            sc.add("act", lambda e: e.activation(out=lnv[:], in_=lnv[:], func=AF.Ln),
                   reads=["lnv"], writes=["lnv"])
            sc.add("act", lambda e: e.activation(out=rstd[:], in_=lnv[:], func=AF.Exp, scale=-0.5),
                   reads=["lnv"], writes=["rstd"])
            sc.add("dve", lambda e: e.scalar_tensor_tensor(out=r[:], in0=od[:], scalar=gsc[:, 0:1], in1=rstd[:],
                                                           op0=ALU.mult, op1=ALU.mult),
                   reads=["od", "rstd", "gsc"], writes=[("res", qt % 2)])
            sc.add("sp", lambda e: e.dma_start(out=out[:, q0:q0 + TT], in_=r[:]),
                   reads=[("res", qt % 2)], dma=True)

        load_x(0)
        if stage == 1:
            sc.add("sp", lambda e: e.dma_start(out=out[:, 0:TT], in_=rc[0][:]), reads=[("rc", 0), "neglam", "gsc", "wsb", "pm", "ones", ("xb", 0)], dma=True)
            sc.emit()
            return nc
        qkv_mm(0)
        qkv_post(0)
        for tt in range(nt):
            if tt + 1 < nt:
                load_x(tt + 1)
            if stage == 2:
                sc.add("act", lambda e: e.copy(out=res[0][:], in_=qT[:, 0:TT]), reads=[("qT", 0), ("kT", 0), ("vS", 0)], writes=[("res", 0)])
                sc.add("sp", lambda e: e.dma_start(out=out[:, 0:TT], in_=res[0][:]), reads=[("res", 0)], dma=True)
                sc.emit()
                return nc
            nkb = 4 * tt + 4
            last = nkb - 1
            scores(tt, 0, 0)
            for kb in range(nkb):
                if kb + 1 < nkb:
                    scores(tt, kb + 1, (kb + 1) % 2)
                expo(tt, kb, kb % 2)
                pv(tt, kb, kb % 2, last)
            epilogue_a(tt)
            if tt + 1 < nt:
                qkv_mm(tt + 1)
            epilogue_b(tt)
            if tt + 1 < nt:
                qkv_post(tt + 1)
            epilogue_post(tt)
        sc.emit()
    return nc


def rope_tables(n):
    half = 8
    pos = np.arange(n, dtype=np.float32)
    inv = (ROPE_THETA ** (-np.arange(0, 16, 2, dtype=np.float32) / np.float32(16))).astype(np.float32)
    ang = (pos[:, None] * inv[None, :]).astype(np.float32)
    cos = np.cos(ang).astype(np.float32).T
    sin = np.sin(ang).astype(np.float32).T
    C = np.ones((128, n), np.float32)
    Sg = np.zeros((128, n), np.float32)
    for c in range(2):
        b = c * 64
        C[b:b + 8] = cos
        C[b + 8:b + 16] = cos
        Sg[b:b + 8] = sin
        Sg[b + 8:b + 16] = -sin
    P = np.zeros((128, 128), np.float32)
    for c in range(2):
        b = c * 64
        for i in range(half):
            P[b + i + 8, b + i] = 1.0
            P[b + i, b + i + 8] = 1.0
    return C, Sg, P


def run_A(x, attn_w_qkv, lq1, lk1, lq2, lk2, subln_g, nt=S // 512):
    SS = nt * 512
    xT = np.ascontiguousarray(x.reshape(S, D)[:SS].T)
    C, Sg, P = rope_tables(SS)
    lam4 = np.concatenate([lq1.reshape(-1), lk1.reshape(-1), lq2.reshape(-1), lk2.reshape(-1)]).reshape(1, 256)
    lam4 = np.ascontiguousarray(lam4.astype(np.float32))
    subg = np.ascontiguousarray(subln_g.reshape(128, 1).astype(np.float32))
    w = attn_w_qkv.reshape(D, 3 * D)
    in_maps = []
    for h in range(NCORES):
        wh = np.concatenate([w[:, h * 128:(h + 1) * 128], w[:, D + h * 128:D + (h + 1) * 128],
                             w[:, 2 * D + h * 128:2 * D + (h + 1) * 128]], axis=1)
        in_maps.append({"xT": xT, "wqkv": np.ascontiguousarray(wh), "ropeC": C, "ropeS": Sg, "perm": P,
                        "ident": np.eye(128, dtype=np.float32),
                        "lam4": lam4, "subg": subg})
    nc = build_A(nt)
    r = run_bass_kernel_spmd(nc, in_maps, core_ids=list(range(NCORES)))
    return np.concatenate([r.results[h]["attnT"] for h in range(NCORES)], axis=0)


TOK = S // NCORES
TG = 1024
TT = 512
WCAP = 6144


def build_BC(mode, tok=TOK):
    ngroups = tok // TG
    ntile = TG // TT
    nc = bass.Bass("TRN2", target_bir_lowering=False)
    hin = nc.dram_tensor("hin", [D, tok], F32, kind="ExternalInput").ap()
    if mode == "B":
        ain = nc.dram_tensor("ain", [D, tok], F32, kind="ExternalInput").ap()
        wmix = nc.dram_tensor("wmix", [D, D], F32, kind="ExternalInput").ap()
    else:
        halo = nc.dram_tensor("halo", [D, 2], F32, kind="ExternalInput").ap()
        w_in = nc.dram_tensor("w_in", [D, 3 * D], F32, kind="ExternalInput").ap()
        cw = nc.dram_tensor("cw", [128, 24], F32, kind="ExternalInput").ap()
        wmix = nc.dram_tensor("wmix", [D, D], F32, kind="ExternalInput").ap()
    wg = nc.dram_tensor("wg", [D, DFF], F32, kind="ExternalInput").ap()
    wu = nc.dram_tensor("wu", [D, DFF], F32, kind="ExternalInput").ap()
    wd = nc.dram_tensor("wd", [DFF, D], F32, kind="ExternalInput").ap()
    lnp = nc.dram_tensor("lnp", [128, 32], F32, kind="ExternalInput").ap()
    hout = nc.dram_tensor("hout", [D, tok], F32, kind="ExternalOutput").ap()

    hin_r = hin.rearrange("(k p) t -> p k t", p=128)
    hout_r = hout.rearrange("(k p) t -> p k t", p=128)
    wmix_r = wmix.rearrange("(k p) n -> p k n", p=128)
    wg_r = wg.rearrange("(k p) n -> p k n", p=128)
    wu_r = wu.rearrange("(k p) n -> p k n", p=128)
    wd_r = wd.rearrange("(k p) n -> p k n", p=128)
    NF = DFF // 128

    with ExitStack() as st:
        sc = Sched(nc, st)
        hres = _sb(nc, st, "hres", [128, 8, TG], F32)
        hb = _sb(nc, st, "hb", [128, 8, TG], BF16)
        ab = _sb(nc, st, "ab", [128, 8, TG], BF16)
        hid = _sb(nc, st, "hid", [128, NF, TG], BF16)
        wsl = [_sb(nc, st, "wsl%d" % i, [128, WCAP], BF16) for i in range(3)]
        zb = _sb(nc, st, "zb", [128, 8, TT], BF16)
        zsq = _sb(nc, st, "zsq", [128, 8, TT], BF16)
        mean = _sb(nc, st, "mean", [128, TT], F32)
        msq = _sb(nc, st, "msq", [128, TT], F32)
        var = _sb(nc, st, "var", [128, TT], F32)
        rstd = _sb(nc, st, "rstd", [128, TT], F32)
        nmr = _sb(nc, st, "nmr", [128, TT], F32)
        tmp = [_sb(nc, st, "tmp%d" % i, [128, TT], F32) for i in range(2)]
        tmp2 = [_sb(nc, st, "tmq%d" % i, [128, TT], F32) for i in range(2)]
        sg = [_sb(nc, st, "sg%d" % i, [128, TT], F32) for i in range(2)]
        ones = _sb(nc, st, "ones", [128, 128], BF16)
        lnt = _sb(nc, st, "lnt", [128, 32], F32)
        B = [_ps(nc, st, "B%d" % i) for i in range(8)]
        if mode == "C":
            cwt = _sb(nc, st, "cwt", [128, 24], F32)
            hbh = _sb(nc, st, "hbh", [128, 8, 2], BF16)
            uprev = _sb(nc, st, "uprev", [128, 8, 2], F32)
            ubuf = [_sb(nc, st, "ubuf%d" % i, [128, TT + 2], F32) for i in range(2)]
            cs = [_sb(nc, st, "cs%d" % i, [128, TT], F32) for i in range(2)]
            ybuf = [_sb(nc, st, "ybuf%d" % i, [128, TT], F32) for i in range(2)]
            hcs = _sb(nc, st, "hcs", [128, 2], F32)

        state = {"bank": 0, "job": 0, "tmp": 0}

        def bank():
            b = state["bank"]
            state["bank"] = (b + 1) % 8
            return b

        def mm(outp, lhsT, rhs, start, stop, reads, writes):
            sc.add("pe", lambda e: e.matmul(outp, lhsT, rhs, start=start, stop=stop),
                   reads=reads, writes=writes)

        sc.add("pool", lambda e: e.memset(ones[:], 1.0), writes=["ones"])
        sc.add("sp", lambda e: e.dma_start(out=lnt[:], in_=lnp), writes=["lnt"], dma=True)
        if mode == "C":
            sc.add("sp", lambda e: e.dma_start(out=cwt[:], in_=cw), writes=["cwt"], dma=True)

        jobs = []

        def run_jobs():
            n = len(jobs)
            base = state["job"]
            if n:
                jobs[0][0]((base) % 3)
            for i in range(n):
                if i + 1 < n:
                    jobs[i + 1][0]((base + i + 1) % 3)
                jobs[i][1]((base + i) % 3)
            state["job"] = base + n
            del jobs[:]

        def wload(slot, view, src):
            sc.add("pool", lambda e: e.dma_start(out=view, in_=src), writes=[("wsl", slot)], dma=True)

        def tcols(t):
            return slice(t * TT, (t + 1) * TT)

        def add_proj_jobs(w_r, src, srckey):
            for nb in range(2):
                def load(slot, nb=nb):
                    v = wsl[slot][:, 0:4096].rearrange("p (k n) -> p k n", k=8)
                    wload(slot, v, w_r[:, :, nb * 512:(nb + 1) * 512])

                def comp(slot, nb=nb):
                    v = wsl[slot][:, 0:4096].rearrange("p (k n) -> p k n", k=8)
                    for t in range(ntile):
                        for j in range(4):
                            n = nb * 4 + j
                            b = bank()
                            for k in range(8):
                                mm(B[b][:, :], v[:, k, j * 128:(j + 1) * 128], src[:, k, tcols(t)], k == 0, k == 7,
                                   [("wsl", slot), (srckey, k, t)], [("B", b)])
                            sc.add("dve", lambda e, n=n, t=t, b=b: e.scalar_tensor_tensor(
                                out=hres[:, n, tcols(t)], in0=hres[:, n, tcols(t)], scalar=ALPHA, in1=B[b][:, :],
                                op0=ALU.mult, op1=ALU.add),
                                reads=[("B", b), ("hres", n, t)], writes=[("hres", n, t)])
                jobs.append((load, comp))

        def layer_norm(q, want_hb):
            for t in range(ntile):
                for k in range(8):
                    sc.add("act", lambda e, k=k, t=t: e.copy(out=zb[:, k, :], in_=hres[:, k, tcols(t)]),
                           reads=[("hres", k, t)], writes=[("zb", k)])
                    sc.add("pool", lambda e, k=k, t=t: e.tensor_tensor(out=zsq[:, k, :], in0=hres[:, k, tcols(t)],
                                                                         in1=hres[:, k, tcols(t)], op=ALU.mult),
                           reads=[("hres", k, t)], writes=[("zsq", k)])
                ba = bank()
                for k in range(8):
                    mm(B[ba][:, :], ones[:, :], zb[:, k, :], k == 0, k == 7, ["ones", ("zb", k)], [("B", ba)])
                bb = bank()
                for k in range(8):
                    mm(B[bb][:, :], ones[:, :], zsq[:, k, :], k == 0, k == 7, ["ones", ("zsq", k)], [("B", bb)])
                sc.add("dve", lambda e, ba=ba: e.tensor_scalar(out=mean[:], in0=B[ba][:, :], scalar1=1.0 / D,
                                                               scalar2=None, op0=ALU.mult),
                       reads=[("B", ba)], writes=["mean"])
                sc.add("dve", lambda e: e.tensor_tensor(out=msq[:], in0=mean[:], in1=mean[:], op=ALU.mult),
                       reads=["mean"], writes=["msq"])
                sc.add("dve", lambda e, bb=bb: e.scalar_tensor_tensor(out=var[:], in0=B[bb][:, :], scalar=1.0 / D,
                                                                      in1=msq[:], op0=ALU.mult, op1=ALU.subtract),
                       reads=[("B", bb), "msq"], writes=["var"])
                sc.add("dve", lambda e: e.tensor_scalar(out=var[:], in0=var[:], scalar1=LN_EPS, scalar2=None,
                                                        op0=ALU.add), reads=["var"], writes=["var"])
                sc.add("act", lambda e: e.activation(out=var[:], in_=var[:], func=AF.Ln),
                       reads=["var"], writes=["var"])
                sc.add("act", lambda e: e.activation(out=rstd[:], in_=var[:], func=AF.Exp, scale=-0.5),
                       reads=["var"], writes=["rstd"])
                sc.add("dve", lambda e: e.scalar_tensor_tensor(out=nmr[:], in0=mean[:], scalar=-1.0, in1=rstd[:],
                                                               op0=ALU.mult, op1=ALU.mult),
                       reads=["mean", "rstd"], writes=["nmr"])
                for k in range(8):
                    i = state["tmp"]
                    state["tmp"] = 1 - i
                    sc.add("dve", lambda e, k=k, t=t, i=i: e.tensor_tensor(out=tmp[i][:], in0=hres[:, k, tcols(t)],
                                                                           in1=rstd[:], op=ALU.mult),
                           reads=[("hres", k, t), "rstd"], writes=[("tmp", i)])
                    sc.add("pool", lambda e, i=i: e.tensor_tensor(out=tmp2[i][:], in0=tmp[i][:], in1=nmr[:],
                                                                  op=ALU.add),
                           reads=[("tmp", i), "nmr"], writes=[("tmp2", i)])
                    gcol = q * 16 + k
                    bcol = q * 16 + 8 + k
                    sc.add("act", lambda e, k=k, t=t, i=i, gcol=gcol, bcol=bcol: e.activation(
                        out=hres[:, k, tcols(t)], in_=tmp2[i][:], func=AF.Identity,
                        scale=lnt[:, gcol:gcol + 1], bias=lnt[:, bcol:bcol + 1]),
                        reads=[("tmp2", i), "lnt"], writes=[("hres", k, t)])
                    if want_hb:
                        sc.add("dve", lambda e, k=k, t=t, i=i, gcol=gcol, bcol=bcol: e.tensor_scalar(
                            out=hb[:, k, tcols(t)], in0=tmp2[i][:], scalar1=lnt[:, gcol:gcol + 1],
                            scalar2=lnt[:, bcol:bcol + 1], op0=ALU.mult, op1=ALU.add),
                            reads=[("tmp2", i), "lnt"], writes=[("hb", k, t)])

        def add_ffn_jobs():
            f0 = 0
            while f0 < NF:
                fb = min(3, NF - f0)

                def load(slot, f0=f0, fb=fb):
                    v = wsl[slot][:, 0:2 * 8 * 384].rearrange("p (g k n) -> p g k n", g=2, k=8)
                    wload(slot, v[:, 0, :, 0:fb * 128], wg_r[:, :, f0 * 128:(f0 + fb) * 128])
                    wload(slot, v[:, 1, :, 0:fb * 128], wu_r[:, :, f0 * 128:(f0 + fb) * 128])

                def comp(slot, f0=f0, fb=fb):
                    v = wsl[slot][:, 0:2 * 8 * 384].rearrange("p (g k n) -> p g k n", g=2, k=8)
                    for t in range(ntile):
                        for j in range(fb):
                            fc = f0 + j
                            bg = bank()
                            for k in range(8):
                                mm(B[bg][:, :], v[:, 0, k, j * 128:(j + 1) * 128], hb[:, k, tcols(t)], k == 0, k == 7,
                                   [("wsl", slot), ("hb", k, t)], [("B", bg)])
                            bu = bank()
                            for k in range(8):
                                mm(B[bu][:, :], v[:, 1, k, j * 128:(j + 1) * 128], hb[:, k, tcols(t)], k == 0, k == 7,
                                   [("wsl", slot), ("hb", k, t)], [("B", bu)])
                            i = state["tmp"]
                            state["tmp"] = 1 - i
                            sc.add("act", lambda e, i=i, bg=bg: e.activation(out=sg[i][:], in_=B[bg][:, :],
                                                                             func=AF.Silu),
                                   reads=[("B", bg)], writes=[("sg", i)])
                            sc.add("dve", lambda e, i=i, bu=bu, fc=fc, t=t: e.tensor_tensor(
                                out=hid[:, fc, tcols(t)], in0=B[bu][:, :], in1=sg[i][:], op=ALU.mult),
                                reads=[("B", bu), ("sg", i)], writes=[("hid", fc, t)])
                jobs.append((load, comp))
                f0 += fb
            for nb in range(4):
                def load(slot, nb=nb):
                    v = wsl[slot][:, 0:NF * 256].rearrange("p (k n) -> p k n", k=NF)
                    wload(slot, v, wd_r[:, :, nb * 256:(nb + 1) * 256])

                def comp(slot, nb=nb):
                    v = wsl[slot][:, 0:NF * 256].rearrange("p (k n) -> p k n", k=NF)
                    for t in range(ntile):
                        for j in range(2):
                            n = nb * 2 + j
                            b = bank()
                            for fc in range(NF):
                                mm(B[b][:, :], v[:, fc, j * 128:(j + 1) * 128], hid[:, fc, tcols(t)], fc == 0,
                                   fc == NF - 1, [("wsl", slot), ("hid", fc, t)], [("B", b)])
                            sc.add("dve", lambda e, n=n, t=t, b=b: e.scalar_tensor_tensor(
                                out=hres[:, n, tcols(t)], in0=hres[:, n, tcols(t)], scalar=ALPHA, in1=B[b][:, :],
                                op0=ALU.mult, op1=ALU.add),
                                reads=[("B", b), ("hres", n, t)], writes=[("hres", n, t)])
                jobs.append((load, comp))

        def add_conv_jobs(g):
            w_in_r = w_in.rearrange("(k p) (g n) -> p k g n", p=128, g=3)
            for jb in range(4):
                def load(slot, jb=jb):
                    v = wsl[slot][:, 0:WCAP].rearrange("p (k g n) -> p k g n", k=8, g=3)
                    for gg in range(3):
                        wload(slot, v[:, :, gg, :], w_in_r[:, :, gg, jb * 256:(jb + 1) * 256])

                def comp(slot, jb=jb):
                    v = wsl[slot][:, 0:WCAP].rearrange("p (k g n) -> p k g n", k=8, g=3)
                    for jj in range(2):
                        ch = jb * 2 + jj
                        cols = slice(jj * 128, (jj + 1) * 128)
                        if g == 0:
                            bc = bank()
                            for k in range(8):
                                mm(B[bc][:, 0:2], v[:, k, 1, cols], hbh[:, k, :], k == 0, k == 7,
                                   [("wsl", slot), "hbh"], [("B", bc)])
                            bx = bank()
                            for k in range(8):
                                mm(B[bx][:, 0:2], v[:, k, 2, cols], hbh[:, k, :], k == 0, k == 7,
                                   [("wsl", slot), "hbh"], [("B", bx)])
                            sc.add("act", lambda e, bc=bc: e.copy(out=hcs[:], in_=B[bc][:, 0:2]),
                                   reads=[("B", bc)], writes=["hcs"])
                            sc.add("dve", lambda e, bx=bx, ch=ch: e.tensor_tensor(out=uprev[:, ch, :], in0=B[bx][:, 0:2],
                                                                                  in1=hcs[:], op=ALU.mult),
                                   reads=[("B", bx), "hcs"], writes=[("uprev", ch)])
                        for t in range(ntile):
                            bbk = bank()
                            for k in range(8):
                                mm(B[bbk][:, :], v[:, k, 0, cols], hb[:, k, tcols(t)], k == 0, k == 7,
                                   [("wsl", slot), ("hb", k, t)], [("B", bbk)])
                            bc = bank()
                            for k in range(8):
                                mm(B[bc][:, :], v[:, k, 1, cols], hb[:, k, tcols(t)], k == 0, k == 7,
                                   [("wsl", slot), ("hb", k, t)], [("B", bc)])
                            bx = bank()
                            for k in range(8):
                                mm(B[bx][:, :], v[:, k, 2, cols], hb[:, k, tcols(t)], k == 0, k == 7,
                                   [("wsl", slot), ("hb", k, t)], [("B", bx)])
                            i = state["tmp"]
                            state["tmp"] = 1 - i
                            u = ubuf[i]
                            sc.add("act", lambda e, i=i, bc=bc: e.copy(out=cs[i][:], in_=B[bc][:, :]),
                                   reads=[("B", bc)], writes=[("cs", i)])
                            sc.add("pool", lambda e, u=u, ch=ch: e.tensor_copy(out=u[:, 0:2], in_=uprev[:, ch, :]),
                                   reads=[("uprev", ch)], writes=[("u", i)])
                            sc.add("dve", lambda e, u=u, i=i, bx=bx: e.tensor_tensor(out=u[:, 2:TT + 2], in0=B[bx][:, :],
                                                                                     in1=cs[i][:], op=ALU.mult),
                                   reads=[("B", bx), ("cs", i)], writes=[("u", i)])
                            sc.add("pool", lambda e, u=u, ch=ch: e.tensor_copy(out=uprev[:, ch, :], in_=u[:, TT:TT + 2]),
                                   reads=[("u", i)], writes=[("uprev", ch)])
                            y = ybuf[i]
                            c3 = ch * 3
                            sc.add("dve", lambda e, u=u, y=y, c3=c3: e.tensor_scalar(
                                out=y[:], in0=u[:, 2:TT + 2], scalar1=cwt[:, c3 + 2:c3 + 3], scalar2=None, op0=ALU.mult),
                                reads=[("u", i), "cwt"], writes=[("y", i)])
                            sc.add("dve", lambda e, u=u, y=y, c3=c3: e.scalar_tensor_tensor(
                                out=y[:], in0=u[:, 1:TT + 1], scalar=cwt[:, c3 + 1:c3 + 2], in1=y[:],
                                op0=ALU.mult, op1=ALU.add), reads=[("u", i), "cwt", ("y", i)], writes=[("y", i)])
                            sc.add("dve", lambda e, u=u, y=y, c3=c3: e.scalar_tensor_tensor(
                                out=y[:], in0=u[:, 0:TT], scalar=cwt[:, c3:c3 + 1], in1=y[:],
                                op0=ALU.mult, op1=ALU.add), reads=[("u", i), "cwt", ("y", i)], writes=[("y", i)])
                            sc.add("dve", lambda e, y=y, bbk=bbk, ch=ch, t=t: e.tensor_tensor(
                                out=ab[:, ch, tcols(t)], in0=B[bbk][:, :], in1=y[:], op=ALU.mult),
                                reads=[("B", bbk), ("y", i)], writes=[("ab", ch, t)])
                jobs.append((load, comp))

        for g in range(ngroups):
            g0 = g * TG
            for t in range(ntile):
                c0 = g0 + t * TT
                sc.add("sp", lambda e, t=t, c0=c0: e.dma_start(out=hres[:, :, tcols(t)], in_=hin_r[:, :, c0:c0 + TT]),
                       writes=[("hres", k, t) for k in range(8)], dma=True)
            if mode == "B":
                ain_r = ain.rearrange("(k p) t -> p k t", p=128)
                for t in range(ntile):
                    c0 = g0 + t * TT
                    sc.add("pool", lambda e, t=t, c0=c0: e.dma_start(out=ab[:, :, tcols(t)], in_=ain_r[:, :, c0:c0 + TT]),
                           writes=[("ab", k, t) for k in range(8)], dma=True)
                add_proj_jobs(wmix_r, ab, "ab")
                run_jobs()
            else:
                for t in range(ntile):
                    c0 = g0 + t * TT
                    sc.add("pool", lambda e, t=t, c0=c0: e.dma_start(out=hb[:, :, tcols(t)], in_=hin_r[:, :, c0:c0 + TT]),
                           writes=[("hb", k, t) for k in range(8)], dma=True)
                if g == 0:
                    halo_r = halo.rearrange("(k p) t -> p k t", p=128)
                    sc.add("pool", lambda e: e.dma_start(out=hbh[:], in_=halo_r), writes=["hbh"], dma=True)
                add_conv_jobs(g)
                add_proj_jobs(wmix_r, ab, "ab")
                run_jobs()
            layer_norm(0, True)
            add_ffn_jobs()
            run_jobs()
            layer_norm(1, False)
            for t in range(ntile):
                c0 = g0 + t * TT
                sc.add("sp", lambda e, t=t, c0=c0: e.dma_start(out=hout_r[:, :, c0:c0 + TT], in_=hres[:, :, tcols(t)]),
                       reads=[("hres", k, t) for k in range(8)], dma=True)
        sc.emit()
    return nc


def _lnp(ln1_g, ln1_b, ln2_g, ln2_b, i):
    cols = []
    for v in (ln1_g[i], ln1_b[i], ln2_g[i], ln2_b[i]):
        cols.append(np.asarray(v, np.float32).reshape(8, 128).T)
    return np.ascontiguousarray(np.concatenate(cols, axis=1))


def run_B(xT, attnT, w_o, wg, wu, wd, lnp):
    in_maps = []
    for c in range(NCORES):
        sl = slice(c * TOK, (c + 1) * TOK)
        in_maps.append({"hin": np.ascontiguousarray(xT[:, sl]), "ain": np.ascontiguousarray(attnT[:, sl]),
                        "wmix": w_o, "wg": wg, "wu": wu, "wd": wd, "lnp": lnp})
    nc = build_BC("B")
    r = run_bass_kernel_spmd(nc, in_maps, core_ids=list(range(NCORES)))
    return np.concatenate([r.results[c]["hout"] for c in range(NCORES)], axis=1)


def run_C(hT, w_in, cw, w_out, wg, wu, wd, lnp):
    in_maps = []
    for c in range(NCORES):
        sl = slice(c * TOK, (c + 1) * TOK)
        halo = np.zeros((D, 2), np.float32) if c == 0 else np.ascontiguousarray(hT[:, c * TOK - 2:c * TOK])
        in_maps.append({"hin": np.ascontiguousarray(hT[:, sl]), "halo": halo, "w_in": w_in, "cw": cw,
                        "wmix": w_out, "wg": wg, "wu": wu, "wd": wd, "lnp": lnp})
    nc = build_BC("C")
    r = run_bass_kernel_spmd(nc, in_maps, core_ids=list(range(NCORES)))
    return np.concatenate([r.results[c]["hout"] for c in range(NCORES)], axis=1)


def kernel3(x, attn_w_qkv, attn_w_o, attn_lambda_q1, attn_lambda_k1, attn_lambda_q2, attn_lambda_k2,
           attn_subln_g, conv_w_in, conv_w, conv_w_out, ffn_w_gate, ffn_w_up, ffn_w_down,
           ln1_g, ln1_b, ln2_g, ln2_b):
    f = lambda a: np.ascontiguousarray(np.asarray(a, np.float32))
    x = f(x)
    attnT = run_A(x, f(attn_w_qkv), f(attn_lambda_q1), f(attn_lambda_k1), f(attn_lambda_q2), f(attn_lambda_k2),
                  f(attn_subln_g))
    xT = np.ascontiguousarray(x.reshape(S, D).T)
    h1T = run_B(xT, attnT, f(attn_w_o)[0], f(ffn_w_gate)[0], f(ffn_w_up)[0], f(ffn_w_down)[0],
                _lnp(ln1_g, ln1_b, ln2_g, ln2_b, 0))
    cw = np.ascontiguousarray(f(conv_w)[0].reshape(3, 8, 128).transpose(2, 1, 0).reshape(128, 24))
    h2T = run_C(h1T, f(conv_w_in)[0], cw, f(conv_w_out)[0], f(ffn_w_gate)[1], f(ffn_w_up)[1], f(ffn_w_down)[1],
                _lnp(ln1_g, ln1_b, ln2_g, ln2_b, 1))
    return np.ascontiguousarray(h2T.T).reshape(1, S, D).astype(np.float32)


def build_T(tok=TOK):
    ngroups = tok // TG
    TGW = TG + 2
    nc = bass.Bass("TRN2", target_bir_lowering=False)
    hin = nc.dram_tensor("hin", [D, tok], F32, kind="ExternalInput").ap()
    xhalo = nc.dram_tensor("xhalo", [D, 2], F32, kind="ExternalInput").ap()
    ain = nc.dram_tensor("ain", [D, tok], F32, kind="ExternalInput").ap()
    ahalo = nc.dram_tensor("ahalo", [D, 2], F32, kind="ExternalInput").ap()
    uflag = nc.dram_tensor("uflag", [128, 1], F32, kind="ExternalInput").ap()
    w_o = nc.dram_tensor("w_o", [D, D], F32, kind="ExternalInput").ap()
    w_in = nc.dram_tensor("w_in", [D, 3 * D], F32, kind="ExternalInput").ap()
    cw = nc.dram_tensor("cw", [128, 24], F32, kind="ExternalInput").ap()
    w_out = nc.dram_tensor("w_out", [D, D], F32, kind="ExternalInput").ap()
    wgs = [nc.dram_tensor("wg%d" % i, [D, DFF], F32, kind="ExternalInput").ap() for i in range(2)]
    wus = [nc.dram_tensor("wu%d" % i, [D, DFF], F32, kind="ExternalInput").ap() for i in range(2)]
    wds = [nc.dram_tensor("wd%d" % i, [DFF, D], F32, kind="ExternalInput").ap() for i in range(2)]
    lnp = nc.dram_tensor("lnp", [128, 64], F32, kind="ExternalInput").ap()
    hout = nc.dram_tensor("hout", [D, tok], F32, kind="ExternalOutput").ap()

    kp = lambda a: a.rearrange("(k p) t -> p k t", p=128)
    hin_r, ain_r, hout_r = kp(hin), kp(ain), kp(hout)
    xhalo_r, ahalo_r = kp(xhalo), kp(ahalo)
    w_o_r, w_out_r = kp(w_o), kp(w_out)
    wg_r = [kp(a) for a in wgs]
    wu_r = [kp(a) for a in wus]
    wd_r = [kp(a) for a in wds]
    w_in_r = w_in.rearrange("(k p) (g n) -> p k g n", p=128, g=3)
    NF = DFF // 128

    with ExitStack() as st:
        sc = Sched(nc, st)
        hres = _sb(nc, st, "hres", [128, 8, TGW], F32)
        hb = _sb(nc, st, "hb", [128, 8, TGW], BF16)
        ab = _sb(nc, st, "ab", [128, 8, TGW], BF16)
        hid = _sb(nc, st, "hid", [128, NF, TGW], BF16)
        wsl = [_sb(nc, st, "wsl%d" % i, [128, WCAP], BF16) for i in range(3)]
        zb = _sb(nc, st, "zb", [128, 8, TT], BF16)
        zsq = _sb(nc, st, "zsq", [128, 8, TT], BF16)
        mean = _sb(nc, st, "mean", [128, TT], F32)
        msq = _sb(nc, st, "msq", [128, TT], F32)
        var = _sb(nc, st, "var", [128, TT], F32)
        rstd = _sb(nc, st, "rstd", [128, TT], F32)
        nmr = _sb(nc, st, "nmr", [128, TT], F32)
        tmp = [_sb(nc, st, "tmp%d" % i, [128, TT], F32) for i in range(2)]
        tmp2 = [_sb(nc, st, "tmq%d" % i, [128, TT], F32) for i in range(2)]
        sg = [_sb(nc, st, "sg%d" % i, [128, TT], F32) for i in range(2)]
        ones = _sb(nc, st, "ones", [128, 128], BF16)
        lnt = _sb(nc, st, "lnt", [128, 64], F32)
        cwt = _sb(nc, st, "cwt", [128, 24], F32)
        ufl = _sb(nc, st, "ufl", [128, 1], F32)
        uprev = _sb(nc, st, "uprev", [128, 8, 2], F32)
        ubuf = [_sb(nc, st, "ubuf%d" % i, [128, TT + 2], F32) for i in range(2)]
        cs = [_sb(nc, st, "cs%d" % i, [128, TT], F32) for i in range(2)]
        ybuf = [_sb(nc, st, "ybuf%d" % i, [128, TT], F32) for i in range(2)]
        hcs = _sb(nc, st, "hcs", [128, 2], F32)
        B = [_ps(nc, st, "B%d" % i) for i in range(8)]

        state = {"bank": 0, "job": 0, "tmp": 0}

        def bank():
            b = state["bank"]
            state["bank"] = (b + 1) % 6
            return b

        def flip():
            i = state["tmp"]
            state["tmp"] = 1 - i
            return i

        def mm(outp, lhsT, rhs, start, stop, reads, writes):
            sc.add("pe", lambda e: e.matmul(outp, lhsT, rhs, start=start, stop=stop),
                   reads=reads, writes=writes)

        sc.add("pool", lambda e: e.memset(ones[:], 1.0), writes=["ones"])
        sc.add("sp", lambda e: e.dma_start(out=lnt[:], in_=lnp), writes=["lnt"], dma=True)
        sc.add("sp", lambda e: e.dma_start(out=cwt[:], in_=cw), writes=["cwt"], dma=True)
        sc.add("sp", lambda e: e.dma_start(out=ufl[:], in_=uflag), writes=["ufl"], dma=True)

        def tiles(g, layer):
            ts = [(0, 0, TT), (1, TT, TT)]
            if g == 0 and layer == 0:
                ts.append((2, TG, 2))
            return ts

        jobs = []
        side = []

        def hook():
            for _ in range(state.get("rate", 0)):
                if side:
                    side.pop(0)()

        def flush():
            while side:
                side.pop(0)()
            state["rate"] = 0

        def enqueue(pieces, nsteps):
            side.extend(pieces)
            state["rate"] = max(1, -(-2 * len(side) // max(1, nsteps)))

        def split(tl):
            first = [x for x in tl if x[0] != 1]
            second = [x for x in tl if x[0] == 1]
            return first, second

        def ensure_loaded(idx):
            if idx < len(jobs) and not jobs[idx]["loaded"]:
                jobs[idx]["loaded"] = True
                jobs[idx]["load"]((state["job"] + idx) % 3)

        def run_segment(lo, hi, ln_q, ln_tl, pre_side, after_first=None):
            npairs = (hi - lo) // 2
            lf, ls = split(ln_tl)
            for p in range(npairs):
                ia, ib = lo + 2 * p, lo + 2 * p + 1
                ja, jb = jobs[ia], jobs[ib]
                sa, sb = (state["job"] + ia) % 3, (state["job"] + ib) % 3
                ensure_loaded(ia)
                ensure_loaded(ib)
                ensure_loaded(ib + 1)
                fa, sea = split(ja["tl"])
                fb_, seb = split(jb["tl"])
                if p == 0 and pre_side:
                    enqueue(pre_side, ja["spt"] * len(fa) + jb["spt"] * len(fb_))
                ja["comp"](sa, fa)
                jb["comp"](sb, fb_)
                if p == 0:
                    flush()
                    if after_first is not None:
                        after_first()
                if p == npairs - 1:
                    pcs = []
                    for tile in lf:
                        pcs += ln_pieces(ln_q, tile)
                    enqueue(pcs, ja["spt"] * len(sea) + jb["spt"] * len(seb))
                ja["comp"](sa, sea)
                ensure_loaded(ib + 2)
                jb["comp"](sb, seb)
                if p == npairs - 1:
                    flush()
            pcs = []
            for tile in ls:
                pcs += ln_pieces(ln_q, tile)
            return pcs

        def wload(slot, view, src):
            sc.add("pool", lambda e: e.dma_start(out=view, in_=src), writes=[("wsl", slot)], dma=True)

        def add_proj_jobs(w_r, tl):
            for nb in range(2):
                def load(slot, nb=nb):
                    v = wsl[slot][:, 0:4096].rearrange("p (k n) -> p k n", k=8)
                    wload(slot, v, w_r[:, :, nb * 512:(nb + 1) * 512])

                def comp(slot, tls, nb=nb):
                    v = wsl[slot][:, 0:4096].rearrange("p (k n) -> p k n", k=8)
                    for (t, c0, w) in tls:
                        for j in range(4):
                            n = nb * 4 + j
                            b = bank()
                            for k in range(8):
                                mm(B[b][:, 0:w], v[:, k, j * 128:(j + 1) * 128], ab[:, k, c0:c0 + w], k == 0, k == 7,
                                   [("wsl", slot), ("ab", k, t)], [("B", b)])
                            sc.add("dve", lambda e, n=n, c0=c0, w=w, b=b: e.scalar_tensor_tensor(
                                out=hres[:, n, c0:c0 + w], in0=hres[:, n, c0:c0 + w], scalar=ALPHA, in1=B[b][:, 0:w],
                                op0=ALU.mult, op1=ALU.add),
                                reads=[("B", b), ("hres", n, t)], writes=[("hres", n, t)])
                            hook()
                jobs.append({"load": load, "comp": comp, "tl": tl, "spt": 4, "loaded": False})

        def ln_pieces(q, tile):
            (t, c0, w) = tile
            pcs = []
            shared = {}

            def prep(k):
                sc.add("act", lambda e: e.activation(out=zsq[:, k, 0:w], in_=hres[:, k, c0:c0 + w], func=AF.Square),
                       reads=[("hres", k, t)], writes=[("zsq", k)])
                sc.add("dve", lambda e: e.tensor_copy(out=zb[:, k, 0:w], in_=hres[:, k, c0:c0 + w]),
                       reads=[("hres", k, t)], writes=[("zb", k)])

            def stats():
                ba = bank()
                for k in range(8):
                    mm(B[ba][:, 0:w], ones[:, :], zb[:, k, 0:w], k == 0, k == 7, ["ones", ("zb", k)], [("B", ba)])
                bb = bank()
                for k in range(8):
                    mm(B[bb][:, 0:w], ones[:, :], zsq[:, k, 0:w], k == 0, k == 7, ["ones", ("zsq", k)], [("B", bb)])
                sc.add("dve", lambda e: e.tensor_scalar(out=mean[:, 0:w], in0=B[ba][:, 0:w], scalar1=1.0 / D,
                                                        scalar2=None, op0=ALU.mult),
                       reads=[("B", ba)], writes=["mean"])
                sc.add("dve", lambda e: e.tensor_tensor(out=msq[:, 0:w], in0=mean[:, 0:w], in1=mean[:, 0:w],
                                                        op=ALU.mult), reads=["mean"], writes=["msq"])
                sc.add("dve", lambda e: e.scalar_tensor_tensor(
                    out=var[:, 0:w], in0=B[bb][:, 0:w], scalar=1.0 / D, in1=msq[:, 0:w], op0=ALU.mult,
                    op1=ALU.subtract), reads=[("B", bb), "msq"], writes=["var"])
                sc.add("dve", lambda e: e.tensor_scalar(out=var[:, 0:w], in0=var[:, 0:w], scalar1=LN_EPS,
                                                        scalar2=None, op0=ALU.add), reads=["var"], writes=["var"])
                sc.add("act", lambda e: e.activation(out=var[:, 0:w], in_=var[:, 0:w], func=AF.Ln),
                       reads=["var"], writes=["var"])
                br = 7
                shared["br"] = br
                sc.add("act", lambda e: e.activation(out=B[br][:, 0:w], in_=var[:, 0:w], func=AF.Exp, scale=-0.5),
                       reads=["var"], writes=[("B", br)])
                sc.add("dve", lambda e: e.scalar_tensor_tensor(out=B[6][:, 0:w], in0=mean[:, 0:w], scalar=-1.0,
                                                               in1=B[br][:, 0:w], op0=ALU.mult, op1=ALU.mult),
                       reads=["mean", ("B", br)], writes=[("B", 6)])

            def norm(k):
                br = shared["br"]
                i = flip()
                sc.add("dve", lambda e: e.tensor_tensor(out=tmp[i][:, 0:w], in0=hres[:, k, c0:c0 + w], in1=B[br][:, 0:w],
                                                        op=ALU.mult),
                       reads=[("hres", k, t), ("B", br)], writes=[("tmp", i)])
                sc.add("dve", lambda e: e.tensor_tensor(out=tmp2[i][:, 0:w], in0=tmp[i][:, 0:w], in1=B[6][:, 0:w],
                                                        op=ALU.add),
                       reads=[("tmp", i), ("B", 6)], writes=[("tmp2", i)])
                gcol = q * 16 + k
                bcol = q * 16 + 8 + k
                sc.add("act", lambda e: e.activation(out=hres[:, k, c0:c0 + w], in_=tmp2[i][:, 0:w], func=AF.Identity,
                                                     scale=lnt[:, gcol:gcol + 1], bias=lnt[:, bcol:bcol + 1]),
                       reads=[("tmp2", i), "lnt"], writes=[("hres", k, t)])
                if True:
                    sc.add("act", lambda e: e.activation(out=hb[:, k, c0:c0 + w], in_=tmp2[i][:, 0:w], func=AF.Identity,
                                                         scale=lnt[:, gcol:gcol + 1], bias=lnt[:, bcol:bcol + 1]),
                           reads=[("tmp2", i), "lnt"], writes=[("hb", k, t)])

            for k in range(8):
                pcs.append(lambda k=k: prep(k))
            pcs.append(stats)
            for k in range(8):
                pcs.append(lambda k=k: norm(k))
            return pcs

        def add_ffn_jobs(layer, tl):
            f0 = 0
            while f0 < NF:
                fb = min(3, NF - f0)

                def load(slot, f0=f0, fb=fb):
                    v = wsl[slot][:, 0:2 * 8 * 384].rearrange("p (g k n) -> p g k n", g=2, k=8)
                    wload(slot, v[:, 0, :, 0:fb * 128], wg_r[layer][:, :, f0 * 128:(f0 + fb) * 128])
                    wload(slot, v[:, 1, :, 0:fb * 128], wu_r[layer][:, :, f0 * 128:(f0 + fb) * 128])

                def comp(slot, tls, f0=f0, fb=fb):
                    v = wsl[slot][:, 0:2 * 8 * 384].rearrange("p (g k n) -> p g k n", g=2, k=8)
                    for (t, c0, w) in tls:
                        for j in range(fb):
                            fc = f0 + j
                            bg = bank()
                            for k in range(8):
                                mm(B[bg][:, 0:w], v[:, 0, k, j * 128:(j + 1) * 128], hb[:, k, c0:c0 + w], k == 0, k == 7,
                                   [("wsl", slot), ("hb", k, t)], [("B", bg)])
                            bu = bank()
                            for k in range(8):
                                mm(B[bu][:, 0:w], v[:, 1, k, j * 128:(j + 1) * 128], hb[:, k, c0:c0 + w], k == 0, k == 7,
                                   [("wsl", slot), ("hb", k, t)], [("B", bu)])
                            i = flip()
                            sc.add("act", lambda e, i=i, bg=bg, w=w: e.activation(out=sg[i][:, 0:w], in_=B[bg][:, 0:w],
                                                                                  func=AF.Silu),
                                   reads=[("B", bg)], writes=[("sg", i)])
                            sc.add("dve", lambda e, i=i, bu=bu, fc=fc, c0=c0, w=w: e.tensor_tensor(
                                out=hid[:, fc, c0:c0 + w], in0=B[bu][:, 0:w], in1=sg[i][:, 0:w], op=ALU.mult),
                                reads=[("B", bu), ("sg", i)], writes=[("hid", fc, t)])
                            hook()
                jobs.append({"load": load, "comp": comp, "tl": tl, "spt": fb, "loaded": False})
                f0 += fb
            for nb in range(4):
                def load(slot, nb=nb):
                    v = wsl[slot][:, 0:NF * 256].rearrange("p (k n) -> p k n", k=NF)
                    wload(slot, v, wd_r[layer][:, :, nb * 256:(nb + 1) * 256])

                def comp(slot, tls, nb=nb):
                    v = wsl[slot][:, 0:NF * 256].rearrange("p (k n) -> p k n", k=NF)
                    for (t, c0, w) in tls:
                        for j in range(2):
                            n = nb * 2 + j
                            b = bank()
                            for fc in range(NF):
                                mm(B[b][:, 0:w], v[:, fc, j * 128:(j + 1) * 128], hid[:, fc, c0:c0 + w], fc == 0,
                                   fc == NF - 1, [("wsl", slot), ("hid", fc, t)], [("B", b)])
                            sc.add("dve", lambda e, n=n, c0=c0, w=w, b=b: e.scalar_tensor_tensor(
                                out=hres[:, n, c0:c0 + w], in0=hres[:, n, c0:c0 + w], scalar=ALPHA, in1=B[b][:, 0:w],
                                op0=ALU.mult, op1=ALU.add),
                                reads=[("B", b), ("hres", n, t)], writes=[("hres", n, t)])
                            hook()
                jobs.append({"load": load, "comp": comp, "tl": tl, "spt": 2, "loaded": False})

        def add_conv_jobs(g):
            for jb in range(4):
                def load(slot, jb=jb):
                    v = wsl[slot][:, 0:WCAP].rearrange("p (k g n) -> p k g n", k=8, g=3)
                    for gg in range(3):
                        wload(slot, v[:, :, gg, :], w_in_r[:, :, gg, jb * 256:(jb + 1) * 256])

                def comp(slot, tls, jb=jb):
                    v = wsl[slot][:, 0:WCAP].rearrange("p (k g n) -> p k g n", k=8, g=3)
                    for jj in range(2):
                        ch = jb * 2 + jj
                        cols = slice(jj * 128, (jj + 1) * 128)
                        if g == 0 and any(x[0] == 0 for x in tls):
                            bc = bank()
                            for k in range(8):
                                mm(B[bc][:, 0:2], v[:, k, 1, cols], hb[:, k, TG:TG + 2], k == 0, k == 7,
                                   [("wsl", slot), ("hb", k, 2)], [("B", bc)])
                            bx = bank()
                            for k in range(8):
                                mm(B[bx][:, 0:2], v[:, k, 2, cols], hb[:, k, TG:TG + 2], k == 0, k == 7,
                                   [("wsl", slot), ("hb", k, 2)], [("B", bx)])
                            sc.add("act", lambda e, bc=bc: e.copy(out=hcs[:], in_=B[bc][:, 0:2]),
                                   reads=[("B", bc)], writes=["hcs"])
                            sc.add("dve", lambda e, bx=bx, ch=ch: e.scalar_tensor_tensor(
                                out=uprev[:, ch, :], in0=B[bx][:, 0:2], scalar=ufl[:, 0:1], in1=hcs[:],
                                op0=ALU.mult, op1=ALU.mult),
                                reads=[("B", bx), "hcs", "ufl"], writes=[("uprev", ch)])
                        for (t, c0, w) in tls:
                            bbk = bank()
                            for k in range(8):
                                mm(B[bbk][:, :], v[:, k, 0, cols], hb[:, k, c0:c0 + w], k == 0, k == 7,
                                   [("wsl", slot), ("hb", k, t)], [("B", bbk)])
                            bc = bank()
                            for k in range(8):
                                mm(B[bc][:, :], v[:, k, 1, cols], hb[:, k, c0:c0 + w], k == 0, k == 7,
                                   [("wsl", slot), ("hb", k, t)], [("B", bc)])
                            bx = bank()
                            for k in range(8):
                                mm(B[bx][:, :], v[:, k, 2, cols], hb[:, k, c0:c0 + w], k == 0, k == 7,
                                   [("wsl", slot), ("hb", k, t)], [("B", bx)])
                            i = flip()
                            u = ubuf[i]
                            y = ybuf[i]
                            c3 = ch * 3
                            sc.add("act", lambda e, i=i, bc=bc: e.copy(out=cs[i][:], in_=B[bc][:, :]),
                                   reads=[("B", bc)], writes=[("cs", i)])
                            sc.add("act", lambda e, u=u, ch=ch: e.copy(out=u[:, 0:2], in_=uprev[:, ch, :]),
                                   reads=[("uprev", ch)], writes=[("u", i)])
                            sc.add("dve", lambda e, u=u, i=i, bx=bx: e.tensor_tensor(out=u[:, 2:TT + 2], in0=B[bx][:, :],
                                                                                     in1=cs[i][:], op=ALU.mult),
                                   reads=[("B", bx), ("cs", i)], writes=[("u", i)])
                            sc.add("act", lambda e, u=u, ch=ch: e.copy(out=uprev[:, ch, :], in_=u[:, TT:TT + 2]),
                                   reads=[("u", i)], writes=[("uprev", ch)])
                            sc.add("dve", lambda e, u=u, y=y, c3=c3: e.tensor_scalar(
                                out=y[:], in0=u[:, 2:TT + 2], scalar1=cwt[:, c3 + 2:c3 + 3], scalar2=None, op0=ALU.mult),
                                reads=[("u", i), "cwt"], writes=[("y", i)])
                            sc.add("dve", lambda e, u=u, y=y, c3=c3: e.scalar_tensor_tensor(
                                out=y[:], in0=u[:, 1:TT + 1], scalar=cwt[:, c3 + 1:c3 + 2], in1=y[:],
                                op0=ALU.mult, op1=ALU.add), reads=[("u", i), "cwt", ("y", i)], writes=[("y", i)])
                            sc.add("dve", lambda e, u=u, y=y, c3=c3: e.scalar_tensor_tensor(
                                out=y[:], in0=u[:, 0:TT], scalar=cwt[:, c3:c3 + 1], in1=y[:],
                                op0=ALU.mult, op1=ALU.add), reads=[("u", i), "cwt", ("y", i)], writes=[("y", i)])
                            sc.add("dve", lambda e, y=y, bbk=bbk, ch=ch, c0=c0, w=w: e.tensor_tensor(
                                out=ab[:, ch, c0:c0 + w], in0=B[bbk][:, :], in1=y[:], op=ALU.mult),
                                reads=[("B", bbk), ("y", i)], writes=[("ab", ch, t)])
                            hook()
                jobs.append({"load": load, "comp": comp, "tl": tiles(g, 1), "spt": 2, "loaded": False})

        def load_tile(g, tile):
            (t, c0, w) = tile
            g0 = g * TG
            if t < 2:
                sc.add("sp", lambda e: e.dma_start(out=hres[:, :, c0:c0 + w], in_=hin_r[:, :, g0 + c0:g0 + c0 + w]),
                       writes=[("hres", k, t) for k in range(8)], dma=True)
                sc.add("pool", lambda e: e.dma_start(out=ab[:, :, c0:c0 + w], in_=ain_r[:, :, g0 + c0:g0 + c0 + w]),
                       writes=[("ab", k, t) for k in range(8)], dma=True)
            else:
                sc.add("sp", lambda e: e.dma_start(out=hres[:, :, c0:c0 + w], in_=xhalo_r),
                       writes=[("hres", k, t) for k in range(8)], dma=True)
                sc.add("pool", lambda e: e.dma_start(out=ab[:, :, c0:c0 + w], in_=ahalo_r),
                       writes=[("ab", k, t) for k in range(8)], dma=True)

        def store_tile(g, tile):
            (t, c0, w) = tile
            g0 = g * TG
            sc.add("sp", lambda e: e.dma_start(out=hout_r[:, :, g0 + c0:g0 + c0 + w], in_=hres[:, :, c0:c0 + w]),
                   reads=[("hres", k, t) for k in range(8)], dma=True)

        pre = []
        for g in range(ngroups):
            tl0 = tiles(g, 0)
            tl1 = tiles(g, 1)
            f0_, s0_ = split(tl0)
            if g == 0:
                f0_, s0_ = tl0, []
            for tile in f0_:
                load_tile(g, tile)

            def after_first(g=g, s0_=s0_):
                if g > 0:
                    for tile in split(tiles(g - 1, 1))[1]:
                        store_tile(g - 1, tile)
                for tile in s0_:
                    load_tile(g, tile)

            base = len(jobs)
            add_proj_jobs(w_o_r, tl0)
            n1 = len(jobs)
            add_ffn_jobs(0, tl0)
            n2 = len(jobs)
            add_conv_jobs(g)
            add_proj_jobs(w_out_r, tl1)
            n3 = len(jobs)
            add_ffn_jobs(1, tl1)
            n4 = len(jobs)
            pre = run_segment(base, n1, 0, tl0, pre, after_first)
            pre = run_segment(n1, n2, 1, tl0, pre)
            pre = run_segment(n2, n3, 2, tl1, pre)
            pre = run_segment(n3, n4, 3, tl1, pre)
            for tile in split(tl1)[0]:
                store_tile(g, tile)
        enqueue(pre, 1)
        flush()
        for tile in split(tiles(ngroups - 1, 1))[1]:
            store_tile(ngroups - 1, tile)
        sc.emit()
    return nc


def run_T(xT, attnT, inp):
    f = lambda a: np.ascontiguousarray(np.asarray(a, np.float32))
    lnp = np.ascontiguousarray(np.concatenate(
        [_lnp(inp["ln1_g"], inp["ln1_b"], inp["ln2_g"], inp["ln2_b"], i) for i in range(2)], axis=1))
    cw = np.ascontiguousarray(f(inp["conv_w"])[0].reshape(3, 8, 128).transpose(2, 1, 0).reshape(128, 24))
    shared = {"w_o": f(inp["attn_w_o"])[0], "w_in": f(inp["conv_w_in"])[0], "cw": cw, "w_out": f(inp["conv_w_out"])[0],
              "wg0": f(inp["ffn_w_gate"])[0], "wu0": f(inp["ffn_w_up"])[0], "wd0": f(inp["ffn_w_down"])[0],
              "wg1": f(inp["ffn_w_gate"])[1], "wu1": f(inp["ffn_w_up"])[1], "wd1": f(inp["ffn_w_down"])[1], "lnp": lnp}
    in_maps = []
    for c in range(NCORES):
        sl = slice(c * TOK, (c + 1) * TOK)
        if c == 0:
            xh = np.zeros((D, 2), np.float32)
            ah = np.zeros((D, 2), np.float32)
        else:
            xh = np.ascontiguousarray(xT[:, c * TOK - 2:c * TOK])
            ah = np.ascontiguousarray(attnT[:, c * TOK - 2:c * TOK])
        m = {"hin": np.ascontiguousarray(xT[:, sl]), "ain": np.ascontiguousarray(attnT[:, sl]), "xhalo": xh,
             "ahalo": ah, "uflag": np.full((128, 1), 0.0 if c == 0 else 1.0, np.float32)}
        m.update(shared)
        in_maps.append(m)
    nc = build_T()
    r = run_bass_kernel_spmd(nc, in_maps, core_ids=list(range(NCORES)))
    return np.concatenate([r.results[c]["hout"] for c in range(NCORES)], axis=1)


def kernel(**inp):
    f = lambda a: np.ascontiguousarray(np.asarray(a, np.float32))
    x = f(inp["x"])
    attnT = run_A(x, f(inp["attn_w_qkv"]), f(inp["attn_lambda_q1"]), f(inp["attn_lambda_k1"]),
                  f(inp["attn_lambda_q2"]), f(inp["attn_lambda_k2"]), f(inp["attn_subln_g"]))
    if os.environ.get("K_PROBE_ONLY_A"):
        return np.zeros((1, S, D), np.float32)
    xT = np.ascontiguousarray(x.reshape(S, D).T)
    h2T = run_T(xT, attnT, inp)
    return np.ascontiguousarray(h2T.T).reshape(1, S, D).astype(np.float32)
```

```python
import math
from contextlib import ExitStack

import numpy as np
import concourse.bass as bass
import concourse.mybir as mybir
from concourse.bass_utils import run_bass_kernel_spmd

F32 = mybir.dt.float32
BF16 = mybir.dt.bfloat16
AF = mybir.ActivationFunctionType
ALU = mybir.AluOpType
AX = mybir.AxisListType

D = 1024
S = 16384
NCORES = 8
DFF = 2816
ALPHA = (2 * 2) ** 0.25
LN_EPS = 1e-5
ROPE_THETA = 500000.0
LAM_INIT0 = 0.8 - 0.6 * math.exp(-0.3 * 0)


class _Op:
    __slots__ = ("eng", "fn", "deps", "dma", "sig", "sem", "val", "prev")


class Sched:
    ENGS = ("sp", "pe", "act", "dve", "pool")
    NDS = 12

    def __init__(self, nc, stack):
        self.nc = nc
        self.stack = stack
        self.ops = []
        self.last_w = {}
        self.readers = {}

    def add(self, eng, fn, reads=(), writes=(), dma=False):
        op = _Op()
        op.eng, op.fn, op.dma, op.sig = eng, fn, dma, False
        op.sem = None
        op.val = 0
        op.prev = 0
        deps = []
        seen = set()

        def dep(o):
            if o is not None and id(o) not in seen:
                seen.add(id(o))
                deps.append(o)

        for r in reads:
            dep(self.last_w.get(r))
            if isinstance(r, tuple) and r[0] == "B":
                for rd in self.readers.get(r, ()):
                    if rd.eng != eng:
                        dep(rd)
        for w in writes:
            dep(self.last_w.get(w))
            for rd in self.readers.get(w, ()):
                dep(rd)
        op.deps = deps
        for r in reads:
            self.readers.setdefault(r, []).append(op)
        for w in writes:
            self.last_w[w] = op
            self.readers[w] = []
        self.ops.append(op)
        return op

    def emit(self):
        nc = self.nc
        st = self.stack
        esem = {e: st.enter_context(nc.semaphore("e_" + e)) for e in ("pe", "act", "dve", "pool")}
        dsem = {e: [st.enter_context(nc.semaphore("d_%s%d" % (e, i))) for i in range(self.NDS)]
                for e in ("sp", "act", "pool")}
        for op in self.ops:
            for d in op.deps:
                if d.dma:
                    continue
                if d.eng == "pe" and op.eng == "pe" and not op.dma:
                    continue
                d.sig = True
        cnt = {e: 0 for e in self.ENGS}
        dcnt = {e: 0 for e in self.ENGS}
        for op in self.ops:
            if op.dma:
                n = dcnt[op.eng]
                dcnt[op.eng] += 1
                op.sem = dsem[op.eng][n % self.NDS]
                op.val = 16 * (n // self.NDS + 1)
                op.prev = 16 * (n // self.NDS)
            elif op.sig:
                cnt[op.eng] += 1
                op.sem = esem[op.eng]
                op.val = cnt[op.eng]
        finals = []
        for e in dsem:
            n = dcnt[e]
            for i in range(min(n, self.NDS)):
                tot = (n - 1 - i) // self.NDS + 1
                finals.append((dsem[e][i], 16 * tot))

        block = st.enter_context(nc.Block())
        decos = {"sp": block.sync, "pe": block.tensor, "act": block.scalar,
                 "dve": block.vector, "pool": block.gpsimd}
        for ename in self.ENGS:
            ops_e = [op for op in self.ops if op.eng == ename]

            def body(eng, ops_e=ops_e, ename=ename):
                waited = {}

                def wait(sem, val):
                    if val > 0 and waited.get(id(sem), 0) < val:
                        eng.wait_ge(sem, val)
                        waited[id(sem)] = val

                for op in ops_e:
                    need = {}
                    for d in op.deps:
                        if (not d.dma) and d.eng == "pe" and ename == "pe" and not op.dma:
                            continue
                        k = id(d.sem)
                        if k not in need or need[k][1] < d.val:
                            need[k] = (d.sem, d.val)
                    if op.dma:
                        wait(op.sem, op.prev)
                    for sem, val in need.values():
                        wait(sem, val)
                    ins = op.fn(eng)
                    if op.dma:
                        ins.then_inc(op.sem, 16)
                    elif op.sig:
                        ins.then_inc(op.sem, 1)
                if ename == "sp":
                    for sem, val in finals:
                        wait(sem, val)

            decos[ename](body)


def _sb(nc, st, name, shape, dt):
    return st.enter_context(nc.sbuf_tensor(name, list(shape), dt))


def _ps(nc, st, name):
    return st.enter_context(nc.psum_tensor(name, [128, 512], F32))


def build_A(nt=S // 512):
    TT = 512
    SS = nt * TT
    nc = bass.Bass("TRN2", target_bir_lowering=False)
    xT = nc.dram_tensor("xT", [D, SS], F32, kind="ExternalInput").ap()
    wqkv = nc.dram_tensor("wqkv", [D, 384], F32, kind="ExternalInput").ap()
    ropeC = nc.dram_tensor("ropeC", [128, SS], F32, kind="ExternalInput").ap()
    ropeS = nc.dram_tensor("ropeS", [128, SS], F32, kind="ExternalInput").ap()
    perm = nc.dram_tensor("perm", [128, 128], F32, kind="ExternalInput").ap()
    identd = nc.dram_tensor("ident", [128, 128], F32, kind="ExternalInput").ap()
    lam4 = nc.dram_tensor("lam4", [1, 256], F32, kind="ExternalInput").ap()
    subg = nc.dram_tensor("subg", [128, 1], F32, kind="ExternalInput").ap()
    out = nc.dram_tensor("attnT", [128, SS], F32, kind="ExternalOutput").ap()
    xTr = xT.rearrange("(k p) t -> p k t", p=128)
    wr = wqkv.rearrange("(k p) n -> p k n", p=128)

    with ExitStack() as st:
        sc = Sched(nc, st)
        qT = _sb(nc, st, "qT", [128, SS], BF16)
        kT = _sb(nc, st, "kT", [128, SS], BF16)
        vS = _sb(nc, st, "vS", [128, nt * 4, 128], BF16)
        xb = [_sb(nc, st, "xb%d" % i, [128, 8, TT], BF16) for i in range(2)]
        wsb = _sb(nc, st, "wsb", [128, 8, 384], BF16)
        pm = _sb(nc, st, "pm", [128, 128], BF16)
        ones = _sb(nc, st, "ones", [128, 128], BF16)
        ones32 = _sb(nc, st, "ones32", [128, 128], F32)
        zs = _sb(nc, st, "zs", [128, TT], F32)
        rc = [_sb(nc, st, "rc%d" % i, [128, TT], F32) for i in range(2)]
        rs = [_sb(nc, st, "rs%d" % i, [128, TT], F32) for i in range(2)]
        qb = _sb(nc, st, "qb", [128, TT], BF16)
        kb_ = _sb(nc, st, "kb_", [128, TT], BF16)
        qs_ = _sb(nc, st, "qs_", [128, TT], BF16)
        ks_ = _sb(nc, st, "ks_", [128, TT], BF16)
        idn = _sb(nc, st, "idn", [128, 128], BF16)
        pT = [[_sb(nc, st, "pT%d%d" % (c, s), [128, TT], BF16) for s in range(2)] for c in range(2)]
        rz = [_sb(nc, st, "rz%d" % i, [128, TT], F32) for i in range(2)]
        oo = [_sb(nc, st, "oo%d" % i, [128, TT], F32) for i in range(2)]
        od = _sb(nc, st, "od", [128, TT], F32)
        osq = _sb(nc, st, "osq", [128, TT], BF16)
        lnv = _sb(nc, st, "lnv", [128, TT], F32)
        rstd = _sb(nc, st, "rstd", [128, TT], F32)
        res = [_sb(nc, st, "res%d" % i, [128, TT], F32) for i in range(2)]
        lamt = _sb(nc, st, "lamt", [128, 256], F32)
        lprod = _sb(nc, st, "lprod", [128, 128], F32)
        lsum = _sb(nc, st, "lsum", [128, 2], F32)
        lexp = _sb(nc, st, "lexp", [128, 2], F32)
        neglam = _sb(nc, st, "neglam", [128, 1], F32)
        gsc = _sb(nc, st, "gsc", [128, 1], F32)
        gld = _sb(nc, st, "gld", [128, 1], F32)
        B = [_ps(nc, st, "B%d" % i) for i in range(8)]

        sc.add("pool", lambda e: e.dma_start(out=wsb[:], in_=wr), writes=["wsb"], dma=True)
        sc.add("pool", lambda e: e.dma_start(out=pm[:], in_=perm), writes=["pm"], dma=True)
        sc.add("pool", lambda e: e.dma_start(out=idn[:], in_=identd), writes=["idn"], dma=True)
        sc.add("sp", lambda e: e.dma_start(out=lamt[:], in_=lam4.partition_broadcast(128)),
               writes=["lamt"], dma=True)
        sc.add("sp", lambda e: e.dma_start(out=gld[:], in_=subg), writes=["gld"], dma=True)
        sc.add("pool", lambda e: e.memset(ones[:], 1.0), writes=["ones"])
        sc.add("pool", lambda e: e.memset(ones32[:], 1.0), writes=["ones32"])
        sc.add("dve", lambda e: e.tensor_tensor(out=lprod[:, 0:64], in0=lamt[:, 0:64], in1=lamt[:, 64:128],
                                                op=ALU.mult), reads=["lamt"], writes=["lprod"])
        sc.add("dve", lambda e: e.tensor_tensor(out=lprod[:, 64:128], in0=lamt[:, 128:192], in1=lamt[:, 192:256],
                                                op=ALU.mult), reads=["lamt"], writes=["lprod"])
        sc.add("dve", lambda e: e.reduce_sum(out=lsum[:, 0:1], in_=lprod[:, 0:64], axis=AX.X),
               reads=["lprod"], writes=["lsum"])
        sc.add("dve", lambda e: e.reduce_sum(out=lsum[:, 1:2], in_=lprod[:, 64:128], axis=AX.X),
               reads=["lprod"], writes=["lsum"])
        sc.add("act", lambda e: e.activation(out=lexp[:], in_=lsum[:], func=AF.Exp),
               reads=["lsum"], writes=["lexp"])
        sc.add("dve", lambda e: e.tensor_tensor(out=neglam[:], in0=lexp[:, 1:2], in1=lexp[:, 0:1],
                                                op=ALU.subtract), reads=["lexp"], writes=["neglam"])
        sc.add("dve", lambda e: e.tensor_scalar(out=neglam[:], in0=neglam[:], scalar1=-LAM_INIT0, scalar2=None,
                                                op0=ALU.add), reads=["neglam"], writes=["neglam"])
        sc.add("dve", lambda e: e.tensor_scalar(out=gsc[:], in0=gld[:], scalar1=(1.0 - LAM_INIT0), scalar2=None,
                                                op0=ALU.mult), reads=["gld"], writes=["gsc"])

        def load_x(tt):
            s = tt % 2
            c0 = tt * TT
            sc.add("pool", lambda e: e.dma_start(out=xb[s][:], in_=xTr[:, :, c0:c0 + TT]),
                   writes=[("xb", s)], dma=True)
            sc.add("sp", lambda e: e.dma_start(out=rc[s][:], in_=ropeC[:, c0:c0 + TT]),
                   writes=[("rc", s)], dma=True)
            sc.add("sp", lambda e: e.dma_start(out=rs[s][:], in_=ropeS[:, c0:c0 + TT]),
                   writes=[("rs", s)], dma=True)

        def mm(outp, lhsT, rhs, start, stop, reads, writes):
            sc.add("pe", lambda e: e.matmul(outp, lhsT, rhs, start=start, stop=stop),
                   reads=reads, writes=writes)

        def qkv_mm(tt):
            s = tt % 2
            c0 = tt * TT
            xs = xb[s]
            for k in range(8):
                mm(B[0][:, :], wsb[:, k, 0:128], xs[:, k, :], k == 0, k == 7, ["wsb", ("xb", s)], [("B", 0)])
            for k in range(8):
                mm(B[1][:, :], wsb[:, k, 128:256], xs[:, k, :], k == 0, k == 7, ["wsb", ("xb", s)], [("B", 1)])
            for sub in range(4):
                for k in range(8):
                    mm(B[2][:, sub * 128:(sub + 1) * 128], xs[:, k, sub * 128:(sub + 1) * 128],
                       wsb[:, k, 256:384], k == 0, k == 7, ["wsb", ("xb", s)], [("B", 2)])

        def qkv_post(tt):
            s = tt % 2
            c0 = tt * TT
            sc.add("dve", lambda e: e.tensor_tensor(out=qb[:], in0=B[0][:, :], in1=rc[s][:], op=ALU.mult),
                   reads=[("B", 0), ("rc", s)], writes=["qb"])
            sc.add("dve", lambda e: e.tensor_tensor(out=qs_[:], in0=B[0][:, :], in1=rs[s][:], op=ALU.mult),
                   reads=[("B", 0), ("rs", s)], writes=["qs_"])
            sc.add("dve", lambda e: e.tensor_tensor(out=kb_[:], in0=B[1][:, :], in1=rc[s][:], op=ALU.mult),
                   reads=[("B", 1), ("rc", s)], writes=["kb_"])
            sc.add("dve", lambda e: e.tensor_tensor(out=ks_[:], in0=B[1][:, :], in1=rs[s][:], op=ALU.mult),
                   reads=[("B", 1), ("rs", s)], writes=["ks_"])
            sc.add("act", lambda e: e.copy(out=vS[:, tt * 4:(tt + 1) * 4, :],
                                           in_=B[2][:, :].rearrange("p (a b) -> p a b", a=4)),
                   reads=[("B", 2)], writes=[("vS", tt)])
            mm(B[3][:, :], idn[:, :], qb[:, :], True, False, ["idn", "qb"], [("B", 3)])
            mm(B[3][:, :], pm[:, :], qs_[:, :], False, True, ["pm", "qs_"], [("B", 3)])
            sc.add("act", lambda e: e.copy(out=qT[:, c0:c0 + TT], in_=B[3][:, :]), reads=[("B", 3)], writes=[("qT", tt)])
            mm(B[2][:, :], idn[:, :], kb_[:, :], True, False, ["idn", "kb_"], [("B", 2)])
            mm(B[2][:, :], pm[:, :], ks_[:, :], False, True, ["pm", "ks_"], [("B", 2)])
            sc.add("act", lambda e: e.copy(out=kT[:, c0:c0 + TT], in_=B[2][:, :]), reads=[("B", 2)], writes=[("kT", tt)])

        def scores(qt, kb, slot):
            j = kb - 4 * qt
            f0 = 128 * j if j > 0 else 0
            q0 = qt * TT
            for c in range(2):
                bi = c * 2 + slot
                mm(B[bi][:, f0:TT], kT[c * 64:(c + 1) * 64, kb * 128:(kb + 1) * 128],
                   qT[c * 64:(c + 1) * 64, q0 + f0:q0 + TT], True, True,
                   [("kT", kb // 4), ("qT", qt)], [("B", bi)])

        def expo(qt, kb, slot):
            j = kb - 4 * qt
            f0 = 128 * j if j > 0 else 0
            for c in range(2):
                bi = c * 2 + slot
                sc.add("act", lambda e, c=c, bi=bi: e.activation(out=pT[c][slot][:, f0:TT], in_=B[bi][:, f0:TT],
                                                                 func=AF.Exp, scale=0.125),
                       reads=[("B", bi)], writes=[("pT", c, slot)])
                if j >= 0:
                    sc.add("pool", lambda e, c=c: e.memset(pT[c][slot][64:128, f0:f0 + 64], 0.0),
                           writes=[("pT", c, slot)])

        def pv(qt, kb, slot, last):
            j = kb - 4 * qt
            f0 = 128 * j if j > 0 else 0
            for c in range(2):
                mm(B[4 + c][:, f0:TT], vS[:, kb, :], pT[c][slot][:, f0:TT], kb == 0, kb == last,
                   [("vS", kb // 4), ("pT", c, slot)], [("B", 4 + c)])
                if c == 0:
                    mm(B[6][:, f0:TT], ones[:, :], pT[0][slot][:, f0:TT], kb == 0, kb == last,
                       ["ones", ("pT", 0, slot)], [("B", 6)])
                elif kb == 0:
                    sc.add("dve", lambda e: e.tensor_copy(out=B[7][:, :], in_=pT[1][slot][:, :]),
                           reads=[("pT", 1, slot)], writes=[("B", 7)])
                else:
                    sc.add("dve", lambda e: e.tensor_tensor(out=B[7][:, f0:TT], in0=B[7][:, f0:TT],
                                                            in1=pT[1][slot][:, f0:TT], op=ALU.add),
                           reads=[("pT", 1, slot), ("B", 7)], writes=[("B", 7)])

        def epilogue_a(qt):
            sc.add("dve", lambda e: e.tensor_copy(out=zs[:], in_=B[7][:, :]), reads=[("B", 7)], writes=["zs"])
            mm(B[7][:, :], ones32[:, :], zs[:, :], True, True, ["ones32", "zs"], [("B", 7)])
            for c in range(2):
                sc.add("act", lambda e, c=c: e.activation(out=rz[c][:], in_=B[6 + c][:, :], func=AF.Ln),
                       reads=[("B", 6 + c)], writes=[("rz", c)])
                sc.add("act", lambda e, c=c: e.activation(out=rz[c][:], in_=rz[c][:], func=AF.Exp, scale=-1.0),
                       reads=[("rz", c)], writes=[("rz", c)])

        def epilogue_b(qt):
            for c in range(2):
                sc.add("dve", lambda e, c=c: e.tensor_tensor(out=oo[c][:], in0=B[4 + c][:, :], in1=rz[c][:],
                                                             op=ALU.mult),
                       reads=[("B", 4 + c), ("rz", c)], writes=[("oo", c)])
            sc.add("dve", lambda e: e.scalar_tensor_tensor(out=od[:], in0=oo[1][:], scalar=neglam[:, 0:1],
                                                           in1=oo[0][:], op0=ALU.mult, op1=ALU.add),
                   reads=[("oo", 0), ("oo", 1), "neglam"], writes=["od"])
            sc.add("act", lambda e: e.activation(out=osq[:], in_=od[:], func=AF.Square),
                   reads=["od"], writes=["osq"])

        def epilogue_post(qt):
            q0 = qt * TT
            r = res[qt % 2]
            mm(B[3][:, :], ones[:, :], osq[:, :], True, True, ["ones", "osq"], [("B", 3)])
            sc.add("dve", lambda e: e.tensor_scalar(out=lnv[:], in0=B[3][:, :], scalar1=1.0 / 128.0,
                                                    scalar2=LN_EPS, op0=ALU.mult, op1=ALU.add),
                   reads=[("B", 3)], writes=["lnv"])
            sc.add("act", lambda e: e.activation(out=lnv[:], in_=lnv[:], func=AF.Ln),
                   reads=["lnv"], writes=["lnv"])
            sc.add("act", lambda e: e.activation(out=rstd[:], in_=lnv[:], func=AF.Exp, scale=-0.5),
                   reads=["lnv"], writes=["rstd"])
            sc.add("dve", lambda e: e.scalar_tensor_tensor(out=r[:], in0=od[:], scalar=gsc[:, 0:1], in1=rstd[:],
                                                           op0=ALU.mult, op1=ALU.mult),
                   reads=["od", "rstd", "gsc"], writes=[("res", qt % 2)])
            sc.add("sp", lambda e: e.dma_start(out=out[:, q0:q0 + TT], in_=r[:]),
                   reads=[("res", qt % 2)], dma=True)

        load_x(0)
        qkv_mm(0)
        qkv_post(0)
        for tt in range(nt):
            if tt + 1 < nt:
                load_x(tt + 1)
            nkb = 4 * tt + 4
            last = nkb - 1
            scores(tt, 0, 0)
            for kb in range(nkb):
                if kb + 1 < nkb:
                    scores(tt, kb + 1, (kb + 1) % 2)
                expo(tt, kb, kb % 2)
                pv(tt, kb, kb % 2, last)
            epilogue_a(tt)
            if tt + 1 < nt:
                qkv_mm(tt + 1)
            epilogue_b(tt)
            if tt + 1 < nt:
                qkv_post(tt + 1)
            epilogue_post(tt)
        sc.emit()
    return nc


def rope_tables(n):
    half = 8
    pos = np.arange(n, dtype=np.float32)
    inv = (ROPE_THETA ** (-np.arange(0, 16, 2, dtype=np.float32) / np.float32(16))).astype(np.float32)
    ang = (pos[:, None] * inv[None, :]).astype(np.float32)
    cos = np.cos(ang).astype(np.float32).T
    sin = np.sin(ang).astype(np.float32).T
    C = np.ones((128, n), np.float32)
    Sg = np.zeros((128, n), np.float32)
    for c in range(2):
        b = c * 64
        C[b:b + 8] = cos
        C[b + 8:b + 16] = cos
        Sg[b:b + 8] = sin
        Sg[b + 8:b + 16] = -sin
    P = np.zeros((128, 128), np.float32)
    for c in range(2):
        b = c * 64
        for i in range(half):
            P[b + i + 8, b + i] = 1.0
            P[b + i, b + i + 8] = 1.0
    return C, Sg, P


def run_A(x, attn_w_qkv, lq1, lk1, lq2, lk2, subln_g, nt=S // 512):
    SS = nt * 512
    xT = np.ascontiguousarray(x.reshape(S, D)[:SS].T)
    C, Sg, P = rope_tables(SS)
    lam4 = np.concatenate([lq1.reshape(-1), lk1.reshape(-1), lq2.reshape(-1), lk2.reshape(-1)]).reshape(1, 256)
    lam4 = np.ascontiguousarray(lam4.astype(np.float32))
    subg = np.ascontiguousarray(subln_g.reshape(128, 1).astype(np.float32))
    w = attn_w_qkv.reshape(D, 3 * D)
    in_maps = []
    for h in range(NCORES):
        wh = np.concatenate([w[:, h * 128:(h + 1) * 128], w[:, D + h * 128:D + (h + 1) * 128],
                             w[:, 2 * D + h * 128:2 * D + (h + 1) * 128]], axis=1)
        in_maps.append({"xT": xT, "wqkv": np.ascontiguousarray(wh), "ropeC": C, "ropeS": Sg, "perm": P,
                        "ident": np.eye(128, dtype=np.float32),
                        "lam4": lam4, "subg": subg})
    nc = build_A(nt)
    r = run_bass_kernel_spmd(nc, in_maps, core_ids=list(range(NCORES)))
    return np.concatenate([r.results[h]["attnT"] for h in range(NCORES)], axis=0)


TOK = S // NCORES
TG = 1024
TT = 512
WCAP = 6144


def _lnp(ln1_g, ln1_b, ln2_g, ln2_b, i):
    cols = []
    for v in (ln1_g[i], ln1_b[i], ln2_g[i], ln2_b[i]):
        cols.append(np.asarray(v, np.float32).reshape(8, 128).T)
    return np.ascontiguousarray(np.concatenate(cols, axis=1))


def build_T(tok=TOK):
    ngroups = tok // TG
    TGW = TG + 2
    nc = bass.Bass("TRN2", target_bir_lowering=False)
    hin = nc.dram_tensor("hin", [D, tok], F32, kind="ExternalInput").ap()
    xhalo = nc.dram_tensor("xhalo", [D, 2], F32, kind="ExternalInput").ap()
    ain = nc.dram_tensor("ain", [D, tok], F32, kind="ExternalInput").ap()
    ahalo = nc.dram_tensor("ahalo", [D, 2], F32, kind="ExternalInput").ap()
    uflag = nc.dram_tensor("uflag", [128, 1], F32, kind="ExternalInput").ap()
    w_o = nc.dram_tensor("w_o", [D, D], F32, kind="ExternalInput").ap()
    w_in = nc.dram_tensor("w_in", [D, 3 * D], F32, kind="ExternalInput").ap()
    cw = nc.dram_tensor("cw", [128, 24], F32, kind="ExternalInput").ap()
    w_out = nc.dram_tensor("w_out", [D, D], F32, kind="ExternalInput").ap()
    wgs = [nc.dram_tensor("wg%d" % i, [D, DFF], F32, kind="ExternalInput").ap() for i in range(2)]
    wus = [nc.dram_tensor("wu%d" % i, [D, DFF], F32, kind="ExternalInput").ap() for i in range(2)]
    wds = [nc.dram_tensor("wd%d" % i, [DFF, D], F32, kind="ExternalInput").ap() for i in range(2)]
    lnp = nc.dram_tensor("lnp", [128, 64], F32, kind="ExternalInput").ap()
    hout = nc.dram_tensor("hout", [D, tok], F32, kind="ExternalOutput").ap()

    kp = lambda a: a.rearrange("(k p) t -> p k t", p=128)
    hin_r, ain_r, hout_r = kp(hin), kp(ain), kp(hout)
    xhalo_r, ahalo_r = kp(xhalo), kp(ahalo)
    w_o_r, w_out_r = kp(w_o), kp(w_out)
    wg_r = [kp(a) for a in wgs]
    wu_r = [kp(a) for a in wus]
    wd_r = [kp(a) for a in wds]
    w_in_r = w_in.rearrange("(k p) (g n) -> p k g n", p=128, g=3)
    NF = DFF // 128

    with ExitStack() as st:
        sc = Sched(nc, st)
        hres = _sb(nc, st, "hres", [128, 8, TGW], F32)
        hb = _sb(nc, st, "hb", [128, 8, TGW], BF16)
        ab = _sb(nc, st, "ab", [128, 8, TGW], BF16)
        hid = _sb(nc, st, "hid", [128, NF, TGW], BF16)
        wsl = [_sb(nc, st, "wsl%d" % i, [128, WCAP], BF16) for i in range(3)]
        zb = _sb(nc, st, "zb", [128, 8, TT], BF16)
        zsq = _sb(nc, st, "zsq", [128, 8, TT], BF16)
        mean = _sb(nc, st, "mean", [128, TT], F32)
        msq = _sb(nc, st, "msq", [128, TT], F32)
        var = _sb(nc, st, "var", [128, TT], F32)
        rstd = _sb(nc, st, "rstd", [128, TT], F32)
        nmr = _sb(nc, st, "nmr", [128, TT], F32)
        tmp = [_sb(nc, st, "tmp%d" % i, [128, TT], F32) for i in range(2)]
        tmp2 = [_sb(nc, st, "tmq%d" % i, [128, TT], F32) for i in range(2)]
        sg = [_sb(nc, st, "sg%d" % i, [128, TT], F32) for i in range(2)]
        ones = _sb(nc, st, "ones", [128, 128], BF16)
        lnt = _sb(nc, st, "lnt", [128, 64], F32)
        cwt = _sb(nc, st, "cwt", [128, 24], F32)
        ufl = _sb(nc, st, "ufl", [128, 1], F32)
        uprev = _sb(nc, st, "uprev", [128, 8, 2], F32)
        ubuf = [_sb(nc, st, "ubuf%d" % i, [128, TT + 2], F32) for i in range(2)]
        cs = [_sb(nc, st, "cs%d" % i, [128, TT], F32) for i in range(2)]
        ybuf = [_sb(nc, st, "ybuf%d" % i, [128, TT], F32) for i in range(2)]
        hcs = _sb(nc, st, "hcs", [128, 2], F32)
        B = [_ps(nc, st, "B%d" % i) for i in range(8)]

        state = {"bank": 0, "job": 0, "tmp": 0}

        def bank():
            b = state["bank"]
            state["bank"] = (b + 1) % 6
            return b

        def flip():
            i = state["tmp"]
            state["tmp"] = 1 - i
            return i

        def mm(outp, lhsT, rhs, start, stop, reads, writes):
            sc.add("pe", lambda e: e.matmul(outp, lhsT, rhs, start=start, stop=stop),
                   reads=reads, writes=writes)

        sc.add("pool", lambda e: e.memset(ones[:], 1.0), writes=["ones"])
        sc.add("sp", lambda e: e.dma_start(out=lnt[:], in_=lnp), writes=["lnt"], dma=True)
        sc.add("sp", lambda e: e.dma_start(out=cwt[:], in_=cw), writes=["cwt"], dma=True)
        sc.add("sp", lambda e: e.dma_start(out=ufl[:], in_=uflag), writes=["ufl"], dma=True)

        def tiles(g, layer):
            ts = [(0, 0, TT), (1, TT, TT)]
            if g == 0 and layer == 0:
                ts.append((2, TG, 2))
            return ts

        jobs = []
        side = []

        def hook():
            for _ in range(state.get("rate", 0)):
                if side:
                    side.pop(0)()

        def flush():
            while side:
                side.pop(0)()
            state["rate"] = 0

        def enqueue(pieces, nsteps):
            side.extend(pieces)
            state["rate"] = max(1, -(-2 * len(side) // max(1, nsteps)))

        def split(tl):
            first = [x for x in tl if x[0] != 1]
            second = [x for x in tl if x[0] == 1]
            return first, second

        def ensure_loaded(idx):
            if idx < len(jobs) and not jobs[idx]["loaded"]:
                jobs[idx]["loaded"] = True
                jobs[idx]["load"]((state["job"] + idx) % 3)

        def run_segment(lo, hi, ln_q, ln_tl, pre_side, after_first=None):
            npairs = (hi - lo) // 2
            lf, ls = split(ln_tl)
            for p in range(npairs):
                ia, ib = lo + 2 * p, lo + 2 * p + 1
                ja, jb = jobs[ia], jobs[ib]
                sa, sb = (state["job"] + ia) % 3, (state["job"] + ib) % 3
                ensure_loaded(ia)
                ensure_loaded(ib)
                ensure_loaded(ib + 1)
                fa, sea = split(ja["tl"])
                fb_, seb = split(jb["tl"])
                if p == 0 and pre_side:
                    enqueue(pre_side, ja["spt"] * len(fa) + jb["spt"] * len(fb_))
                ja["comp"](sa, fa)
                jb["comp"](sb, fb_)
                if p == 0:
                    flush()
                    if after_first is not None:
                        after_first()
                if p == npairs - 1:
                    pcs = []
                    for tile in lf:
                        pcs += ln_pieces(ln_q, tile)
                    enqueue(pcs, ja["spt"] * len(sea) + jb["spt"] * len(seb))
                ja["comp"](sa, sea)
                ensure_loaded(ib + 2)
                jb["comp"](sb, seb)
                if p == npairs - 1:
                    flush()
            pcs = []
            for tile in ls:
                pcs += ln_pieces(ln_q, tile)
            return pcs

        def wload(slot, view, src):
            sc.add("pool", lambda e: e.dma_start(out=view, in_=src), writes=[("wsl", slot)], dma=True)

        def add_proj_jobs(w_r, tl):
            for nb in range(2):
                def load(slot, nb=nb):
                    v = wsl[slot][:, 0:4096].rearrange("p (k n) -> p k n", k=8)
                    wload(slot, v, w_r[:, :, nb * 512:(nb + 1) * 512])

                def comp(slot, tls, nb=nb):
                    v = wsl[slot][:, 0:4096].rearrange("p (k n) -> p k n", k=8)
                    for (t, c0, w) in tls:
                        for j in range(4):
                            n = nb * 4 + j
                            b = bank()
                            for k in range(8):
                                mm(B[b][:, 0:w], v[:, k, j * 128:(j + 1) * 128], ab[:, k, c0:c0 + w], k == 0, k == 7,
                                   [("wsl", slot), ("ab", k, t)], [("B", b)])
                            sc.add("dve", lambda e, n=n, c0=c0, w=w, b=b: e.scalar_tensor_tensor(
                                out=hres[:, n, c0:c0 + w], in0=hres[:, n, c0:c0 + w], scalar=ALPHA, in1=B[b][:, 0:w],
                                op0=ALU.mult, op1=ALU.add),
                                reads=[("B", b), ("hres", n, t)], writes=[("hres", n, t)])
                            hook()
                jobs.append({"load": load, "comp": comp, "tl": tl, "spt": 4, "loaded": False})

        def ln_pieces(q, tile):
            (t, c0, w) = tile
            pcs = []
            shared = {}

            def prep(k):
                sc.add("act", lambda e: e.activation(out=zsq[:, k, 0:w], in_=hres[:, k, c0:c0 + w], func=AF.Square),
                       reads=[("hres", k, t)], writes=[("zsq", k)])
                sc.add("dve", lambda e: e.tensor_copy(out=zb[:, k, 0:w], in_=hres[:, k, c0:c0 + w]),
                       reads=[("hres", k, t)], writes=[("zb", k)])

            def stats():
                ba = bank()
                for k in range(8):
                    mm(B[ba][:, 0:w], ones[:, :], zb[:, k, 0:w], k == 0, k == 7, ["ones", ("zb", k)], [("B", ba)])
                bb = bank()
                for k in range(8):
                    mm(B[bb][:, 0:w], ones[:, :], zsq[:, k, 0:w], k == 0, k == 7, ["ones", ("zsq", k)], [("B", bb)])
                sc.add("dve", lambda e: e.tensor_scalar(out=mean[:, 0:w], in0=B[ba][:, 0:w], scalar1=1.0 / D,
                                                        scalar2=None, op0=ALU.mult),
                       reads=[("B", ba)], writes=["mean"])
                sc.add("dve", lambda e: e.tensor_tensor(out=msq[:, 0:w], in0=mean[:, 0:w], in1=mean[:, 0:w],
                                                        op=ALU.mult), reads=["mean"], writes=["msq"])
                sc.add("dve", lambda e: e.scalar_tensor_tensor(
                    out=var[:, 0:w], in0=B[bb][:, 0:w], scalar=1.0 / D, in1=msq[:, 0:w], op0=ALU.mult,
                    op1=ALU.subtract), reads=[("B", bb), "msq"], writes=["var"])
                sc.add("dve", lambda e: e.tensor_scalar(out=var[:, 0:w], in0=var[:, 0:w], scalar1=LN_EPS,
                                                        scalar2=None, op0=ALU.add), reads=["var"], writes=["var"])
                sc.add("act", lambda e: e.activation(out=var[:, 0:w], in_=var[:, 0:w], func=AF.Ln),
                       reads=["var"], writes=["var"])
                br = 7
                shared["br"] = br
                sc.add("act", lambda e: e.activation(out=B[br][:, 0:w], in_=var[:, 0:w], func=AF.Exp, scale=-0.5),
                       reads=["var"], writes=[("B", br)])
                sc.add("dve", lambda e: e.scalar_tensor_tensor(out=B[6][:, 0:w], in0=mean[:, 0:w], scalar=-1.0,
                                                               in1=B[br][:, 0:w], op0=ALU.mult, op1=ALU.mult),
                       reads=["mean", ("B", br)], writes=[("B", 6)])

            def norm(k):
                br = shared["br"]
                i = flip()
                sc.add("dve", lambda e: e.tensor_tensor(out=tmp[i][:, 0:w], in0=hres[:, k, c0:c0 + w], in1=B[br][:, 0:w],
                                                        op=ALU.mult),
                       reads=[("hres", k, t), ("B", br)], writes=[("tmp", i)])
                sc.add("dve", lambda e: e.tensor_tensor(out=tmp2[i][:, 0:w], in0=tmp[i][:, 0:w], in1=B[6][:, 0:w],
                                                        op=ALU.add),
                       reads=[("tmp", i), ("B", 6)], writes=[("tmp2", i)])
                gcol = q * 16 + k
                bcol = q * 16 + 8 + k
                sc.add("act", lambda e: e.activation(out=hres[:, k, c0:c0 + w], in_=tmp2[i][:, 0:w], func=AF.Identity,
                                                     scale=lnt[:, gcol:gcol + 1], bias=lnt[:, bcol:bcol + 1]),
                       reads=[("tmp2", i), "lnt"], writes=[("hres", k, t)])
                if True:
                    sc.add("act", lambda e: e.activation(out=hb[:, k, c0:c0 + w], in_=tmp2[i][:, 0:w], func=AF.Identity,
                                                         scale=lnt[:, gcol:gcol + 1], bias=lnt[:, bcol:bcol + 1]),
                           reads=[("tmp2", i), "lnt"], writes=[("hb", k, t)])

            for k in range(8):
                pcs.append(lambda k=k: prep(k))
            pcs.append(stats)
            for k in range(8):
                pcs.append(lambda k=k: norm(k))
            return pcs

        def add_ffn_jobs(layer, tl):
            f0 = 0
            while f0 < NF:
                fb = min(3, NF - f0)

                def load(slot, f0=f0, fb=fb):
                    v = wsl[slot][:, 0:2 * 8 * 384].rearrange("p (g k n) -> p g k n", g=2, k=8)
                    wload(slot, v[:, 0, :, 0:fb * 128], wg_r[layer][:, :, f0 * 128:(f0 + fb) * 128])
                    wload(slot, v[:, 1, :, 0:fb * 128], wu_r[layer][:, :, f0 * 128:(f0 + fb) * 128])

                def comp(slot, tls, f0=f0, fb=fb):
                    v = wsl[slot][:, 0:2 * 8 * 384].rearrange("p (g k n) -> p g k n", g=2, k=8)
                    for (t, c0, w) in tls:
                        for j in range(fb):
                            fc = f0 + j
                            bg = bank()
                            for k in range(8):
                                mm(B[bg][:, 0:w], v[:, 0, k, j * 128:(j + 1) * 128], hb[:, k, c0:c0 + w], k == 0, k == 7,
                                   [("wsl", slot), ("hb", k, t)], [("B", bg)])
                            bu = bank()
                            for k in range(8):
                                mm(B[bu][:, 0:w], v[:, 1, k, j * 128:(j + 1) * 128], hb[:, k, c0:c0 + w], k == 0, k == 7,
                                   [("wsl", slot), ("hb", k, t)], [("B", bu)])
                            i = flip()
                            sc.add("act", lambda e, i=i, bg=bg, w=w: e.activation(out=sg[i][:, 0:w], in_=B[bg][:, 0:w],
                                                                                  func=AF.Silu),
                                   reads=[("B", bg)], writes=[("sg", i)])
                            sc.add("dve", lambda e, i=i, bu=bu, fc=fc, c0=c0, w=w: e.tensor_tensor(
                                out=hid[:, fc, c0:c0 + w], in0=B[bu][:, 0:w], in1=sg[i][:, 0:w], op=ALU.mult),
                                reads=[("B", bu), ("sg", i)], writes=[("hid", fc, t)])
                            hook()
                jobs.append({"load": load, "comp": comp, "tl": tl, "spt": fb, "loaded": False})
                f0 += fb
            for nb in range(4):
                def load(slot, nb=nb):
                    v = wsl[slot][:, 0:NF * 256].rearrange("p (k n) -> p k n", k=NF)
                    wload(slot, v, wd_r[layer][:, :, nb * 256:(nb + 1) * 256])

                def comp(slot, tls, nb=nb):
                    v = wsl[slot][:, 0:NF * 256].rearrange("p (k n) -> p k n", k=NF)
                    for (t, c0, w) in tls:
                        for j in range(2):
                            n = nb * 2 + j
                            b = bank()
                            for fc in range(NF):
                                mm(B[b][:, 0:w], v[:, fc, j * 128:(j + 1) * 128], hid[:, fc, c0:c0 + w], fc == 0,
                                   fc == NF - 1, [("wsl", slot), ("hid", fc, t)], [("B", b)])
                            sc.add("dve", lambda e, n=n, c0=c0, w=w, b=b: e.scalar_tensor_tensor(
                                out=hres[:, n, c0:c0 + w], in0=hres[:, n, c0:c0 + w], scalar=ALPHA, in1=B[b][:, 0:w],
                                op0=ALU.mult, op1=ALU.add),
                                reads=[("B", b), ("hres", n, t)], writes=[("hres", n, t)])
                            hook()
                jobs.append({"load": load, "comp": comp, "tl": tl, "spt": 2, "loaded": False})

        def add_conv_jobs(g):
            for jb in range(4):
                def load(slot, jb=jb):
                    v = wsl[slot][:, 0:WCAP].rearrange("p (k g n) -> p k g n", k=8, g=3)
                    for gg in range(3):
                        wload(slot, v[:, :, gg, :], w_in_r[:, :, gg, jb * 256:(jb + 1) * 256])

                def comp(slot, tls, jb=jb):
                    v = wsl[slot][:, 0:WCAP].rearrange("p (k g n) -> p k g n", k=8, g=3)
                    for jj in range(2):
                        ch = jb * 2 + jj
                        cols = slice(jj * 128, (jj + 1) * 128)
                        if g == 0 and any(x[0] == 0 for x in tls):
                            bc = bank()
                            for k in range(8):
                                mm(B[bc][:, 0:2], v[:, k, 1, cols], hb[:, k, TG:TG + 2], k == 0, k == 7,
                                   [("wsl", slot), ("hb", k, 2)], [("B", bc)])
                            bx = bank()
                            for k in range(8):
                                mm(B[bx][:, 0:2], v[:, k, 2, cols], hb[:, k, TG:TG + 2], k == 0, k == 7,
                                   [("wsl", slot), ("hb", k, 2)], [("B", bx)])
                            sc.add("act", lambda e, bc=bc: e.copy(out=hcs[:], in_=B[bc][:, 0:2]),
                                   reads=[("B", bc)], writes=["hcs"])
                            sc.add("dve", lambda e, bx=bx, ch=ch: e.scalar_tensor_tensor(
                                out=uprev[:, ch, :], in0=B[bx][:, 0:2], scalar=ufl[:, 0:1], in1=hcs[:],
                                op0=ALU.mult, op1=ALU.mult),
                                reads=[("B", bx), "hcs", "ufl"], writes=[("uprev", ch)])
                        for (t, c0, w) in tls:
                            bbk = bank()
                            for k in range(8):
                                mm(B[bbk][:, :], v[:, k, 0, cols], hb[:, k, c0:c0 + w], k == 0, k == 7,
                                   [("wsl", slot), ("hb", k, t)], [("B", bbk)])
                            bc = bank()
                            for k in range(8):
                                mm(B[bc][:, :], v[:, k, 1, cols], hb[:, k, c0:c0 + w], k == 0, k == 7,
                                   [("wsl", slot), ("hb", k, t)], [("B", bc)])
                            bx = bank()
                            for k in range(8):
                                mm(B[bx][:, :], v[:, k, 2, cols], hb[:, k, c0:c0 + w], k == 0, k == 7,
                                   [("wsl", slot), ("hb", k, t)], [("B", bx)])
                            i = flip()
                            u = ubuf[i]
                            y = ybuf[i]
                            c3 = ch * 3
                            sc.add("act", lambda e, i=i, bc=bc: e.copy(out=cs[i][:], in_=B[bc][:, :]),
                                   reads=[("B", bc)], writes=[("cs", i)])
                            sc.add("act", lambda e, u=u, ch=ch: e.copy(out=u[:, 0:2], in_=uprev[:, ch, :]),
                                   reads=[("uprev", ch)], writes=[("u", i)])
                            sc.add("dve", lambda e, u=u, i=i, bx=bx: e.tensor_tensor(out=u[:, 2:TT + 2], in0=B[bx][:, :],
                                                                                     in1=cs[i][:], op=ALU.mult),
                                   reads=[("B", bx), ("cs", i)], writes=[("u", i)])
                            sc.add("act", lambda e, u=u, ch=ch: e.copy(out=uprev[:, ch, :], in_=u[:, TT:TT + 2]),
                                   reads=[("u", i)], writes=[("uprev", ch)])
                            sc.add("dve", lambda e, u=u, y=y, c3=c3: e.tensor_scalar(
                                out=y[:], in0=u[:, 2:TT + 2], scalar1=cwt[:, c3 + 2:c3 + 3], scalar2=None, op0=ALU.mult),
                                reads=[("u", i), "cwt"], writes=[("y", i)])
                            sc.add("dve", lambda e, u=u, y=y, c3=c3: e.scalar_tensor_tensor(
                                out=y[:], in0=u[:, 1:TT + 1], scalar=cwt[:, c3 + 1:c3 + 2], in1=y[:],
                                op0=ALU.mult, op1=ALU.add), reads=[("u", i), "cwt", ("y", i)], writes=[("y", i)])
                            sc.add("dve", lambda e, u=u, y=y, c3=c3: e.scalar_tensor_tensor(
                                out=y[:], in0=u[:, 0:TT], scalar=cwt[:, c3:c3 + 1], in1=y[:],
                                op0=ALU.mult, op1=ALU.add), reads=[("u", i), "cwt", ("y", i)], writes=[("y", i)])
                            sc.add("dve", lambda e, y=y, bbk=bbk, ch=ch, c0=c0, w=w: e.tensor_tensor(
                                out=ab[:, ch, c0:c0 + w], in0=B[bbk][:, :], in1=y[:], op=ALU.mult),
                                reads=[("B", bbk), ("y", i)], writes=[("ab", ch, t)])
                            hook()
                jobs.append({"load": load, "comp": comp, "tl": tiles(g, 1), "spt": 2, "loaded": False})

        def load_tile(g, tile):
            (t, c0, w) = tile
            g0 = g * TG
            if t < 2:
                sc.add("sp", lambda e: e.dma_start(out=hres[:, :, c0:c0 + w], in_=hin_r[:, :, g0 + c0:g0 + c0 + w]),
                       writes=[("hres", k, t) for k in range(8)], dma=True)
                sc.add("pool", lambda e: e.dma_start(out=ab[:, :, c0:c0 + w], in_=ain_r[:, :, g0 + c0:g0 + c0 + w]),
                       writes=[("ab", k, t) for k in range(8)], dma=True)
            else:
                sc.add("sp", lambda e: e.dma_start(out=hres[:, :, c0:c0 + w], in_=xhalo_r),
                       writes=[("hres", k, t) for k in range(8)], dma=True)
                sc.add("pool", lambda e: e.dma_start(out=ab[:, :, c0:c0 + w], in_=ahalo_r),
                       writes=[("ab", k, t) for k in range(8)], dma=True)

        def store_tile(g, tile):
            (t, c0, w) = tile
            g0 = g * TG
            sc.add("sp", lambda e: e.dma_start(out=hout_r[:, :, g0 + c0:g0 + c0 + w], in_=hres[:, :, c0:c0 + w]),
                   reads=[("hres", k, t) for k in range(8)], dma=True)

        pre = []
        for g in range(ngroups):
            tl0 = tiles(g, 0)
            tl1 = tiles(g, 1)
            f0_, s0_ = split(tl0)
            if g == 0:
                f0_, s0_ = tl0, []
            for tile in f0_:
                load_tile(g, tile)

            def after_first(g=g, s0_=s0_):
                if g > 0:
                    for tile in split(tiles(g - 1, 1))[1]:
                        store_tile(g - 1, tile)
                for tile in s0_:
                    load_tile(g, tile)

            base = len(jobs)
            add_proj_jobs(w_o_r, tl0)
            n1 = len(jobs)
            add_ffn_jobs(0, tl0)
            n2 = len(jobs)
            add_conv_jobs(g)
            add_proj_jobs(w_out_r, tl1)
            n3 = len(jobs)
            add_ffn_jobs(1, tl1)
            n4 = len(jobs)
            pre = run_segment(base, n1, 0, tl0, pre, after_first)
            pre = run_segment(n1, n2, 1, tl0, pre)
            pre = run_segment(n2, n3, 2, tl1, pre)
            pre = run_segment(n3, n4, 3, tl1, pre)
            for tile in split(tl1)[0]:
                store_tile(g, tile)
        enqueue(pre, 1)
        flush()
        for tile in split(tiles(ngroups - 1, 1))[1]:
            store_tile(ngroups - 1, tile)
        sc.emit()
    return nc


def run_T(xT, attnT, inp):
    f = lambda a: np.ascontiguousarray(np.asarray(a, np.float32))
    lnp = np.ascontiguousarray(np.concatenate(
        [_lnp(inp["ln1_g"], inp["ln1_b"], inp["ln2_g"], inp["ln2_b"], i) for i in range(2)], axis=1))
    cw = np.ascontiguousarray(f(inp["conv_w"])[0].reshape(3, 8, 128).transpose(2, 1, 0).reshape(128, 24))
    shared = {"w_o": f(inp["attn_w_o"])[0], "w_in": f(inp["conv_w_in"])[0], "cw": cw, "w_out": f(inp["conv_w_out"])[0],
              "wg0": f(inp["ffn_w_gate"])[0], "wu0": f(inp["ffn_w_up"])[0], "wd0": f(inp["ffn_w_down"])[0],
              "wg1": f(inp["ffn_w_gate"])[1], "wu1": f(inp["ffn_w_up"])[1], "wd1": f(inp["ffn_w_down"])[1], "lnp": lnp}
    in_maps = []
    for c in range(NCORES):
        sl = slice(c * TOK, (c + 1) * TOK)
        if c == 0:
            xh = np.zeros((D, 2), np.float32)
            ah = np.zeros((D, 2), np.float32)
        else:
            xh = np.ascontiguousarray(xT[:, c * TOK - 2:c * TOK])
            ah = np.ascontiguousarray(attnT[:, c * TOK - 2:c * TOK])
        m = {"hin": np.ascontiguousarray(xT[:, sl]), "ain": np.ascontiguousarray(attnT[:, sl]), "xhalo": xh,
             "ahalo": ah, "uflag": np.full((128, 1), 0.0 if c == 0 else 1.0, np.float32)}
        m.update(shared)
        in_maps.append(m)
    nc = build_T()
    r = run_bass_kernel_spmd(nc, in_maps, core_ids=list(range(NCORES)))
    return np.concatenate([r.results[c]["hout"] for c in range(NCORES)], axis=1)


def kernel(**inp):
    f = lambda a: np.ascontiguousarray(np.asarray(a, np.float32))
    x = f(inp["x"])
    attnT = run_A(x, f(inp["attn_w_qkv"]), f(inp["attn_lambda_q1"]), f(inp["attn_lambda_k1"]),
                  f(inp["attn_lambda_q2"]), f(inp["attn_lambda_k2"]), f(inp["attn_subln_g"]))
    xT = np.ascontiguousarray(x.reshape(S, D).T)
    h2T = run_T(xT, attnT, inp)
    return np.ascontiguousarray(h2T.T).reshape(1, S, D).astype(np.float32)
```
